# Optimizing a Trainium2 kernel written in Bass

```python
import math
import jax
import jax.numpy as jnp
from jax import lax
import numpy as np

D_MODEL = 1024
BATCH = 16
SEQ = 2048
DEPTH = 1

CTX_LEN = 256
GRID_W = 64
D_MIX = D_MODEL
D_S5 = D_MIX // 2
S5_GROUP = 16
S5_GROUPS = D_S5 // S5_GROUP
S5_STATE = 64
D_GDN = D_MIX - D_S5
GDN_HEAD = 128
GDN_HEADS = D_GDN // GDN_HEAD
CHUNK = 64
CONV_K = 3
N_DIR = 2
P_IN = 2 * D_S5 + 4 * D_GDN + 2 * N_DIR * GDN_HEADS
DEEPNORM_ALPHA = (2.0 * DEPTH) ** 0.25
DEEPNORM_BETA = (8.0 * DEPTH) ** -0.25
LN_EPS = 1e-5
NORM_EPS = 1e-6

kernel_name = "hybrid_s5_gdn_prefix_dit_block"


def _proj_splits():
    widths = (D_S5, D_S5, 3 * D_GDN, D_GDN, N_DIR * GDN_HEADS, N_DIR * GDN_HEADS)
    return tuple(int(v) for v in np.cumsum(widths)[:-1])


def _layer_norm(x, g, b):
    xf = x.astype(jnp.float32)
    mu = jnp.mean(xf, -1, keepdims=True)
    var = jnp.mean(jnp.square(xf - mu), -1, keepdims=True)
    return ((xf - mu) * lax.rsqrt(var + LN_EPS) * g.astype(jnp.float32) + b.astype(jnp.float32)).astype(x.dtype)


def _ada(cond, w_ada, b_ada):
    m = jax.nn.silu(cond) @ w_ada + b_ada
    return jnp.split(m, 3, axis=-1)


def _maybe_flip(t, rev):
    return jnp.flip(t, 1) if rev else t


def _s5_zoh(lam_re, lam_im, log_dt, b_re, b_im):
    lam = lax.complex(lam_re.astype(jnp.float32), lam_im.astype(jnp.float32))
    dt = jnp.exp(log_dt.astype(jnp.float32))[:, None]
    abar = jnp.exp(lam * dt)
    bmat = lax.complex(b_re.astype(jnp.float32), b_im.astype(jnp.float32))
    bbar = ((abar - 1.0) / lam)[..., None] * bmat
    return abar, bbar


def _s5_combine(e1, e2):
    a1, b1 = e1
    a2, b2 = e2
    return a1 * a2, a2 * b1 + b2


def _s5_states(u, abar, bbar, h0, rev):
    bu = jnp.einsum('blgc,gpc->blgp', u, bbar)
    bu = _maybe_flip(bu, rev)
    bu = bu.at[:, 0].add(abar * h0)
    a = jnp.broadcast_to(abar, (1,) + bu.shape[1:])
    _, h = lax.associative_scan(_s5_combine, (a, bu), axis=1)
    return _maybe_flip(h, rev), h[:, -1]


def _s5_readout(h, c_re, c_im):
    cmat = lax.complex(c_re.astype(jnp.float32), c_im.astype(jnp.float32))
    y = jnp.real(jnp.einsum('blgp,gcp->blgc', h, cmat))
    return y.reshape(y.shape[0], y.shape[1], D_S5)


def _s5_glu(y, w_glu, b_glu):
    g = jax.nn.gelu(y, approximate=False)
    return g * jax.nn.sigmoid(g @ w_glu.astype(jnp.float32) + b_glu.astype(jnp.float32))


def _s5_mixer(u, uc, z, zc, p, with_ctx):
    B_ = u.shape[0]
    uf = u.astype(jnp.float32)
    ucf = uc.astype(jnp.float32)
    ul = uf.reshape(B_, -1, S5_GROUPS, S5_GROUP)
    ucg = ucf.reshape(B_, -1, S5_GROUPS, S5_GROUP)
    d_skip = p['s5_d'].astype(jnp.float32)
    y_lat = d_skip * uf
    y_ctx = d_skip * ucf if with_ctx else None
    for d in range(N_DIR):
        abar, bbar = _s5_zoh(p['s5_lambda_re'][d], p['s5_lambda_im'][d], p['s5_log_dt'][d],
                             p['s5_b_re'][d], p['s5_b_im'][d])
        h0 = jnp.zeros((B_, S5_GROUPS, S5_STATE), jnp.complex64)
        hs_ctx, h_ctx_end = _s5_states(ucg, abar, bbar, h0, d == 1)
        hs_lat, _ = _s5_states(ul, abar, bbar, h_ctx_end, d == 1)
        y_lat = y_lat + _s5_readout(hs_lat, p['s5_c_re'][d], p['s5_c_im'][d])
        if with_ctx:
            y_ctx = y_ctx + _s5_readout(hs_ctx, p['s5_c_re'][d], p['s5_c_im'][d])
    out_lat = _s5_glu(y_lat, p['w_glu'], p['b_glu']) * jax.nn.silu(z.astype(jnp.float32))
    out_ctx = None
    if with_ctx:
        out_ctx = _s5_glu(y_ctx, p['w_glu'], p['b_glu']) * jax.nn.silu(zc.astype(jnp.float32))
    return out_lat, out_ctx


def _conv_latent(t, w):
    B_, L, C = t.shape
    rows = L // GRID_W
    img = t.reshape(B_, rows, GRID_W, C)
    out = lax.conv_general_dilated(img, w[:, :, None, :].astype(t.dtype), (1, 1), 'SAME',
                                   dimension_numbers=('NHWC', 'HWIO', 'NHWC'), feature_group_count=C)
    return out.reshape(B_, L, C)


def _conv_context(t, w):
    C = t.shape[-1]
    return lax.conv_general_dilated(t, w[CONV_K // 2][:, None, :].astype(t.dtype), (1,), 'SAME',
                                    dimension_numbers=('NWC', 'WIO', 'NWC'), feature_group_count=C)


def _l2norm(t):
    return t * lax.rsqrt(jnp.sum(jnp.square(t), -1, keepdims=True) + NORM_EPS)


def _gdn_qkv_heads(qkv):
    B_, L, _ = qkv.shape
    q, k, v = jnp.split(qkv.astype(jnp.float32), 3, axis=-1)
    q = _l2norm(q.reshape(B_, L, GDN_HEADS, GDN_HEAD)) * (GDN_HEAD ** -0.5)
    k = _l2norm(k.reshape(B_, L, GDN_HEADS, GDN_HEAD))
    v = v.reshape(B_, L, GDN_HEADS, GDN_HEAD)
    return q, k, v


def _gdn_gates(beta_logit, alpha_logit, a_log, dt_bias):
    B_, L, _ = beta_logit.shape
    beta = jax.nn.sigmoid(beta_logit.astype(jnp.float32)).reshape(B_, L, N_DIR, GDN_HEADS)
    a = alpha_logit.astype(jnp.float32).reshape(B_, L, N_DIR, GDN_HEADS)
    g = -jnp.exp(a_log.astype(jnp.float32)) * jax.nn.softplus(a + dt_bias.astype(jnp.float32))
    return beta, g


def _gdn_chunked(q, k, v, beta, g, s0):
    B_, L, H, _ = q.shape
    Dv = v.shape[-1]
    N = L // CHUNK

    def chunk(t):
        return t.reshape(B_, N, CHUNK, H, -1).transpose(1, 0, 3, 2, 4)

    qc, kc, vc = chunk(q), chunk(k), chunk(v)
    bc = chunk(beta[..., None])[..., 0]
    gcum = jnp.cumsum(chunk(g[..., None])[..., 0], axis=-1)
    idx = jnp.arange(CHUNK)
    lower = idx[:, None] >= idx[None, :]
    strict = idx[:, None] > idx[None, :]
    decay = jnp.exp(jnp.where(lower, gcum[..., :, None] - gcum[..., None, :], -jnp.inf))
    kk = jnp.einsum('nbhik,nbhjk->nbhij', kc, kc)
    a_mat = jnp.where(strict, bc[..., :, None] * kk * decay, 0.0)
    gamma = jnp.exp(gcum)
    rhs = jnp.concatenate([bc[..., None] * vc, (bc * gamma)[..., None] * kc], axis=-1)
    eye = jnp.eye(CHUNK, dtype=jnp.float32)
    sol = lax.linalg.triangular_solve(a_mat + eye, rhs, left_side=True, lower=True, unit_diagonal=True)
    u0, w = sol[..., :Dv], sol[..., Dv:]
    qk = jnp.einsum('nbhik,nbhjk->nbhij', qc, kc) * decay
    k_out = kc * jnp.exp(gcum[..., -1:] - gcum)[..., None]
    gamma_last = gamma[..., -1]

    def step(s, inp):
        qi, ki, ui0, wi, qki, gi, gli = inp
        u = ui0 - jnp.einsum('bhck,bhkv->bhcv', wi, s)
        o = gi[..., None] * jnp.einsum('bhck,bhkv->bhcv', qi, s) + jnp.einsum('bhij,bhjv->bhiv', qki, u)
        s = gli[..., None, None] * s + jnp.einsum('bhck,bhcv->bhkv', ki, u)
        return s, o

    s_fin, o = lax.scan(step, s0, (qc, k_out, u0, w, qk, gamma, gamma_last))
    o = o.transpose(1, 0, 3, 2, 4).reshape(B_, L, H, Dv)
    return o, s_fin


def _gated_rmsnorm(o, z, w):
    o = o * lax.rsqrt(jnp.mean(jnp.square(o), -1, keepdims=True) + NORM_EPS) * w.astype(jnp.float32)
    return o.reshape(z.shape) * jax.nn.silu(z.astype(jnp.float32))


def _gdn_mixer(qkv, qkvc, z, zc, beta_l, beta_c, alpha_l, alpha_c, p, with_ctx):
    qkv = jax.nn.silu(_conv_latent(qkv, p['conv_w']))
    qkvc = jax.nn.silu(_conv_context(qkvc, p['conv_w']))
    ql, kl, vl = _gdn_qkv_heads(qkv)
    qc, kc, vc = _gdn_qkv_heads(qkvc)
    bl, gl = _gdn_gates(beta_l, alpha_l, p['gdn_a_log'], p['gdn_dt_bias'])
    bcx, gcx = _gdn_gates(beta_c, alpha_c, p['gdn_a_log'], p['gdn_dt_bias'])
    B_ = ql.shape[0]
    o_lat = jnp.zeros(vl.shape, jnp.float32)
    o_ctx = jnp.zeros(vc.shape, jnp.float32)
    for d in range(N_DIR):
        rev = d == 1
        f = lambda t: _maybe_flip(t, rev)
        s0 = jnp.zeros((B_, GDN_HEADS, GDN_HEAD, GDN_HEAD), jnp.float32)
        oc, s_ctx = _gdn_chunked(f(qc), f(kc), f(vc), f(bcx[:, :, d]), f(gcx[:, :, d]), s0)
        ol, _ = _gdn_chunked(f(ql), f(kl), f(vl), f(bl[:, :, d]), f(gl[:, :, d]), s_ctx)
        o_lat = o_lat + f(ol)
        o_ctx = o_ctx + f(oc)
    out_lat = _gated_rmsnorm(o_lat, z, p['gdn_norm_w'])
    out_ctx = _gated_rmsnorm(o_ctx, zc, p['gdn_norm_w']) if with_ctx else None
    return out_lat, out_ctx


def _layer(x, ctx, c, c_ctx, p, with_ctx):
    shift, scale, gate = _ada(c, p['w_ada'], p['b_ada'])
    shift_c, scale_c, gate_c = _ada(c_ctx, p['w_ada'], p['b_ada'])
    h = x * (1.0 + scale[:, None]) + shift[:, None]
    hc = ctx * (1.0 + scale_c) + shift_c
    u, z_s5, qkv, z_gdn, beta_l, alpha_l = jnp.split(h @ p['w_in'], _proj_splits(), axis=-1)
    uc, z_s5c, qkvc, z_gdnc, beta_c, alpha_c = jnp.split(hc @ p['w_in'], _proj_splits(), axis=-1)
    s5_lat, s5_ctx = _s5_mixer(u, uc, z_s5, z_s5c, p, with_ctx)
    gdn_lat, gdn_ctx = _gdn_mixer(qkv, qkvc, z_gdn, z_gdnc, beta_l, beta_c, alpha_l, alpha_c, p, with_ctx)
    w_out = p['w_out'].astype(jnp.float32)
    y = jnp.concatenate([s5_lat, gdn_lat], axis=-1) @ w_out
    x_new = _layer_norm(DEEPNORM_ALPHA * x + gate[:, None] * y.astype(x.dtype), p['ln_g'], p['ln_b'])
    if not with_ctx:
        return x_new, ctx
    yc = jnp.concatenate([s5_ctx, gdn_ctx], axis=-1) @ w_out
    ctx_new = _layer_norm(DEEPNORM_ALPHA * ctx + gate_c * yc.astype(ctx.dtype), p['ln_g'], p['ln_b'])
    return x_new, ctx_new


def setup_inputs(seed: int = 0) -> dict:
    key = jax.random.key(seed)
    ks = jax.random.split(key, 24)
    f32 = jnp.float32

    def nrm(k, shape, s):
        return jax.random.normal(k, shape, f32) * s

    lam_shape = (DEPTH, N_DIR, S5_GROUPS, S5_STATE)
    n = jnp.arange(S5_STATE, dtype=f32)
    dt = jnp.exp(jax.random.uniform(ks[19], (DEPTH, N_DIR, GDN_HEADS), f32, math.log(1e-3), math.log(1e-1)))
    return {
        "x": nrm(ks[0], (BATCH, SEQ, D_MODEL), 1.0),
        "c": nrm(ks[1], (BATCH, D_MODEL), 1.0),
        "ctx": nrm(ks[2], (BATCH, CTX_LEN, D_MODEL), 1.0),
        "c_ctx": nrm(ks[3], (D_MODEL,), 1.0),
        "w_ada": nrm(ks[4], (DEPTH, D_MODEL, 3 * D_MODEL), 0.5 * D_MODEL ** -0.5),
        "b_ada": nrm(ks[5], (DEPTH, 3 * D_MODEL), 0.01),
        "w_in": nrm(ks[6], (DEPTH, D_MODEL, P_IN), D_MODEL ** -0.5),
        "s5_lambda_re": -0.5 + nrm(ks[7], lam_shape, 0.01),
        "s5_lambda_im": math.pi * n + nrm(ks[8], lam_shape, 0.01),
        "s5_log_dt": jax.random.uniform(ks[9], (DEPTH, N_DIR, S5_GROUPS), f32, math.log(1e-3), math.log(1e-1)),
        "s5_b_re": nrm(ks[10], (DEPTH, N_DIR, S5_GROUPS, S5_STATE, S5_GROUP), (2 * S5_GROUP) ** -0.5),
        "s5_b_im": nrm(ks[11], (DEPTH, N_DIR, S5_GROUPS, S5_STATE, S5_GROUP), (2 * S5_GROUP) ** -0.5),
        "s5_c_re": nrm(ks[12], (DEPTH, N_DIR, S5_GROUPS, S5_GROUP, S5_STATE), (2 * S5_STATE) ** -0.5),
        "s5_c_im": nrm(ks[13], (DEPTH, N_DIR, S5_GROUPS, S5_GROUP, S5_STATE), (2 * S5_STATE) ** -0.5),
        "s5_d": nrm(ks[14], (DEPTH, D_S5), 1.0),
        "w_glu": nrm(ks[15], (DEPTH, D_S5, D_S5), D_S5 ** -0.5),
        "b_glu": nrm(ks[16], (DEPTH, D_S5), 0.01),
        "conv_w": nrm(ks[17], (DEPTH, CONV_K, CONV_K, 3 * D_GDN), 1.0 / CONV_K),
        "gdn_a_log": jnp.log(jax.random.uniform(ks[18], (DEPTH, N_DIR, GDN_HEADS), f32, 1.0, 16.0)),
        "gdn_dt_bias": dt + jnp.log(-jnp.expm1(-dt)),
        "gdn_norm_w": 1.0 + nrm(ks[20], (DEPTH, GDN_HEAD), 0.01),
        "w_out": nrm(ks[21], (DEPTH, D_MIX, D_MODEL), DEEPNORM_BETA * D_MIX ** -0.5),
        "ln_g": 1.0 + nrm(ks[22], (DEPTH, D_MODEL), 0.01),
        "ln_b": nrm(ks[23], (DEPTH, D_MODEL), 0.01),
    }


def reference(x, c, ctx, c_ctx, w_ada, b_ada, w_in, s5_lambda_re, s5_lambda_im, s5_log_dt,
              s5_b_re, s5_b_im, s5_c_re, s5_c_im, s5_d, w_glu, b_glu, conv_w, gdn_a_log,
              gdn_dt_bias, gdn_norm_w, w_out, ln_g, ln_b):
    for l in range(DEPTH):
        p = {
            'w_ada': w_ada[l], 'b_ada': b_ada[l], 'w_in': w_in[l],
            's5_lambda_re': s5_lambda_re[l], 's5_lambda_im': s5_lambda_im[l], 's5_log_dt': s5_log_dt[l],
            's5_b_re': s5_b_re[l], 's5_b_im': s5_b_im[l], 's5_c_re': s5_c_re[l], 's5_c_im': s5_c_im[l],
            's5_d': s5_d[l], 'w_glu': w_glu[l], 'b_glu': b_glu[l], 'conv_w': conv_w[l],
            'gdn_a_log': gdn_a_log[l], 'gdn_dt_bias': gdn_dt_bias[l], 'gdn_norm_w': gdn_norm_w[l],
            'w_out': w_out[l], 'ln_g': ln_g[l], 'ln_b': ln_b[l],
        }
        x, ctx = _layer(x, ctx, c, c_ctx, p, l < DEPTH - 1)
    return x
```

```python
import contextlib
import math
import numpy as np
import concourse.bass as bass
import concourse.mybir as mybir
from concourse.bass_utils import run_bass_kernel_spmd

F32 = mybir.dt.float32
BF16 = mybir.dt.bfloat16
ALU = mybir.AluOpType
AF = mybir.ActivationFunctionType
AX = mybir.AxisListType

ENG_NAMES = ["pe", "act", "dve", "pool", "sp"]
SAME_ENGINE_SYNC = True
N_DMA_SEMS = 12

NSEQ = 2
LCTX = 256
LLAT = 2048
LTOT = LCTX + LLAT
DEEP_ALPHA = 2.0 ** 0.25


class Prog:
    def __init__(self):
        self.ops = []
        self.last_w = {}
        self.readers = {}
        self.barrier_idx = None

    def barrier(self, eng, fn):
        deps = set(self.last_w.values())
        for v in self.readers.values():
            deps.update(v)
        if self.barrier_idx is not None:
            deps.add(self.barrier_idx)
        idx = len(self.ops)
        self.ops.append(dict(eng=eng, fn=fn, deps=deps, dma=False))
        self.barrier_idx = idx
        self.last_w = {}
        self.readers = {}
        return idx

    def op(self, eng, fn, reads=(), writes=(), dma=False):
        writes = list(writes) + [k for k in reads if k.startswith("ps")]
        idx = len(self.ops)
        deps = set()
        if self.barrier_idx is not None:
            deps.add(self.barrier_idx)
        for k in reads:
            if k in self.last_w:
                deps.add(self.last_w[k])
        for k in writes:
            if k in self.last_w:
                deps.add(self.last_w[k])
            deps.update(self.readers.get(k, ()))
        self.ops.append(dict(eng=eng, fn=fn, deps=deps, dma=dma))
        for k in reads:
            self.readers.setdefault(k, []).append(idx)
        for k in writes:
            self.last_w[k] = idx
            self.readers[k] = []
        return idx

    def emit(self, nc):
        ops = self.ops
        pos = {}
        seqcount = {e: 0 for e in ENG_NAMES}
        for i, o in enumerate(ops):
            if not o["dma"]:
                seqcount[o["eng"]] += 1
                pos[i] = seqcount[o["eng"]]
        dma_ops = [i for i, o in enumerate(ops) if o["dma"]]
        dma_sem = {}
        dma_val = {}
        semcnt = [0] * N_DMA_SEMS
        prev_on_sem = {}
        for n, i in enumerate(dma_ops):
            s = n % N_DMA_SEMS
            semcnt[s] += 16
            dma_sem[i] = s
            dma_val[i] = semcnt[s]
            if s in prev_on_sem:
                ops[i]["deps"].add(prev_on_sem[s])
            prev_on_sem[s] = i
        known = {e: {p: 0 for p in ENG_NAMES} for e in ENG_NAMES}
        known_dma = {e: [0] * N_DMA_SEMS for e in ENG_NAMES}
        flagged = set()
        waits = [[] for _ in ops]
        for i, o in enumerate(ops):
            e = o["eng"]
            for d in sorted(o["deps"]):
                od = ops[d]
                if od["dma"]:
                    s = dma_sem[d]
                    if dma_val[d] > known_dma[e][s]:
                        known_dma[e][s] = dma_val[d]
                        waits[i].append(("dma", s, dma_val[d]))
                else:
                    p = od["eng"]
                    if p == e and (e == "pe" or not SAME_ENGINE_SYNC):
                        continue
                    if pos[d] > known[e][p]:
                        known[e][p] = pos[d]
                        flagged.add(d)
                        waits[i].append(("eng", p, d))
        cnt = {e: 0 for e in ENG_NAMES}
        val = {}
        for i, o in enumerate(ops):
            if i in flagged:
                cnt[o["eng"]] += 1
                val[i] = cnt[o["eng"]]
        per_eng = {e: [] for e in ENG_NAMES}
        for i, o in enumerate(ops):
            per_eng[o["eng"]].append(i)
        self.stats = {e: len(per_eng[e]) for e in ENG_NAMES}

        with contextlib.ExitStack() as st:
            esem = {e: st.enter_context(nc.semaphore("s_" + e)) for e in ENG_NAMES}
            dsem = [st.enter_context(nc.semaphore("d_%d" % k)) for k in range(N_DMA_SEMS)]
            block = st.enter_context(nc.Block())

            def run(e, engobj):
                for i in per_eng[e]:
                    o = ops[i]
                    mx = {}
                    for w in waits[i]:
                        if w[0] == "dma":
                            key = ("d", w[1]); v = w[2]
                        else:
                            key = ("e", w[1]); v = val[w[2]]
                        mx[key] = max(mx.get(key, 0), v)
                    for key, v in mx.items():
                        sem = dsem[key[1]] if key[0] == "d" else esem[key[1]]
                        engobj.wait_ge(sem, v)
                    ins = o["fn"](engobj)
                    if o["dma"]:
                        ins.then_inc(dsem[dma_sem[i]], 16)
                    elif i in flagged:
                        ins.then_inc(esem[e], 1)
                if e == "sp":
                    for s in range(N_DMA_SEMS):
                        if semcnt[s] > 0:
                            engobj.wait_ge(dsem[s], semcnt[s])

            @block.tensor
            def _(eng):
                run("pe", eng)

            @block.scalar
            def _(eng):
                run("act", eng)

            @block.vector
            def _(eng):
                run("dve", eng)

            @block.gpsimd
            def _(eng):
                run("pool", eng)

            @block.sync
            def _(eng):
                run("sp", eng)


def _consts():
    c = {}
    c["ident"] = np.eye(128, dtype=np.float32)
    k = np.arange(64)
    tri = np.stack([(k[:, None] <= k[None, :]), (k[:, None] >= k[None, :])]).astype(np.float32)
    su = np.stack([(k[:, None] > k[None, :]), (k[:, None] < k[None, :])]).astype(np.float32)
    mi = np.stack([(k[None, :] >= k[:, None]), (k[None, :] <= k[:, None])]).astype(np.float32)
    ms = np.stack([(k[None, :] > k[:, None]), (k[None, :] < k[:, None])]).astype(np.float32)
    c["masks"] = np.concatenate([tri[0], tri[1], su[0], su[1], mi[0], mi[1], ms[0], ms[1]], axis=1).astype(np.float32)
    c["ones"] = np.ones((128, 128), np.float32)
    return c


def _prep_shared(inp):
    f = lambda a: np.ascontiguousarray(np.asarray(a, dtype=np.float32))
    sh = {}
    sh["w_ada"] = f(inp["w_ada"][0])
    b_ada = f(inp["b_ada"][0])
    sh["bcol"] = f(b_ada.reshape(24, 128).T)
    sh["bgate_bc"] = f(np.broadcast_to(b_ada[2048:3072][None, :], (128, 1024)))
    w_in = f(inp["w_in"][0])
    sh["w_in"] = w_in
    sh["wgate"] = f(w_in[:, 3072:3088].reshape(8, 128, 16).transpose(1, 0, 2))
    lre = f(inp["s5_lambda_re"][0]); lim = f(inp["s5_lambda_im"][0]); ldt = f(inp["s5_log_dt"][0])

    def smaj(a):
        o = np.zeros((128, 32), np.float32)
        for pair in range(16):
            for d in range(2):
                for gl in range(2):
                    o[gl * 64:(gl + 1) * 64, pair * 2 + d] = a[d, 2 * pair + gl, :]
        return o
    sh["lam_re"] = smaj(lre)
    sh["lam_im"] = smaj(lim)
    sh["logdt"] = smaj(np.broadcast_to(ldt[:, :, None], (2, 32, 64)))
    bre = f(inp["s5_b_re"][0]); bim = f(inp["s5_b_im"][0])
    cre = f(inp["s5_c_re"][0]); cim = f(inp["s5_c_im"][0])
    Bp = np.zeros((128, 4, 4, 2, 2, 128), np.float32)
    Cp = np.zeros((128, 4, 4, 2, 2, 128), np.float32)
    for j in range(4):
        for gp in range(4):
            for gl in range(2):
                gq = 2 * gp + gl
                g = 8 * j + gq
                for d in range(2):
                    Bp[gq * 16:(gq + 1) * 16, j, gp, d, 0, gl * 64:(gl + 1) * 64] = bre[d, g].T
                    Bp[gq * 16:(gq + 1) * 16, j, gp, d, 1, gl * 64:(gl + 1) * 64] = bim[d, g].T
                    Cp[gl * 64:(gl + 1) * 64, j, gp, d, 0, gq * 16:(gq + 1) * 16] = cre[d, g].T
                    Cp[gl * 64:(gl + 1) * 64, j, gp, d, 1, gq * 16:(gq + 1) * 16] = cim[d, g].T
    sh["Bp"] = f(Bp.reshape(128, 8192))
    sh["Cp"] = f(Cp.reshape(128, 8192))
    sh["dcol"] = f(inp["s5_d"][0].reshape(4, 128).T)
    sh["wglu"] = f(inp["w_glu"][0].reshape(4, 128, 512).transpose(1, 0, 2))
    sh["bglu"] = f(inp["b_glu"][0].reshape(4, 128).T)
    sh["convw"] = f(inp["conv_w"][0].reshape(9, 12, 128).transpose(2, 1, 0))
    sh["alog"] = f(np.broadcast_to(inp["gdn_a_log"][0].reshape(1, 8), (64, 8)))
    sh["dtb"] = f(np.broadcast_to(inp["gdn_dt_bias"][0].reshape(1, 8), (64, 8)))
    sh["normw"] = f(np.broadcast_to(inp["gdn_norm_w"][0].reshape(1, 128), (64, 128)))
    sh["w_out"] = f(inp["w_out"][0])
    sh["lng"] = f(np.broadcast_to(inp["ln_g"][0][None, :], (128, 1024)))
    sh["lnb"] = f(np.broadcast_to(inp["ln_b"][0][None, :], (128, 1024)))
    sh.update(_consts())
    return sh


IN_SHAPES = {
    "x2": [NSEQ, LLAT, 1024], "ctx2": [NSEQ, LCTX, 1024], "cT": [128, 8, 3],
    "w_ada": [1024, 3072], "bcol": [128, 24], "bgate_bc": [128, 1024], "w_in": [1024, 3088],
    "wgate": [128, 8, 16], "lam_re": [128, 32], "lam_im": [128, 32], "logdt": [128, 32],
    "Bp": [128, 8192], "Cp": [128, 8192], "dcol": [128, 4], "wglu": [128, 4, 512], "bglu": [128, 4],
    "convw": [128, 12, 9], "alog": [64, 8], "dtb": [64, 8], "normw": [64, 128],
    "w_out": [1024, 1024], "lng": [128, 1024], "lnb": [128, 1024],
    "ident": [128, 128], "masks": [64, 512], "ones": [128, 128],
}


def build_program(dbg=None, stages=("s5", "gdn", "out")):
    nc = bass.Bass("TRN2", target_bir_lowering=False)
    D = {k: nc.dram_tensor(k, v, F32, kind="ExternalInput").ap() for k, v in IN_SHAPES.items()}
    yout = nc.dram_tensor("yout", [NSEQ, LLAT, 1024], F32, kind="ExternalOutput").ap()
    dbg_t = {}
    if dbg:
        for k, shp in dbg.items():
            dbg_t[k] = nc.dram_tensor("dbg_" + k, shp, F32, kind="ExternalOutput").ap()
    P = Prog()
    st = contextlib.ExitStack()
    uid = [0]

    def sb(name, shape, dt=F32):
        return st.enter_context(nc.sbuf_tensor("sb_" + name, shape, dt))

    def psum(name):
        return st.enter_context(nc.psum_tensor(name, [128, 512], F32))

    with st:
        hT = sb("hT", [128, 8, LTOT], BF16)
        mixT = sb("mixT", [128, 8, LLAT], BF16)
        ident = sb("ident", [128, 128])
        ones = sb("ones", [128, 128])
        masks = sb("masks", [64, 512])
        identb = sb("identb", [128, 128], BF16)
        masksb = sb("masksb", [64, 512], BF16)
        onesb = sb("onesb", [64, 128], BF16)
        xst = [sb("xst%d" % i, [128, 1024]) for i in range(2)]
        wst = [sb("wst%d" % i, [128, 8, 128]) for i in range(2)]
        wbf = [sb("wbf%d" % i, [128, 8, 128], BF16) for i in range(2)]
        modsc = sb("modsc", [128, 16, 3])
        siluc = sb("siluc", [128, 8, 3])
        dbgs = sb("dbgs", [128, 512]) if dbg else None
        bcol = sb("bcol", [128, 24])
        small = sb("small", [128, 64])
        ARENA_WORDS = 28800
        arena = sb("arena", [128, ARENA_WORDS])
        PS = [psum("ps%d" % i) for i in range(8)]
        PK = ["ps%d" % i for i in range(8)]

        def dma(out, in_, r=(), w=()):
            P.op("sp", lambda e: e.dma_start(out=out, in_=in_), reads=r, writes=w, dma=True)

        def act(out, in_, func, r, w, bias=None, scale=None):
            kw = {}
            if bias is not None:
                kw["bias"] = bias
            if scale is not None:
                kw["scale"] = scale
            P.op("act", lambda e: e.activation(out=out, in_=in_, func=func, **kw), reads=r, writes=w)

        def tt(eng, out, in0, in1, op, r, w):
            P.op(eng, lambda e: e.tensor_tensor(out=out, in0=in0, in1=in1, op=op), reads=r, writes=w)

        def ts(eng, out, in0, s1, op0, r, w, s2=None, op1=None):
            if op1 is None:
                P.op(eng, lambda e: e.tensor_scalar(out=out, in0=in0, scalar1=s1, scalar2=None, op0=op0), reads=r, writes=w)
            else:
                P.op(eng, lambda e: e.tensor_scalar(out=out, in0=in0, scalar1=s1, scalar2=s2, op0=op0, op1=op1), reads=r, writes=w)

        def stt(eng, out, in0, scalar, in1, op0, op1, r, w):
            eng = "dve"
            P.op(eng, lambda e: e.scalar_tensor_tensor(out=out, in0=in0, scalar=scalar, in1=in1, op0=op0, op1=op1), reads=r, writes=w)

        def cp(eng, out, in_, r, w):
            if eng == "act":
                act(out, in_, AF.Copy, r, w)
            else:
                P.op(eng, lambda e: e.tensor_copy(out=out, in_=in_), reads=r, writes=w)

        def mm(out, lhsT, rhs, start, stop, r, w):
            P.op("pe", lambda e: e.matmul(out, lhsT=lhsT, rhs=rhs, start=start, stop=stop), reads=r, writes=w)

        def tr(out, in_, idn, r, w):
            P.op("pe", lambda e: e.transpose(out, in_, idn), reads=r, writes=w)

        def dump(name, ap, r):
            if name in dbg_t:
                if ap.dtype == BF16:
                    n = ap.shape[-1]
                    cp("dve", dbgs[:, 0:n], ap, r, ["dbgs"])
                    dma(dbg_t[name], dbgs[:, 0:n], r=["dbgs"])
                else:
                    dma(dbg_t[name], ap, r=r)

        def barrier():
            P.barrier("dve", lambda e: e.memset(small[:, 63:64], 0.0))

        dma(ident[:], D["ident"][:, :], w=["ident"])
        dma(ones[:], D["ones"][:, :], w=["ones"])
        dma(masks[:], D["masks"][:, :], w=["masks"])
        dma(bcol[:], D["bcol"][:, :], w=["bcol"])
        cp("dve", identb[:], ident[:], ["ident"], ["identb"])
        cp("dve", masksb[:], masks[:], ["masks"], ["masksb"])
        cp("dve", onesb[:], ones[0:64, :], ["ones"], ["onesb"])
        SUb = [masksb[:, 128:192], masksb[:, 192:256]]
        TRI = [masks[:, 0:64], masks[:, 64:128]]
        SU = [masks[:, 128:192], masks[:, 192:256]]
        MI = [masks[:, 256:320], masks[:, 320:384]]
        MS = [masks[:, 384:448], masks[:, 448:512]]

        dma(siluc[:], D["cT"][:, :, :], w=["siluc"])
        act(siluc[:], siluc[:], AF.Silu, ["siluc"], ["siluc"])
        wada = D["w_ada"].rearrange("(k p) c -> p k c", p=128)
        for t in range(16):
            b = t % 2
            dma(wst[b][:], wada[:, :, t * 128:(t + 1) * 128], w=["wst%d" % b])
            for k in range(8):
                mm(PS[0][:, t * 4:t * 4 + 3], wst[b][:, k, :], siluc[:, k, :], k == 0, k == 7, ["wst%d" % b, "siluc"], [PK[0]])
        for t in range(16):
            ts("dve", modsc[:, t, :], PS[0][:, t * 4:t * 4 + 3], bcol[:, t:t + 1], ALU.add, [PK[0], "bcol"], ["modsc"],
               s2=(1.0 if t >= 8 else 0.0), op1=ALU.add)

        win = D["w_in"].rearrange("(k p) c -> p k c", p=128)
        wcount = [0]

        def load_w(c0, ncols=128):
            b = wcount[0] % 2
            wcount[0] += 1
            dma(wst[b][:, :, 0:ncols], win[:, :, c0:c0 + ncols], w=["wst%d" % b])
            cp("act", wbf[b][:, :, 0:ncols], wst[b][:, :, 0:ncols], ["wst%d" % b], ["wbf%d" % b])
            return b

        def proj_fm(b, ps_ap, tok0, ntok, pk):
            for k in range(8):
                mm(ps_ap, wbf[b][:, k, :], hT[:, k, tok0:tok0 + ntok], k == 0, k == 7, ["wbf%d" % b, "hT"], [pk])

        TOKBLK = [(0, 256), (256, 512), (768, 512), (1280, 512), (1792, 512)]

        for s in range(NSEQ):
            barrier()
            for tt_i in range(18):
                b = tt_i % 2
                if tt_i < 2:
                    src = D["ctx2"][s, tt_i * 128:(tt_i + 1) * 128, :]
                    jcol = 2
                else:
                    src = D["x2"][s, (tt_i - 2) * 128:(tt_i - 1) * 128, :]
                    jcol = s
                dma(xst[b][:], src, w=["xst%d" % b])
                for half in range(2):
                    pk = 1 + half
                    for kk in range(4):
                        k = half * 4 + kk
                        tr(PS[pk][:, kk * 128:(kk + 1) * 128], xst[b][:, k * 128:(k + 1) * 128], ident[:], ["xst%d" % b, "ident"], [PK[pk]])
                    for kk in range(4):
                        k = half * 4 + kk
                        act(hT[:, k, tt_i * 128:(tt_i + 1) * 128], PS[pk][:, kk * 128:(kk + 1) * 128], AF.Identity,
                            [PK[pk], "modsc"], ["hT"], bias=modsc[:, k, jcol:jcol + 1], scale=modsc[:, 8 + k, jcol:jcol + 1])
            if s == 0:
                dump("hT", hT[:, 0, 0:512], ["hT"])

            if "s5" in stages:
                barrier()
                o = [0]

                def carve(n, dt=F32):
                    words = n if dt == F32 else (n + 1) // 2
                    a = arena[:, o[0]:o[0] + words]
                    o[0] += words
                    assert o[0] <= ARENA_WORDS, o[0]
                    return a.bitcast(BF16)[:, 0:n] if dt == BF16 else a
                Bpb = carve(2048, BF16)
                Cpb = carve(2048, BF16)
                uT = carve(4 * LTOT, BF16).rearrange("p (j t) -> p j t", j=4)
                o_hb = o[0]
                HbD = [[carve(LTOT) for _ in range(2)] for _ in range(2)]
                HbB = [[carve(LLAT, BF16) for _ in range(2)] for _ in range(2)]
                XaD = [[carve(288) for _ in range(2)] for _ in range(2)]
                XbD = [[carve(288) for _ in range(2)] for _ in range(2)]
                prm = carve(32 * 40).rearrange("p (q c) -> p q c", c=32)
                prm2 = carve(32 * 28).rearrange("p (q c) -> p q c", c=32)
                ytmp = carve(512)
                LR, LI, DT, ER, TH, AR, AI, CR, CI, T0, T1, T2 = range(12)
                PW = 13
                NP = PW + 16
                P.op("dve", lambda e: e.memset(prm[:, :, :], 0.0), writes=["prm"])
                for q_, nm in ((LR, "lam_re"), (LI, "lam_im"), (DT, "logdt")):
                    dma(prm[:, q_, :], D[nm][:, :], w=["prm"])

                def ptt(dst, a_, b_, op):
                    tt("dve", prm[:, dst, :], prm[:, a_, :], prm[:, b_, :], op, ["prm"], ["prm"])

                def pts(dst, a_, s1, op0, s2=None, op1=None, eng="dve"):
                    ts(eng, prm[:, dst, :], prm[:, a_, :], s1, op0, ["prm"], ["prm"], s2=s2, op1=op1)
                act(prm[:, DT, :], prm[:, DT, :], AF.Exp, ["prm"], ["prm"])
                ptt(ER, LR, DT, ALU.mult)
                act(prm[:, ER, :], prm[:, ER, :], AF.Exp, ["prm"], ["prm"])
                ptt(TH, LI, DT, ALU.mult)
                I32 = mybir.dt.int32
                kint = prm[:, 12, :].bitcast(I32)
                PI_C = 3.14159
                for (dst, shift_) in ((AI, 0.0), (AR, 0.5 * math.pi)):
                    pts(T0, TH, 1.0 / (2 * math.pi), ALU.mult, s2=shift_ / (2 * math.pi), op1=ALU.add)
                    P.op("dve", lambda e: e.tensor_copy(out=kint, in_=prm[:, T0, :]), reads=["prm"], writes=["prm"])
                    P.op("dve", lambda e: e.tensor_copy(out=prm[:, T1, :], in_=kint), reads=["prm"], writes=["prm"])
                    pts(T2, TH, shift_, ALU.add)
                    stt("dve", prm[:, T0, :], prm[:, T1, :], -2 * math.pi, prm[:, T2, :], ALU.mult, ALU.add, ["prm"], ["prm"])
                    pts(T0, T0, -PI_C, ALU.max, s2=PI_C, op1=ALU.min)
                    act(prm[:, dst, :], prm[:, T0, :], AF.Sin, ["prm"], ["prm"])
                ptt(AR, AR, ER, ALU.mult)
                ptt(AI, AI, ER, ALU.mult)
                pts(T0, AR, -1.0, ALU.add)
                ptt(T1, T0, LR, ALU.mult)
                ptt(T2, AI, LI, ALU.mult)
                ptt(CR, T1, T2, ALU.add)
                ptt(T1, AI, LR, ALU.mult)
                ptt(T2, T0, LI, ALU.mult)
                ptt(CI, T1, T2, ALU.subtract)
                ptt(T1, LR, LR, ALU.mult)
                ptt(T2, LI, LI, ALU.mult)
                ptt(T1, T1, T2, ALU.add)
                P.op("dve", lambda e: e.reciprocal(out=prm[:, T1, :], in_=prm[:, T1, :]), reads=["prm"], writes=["prm"])
                ptt(CR, CR, T1, ALU.mult)
                ptt(CI, CI, T1, ALU.mult)

                def cmul(dst_r, dst_i, ar_, ai_, br_, bi_):
                    ptt(T0, ar_, br_, ALU.mult)
                    ptt(T1, ai_, bi_, ALU.mult)
                    ptt(T2, ar_, bi_, ALU.mult)
                    ptt(dst_r, T0, T1, ALU.subtract)
                    ptt(T0, ai_, br_, ALU.mult)
                    ptt(dst_i, T2, T0, ALU.add)
                pts(PW, AR, 1.0, ALU.mult)
                pts(PW + 1, AI, 1.0, ALU.mult)
                for k in range(2, 9):
                    cmul(PW + 2 * (k - 1), PW + 2 * (k - 1) + 1, PW + 2 * (k - 2), PW + 2 * (k - 2) + 1, AR, AI)
                for k in range(1, 9):
                    pts(NP + k - 1, PW + 2 * (k - 1) + 1, -1.0, ALU.mult)
                ts("dve", prm2[:, 0, :], prm[:, PW + 14, :], 1.0, ALU.mult, ["prm"], ["prm2"])
                ts("dve", prm2[:, 1, :], prm[:, PW + 15, :], 1.0, ALU.mult, ["prm"], ["prm2"])
                for m in range(9):
                    if m > 0:
                        r0, i0_ = prm2[:, 3 * (m - 1), :], prm2[:, 3 * (m - 1) + 1, :]
                        tt("dve", prm[:, T0, :], r0, r0, ALU.mult, ["prm2", "prm"], ["prm"])
                        tt("dve", prm[:, T1, :], i0_, i0_, ALU.mult, ["prm2", "prm"], ["prm"])
                        tt("dve", prm2[:, 3 * m, :], prm[:, T0, :], prm[:, T1, :], ALU.subtract, ["prm"], ["prm2"])
                        tt("dve", prm[:, T0, :], r0, i0_, ALU.mult, ["prm2", "prm"], ["prm"])
                        ts("dve", prm2[:, 3 * m + 1, :], prm[:, T0, :], 2.0, ALU.mult, ["prm"], ["prm2"])
                    ts("dve", prm2[:, 3 * m + 2, :], prm2[:, 3 * m + 1, :], -1.0, ALU.mult, ["prm2"], ["prm2"])
                if s == 0:
                    dump("prm", prm[:, 0:40, :], ["prm"])
                for j in range(4):
                    wb = load_w(j * 128)
                    for bi, (t0, nt) in enumerate(TOKBLK):
                        pk = 1 + bi % 2
                        proj_fm(wb, PS[pk][:, 0:nt], t0, nt, PK[pk])
                        cp("act", uT[:, j, t0:t0 + nt], PS[pk][:, 0:nt], [PK[pk]], ["uT"])
                if s == 0:
                    dump("uT", uT[:, 0, 0:512], ["uT"])
                dcol = small[:, 0:4]
                dma(dcol, D["dcol"][:, :], w=["dcol"])
                gT = mixT[:, 0:4, :]
                Bpv = Bpb.rearrange("p (g d r c) -> p g d r c", g=4, d=2, r=2)
                Cpv = Cpb.rearrange("p (g d r c) -> p g d r c", g=4, d=2, r=2)
                for j in range(4):
                    for pc in range(2):
                        dma(xst[pc][:], D["Bp"][:, j * 2048 + pc * 1024:j * 2048 + (pc + 1) * 1024], w=["xst%d" % pc])
                        cp("act", Bpb[:, pc * 1024:(pc + 1) * 1024], xst[pc][:], ["xst%d" % pc], ["Bpb"])
                    for gp in range(4):
                        b = gp % 2
                        pcg = j * 4 + gp
                        dma(xst[b][:, 0:512], D["Cp"][:, pcg * 512:(pcg + 1) * 512], w=["xst%d" % b])
                        for d in range(2):
                            col = pcg * 2 + d
                            cre_ = xst[b][:, d * 256:d * 256 + 128]
                            cim_ = xst[b][:, d * 256 + 128:d * 256 + 256]
                            t_ = xst[b][:, 512 + d * 128:512 + (d + 1) * 128]
                            base = gp * 512 + d * 256
                            xk = "xst%d" % b
                            ts("dve", t_, cim_, prm[:, CI, col:col + 1], ALU.mult, [xk, "prm"], [xk])
                            stt("dve", Cpb[:, base:base + 128], cre_, prm[:, CR, col:col + 1], t_, ALU.mult, ALU.subtract, [xk, "prm"], ["Cpb"])
                            ts("dve", t_, cre_, prm[:, CI, col:col + 1], ALU.mult, [xk, "prm"], [xk])
                            stt("dve", t_, cim_, prm[:, CR, col:col + 1], t_, ALU.mult, ALU.add, [xk, "prm"], [xk])
                            ts("dve", Cpb[:, base + 128:base + 256], t_, -1.0, ALU.mult, [xk], ["Cpb"])
                    YP = [PS[4 + q] for q in range(4)]
                    YK = [PK[4 + q] for q in range(4)]
                    rcount = [0, 0, 0, 0]

                    def s5_body(gp, d):
                        pair = j * 4 + gp
                        col = pair * 2 + d
                        E = "dve"
                        Hb = HbD[d]
                        hk = "Hb%d" % d
                        Xa, Xb = XaD[d], XbD[d]
                        for ri in range(2):
                            for bi, (t0, nt) in enumerate(TOKBLK):
                                pk = 1 + (bi + ri) % 3
                                mm(PS[pk][:, 0:nt], Bpv[:, gp, d, ri, :], uT[:, j, t0:t0 + nt], True, True, ["Bpb", "uT"], [PK[pk]])
                                cp("act", Hb[ri].rearrange("p (s c) -> p c s", s=8)[:, t0 // 8:(t0 + nt) // 8, :],
                                   PS[pk][:, 0:nt].rearrange("p (c s) -> p c s", s=8), [PK[pk]], [hk])
                        yield "hold"
                        Hr = Hb[0].rearrange("p (s c) -> p c s", s=8)
                        Hi = Hb[1].rearrange("p (s c) -> p c s", s=8)
                        a_r = prm[:, PW, col:col + 1]
                        a_i = prm[:, PW + 1, col:col + 1]
                        a_ni = prm[:, NP, col:col + 1]
                        order = range(1, 8) if d == 0 else range(6, -1, -1)
                        for s_ in order:
                            sp_ = s_ - 1 if d == 0 else s_ + 1
                            stt(E, Hr[:, :, s_], Hr[:, :, sp_], a_r, Hr[:, :, s_], ALU.mult, ALU.add, [hk, "prm"], [hk])
                            yield
                            stt(E, Hi[:, :, s_], Hr[:, :, sp_], a_i, Hi[:, :, s_], ALU.mult, ALU.add, [hk, "prm"], [hk])
                            yield
                            stt(E, Hr[:, :, s_], Hi[:, :, sp_], a_ni, Hr[:, :, s_], ALU.mult, ALU.add, [hk, "prm"], [hk])
                            yield
                            stt(E, Hi[:, :, s_], Hi[:, :, sp_], a_r, Hi[:, :, s_], ALU.mult, ALU.add, [hk, "prm"], [hk])
                            yield
                        se = 7 if d == 0 else 0
                        xak, xbk = "Xa%d" % d, "Xb%d" % d
                        for ri, Hv in enumerate((Hr, Hi)):
                            if d == 0:
                                cp(E, Xa[ri][:, 0:288], Hv[:, :, se], [hk], [xak])
                                yield
                            else:
                                cp(E, Xa[ri][:, 0:256], Hv[:, 32:288, se], [hk], [xak])
                                yield
                                cp(E, Xa[ri][:, 256:288], Hv[:, 0:32, se], [hk], [xak])
                                yield
                        cur, nxt, ck, nk = Xa, Xb, xak, xbk
                        for m in range(9):
                            sh_ = 1 << m
                            A_r = prm2[:, 3 * m, col:col + 1]
                            A_i = prm2[:, 3 * m + 1, col:col + 1]
                            A_ni = prm2[:, 3 * m + 2, col:col + 1]
                            n_ = 288 - sh_
                            if d == 0:
                                dst = slice(sh_, 288); srcs = slice(0, n_); keep = slice(0, sh_)
                            else:
                                dst = slice(0, n_); srcs = slice(sh_, 288); keep = slice(n_, 288)
                            stt(E, nxt[0][:, dst], cur[0][:, srcs], A_r, cur[0][:, dst], ALU.mult, ALU.add, [ck, "prm2"], [nk])
                            yield
                            stt(E, nxt[0][:, dst], cur[1][:, srcs], A_ni, nxt[0][:, dst], ALU.mult, ALU.add, [ck, nk, "prm2"], [nk])
                            yield
                            stt(E, nxt[1][:, dst], cur[0][:, srcs], A_i, cur[1][:, dst], ALU.mult, ALU.add, [ck, "prm2"], [nk])
                            yield
                            stt(E, nxt[1][:, dst], cur[1][:, srcs], A_r, nxt[1][:, dst], ALU.mult, ALU.add, [ck, nk, "prm2"], [nk])
                            yield
                            cp(E, nxt[0][:, keep], cur[0][:, keep], [ck], [nk])
                            yield
                            cp(E, nxt[1][:, keep], cur[1][:, keep], [ck], [nk])
                            yield
                            cur, nxt, ck, nk = nxt, cur, nk, ck
                        HBr = HbB[d][0].rearrange("p (s c) -> p c s", s=8)
                        HBi = HbB[d][1].rearrange("p (s c) -> p c s", s=8)
                        bk = "HbB%d" % d
                        for s_ in range(8):
                            kpow = s_ + 1 if d == 0 else 8 - s_
                            p_r = prm[:, PW + 2 * (kpow - 1), col:col + 1]
                            p_i = prm[:, PW + 2 * (kpow - 1) + 1, col:col + 1]
                            p_ni = prm[:, NP + kpow - 1, col:col + 1]
                            if d == 0:
                                Hs_r, Hs_i = cur[0][:, 31:287], cur[1][:, 31:287]
                            else:
                                Hs_r, Hs_i = cur[0][:, 1:257], cur[1][:, 1:257]
                            stt(E, Hr[:, 32:288, s_], Hs_r, p_r, Hr[:, 32:288, s_], ALU.mult, ALU.add, [hk, ck, "prm"], [hk])
                            yield
                            stt(E, HBr[:, :, s_], Hs_i, p_ni, Hr[:, 32:288, s_], ALU.mult, ALU.add, [hk, ck, "prm"], [bk])
                            yield
                            stt(E, Hi[:, 32:288, s_], Hs_r, p_i, Hi[:, 32:288, s_], ALU.mult, ALU.add, [hk, ck, "prm"], [hk])
                            yield
                            stt(E, HBi[:, :, s_], Hs_i, p_r, Hi[:, 32:288, s_], ALU.mult, ALU.add, [hk, ck, "prm"], [bk])
                            yield
                        for q in range(4):
                            for ri in range(2):
                                mm(YP[q][:, :], Cpv[:, gp, d, ri, :], HbB[d][ri].rearrange("p (s c) -> p c s", s=8)[:, q * 64:(q + 1) * 64, :],
                                   rcount[q] == 0, rcount[q] == 15, ["Cpb", bk], [YK[q]])
                                rcount[q] += 1

                    def s5_stream(d):
                        for gp in range(4):
                            yield from s5_body(gp, d)
                    g0, g1 = s5_stream(0), s5_stream(1)
                    for _ in range(60):
                        next(g0)
                    alive = [g0, g1]
                    hold = {id(g0): 0, id(g1): 0}
                    while alive:
                        progressed = False
                        for g_ in list(alive):
                            if hold[id(g_)] > 0 and len(alive) > 1:
                                hold[id(g_)] -= 1
                                continue
                            try:
                                r_ = next(g_)
                                progressed = True
                                if r_ == "hold":
                                    hold[id(g_)] = 40
                            except StopIteration:
                                alive.remove(g_)
                        if not progressed:
                            for k_ in hold:
                                hold[k_] = 0
                    for q in range(4):
                        stt("dve", ytmp, uT[:, j, LCTX + q * 512:LCTX + (q + 1) * 512], dcol[:, j:j + 1], YP[q][:, :], ALU.mult, ALU.add,
                            ["uT", "dcol", YK[q]], ["ytmp"])
                        if s == 0 and j == 0 and q == 0:
                            dump("y0", ytmp, ["ytmp"])
                        act(gT[:, j, q * 512:(q + 1) * 512], ytmp, AF.Gelu, ["ytmp"], ["gT"])
                barrier()
                o[0] = o_hb
                zt = carve(512)
                sig = carve(4 * 512).rearrange("p (j c) -> p j c", j=4)
                wgl = carve(2048, BF16).rearrange("p (j c) -> p j c", j=4)
                bgl = small[:, 4:8]
                dma(bgl, D["bglu"][:, :], w=["bglu"])
                for j in range(4):
                    dma(xst[j % 2][:, 0:512], D["wglu"][:, j, :], w=["xst%d" % (j % 2)])
                    cp("pool", wgl[:, j, :], xst[j % 2][:, 0:512], ["xst%d" % (j % 2)], ["wgl"])
                for q in range(4):
                    for jo in range(4):
                        for ji in range(4):
                            mm(PS[4 + jo][:, :], wgl[:, ji, jo * 128:(jo + 1) * 128], gT[:, ji, q * 512:(q + 1) * 512], ji == 0, ji == 3,
                               ["wgl", "gT"], [PK[4 + jo]])
                        act(sig[:, jo, :], PS[4 + jo][:, :], AF.Sigmoid, [PK[4 + jo], "bglu"], ["sig%d" % jo], bias=bgl[:, jo:jo + 1])
                    for jo in range(4):
                        wb = load_w(512 + jo * 128)
                        proj_fm(wb, PS[1 + jo % 2][:, :], LCTX + q * 512, 512, PK[1 + jo % 2])
                        act(zt, PS[1 + jo % 2][:, :], AF.Silu, [PK[1 + jo % 2]], ["zt"])
                        tt("dve", sig[:, jo, :], sig[:, jo, :], zt, ALU.mult, ["sig%d" % jo, "zt"], ["sig%d" % jo])
                    for jo in range(4):
                        tt("pool", gT[:, jo, q * 512:(q + 1) * 512], gT[:, jo, q * 512:(q + 1) * 512], sig[:, jo, :], ALU.mult,
                           ["gT", "sig%d" % jo], ["gT"])
                if s == 0:
                    dump("mix_s5", mixT[:, 0, 0:512], ["gT"])
            if "gdn" in stages:
                barrier()
                o = [0]

                def carve(n, dt=F32):
                    words = n if dt == F32 else (n + 1) // 2
                    a = arena[:, o[0]:o[0] + words]
                    o[0] += words
                    assert o[0] <= ARENA_WORDS, o[0]
                    return a.bitcast(BF16)[:, 0:n] if dt == BF16 else a

                def c3(n_, a_, b_):
                    return carve(n_ * a_ * b_).rearrange("p (n a b) -> p n a b", n=n_, a=a_)
                wgb = carve(128, BF16).rearrange("p (k c) -> p k c", k=8)
                gpar = carve(32)
                normw = carve(128)
                cwt = carve(108).rearrange("p (t k) -> p t k", t=12)
                G_ = {nm: carve(288).rearrange("p (c e) -> p c e", e=8) for nm in ("beta", "g", "gcum", "koutc")}
                G_["gam"] = G_["gcum"]
                XTf = carve(LTOT)
                XT = [carve(LTOT, BF16) for _ in range(3)]
                lpad = carve(34 * 66, BF16).rearrange("p (r c) -> p r c", c=66)
                cpad = carve(258, BF16)
                dgt = carve(9 * 128, BF16).rearrange("p (k c) -> p k c", k=9)
                tmp5 = carve(512)
                tmp5b = carve(512)
                rsb = carve(32)
                Oacc = carve(32 * 128).rearrange("p (c v) -> p c v", v=128)
                Sst = [carve(128) for _ in range(2)]
                NB = 4
                NI = 2 * NB
                mk = lambda w_, dt=F32: carve(NI * w_, dt).rearrange("p (n c) -> p n c", n=NI)
                sh_ = {"Lg": mk(64), "dec": mk(64), "Pm": mk(64), "Lgh": mk(64, BF16), "Lgl": mk(64, BF16),
                       "Qa": mk(64, BF16), "QTa": mk(64, BF16), "Qb": mk(64, BF16), "QTb": mk(64, BF16), "Pmb": mk(64, BF16),
                       "V": mk(128, BF16), "Kg": mk(128, BF16)}
                Av = [dict(sh_, **{"wT": mk(64), "QgT": mk(64), "gbc": mk(64), "QKT": mk(64, BF16), "K": mk(128, BF16), "u0": mk(128)})
                      for _ in range(2)]
                A64v = Av
                A128v = Av
                ubuf = [carve(128, BF16) for _ in range(4)]
                dma(xst[0][:, 0:128], D["wgate"].rearrange("p k c -> p (k c)"), w=["xst0"])
                cp("dve", wgb.rearrange("p k c -> p (k c)"), xst[0][:, 0:128], ["xst0"], ["wgb"])
                dma(gpar[0:64, 0:8], D["alog"][:, :], w=["gpar"])
                dma(gpar[0:64, 8:16], D["dtb"][:, :], w=["gpar"])
                dma(normw[0:64, :], D["normw"][:, :], w=["normw"])
                dma(cwt[:], D["convw"][:, :, :], w=["cwt"])
                act(gpar[0:64, 0:8], gpar[0:64, 0:8], AF.Exp, ["gpar"], ["gpar"])
                ts("dve", gpar[0:64, 0:8], gpar[0:64, 0:8], -1.0, ALU.mult, ["gpar"], ["gpar"])
                P.op("dve", lambda e: e.memset(lpad[:, :, :], 0.0), writes=["lpad"])
                P.op("dve", lambda e: e.memset(cpad[:, :], 0.0), writes=["cpad"])
                for c_ in range(36):
                    bank, cc = (0, c_) if c_ < 32 else (3, c_ - 32)
                    for k in range(8):
                        mm(PS[bank][0:64, cc * 16:(cc + 1) * 16], hT[:, k, c_ * 64:(c_ + 1) * 64], wgb[:, k, :], k == 0, k == 7,
                           ["hT", "wgb"], [PK[bank]])
                for (bank, c0, nch) in ((0, 0, 32), (3, 32, 4)):
                    pv = PS[bank][0:64, 0:nch * 16].rearrange("p (c e) -> p c e", e=16)
                    act(G_["beta"][0:64, c0:c0 + nch, :], pv[:, :, 0:8], AF.Sigmoid, [PK[bank]], ["gates"])
                    tt("dve", G_["g"][0:64, c0:c0 + nch, :], pv[:, :, 8:16],
                       gpar[0:64, 8:16].unsqueeze(1).to_broadcast([64, nch, 8]), ALU.add, [PK[bank], "gpar"], ["gates"])
                gall = lambda nm: G_[nm][0:64, :, :]
                act(gall("g"), gall("g"), AF.Exp, ["gates"], ["gates"])
                act(gall("g"), gall("g"), AF.Ln, ["gates"], ["gates"], bias=1.0)
                tt("dve", gall("g"), gall("g"), gpar[0:64, 0:8].unsqueeze(1).to_broadcast([64, 36, 8]), ALU.mult, ["gates", "gpar"], ["gates"])
                gflat = G_["g"][0:64, :, :].rearrange("p c e -> p (c e)")
                for d in range(2):
                    for hh in range(2):
                        pass
                    mm(PS[1][0:64, 0:288], TRI[d], gflat, True, True, ["masks", "gates"], [PK[1]])
                    pv = PS[1][0:64, 0:288].rearrange("p (c e) -> p c e", e=8)
                    cp("dve", G_["gcum"][0:64, :, d * 4:(d + 1) * 4], pv[:, :, d * 4:(d + 1) * 4], [PK[1]], ["gates"])
                mm(PS[2][0:64, 0:288], ones[0:64, 0:64], gflat, True, True, ["ones", "gates"], [PK[2]])
                tt("dve", gall("koutc"), PS[2][0:64, 0:288].rearrange("p (c e) -> p c e", e=8), gall("gcum"), ALU.subtract, [PK[2], "gates"], ["gates"])
                act(gall("koutc"), gall("koutc"), AF.Exp, ["gates"], ["gates"])
                if s == 0:
                    dump("g", G_["g"][0:64, :, :].rearrange("p c e -> p (c e)"), ["gates"])
                    dump("gcum", G_["gcum"][0:64, :, :].rearrange("p c e -> p (c e)"), ["gates"])
                act(gall("gam"), gall("gcum"), AF.Exp, ["gates"], ["gates"])
                FORD = list(range(36))
                BORD = [3, 2, 1, 0] + list(range(35, 3, -1))
                import os as _os2
                for h in range(int(_os2.environ.get('GDN_NHEADS', 4))):
                    for t in range(3):
                        tile_i = t * 4 + h
                        wb = load_w(1024 + t * 512 + h * 128)
                        proj_fm(wb, PS[1][:, 0:256], 0, 256, PK[1])
                        cp("act", cpad[:, 1:257], PS[1][:, 0:256], [PK[1]], ["cpad"])
                        for q in range(4):
                            pk = 2 + q % 2
                            proj_fm(wb, PS[pk][:, :], 256 + q * 512, 512, PK[pk])
                            cp("act", lpad[:, 1 + 8 * q:9 + 8 * q, 1:65], PS[pk][:, :].rearrange("p (r c) -> p r c", c=64), [PK[pk]], ["lpad"])
                        xk = "XT%d" % t
                        for tap in range(9):
                            ts("dve", dgt[:, tap, :], identb[:, :], cwt[:, tile_i, tap:tap + 1], ALU.mult, ["identb", "cwt"], ["dgt"])
                        for kx in range(3):
                            mm(PS[4][:, 0:256], dgt[:, 3 + kx, :], cpad[:, kx:kx + 256], kx == 0, kx == 2, ["dgt", "cpad"], [PK[4]])
                        if t < 2:
                            act(XTf[:, 0:256], PS[4][:, 0:256], AF.Silu, [PK[4]], ["XTf"])
                        else:
                            act(XT[t][:, 0:256], PS[4][:, 0:256], AF.Silu, [PK[4]], [xk])
                        for q in range(4):
                            pk = 4 + (q + 1) % 2
                            for tap in range(9):
                                ky, kx = tap // 3, tap % 3
                                mm(PS[pk][:, :], dgt[:, tap, :], lpad[:, ky + 8 * q:ky + 8 * q + 8, kx:kx + 64], tap == 0, tap == 8,
                                   ["dgt", "lpad"], [PK[pk]])
                            if t < 2:
                                act(XTf[:, 256 + q * 512:256 + (q + 1) * 512], PS[pk][:, :], AF.Silu, [PK[pk]], ["XTf"])
                            else:
                                act(XT[t][:, 256 + q * 512:256 + (q + 1) * 512], PS[pk][:, :], AF.Silu, [PK[pk]], [xk])
                        if t < 2:
                            for bi, (t0, nt) in enumerate(TOKBLK):
                                pk = 1 + bi % 3
                                tq = tmp5 if bi % 2 == 0 else tmp5b
                                tqk = "tmp5" if bi % 2 == 0 else "tmp5b"
                                tt("dve", tq[:, 0:nt], XTf[:, t0:t0 + nt], XTf[:, t0:t0 + nt], ALU.mult, ["XTf"], [tqk])
                                mm(PS[pk][:, 0:nt], ones[:, :], tq[:, 0:nt], True, True, ["ones", tqk], [PK[pk]])
                                act(tq[:, 0:nt], PS[pk][:, 0:nt], AF.Ln, [PK[pk]], [tqk], bias=1e-6)
                                act(tq[:, 0:nt], tq[:, 0:nt], AF.Exp, [tqk], [tqk], scale=-0.5)
                                stt("dve", XT[t][:, t0:t0 + nt], XTf[:, t0:t0 + nt], (128.0 ** -0.5) if t == 0 else 1.0, tq[:, 0:nt],
                                    ALU.mult, ALU.mult, ["XTf", tqk], [xk])
                    qT, kT, vT = XT
                    if s == 0 and h == 0:
                        dump("qn", qT[:, 0:512], ["XT0"])
                        dump("kn", kT[:, 256:768], ["XT1"])
                        dump("vv", vT[:, 256:768], ["XT2"])
                    for d in range(2):
                        P.op("dve", (lambda d=d: lambda e: e.memset(Sst[d][:, :], 0.0))(), writes=["S%d" % d])

                    def batch_info(b):
                        i0_ = b * NB
                        return [(FORD[i0_], NB), (min(BORD[i0_:i0_ + NB]), NB)]

                    def stageA_batch(b):
                        info = batch_info(b)
                        par = b % 2
                        A64, A128 = A64v[par], A128v[par]
                        KY = lambda s_: s_ + str(par)
                        items = []
                        for d in range(2):
                            cmin, nch = info[d]
                            for q_ in range(nch):
                                items.append((d, cmin + q_, d * NB + q_))
                        e8 = lambda d: d * 4 + h
                        gcols = lambda nm, d: G_[nm][0:64, info[d][0]:info[d][0] + NB, e8(d)]
                        v = lambda nm, d: A64[nm][0:64, d * NB:(d + 1) * NB, :]
                        v2 = lambda nm, d: A128[nm][0:64, d * NB:(d + 1) * NB, :]
                        ps3 = lambda bank, d, w_: PS[bank][0:64, :].rearrange("p (n c) -> p n c", c=w_)[:, (d * NB if w_ == 64 else 0):(d * NB if w_ == 64 else 0) + NB, :]
                        for d in range(2):
                            tt("dve", v("Lg", d), TRI[d].unsqueeze(1).to_broadcast([64, NB, 64]),
                               gcols("g", d).unsqueeze(2).to_broadcast([64, NB, 64]), ALU.mult, ["masks", "gates"], ["aLg"])
                        cp("act", A64["Lgh"][0:64, :, :], A64["Lg"][0:64, :, :], ["aLg"], ["aLgh"])
                        tt("dve", A64["Lgl"][0:64, :, :], A64["Lg"][0:64, :, :], A64["Lgh"][0:64, :, :], ALU.subtract, ["aLg", "aLgh"], ["aLgl"])
                        yield
                        for (d, c_, it) in items:
                            mm(PS[2][0:64, it * 64:(it + 1) * 64], SUb[d], A64["Lgh"][0:64, it, :], True, False, ["masksb", "aLgh"], [PK[2]])
                            mm(PS[2][0:64, it * 64:(it + 1) * 64], SUb[d], A64["Lgl"][0:64, it, :], False, True, ["masksb", "aLgl"], [PK[2]])
                            mm(PS[3][:, it * 64:(it + 1) * 64], onesb[:, :], A64["Lgh"][0:64, it, :], True, False, ["onesb", "aLgh"], [PK[3]])
                            mm(PS[3][:, it * 64:(it + 1) * 64], onesb[:, :], A64["Lgl"][0:64, it, :], False, True, ["onesb", "aLgl"], [PK[3]])
                        act(A64["dec"][0:64, :, :], PS[2][0:64, :].rearrange("p (n c) -> p n c", c=64), AF.Exp, [PK[2]], ["adec"])
                        act(A64["gbc"][:, :, :], PS[3][:, :].rearrange("p (n c) -> p n c", c=64), AF.Exp, [PK[3]], [KY("agbc")])
                        yield
                        for (d, c_, it) in items:
                            tok = slice(c_ * 64, (c_ + 1) * 64)
                            mm(PS[0][0:64, it * 64:(it + 1) * 64], kT[:, tok], kT[:, tok], True, True, ["XT1"], [PK[0]])
                            mm(PS[1][0:64, it * 64:(it + 1) * 64], kT[:, tok], qT[:, tok], True, True, ["XT1", "XT0"], [PK[1]])
                        for d in range(2):
                            tt("pool", v("Lg", d), v("dec", d), MI[d].unsqueeze(1).to_broadcast([64, NB, 64]), ALU.mult, ["adec", "masks"], ["aLg"])
                            tt("pool", v("dec", d), v("dec", d), MS[d].unsqueeze(1).to_broadcast([64, NB, 64]), ALU.mult, ["adec", "masks"], ["adec"])
                        yield
                        for d in range(2):
                            tt("dve", v("Pm", d), PS[0][0:64, :].rearrange("p (n c) -> p n c", c=64)[:, d * NB:(d + 1) * NB, :],
                               gcols("beta", d).unsqueeze(2).to_broadcast([64, NB, 64]), ALU.mult, [PK[0], "gates"], ["aPm"])
                        tt("dve", A64["dec"][0:64, :, :], A64["Pm"][0:64, :, :], A64["dec"][0:64, :, :], ALU.mult, ["aPm", "adec"], ["adec"])
                        tt("dve", A64["QKT"][0:64, :, :], PS[1][0:64, :].rearrange("p (n c) -> p n c", c=64), A64["Lg"][0:64, :, :], ALU.mult,
                           [PK[1], "aLg"], [KY("aQKT")])
                        tt("dve", A64["Pm"][0:64, :, :], ident[0:64, 0:64].unsqueeze(1).to_broadcast([64, NI, 64]), A64["dec"][0:64, :, :],
                           ALU.subtract, ["ident", "adec"], ["aPm"])
                        cp("pool", A64["Pmb"][0:64, :, :], A64["Pm"][0:64, :, :], ["aPm"], ["aPmb"])
                        cp("act", A64["Qa"][0:64, :, :], A64["dec"][0:64, :, :], ["adec"], ["aQa"])
                        yield
                        for (d, c_, it) in items:
                            mm(PS[4][0:64, it * 64:(it + 1) * 64], A64["Qa"][0:64, it, :], identb[0:64, 0:64], True, True, ["aQa", "identb"], [PK[4]])
                        cp("act", A64["QTa"][0:64, :, :], PS[4][0:64, :].rearrange("p (n c) -> p n c", c=64), [PK[4]], ["aQTa"])
                        yield
                        for (d, c_, it) in items:
                            tok = slice(c_ * 64, (c_ + 1) * 64)
                            mm(PS[2 + d][0:64, (it % NB) * 128:(it % NB + 1) * 128], kT[:, tok], identb[:, :], True, True, ["XT1", "identb"], [PK[2 + d]])
                            mm(PS[4 + d][0:64, (it % NB) * 128:(it % NB + 1) * 128], vT[:, tok], identb[:, :], True, True, ["XT2", "identb"], [PK[4 + d]])
                        for d in range(2):
                            cp("act", v2("K", d), PS[2 + d][0:64, :].rearrange("p (n c) -> p n c", c=128), [PK[2 + d]], [KY("aK")])
                            cp("act", v2("V", d), PS[4 + d][0:64, :].rearrange("p (n c) -> p n c", c=128), [PK[4 + d]], ["aV"])
                        for d in range(2):
                            tt("pool", v2("Kg", d), v2("K", d), gcols("gam", d).unsqueeze(2).to_broadcast([64, NB, 128]), ALU.mult, [KY("aK"), "gates"], ["aKg"])
                            tt("pool", v2("K", d), v2("K", d), gcols("koutc", d).unsqueeze(2).to_broadcast([64, NB, 128]), ALU.mult, [KY("aK"), "gates"], [KY("aK")])
                            cmin = info[d][0]
                            tt("pool", A64["QgT"][:, d * NB:(d + 1) * NB, :], qT[:, cmin * 64:(cmin + NB) * 64].rearrange("p (n c) -> p n c", c=64),
                               A64["gbc"][:, d * NB:(d + 1) * NB, :], ALU.mult, ["XT0", KY("agbc")], [KY("aQgT")])
                        yield
                        Q, QT, Qn, QTn = "Qa", "QTa", "Qb", "QTb"
                        kq = {"Qa": "aQa", "QTa": "aQTa", "Qb": "aQb", "QTb": "aQTb"}
                        for lvl in range(5):
                            for (d, c_, it) in items:
                                mm(PS[0][0:64, it * 64:(it + 1) * 64], A64[QT][0:64, it, :], A64[Q][0:64, it, :], True, True, [kq[Q], kq[QT]], [PK[0]])
                                mm(PS[1][0:64, it * 64:(it + 1) * 64], A64[Q][0:64, it, :], A64[QT][0:64, it, :], True, True, [kq[Q], kq[QT]], [PK[1]])
                            yield
                            cp("act", A64[Qn][0:64, :, :], PS[0][0:64, :].rearrange("p (n c) -> p n c", c=64), [PK[0]], [kq[Qn]])
                            cp("dve", A64[QTn][0:64, :, :], PS[1][0:64, :].rearrange("p (n c) -> p n c", c=64), [PK[1]], [kq[QTn]])
                            for (d, c_, it) in items:
                                mm(PS[5][0:64, it * 64:(it + 1) * 64], A64[QTn][0:64, it, :], A64["Pmb"][0:64, it, :], True, True, [kq[QTn], "aPmb"], [PK[5]])
                            tt("dve", A64["Pm"][0:64, :, :], A64["Pm"][0:64, :, :], PS[5][0:64, :].rearrange("p (n c) -> p n c", c=64), ALU.add,
                               ["aPm", PK[5]], ["aPm"])
                            cp("act", A64["Pmb"][0:64, :, :], A64["Pm"][0:64, :, :], ["aPm"], ["aPmb"])
                            yield
                            Q, QT, Qn, QTn = Qn, QTn, Q, QT
                        yield
                        for (d, c_, it) in items:
                            mm(PS[2 + d][0:64, (it % NB) * 128:(it % NB + 1) * 128], A64["Pmb"][0:64, it, :], A128["V"][0:64, it, :], True, True,
                               ["aPmb", "aV"], [PK[2 + d]])
                            mm(PS[4][:, it * 64:(it + 1) * 64], A128["Kg"][0:64, it, :], A64["Pmb"][0:64, it, :], True, True, ["aKg", "aPmb"], [PK[4]])
                        for d in range(2):
                            tt("dve", v2("u0", d), PS[2 + d][0:64, :].rearrange("p (n c) -> p n c", c=128),
                               gcols("beta", d).unsqueeze(2).to_broadcast([64, NB, 128]), ALU.mult, [PK[2 + d], "gates"], [KY("au0")])
                        act(A64["wT"][:, :, :], PS[4][:, :].rearrange("p (n c) -> p n c", c=64), AF.Identity, [PK[4]], [KY("awT")], scale=-1.0)

                    def stageC_gen(i, d):
                        b = i // NB
                        info = batch_info(b)
                        par = b % 2
                        A64, A128 = A64v[par], A128v[par]
                        KY = lambda s_: s_ + str(par)
                        c_ = FORD[i] if d == 0 else BORD[i]
                        it = d * NB + (c_ - info[d][0])
                        e8 = d * 4 + h
                        col = lambda nm: G_[nm][0:64, c_, e8:e8 + 1]
                        pC = PS[6 + d]
                        kC = PK[6 + d]
                        Sk = "S%d" % d
                        S_ = Sst[d]
                        ub = ubuf[(i % 2) * 2 + d]
                        uk = "u%d" % ((i % 2) * 2 + d)
                        mm(pC[0:64, 0:128], A64["wT"][:, it, :], S_[:, :], True, True, [KY("awT"), Sk], [kC])
                        yield
                        stt("dve", ub[0:64, :], pC[0:64, 0:128], col("beta"), A128["u0"][0:64, it, :], ALU.mult, ALU.add, [kC, "gates", KY("au0")], [uk])
                        yield
                        if c_ >= 4:
                            cl = c_ - 4
                            mm(pC[0:64, 128:256], A64["QgT"][:, it, :], S_[:, :], True, False, [KY("aQgT"), Sk], [kC])
                            mm(pC[0:64, 128:256], A64["QKT"][0:64, it, :], ub[0:64, :], False, True, [KY("aQKT"), uk], [kC])
                        mm(pC[:, 256:384], A128["K"][0:64, it, :], ub[0:64, :], True, True, [KY("aK"), uk], [kC])
                        yield
                        lastcol = 63 if d == 0 else 0
                        stt("dve", S_[:, :], S_[:, :], A64["gbc"][:, it, lastcol:lastcol + 1], pC[:, 256:384], ALU.mult, ALU.add,
                            [Sk, KY("agbc"), kC], [Sk])
                        yield
                        if c_ >= 4:
                            firstw = (d == 0 and cl < 16) or (d == 1 and cl >= 16)
                            ok = "O%d" % cl
                            if firstw:
                                cp("act", Oacc[0:64, cl, :], pC[0:64, 128:256], [kC], [ok])
                            else:
                                tt("dve", Oacc[0:64, cl, :], Oacc[0:64, cl, :], pC[0:64, 128:256], ALU.add, [kC, ok], [ok])
                        yield

                    def chain_d(d, steps):
                        for i in steps:
                            yield from stageC_gen(i, d)

                    def stageC_steps(steps):
                        alive = [chain_d(0, steps), chain_d(1, steps)]
                        while alive:
                            for g_ in list(alive):
                                try:
                                    next(g_)
                                except StopIteration:
                                    alive.remove(g_)

                    def chain_both(steps):
                        alive = [chain_d(0, steps), chain_d(1, steps)]
                        while alive:
                            for g_ in list(alive):
                                try:
                                    next(g_)
                                except StopIteration:
                                    alive.remove(g_)
                            yield

                    def run_all(g_):
                        for _ in g_:
                            pass
                    NBT = 36 // NB
                    run_all(stageA_batch(0))
                    for b in range(NBT):
                        gC = chain_both(list(range(b * NB, (b + 1) * NB)))
                        gA = stageA_batch(b + 1) if b + 1 < NBT else iter(())
                        doneA = doneC = False
                        while not (doneA and doneC):
                            for _ in range(2):
                                if not doneA:
                                    try:
                                        next(gA)
                                    except StopIteration:
                                        doneA = True
                            if not doneC:
                                try:
                                    next(gC)
                                except StopIteration:
                                    doneC = True
                    if s == 0 and h == 0:
                        dump("O0", Oacc[0:64, 0, :], ["O0"])
                        dump("O31", Oacc[0:64, 31, :], ["O31"])
                    wb = load_w(2560 + h * 128)
                    for blk in range(8):
                        par_ = blk % 2
                        okeys = ["O%d" % (blk * 4 + cc) for cc in range(4)]
                        O4 = Oacc[0:64, blk * 4:blk * 4 + 4, :]
                        sq4 = XTf[0:64, par_ * 512:(par_ + 1) * 512].rearrange("p (c v) -> p c v", v=128)
                        rs_ = rsb[0:64, 4 * blk:4 * blk + 4]
                        ksq, krs = "fsq%d" % par_, "rs%d" % blk
                        tt("pool", sq4, O4, O4, ALU.mult, okeys + ["XTf"], [ksq])
                        P.op("dve", (lambda sq4=sq4, rs_=rs_: lambda e: e.tensor_reduce(out=rs_, in_=sq4, axis=AX.X, op=ALU.add))(),
                             reads=[ksq, "XTf"], writes=[krs])
                        ts("dve", rs_, rs_, 1.0 / 128.0, ALU.mult, [krs], [krs], s2=1e-6, op1=ALU.add)
                        act(rs_, rs_, AF.Sqrt, [krs], [krs])
                        P.op("dve", (lambda rs_=rs_: lambda e: e.reciprocal(out=rs_, in_=rs_))(), reads=[krs], writes=[krs])
                    for blk in range(8):
                        par_ = blk % 2
                        okeys = ["O%d" % (blk * 4 + cc) for cc in range(4)]
                        O4 = Oacc[0:64, blk * 4:blk * 4 + 4, :]
                        z4 = XTf[0:64, 1024 + par_ * 512:1024 + (par_ + 1) * 512].rearrange("p (c v) -> p c v", v=128)
                        rs_ = rsb[0:64, 4 * blk:4 * blk + 4]
                        kz, krs = "fz%d" % par_, "rs%d" % blk
                        pz, pt = (1, 2) if par_ == 0 else (3, 4)
                        for cc in range(4):
                            tk = slice(LCTX + (blk * 4 + cc) * 64, LCTX + (blk * 4 + cc + 1) * 64)
                            for k in range(8):
                                mm(PS[pz][0:64, cc * 128:(cc + 1) * 128], hT[:, k, tk], wbf[wb][:, k, :], k == 0, k == 7, ["hT", "wbf%d" % wb], [PK[pz]])
                        act(z4, PS[pz][0:64, :].rearrange("p (c v) -> p c v", v=128), AF.Silu, [PK[pz], "XTf"], [kz])
                        tt("dve", O4, O4, rs_.unsqueeze(2).to_broadcast([64, 4, 128]), ALU.mult, okeys + [krs], okeys)
                        tt("pool", O4, O4, normw[0:64, :].unsqueeze(1).to_broadcast([64, 4, 128]), ALU.mult, okeys + ["normw"], okeys)
                        tt("dve", O4, O4, z4, ALU.mult, okeys + [kz, "XTf"], okeys)
                        for cc in range(4):
                            tr(PS[pt][:, cc * 64:(cc + 1) * 64], Oacc[0:64, blk * 4 + cc, :], ident[0:64, 0:64], okeys + ["ident"], [PK[pt]])
                        cp("act", mixT[:, 4 + h, blk * 256:(blk + 1) * 256], PS[pt][:, 0:256], [PK[pt]], ["mixG"])
                    if s == 0 and h == 0:
                        dump("mix_g", mixT[:, 4, 0:512], ["mixG"])

            if "out" in stages:
                barrier()
                o = [0]

                def carve(n, dt=F32):
                    words = n if dt == F32 else (n + 1) // 2
                    a = arena[:, o[0]:o[0] + words]
                    o[0] += words
                    assert o[0] <= ARENA_WORDS, o[0]
                    return a.bitcast(BF16)[:, 0:n] if dt == BF16 else a
                woutb = carve(8192, BF16).rearrange("p (k c) -> p k c", k=8)
                gate_bc = carve(1024)
                lng = carve(1024)
                lnb = carve(1024)
                wg = carve(4096).rearrange("p (k c) -> p k c", k=8)
                silucb = carve(1024).rearrange("p (k c) -> p k c", k=8)
                rbuf = [carve(1024) for _ in range(2)]
                stats = carve(16)
                dma(lng, D["lng"][:, :], w=["lng"])
                dma(lnb, D["lnb"][:, :], w=["lnb"])
                dma(gate_bc, D["bgate_bc"][:, :], w=["gate_bc"])
                wout_v = D["w_out"].rearrange("(k p) c -> p k c", p=128)
                for k in range(8):
                    dma(xst[k % 2][:], wout_v[:, k, :], w=["xst%d" % (k % 2)])
                    cp("act", woutb[:, k, :], xst[k % 2][:], ["xst%d" % (k % 2)], ["woutb"])
                for k in range(8):
                    ts("dve", silucb[:, k, :], ones[:, :], siluc[:, k, s:s + 1], ALU.mult, ["ones", "siluc"], ["silucb"])
                for half in range(2):
                    dma(wg[:], wada[:, :, 2048 + half * 512:2048 + (half + 1) * 512], w=["wg"])
                    for k in range(8):
                        mm(PS[3][:, :], silucb[:, k, :], wg[:, k, :], k == 0, k == 7, ["silucb", "wg"], [PK[3]])
                    tt("dve", gate_bc[:, half * 512:(half + 1) * 512], gate_bc[:, half * 512:(half + 1) * 512], PS[3][:, :], ALU.add,
                       ["gate_bc", PK[3]], ["gate_bc"])
                if s == 0:
                    dump("gate", gate_bc[:, 0:512], ["gate_bc"])
                for ti in range(16):
                    b = ti % 2
                    rk = "r%d" % b
                    dma(xst[b][:], D["x2"][s, ti * 128:(ti + 1) * 128, :], w=["xst%d" % b])
                    for half in range(2):
                        pk = 4 + half + 2 * b
                        for k in range(8):
                            mm(PS[pk][:, :], mixT[:, k, ti * 128:(ti + 1) * 128], woutb[:, k, half * 512:(half + 1) * 512], k == 0, k == 7,
                               ["mixT", "woutb"], [PK[pk]])
                        tt("dve", rbuf[b][:, half * 512:(half + 1) * 512], PS[pk][:, :], gate_bc[:, half * 512:(half + 1) * 512], ALU.mult,
                           [PK[pk], "gate_bc"], [rk])
                    stt("pool", rbuf[b][:, :], xst[b][:, :], DEEP_ALPHA, rbuf[b][:, :], ALU.mult, ALU.add, ["xst%d" % b, rk], [rk])
                    st6 = stats[:, 0:12].rearrange("p (c e) -> p c e", e=6)
                    for half in range(2):
                        P.op("dve", (lambda b=b, half=half, st6=st6: lambda e: e.bn_stats(out=st6[:, half, :], in_=rbuf[b][:, half * 512:(half + 1) * 512]))(),
                             reads=[rk], writes=["stats"])
                    P.op("dve", (lambda st6=st6: lambda e: e.bn_aggr(out=stats[:, 12:14], in_=st6))(), reads=["stats"], writes=["mv"])
                    act(stats[:, 14:15], stats[:, 13:14], AF.Sqrt, ["mv"], ["mv2"], bias=1e-5)
                    P.op("dve", lambda e: e.reciprocal(out=stats[:, 14:15], in_=stats[:, 14:15]), reads=["mv2"], writes=["mv2"])
                    stt("dve", stats[:, 15:16], stats[:, 12:13], -1.0, stats[:, 14:15], ALU.mult, ALU.mult, ["mv", "mv2"], ["mv2"])
                    act(rbuf[b][:, :], rbuf[b][:, :], AF.Identity, [rk, "mv2"], [rk], bias=stats[:, 15:16], scale=stats[:, 14:15])
                    tt("pool", rbuf[b][:, :], rbuf[b][:, :], lng, ALU.mult, [rk, "lng"], [rk])
                    tt("dve", rbuf[b][:, :], rbuf[b][:, :], lnb, ALU.add, [rk, "lnb"], [rk])
                    dma(yout[s, ti * 128:(ti + 1) * 128, :], rbuf[b][:, :], r=[rk])
        P.emit(nc)
    return nc, P


def _core_inputs(inp, sh, core):
    m = dict(sh)
    f = lambda a: np.ascontiguousarray(np.asarray(a, dtype=np.float32))
    b0 = core * NSEQ
    m["x2"] = f(inp["x"][b0:b0 + NSEQ])
    m["ctx2"] = f(inp["ctx"][b0:b0 + NSEQ])
    cc = np.stack([np.asarray(inp["c"][b0], np.float32), np.asarray(inp["c"][b0 + 1], np.float32),
                   np.asarray(inp["c_ctx"], np.float32)], axis=0)
    m["cT"] = f(cc.reshape(3, 8, 128).transpose(2, 1, 0))
    return m


_CACHE = {}


def kernel(**inputs):
    if "nc" not in _CACHE:
        _CACHE["nc"] = build_program()[0]
    nc = _CACHE["nc"]
    sh = _prep_shared(inputs)
    maps = [_core_inputs(inputs, sh, c) for c in range(8)]
    res = run_bass_kernel_spmd(nc, maps, core_ids=list(range(8)))
    out = np.concatenate([r["yout"] for r in res.results], axis=0)
    return out.astype(np.float32)
```

```python
import contextlib
import math
import numpy as np
import concourse.bass as bass
import concourse.mybir as mybir
from concourse.bass_utils import run_bass_kernel_spmd

F32 = mybir.dt.float32
BF16 = mybir.dt.bfloat16
ALU = mybir.AluOpType
AF = mybir.ActivationFunctionType
AX = mybir.AxisListType

ENG_NAMES = ["pe", "act", "dve", "pool", "sp"]
SAME_ENGINE_SYNC = True
N_DMA_SEMS = 12

NSEQ = 2
LCTX = 256
LLAT = 2048
LTOT = LCTX + LLAT
DEEP_ALPHA = 2.0 ** 0.25


class Prog:
    def __init__(self):
        self.ops = []
        self.last_w = {}
        self.readers = {}
        self.barrier_idx = None

    def barrier(self, eng, fn):
        deps = set(self.last_w.values())
        for v in self.readers.values():
            deps.update(v)
        if self.barrier_idx is not None:
            deps.add(self.barrier_idx)
        idx = len(self.ops)
        self.ops.append(dict(eng=eng, fn=fn, deps=deps, dma=False))
        self.barrier_idx = idx
        self.last_w = {}
        self.readers = {}
        return idx

    def op(self, eng, fn, reads=(), writes=(), dma=False):
        writes = list(writes) + [k for k in reads if k.startswith("ps")]
        idx = len(self.ops)
        deps = set()
        if self.barrier_idx is not None:
            deps.add(self.barrier_idx)
        for k in reads:
            if k in self.last_w:
                deps.add(self.last_w[k])
        for k in writes:
            if k in self.last_w:
                deps.add(self.last_w[k])
            deps.update(self.readers.get(k, ()))
        self.ops.append(dict(eng=eng, fn=fn, deps=deps, dma=dma))
        for k in reads:
            self.readers.setdefault(k, []).append(idx)
        for k in writes:
            self.last_w[k] = idx
            self.readers[k] = []
        return idx

    def emit(self, nc):
        ops = self.ops
        pos = {}
        seqcount = {e: 0 for e in ENG_NAMES}
        for i, o in enumerate(ops):
            if not o["dma"]:
                seqcount[o["eng"]] += 1
                pos[i] = seqcount[o["eng"]]
        dma_ops = [i for i, o in enumerate(ops) if o["dma"]]
        dma_sem = {}
        dma_val = {}
        semcnt = [0] * N_DMA_SEMS
        prev_on_sem = {}
        for n, i in enumerate(dma_ops):
            s = n % N_DMA_SEMS
            semcnt[s] += 16
            dma_sem[i] = s
            dma_val[i] = semcnt[s]
            if s in prev_on_sem:
                ops[i]["deps"].add(prev_on_sem[s])
            prev_on_sem[s] = i
        known = {e: {p: 0 for p in ENG_NAMES} for e in ENG_NAMES}
        known_dma = {e: [0] * N_DMA_SEMS for e in ENG_NAMES}
        flagged = set()
        waits = [[] for _ in ops]
        for i, o in enumerate(ops):
            e = o["eng"]
            for d in sorted(o["deps"]):
                od = ops[d]
                if od["dma"]:
                    s = dma_sem[d]
                    if dma_val[d] > known_dma[e][s]:
                        known_dma[e][s] = dma_val[d]
                        waits[i].append(("dma", s, dma_val[d]))
                else:
                    p = od["eng"]
                    if p == e and (e == "pe" or not SAME_ENGINE_SYNC):
                        continue
                    if pos[d] > known[e][p]:
                        known[e][p] = pos[d]
                        flagged.add(d)
                        waits[i].append(("eng", p, d))
        cnt = {e: 0 for e in ENG_NAMES}
        val = {}
        for i, o in enumerate(ops):
            if i in flagged:
                cnt[o["eng"]] += 1
                val[i] = cnt[o["eng"]]
        per_eng = {e: [] for e in ENG_NAMES}
        for i, o in enumerate(ops):
            per_eng[o["eng"]].append(i)
        self.stats = {e: len(per_eng[e]) for e in ENG_NAMES}

        with contextlib.ExitStack() as st:
            esem = {e: st.enter_context(nc.semaphore("s_" + e)) for e in ENG_NAMES}
            dsem = [st.enter_context(nc.semaphore("d_%d" % k)) for k in range(N_DMA_SEMS)]
            block = st.enter_context(nc.Block())

            def run(e, engobj):
                for i in per_eng[e]:
                    o = ops[i]
                    mx = {}
                    for w in waits[i]:
                        if w[0] == "dma":
                            key = ("d", w[1]); v = w[2]
                        else:
                            key = ("e", w[1]); v = val[w[2]]
                        mx[key] = max(mx.get(key, 0), v)
                    for key, v in mx.items():
                        sem = dsem[key[1]] if key[0] == "d" else esem[key[1]]
                        engobj.wait_ge(sem, v)
                    ins = o["fn"](engobj)
                    if o["dma"]:
                        ins.then_inc(dsem[dma_sem[i]], 16)
                    elif i in flagged:
                        ins.then_inc(esem[e], 1)
                if e == "sp":
                    for s in range(N_DMA_SEMS):
                        if semcnt[s] > 0:
                            engobj.wait_ge(dsem[s], semcnt[s])

            @block.tensor
            def _(eng):
                run("pe", eng)

            @block.scalar
            def _(eng):
                run("act", eng)

            @block.vector
            def _(eng):
                run("dve", eng)

            @block.gpsimd
            def _(eng):
                run("pool", eng)

            @block.sync
            def _(eng):
                run("sp", eng)


def _consts():
    c = {}
    c["ident"] = np.eye(128, dtype=np.float32)
    k = np.arange(64)
    tri = np.stack([(k[:, None] <= k[None, :]), (k[:, None] >= k[None, :])]).astype(np.float32)
    su = np.stack([(k[:, None] > k[None, :]), (k[:, None] < k[None, :])]).astype(np.float32)
    mi = np.stack([(k[None, :] >= k[:, None]), (k[None, :] <= k[:, None])]).astype(np.float32)
    ms = np.stack([(k[None, :] > k[:, None]), (k[None, :] < k[:, None])]).astype(np.float32)
    c["masks"] = np.concatenate([tri[0], tri[1], su[0], su[1], mi[0], mi[1], ms[0], ms[1]], axis=1).astype(np.float32)
    c["ones"] = np.ones((128, 128), np.float32)
    return c


def _prep_shared(inp):
    f = lambda a: np.ascontiguousarray(np.asarray(a, dtype=np.float32))
    sh = {}
    sh["w_ada"] = f(inp["w_ada"][0])
    b_ada = f(inp["b_ada"][0])
    sh["bcol"] = f(b_ada.reshape(24, 128).T)
    sh["bgate_bc"] = f(np.broadcast_to(b_ada[2048:3072][None, :], (128, 1024)))
    w_in = f(inp["w_in"][0])
    sh["w_in"] = w_in
    sh["wgate"] = f(w_in[:, 3072:3088].reshape(8, 128, 16).transpose(1, 0, 2))
    lre = f(inp["s5_lambda_re"][0]); lim = f(inp["s5_lambda_im"][0]); ldt = f(inp["s5_log_dt"][0])

    def smaj(a):
        o = np.zeros((128, 32), np.float32)
        for pair in range(16):
            for d in range(2):
                for gl in range(2):
                    o[gl * 64:(gl + 1) * 64, pair * 2 + d] = a[d, 2 * pair + gl, :]
        return o
    sh["lam_re"] = smaj(lre)
    sh["lam_im"] = smaj(lim)
    sh["logdt"] = smaj(np.broadcast_to(ldt[:, :, None], (2, 32, 64)))
    bre = f(inp["s5_b_re"][0]); bim = f(inp["s5_b_im"][0])
    cre = f(inp["s5_c_re"][0]); cim = f(inp["s5_c_im"][0])
    Bp = np.zeros((128, 4, 4, 2, 2, 128), np.float32)
    Cp = np.zeros((128, 4, 4, 2, 2, 128), np.float32)
    for j in range(4):
        for gp in range(4):
            for gl in range(2):
                gq = 2 * gp + gl
                g = 8 * j + gq
                for d in range(2):
                    Bp[gq * 16:(gq + 1) * 16, j, gp, d, 0, gl * 64:(gl + 1) * 64] = bre[d, g].T
                    Bp[gq * 16:(gq + 1) * 16, j, gp, d, 1, gl * 64:(gl + 1) * 64] = bim[d, g].T
                    Cp[gl * 64:(gl + 1) * 64, j, gp, d, 0, gq * 16:(gq + 1) * 16] = cre[d, g].T
                    Cp[gl * 64:(gl + 1) * 64, j, gp, d, 1, gq * 16:(gq + 1) * 16] = cim[d, g].T
    sh["Bp"] = f(Bp.reshape(128, 8192))
    sh["Cp"] = f(Cp.reshape(128, 8192))
    sh["dcol"] = f(inp["s5_d"][0].reshape(4, 128).T)
    sh["wglu"] = f(inp["w_glu"][0].reshape(4, 128, 512).transpose(1, 0, 2))
    sh["bglu"] = f(inp["b_glu"][0].reshape(4, 128).T)
    sh["convw"] = f(inp["conv_w"][0].reshape(9, 12, 128).transpose(2, 1, 0))
    sh["alog"] = f(np.broadcast_to(inp["gdn_a_log"][0].reshape(1, 8), (64, 8)))
    sh["dtb"] = f(np.broadcast_to(inp["gdn_dt_bias"][0].reshape(1, 8), (64, 8)))
    sh["normw"] = f(np.broadcast_to(inp["gdn_norm_w"][0].reshape(1, 128), (64, 128)))
    sh["w_out"] = f(inp["w_out"][0])
    sh["lng"] = f(np.broadcast_to(inp["ln_g"][0][None, :], (128, 1024)))
    sh["lnb"] = f(np.broadcast_to(inp["ln_b"][0][None, :], (128, 1024)))
    sh.update(_consts())
    return sh


IN_SHAPES = {
    "x2": [NSEQ, LLAT, 1024], "ctx2": [NSEQ, LCTX, 1024], "cT": [128, 8, 3],
    "w_ada": [1024, 3072], "bcol": [128, 24], "bgate_bc": [128, 1024], "w_in": [1024, 3088],
    "wgate": [128, 8, 16], "lam_re": [128, 32], "lam_im": [128, 32], "logdt": [128, 32],
    "Bp": [128, 8192], "Cp": [128, 8192], "dcol": [128, 4], "wglu": [128, 4, 512], "bglu": [128, 4],
    "convw": [128, 12, 9], "alog": [64, 8], "dtb": [64, 8], "normw": [64, 128],
    "w_out": [1024, 1024], "lng": [128, 1024], "lnb": [128, 1024],
    "ident": [128, 128], "masks": [64, 512], "ones": [128, 128],
}


def build_program(dbg=None, stages=("s5", "gdn", "out")):
    nc = bass.Bass("TRN2", target_bir_lowering=False)
    D = {k: nc.dram_tensor(k, v, F32, kind="ExternalInput").ap() for k, v in IN_SHAPES.items()}
    yout = nc.dram_tensor("yout", [NSEQ, LLAT, 1024], F32, kind="ExternalOutput").ap()
    dbg_t = {}
    if dbg:
        for k, shp in dbg.items():
            dbg_t[k] = nc.dram_tensor("dbg_" + k, shp, F32, kind="ExternalOutput").ap()
    P = Prog()
    st = contextlib.ExitStack()
    uid = [0]

    def sb(name, shape, dt=F32):
        return st.enter_context(nc.sbuf_tensor("sb_" + name, shape, dt))

    def psum(name):
        return st.enter_context(nc.psum_tensor(name, [128, 512], F32))

    with st:
        hT = sb("hT", [128, 8, LTOT], BF16)
        mixT = sb("mixT", [128, 8, LLAT], BF16)
        ident = sb("ident", [128, 128])
        ones = sb("ones", [128, 128])
        masks = sb("masks", [64, 512])
        identb = sb("identb", [128, 128], BF16)
        masksb = sb("masksb", [64, 512], BF16)
        onesb = sb("onesb", [64, 128], BF16)
        xst = [sb("xst%d" % i, [128, 1024]) for i in range(2)]
        wst = [sb("wst%d" % i, [128, 8, 128]) for i in range(2)]
        wbf = [sb("wbf%d" % i, [128, 8, 128], BF16) for i in range(2)]
        modsc = sb("modsc", [128, 16, 3])
        siluc = sb("siluc", [128, 8, 3])
        dbgs = sb("dbgs", [128, 512]) if dbg else None
        bcol = sb("bcol", [128, 24])
        small = sb("small", [128, 64])
        ARENA_WORDS = 28800
        arena = sb("arena", [128, ARENA_WORDS])
        PS = [psum("ps%d" % i) for i in range(8)]
        PK = ["ps%d" % i for i in range(8)]

        def dma(out, in_, r=(), w=()):
            P.op("sp", lambda e: e.dma_start(out=out, in_=in_), reads=r, writes=w, dma=True)

        def act(out, in_, func, r, w, bias=None, scale=None):
            kw = {}
            if bias is not None:
                kw["bias"] = bias
            if scale is not None:
                kw["scale"] = scale
            P.op("act", lambda e: e.activation(out=out, in_=in_, func=func, **kw), reads=r, writes=w)

        def tt(eng, out, in0, in1, op, r, w):
            P.op(eng, lambda e: e.tensor_tensor(out=out, in0=in0, in1=in1, op=op), reads=r, writes=w)

        def ts(eng, out, in0, s1, op0, r, w, s2=None, op1=None):
            if op1 is None:
                P.op(eng, lambda e: e.tensor_scalar(out=out, in0=in0, scalar1=s1, scalar2=None, op0=op0), reads=r, writes=w)
            else:
                P.op(eng, lambda e: e.tensor_scalar(out=out, in0=in0, scalar1=s1, scalar2=s2, op0=op0, op1=op1), reads=r, writes=w)

        def stt(eng, out, in0, scalar, in1, op0, op1, r, w):
            eng = "dve"
            P.op(eng, lambda e: e.scalar_tensor_tensor(out=out, in0=in0, scalar=scalar, in1=in1, op0=op0, op1=op1), reads=r, writes=w)

        def cp(eng, out, in_, r, w):
            if eng == "act":
                act(out, in_, AF.Copy, r, w)
            else:
                P.op(eng, lambda e: e.tensor_copy(out=out, in_=in_), reads=r, writes=w)

        def mm(out, lhsT, rhs, start, stop, r, w):
            P.op("pe", lambda e: e.matmul(out, lhsT=lhsT, rhs=rhs, start=start, stop=stop), reads=r, writes=w)

        def tr(out, in_, idn, r, w):
            P.op("pe", lambda e: e.transpose(out, in_, idn), reads=r, writes=w)

        def dump(name, ap, r):
            if name in dbg_t:
                if ap.dtype == BF16:
                    n = ap.shape[-1]
                    cp("dve", dbgs[:, 0:n], ap, r, ["dbgs"])
                    dma(dbg_t[name], dbgs[:, 0:n], r=["dbgs"])
                else:
                    dma(dbg_t[name], ap, r=r)

        def barrier():
            P.barrier("dve", lambda e: e.memset(small[:, 63:64], 0.0))

        dma(ident[:], D["ident"][:, :], w=["ident"])
        dma(ones[:], D["ones"][:, :], w=["ones"])
        dma(masks[:], D["masks"][:, :], w=["masks"])
        dma(bcol[:], D["bcol"][:, :], w=["bcol"])
        cp("dve", identb[:], ident[:], ["ident"], ["identb"])
        cp("dve", masksb[:], masks[:], ["masks"], ["masksb"])
        cp("dve", onesb[:], ones[0:64, :], ["ones"], ["onesb"])
        SUb = [masksb[:, 128:192], masksb[:, 192:256]]
        TRI = [masks[:, 0:64], masks[:, 64:128]]
        SU = [masks[:, 128:192], masks[:, 192:256]]
        MI = [masks[:, 256:320], masks[:, 320:384]]
        MS = [masks[:, 384:448], masks[:, 448:512]]

        dma(siluc[:], D["cT"][:, :, :], w=["siluc"])
        act(siluc[:], siluc[:], AF.Silu, ["siluc"], ["siluc"])
        wada = D["w_ada"].rearrange("(k p) c -> p k c", p=128)
        for t in range(16):
            b = t % 2
            dma(wst[b][:], wada[:, :, t * 128:(t + 1) * 128], w=["wst%d" % b])
            for k in range(8):
                mm(PS[0][:, t * 4:t * 4 + 3], wst[b][:, k, :], siluc[:, k, :], k == 0, k == 7, ["wst%d" % b, "siluc"], [PK[0]])
        for t in range(16):
            ts("dve", modsc[:, t, :], PS[0][:, t * 4:t * 4 + 3], bcol[:, t:t + 1], ALU.add, [PK[0], "bcol"], ["modsc"],
               s2=(1.0 if t >= 8 else 0.0), op1=ALU.add)

        win = D["w_in"].rearrange("(k p) c -> p k c", p=128)
        wcount = [0]

        def load_w(c0, ncols=128):
            b = wcount[0] % 2
            wcount[0] += 1
            dma(wst[b][:, :, 0:ncols], win[:, :, c0:c0 + ncols], w=["wst%d" % b])
            cp("pool", wbf[b][:, :, 0:ncols], wst[b][:, :, 0:ncols], ["wst%d" % b], ["wbf%d" % b])
            return b

        def proj_fm(b, ps_ap, tok0, ntok, pk):
            for k in range(8):
                mm(ps_ap, wbf[b][:, k, :], hT[:, k, tok0:tok0 + ntok], k == 0, k == 7, ["wbf%d" % b, "hT"], [pk])

        TOKBLK = [(0, 256), (256, 512), (768, 512), (1280, 512), (1792, 512)]

        for s in range(NSEQ):
            barrier()
            for tt_i in range(18):
                b = tt_i % 2
                if tt_i < 2:
                    src = D["ctx2"][s, tt_i * 128:(tt_i + 1) * 128, :]
                    jcol = 2
                else:
                    src = D["x2"][s, (tt_i - 2) * 128:(tt_i - 1) * 128, :]
                    jcol = s
                dma(xst[b][:], src, w=["xst%d" % b])
                for half in range(2):
                    pk = 1 + half
                    for kk in range(4):
                        k = half * 4 + kk
                        tr(PS[pk][:, kk * 128:(kk + 1) * 128], xst[b][:, k * 128:(k + 1) * 128], ident[:], ["xst%d" % b, "ident"], [PK[pk]])
                    for kk in range(4):
                        k = half * 4 + kk
                        act(hT[:, k, tt_i * 128:(tt_i + 1) * 128], PS[pk][:, kk * 128:(kk + 1) * 128], AF.Identity,
                            [PK[pk], "modsc"], ["hT"], bias=modsc[:, k, jcol:jcol + 1], scale=modsc[:, 8 + k, jcol:jcol + 1])
            if s == 0:
                dump("hT", hT[:, 0, 0:512], ["hT"])

            if "s5" in stages:
                barrier()
                o = [0]

                def carve(n, dt=F32):
                    words = n if dt == F32 else (n + 1) // 2
                    a = arena[:, o[0]:o[0] + words]
                    o[0] += words
                    assert o[0] <= ARENA_WORDS, o[0]
                    return a.bitcast(BF16)[:, 0:n] if dt == BF16 else a
                Bpb = carve(2048, BF16)
                Cpb = carve(2048, BF16)
                uT = carve(4 * LTOT, BF16).rearrange("p (j t) -> p j t", j=4)
                o_hb = o[0]
                HbD = [[carve(LTOT) for _ in range(2)] for _ in range(2)]
                HbB = [[carve(LLAT, BF16) for _ in range(2)] for _ in range(2)]
                XaD = [[carve(288) for _ in range(2)] for _ in range(2)]
                XbD = [[carve(288) for _ in range(2)] for _ in range(2)]
                prm = carve(32 * 40).rearrange("p (q c) -> p q c", c=32)
                prm2 = carve(32 * 28).rearrange("p (q c) -> p q c", c=32)
                ytmp = carve(512)
                LR, LI, DT, ER, TH, AR, AI, CR, CI, T0, T1, T2 = range(12)
                PW = 13
                NP = PW + 16
                P.op("dve", lambda e: e.memset(prm[:, :, :], 0.0), writes=["prm"])
                for q_, nm in ((LR, "lam_re"), (LI, "lam_im"), (DT, "logdt")):
                    dma(prm[:, q_, :], D[nm][:, :], w=["prm"])

                def ptt(dst, a_, b_, op):
                    tt("dve", prm[:, dst, :], prm[:, a_, :], prm[:, b_, :], op, ["prm"], ["prm"])

                def pts(dst, a_, s1, op0, s2=None, op1=None, eng="dve"):
                    ts(eng, prm[:, dst, :], prm[:, a_, :], s1, op0, ["prm"], ["prm"], s2=s2, op1=op1)
                act(prm[:, DT, :], prm[:, DT, :], AF.Exp, ["prm"], ["prm"])
                ptt(ER, LR, DT, ALU.mult)
                act(prm[:, ER, :], prm[:, ER, :], AF.Exp, ["prm"], ["prm"])
                ptt(TH, LI, DT, ALU.mult)
                I32 = mybir.dt.int32
                kint = prm[:, 12, :].bitcast(I32)
                PI_C = 3.14159
                for (dst, shift_) in ((AI, 0.0), (AR, 0.5 * math.pi)):
                    pts(T0, TH, 1.0 / (2 * math.pi), ALU.mult, s2=shift_ / (2 * math.pi), op1=ALU.add)
                    P.op("dve", lambda e: e.tensor_copy(out=kint, in_=prm[:, T0, :]), reads=["prm"], writes=["prm"])
                    P.op("dve", lambda e: e.tensor_copy(out=prm[:, T1, :], in_=kint), reads=["prm"], writes=["prm"])
                    pts(T2, TH, shift_, ALU.add)
                    stt("dve", prm[:, T0, :], prm[:, T1, :], -2 * math.pi, prm[:, T2, :], ALU.mult, ALU.add, ["prm"], ["prm"])
                    pts(T0, T0, -PI_C, ALU.max, s2=PI_C, op1=ALU.min)
                    act(prm[:, dst, :], prm[:, T0, :], AF.Sin, ["prm"], ["prm"])
                ptt(AR, AR, ER, ALU.mult)
                ptt(AI, AI, ER, ALU.mult)
                pts(T0, AR, -1.0, ALU.add)
                ptt(T1, T0, LR, ALU.mult)
                ptt(T2, AI, LI, ALU.mult)
                ptt(CR, T1, T2, ALU.add)
                ptt(T1, AI, LR, ALU.mult)
                ptt(T2, T0, LI, ALU.mult)
                ptt(CI, T1, T2, ALU.subtract)
                ptt(T1, LR, LR, ALU.mult)
                ptt(T2, LI, LI, ALU.mult)
                ptt(T1, T1, T2, ALU.add)
                P.op("dve", lambda e: e.reciprocal(out=prm[:, T1, :], in_=prm[:, T1, :]), reads=["prm"], writes=["prm"])
                ptt(CR, CR, T1, ALU.mult)
                ptt(CI, CI, T1, ALU.mult)

                def cmul(dst_r, dst_i, ar_, ai_, br_, bi_):
                    ptt(T0, ar_, br_, ALU.mult)
                    ptt(T1, ai_, bi_, ALU.mult)
                    ptt(T2, ar_, bi_, ALU.mult)
                    ptt(dst_r, T0, T1, ALU.subtract)
                    ptt(T0, ai_, br_, ALU.mult)
                    ptt(dst_i, T2, T0, ALU.add)
                pts(PW, AR, 1.0, ALU.mult)
                pts(PW + 1, AI, 1.0, ALU.mult)
                for k in range(2, 9):
                    cmul(PW + 2 * (k - 1), PW + 2 * (k - 1) + 1, PW + 2 * (k - 2), PW + 2 * (k - 2) + 1, AR, AI)
                for k in range(1, 9):
                    pts(NP + k - 1, PW + 2 * (k - 1) + 1, -1.0, ALU.mult)
                ts("dve", prm2[:, 0, :], prm[:, PW + 14, :], 1.0, ALU.mult, ["prm"], ["prm2"])
                ts("dve", prm2[:, 1, :], prm[:, PW + 15, :], 1.0, ALU.mult, ["prm"], ["prm2"])
                for m in range(9):
                    if m > 0:
                        r0, i0_ = prm2[:, 3 * (m - 1), :], prm2[:, 3 * (m - 1) + 1, :]
                        tt("dve", prm[:, T0, :], r0, r0, ALU.mult, ["prm2", "prm"], ["prm"])
                        tt("dve", prm[:, T1, :], i0_, i0_, ALU.mult, ["prm2", "prm"], ["prm"])
                        tt("dve", prm2[:, 3 * m, :], prm[:, T0, :], prm[:, T1, :], ALU.subtract, ["prm"], ["prm2"])
                        tt("dve", prm[:, T0, :], r0, i0_, ALU.mult, ["prm2", "prm"], ["prm"])
                        ts("dve", prm2[:, 3 * m + 1, :], prm[:, T0, :], 2.0, ALU.mult, ["prm"], ["prm2"])
                    ts("dve", prm2[:, 3 * m + 2, :], prm2[:, 3 * m + 1, :], -1.0, ALU.mult, ["prm2"], ["prm2"])
                if s == 0:
                    dump("prm", prm[:, 0:40, :], ["prm"])
                for j in range(4):
                    wb = load_w(j * 128)
                    for bi, (t0, nt) in enumerate(TOKBLK):
                        pk = 1 + bi % 2
                        proj_fm(wb, PS[pk][:, 0:nt], t0, nt, PK[pk])
                        cp("act", uT[:, j, t0:t0 + nt], PS[pk][:, 0:nt], [PK[pk]], ["uT"])
                if s == 0:
                    dump("uT", uT[:, 0, 0:512], ["uT"])
                dcol = small[:, 0:4]
                dma(dcol, D["dcol"][:, :], w=["dcol"])
                gT = mixT[:, 0:4, :]
                Bpv = Bpb.rearrange("p (g d r c) -> p g d r c", g=4, d=2, r=2)
                Cpv = Cpb.rearrange("p (g d r c) -> p g d r c", g=4, d=2, r=2)
                for j in range(4):
                    for pc in range(2):
                        dma(xst[pc][:], D["Bp"][:, j * 2048 + pc * 1024:j * 2048 + (pc + 1) * 1024], w=["xst%d" % pc])
                        cp("pool", Bpb[:, pc * 1024:(pc + 1) * 1024], xst[pc][:], ["xst%d" % pc], ["Bpb"])
                    for gp in range(4):
                        b = gp % 2
                        pcg = j * 4 + gp
                        dma(xst[b][:, 0:512], D["Cp"][:, pcg * 512:(pcg + 1) * 512], w=["xst%d" % b])
                        for d in range(2):
                            col = pcg * 2 + d
                            cre_ = xst[b][:, d * 256:d * 256 + 128]
                            cim_ = xst[b][:, d * 256 + 128:d * 256 + 256]
                            t_ = xst[b][:, 512 + d * 128:512 + (d + 1) * 128]
                            base = gp * 512 + d * 256
                            xk = "xst%d" % b
                            ts("dve", t_, cim_, prm[:, CI, col:col + 1], ALU.mult, [xk, "prm"], [xk])
                            stt("dve", Cpb[:, base:base + 128], cre_, prm[:, CR, col:col + 1], t_, ALU.mult, ALU.subtract, [xk, "prm"], ["Cpb"])
                            ts("dve", t_, cre_, prm[:, CI, col:col + 1], ALU.mult, [xk, "prm"], [xk])
                            stt("dve", t_, cim_, prm[:, CR, col:col + 1], t_, ALU.mult, ALU.add, [xk, "prm"], [xk])
                            ts("dve", Cpb[:, base + 128:base + 256], t_, -1.0, ALU.mult, [xk], ["Cpb"])
                    YP = [PS[4 + q] for q in range(4)]
                    YK = [PK[4 + q] for q in range(4)]
                    rcount = [0, 0, 0, 0]

                    def s5_body(gp, d):
                        pair = j * 4 + gp
                        col = pair * 2 + d
                        E = "dve"
                        Hb = HbD[d]
                        hk = "Hb%d" % d
                        Xa, Xb = XaD[d], XbD[d]
                        for ri in range(2):
                            for bi, (t0, nt) in enumerate(TOKBLK):
                                pk = 1 + (bi + ri) % 3
                                mm(PS[pk][:, 0:nt], Bpv[:, gp, d, ri, :], uT[:, j, t0:t0 + nt], True, True, ["Bpb", "uT"], [PK[pk]])
                                cp("act", Hb[ri].rearrange("p (s c) -> p c s", s=8)[:, t0 // 8:(t0 + nt) // 8, :],
                                   PS[pk][:, 0:nt].rearrange("p (c s) -> p c s", s=8), [PK[pk]], [hk])
                        yield "hold"
                        Hr = Hb[0].rearrange("p (s c) -> p c s", s=8)
                        Hi = Hb[1].rearrange("p (s c) -> p c s", s=8)
                        a_r = prm[:, PW, col:col + 1]
                        a_i = prm[:, PW + 1, col:col + 1]
                        a_ni = prm[:, NP, col:col + 1]
                        order = range(1, 8) if d == 0 else range(6, -1, -1)
                        for s_ in order:
                            sp_ = s_ - 1 if d == 0 else s_ + 1
                            stt(E, Hr[:, :, s_], Hr[:, :, sp_], a_r, Hr[:, :, s_], ALU.mult, ALU.add, [hk, "prm"], [hk])
                            yield
                            stt(E, Hi[:, :, s_], Hr[:, :, sp_], a_i, Hi[:, :, s_], ALU.mult, ALU.add, [hk, "prm"], [hk])
                            yield
                            stt(E, Hr[:, :, s_], Hi[:, :, sp_], a_ni, Hr[:, :, s_], ALU.mult, ALU.add, [hk, "prm"], [hk])
                            yield
                            stt(E, Hi[:, :, s_], Hi[:, :, sp_], a_r, Hi[:, :, s_], ALU.mult, ALU.add, [hk, "prm"], [hk])
                            yield
                        se = 7 if d == 0 else 0
                        xak, xbk = "Xa%d" % d, "Xb%d" % d
                        for ri, Hv in enumerate((Hr, Hi)):
                            if d == 0:
                                cp(E, Xa[ri][:, 0:288], Hv[:, :, se], [hk], [xak])
                                yield
                            else:
                                cp(E, Xa[ri][:, 0:256], Hv[:, 32:288, se], [hk], [xak])
                                yield
                                cp(E, Xa[ri][:, 256:288], Hv[:, 0:32, se], [hk], [xak])
                                yield
                        cur, nxt, ck, nk = Xa, Xb, xak, xbk
                        for m in range(9):
                            sh_ = 1 << m
                            A_r = prm2[:, 3 * m, col:col + 1]
                            A_i = prm2[:, 3 * m + 1, col:col + 1]
                            A_ni = prm2[:, 3 * m + 2, col:col + 1]
                            n_ = 288 - sh_
                            if d == 0:
                                dst = slice(sh_, 288); srcs = slice(0, n_); keep = slice(0, sh_)
                            else:
                                dst = slice(0, n_); srcs = slice(sh_, 288); keep = slice(n_, 288)
                            stt(E, nxt[0][:, dst], cur[0][:, srcs], A_r, cur[0][:, dst], ALU.mult, ALU.add, [ck, "prm2"], [nk])
                            yield
                            stt(E, nxt[0][:, dst], cur[1][:, srcs], A_ni, nxt[0][:, dst], ALU.mult, ALU.add, [ck, nk, "prm2"], [nk])
                            yield
                            stt(E, nxt[1][:, dst], cur[0][:, srcs], A_i, cur[1][:, dst], ALU.mult, ALU.add, [ck, "prm2"], [nk])
                            yield
                            stt(E, nxt[1][:, dst], cur[1][:, srcs], A_r, nxt[1][:, dst], ALU.mult, ALU.add, [ck, nk, "prm2"], [nk])
                            yield
                            cp(E, nxt[0][:, keep], cur[0][:, keep], [ck], [nk])
                            yield
                            cp(E, nxt[1][:, keep], cur[1][:, keep], [ck], [nk])
                            yield
                            cur, nxt, ck, nk = nxt, cur, nk, ck
                        HBr = HbB[d][0].rearrange("p (s c) -> p c s", s=8)
                        HBi = HbB[d][1].rearrange("p (s c) -> p c s", s=8)
                        bk = "HbB%d" % d
                        for s_ in range(8):
                            kpow = s_ + 1 if d == 0 else 8 - s_
                            p_r = prm[:, PW + 2 * (kpow - 1), col:col + 1]
                            p_i = prm[:, PW + 2 * (kpow - 1) + 1, col:col + 1]
                            p_ni = prm[:, NP + kpow - 1, col:col + 1]
                            if d == 0:
                                Hs_r, Hs_i = cur[0][:, 31:287], cur[1][:, 31:287]
                            else:
                                Hs_r, Hs_i = cur[0][:, 1:257], cur[1][:, 1:257]
                            stt(E, Hr[:, 32:288, s_], Hs_r, p_r, Hr[:, 32:288, s_], ALU.mult, ALU.add, [hk, ck, "prm"], [hk])
                            yield
                            stt(E, HBr[:, :, s_], Hs_i, p_ni, Hr[:, 32:288, s_], ALU.mult, ALU.add, [hk, ck, "prm"], [bk])
                            yield
                            stt(E, Hi[:, 32:288, s_], Hs_r, p_i, Hi[:, 32:288, s_], ALU.mult, ALU.add, [hk, ck, "prm"], [hk])
                            yield
                            stt(E, HBi[:, :, s_], Hs_i, p_r, Hi[:, 32:288, s_], ALU.mult, ALU.add, [hk, ck, "prm"], [bk])
                            yield
                        for q in range(4):
                            for ri in range(2):
                                mm(YP[q][:, :], Cpv[:, gp, d, ri, :], HbB[d][ri].rearrange("p (s c) -> p c s", s=8)[:, q * 64:(q + 1) * 64, :],
                                   rcount[q] == 0, rcount[q] == 15, ["Cpb", bk], [YK[q]])
                                rcount[q] += 1

                    def s5_stream(d):
                        for gp in range(4):
                            yield from s5_body(gp, d)
                    g0, g1 = s5_stream(0), s5_stream(1)
                    for _ in range(60):
                        next(g0)
                    alive = [g0, g1]
                    hold = {id(g0): 0, id(g1): 0}
                    while alive:
                        progressed = False
                        for g_ in list(alive):
                            if hold[id(g_)] > 0 and len(alive) > 1:
                                hold[id(g_)] -= 1
                                continue
                            try:
                                r_ = next(g_)
                                progressed = True
                                if r_ == "hold":
                                    hold[id(g_)] = 40
                            except StopIteration:
                                alive.remove(g_)
                        if not progressed:
                            for k_ in hold:
                                hold[k_] = 0
                    for q in range(4):
                        stt("dve", ytmp, uT[:, j, LCTX + q * 512:LCTX + (q + 1) * 512], dcol[:, j:j + 1], YP[q][:, :], ALU.mult, ALU.add,
                            ["uT", "dcol", YK[q]], ["ytmp"])
                        if s == 0 and j == 0 and q == 0:
                            dump("y0", ytmp, ["ytmp"])
                        act(gT[:, j, q * 512:(q + 1) * 512], ytmp, AF.Gelu, ["ytmp"], ["gT"])
                barrier()
                o[0] = o_hb
                zt = carve(512)
                sig = carve(4 * 512).rearrange("p (j c) -> p j c", j=4)
                wgl = carve(2048, BF16).rearrange("p (j c) -> p j c", j=4)
                bgl = small[:, 4:8]
                dma(bgl, D["bglu"][:, :], w=["bglu"])
                for j in range(4):
                    dma(xst[j % 2][:, 0:512], D["wglu"][:, j, :], w=["xst%d" % (j % 2)])
                    cp("pool", wgl[:, j, :], xst[j % 2][:, 0:512], ["xst%d" % (j % 2)], ["wgl"])
                for q in range(4):
                    for jo in range(4):
                        for ji in range(4):
                            mm(PS[4 + jo][:, :], wgl[:, ji, jo * 128:(jo + 1) * 128], gT[:, ji, q * 512:(q + 1) * 512], ji == 0, ji == 3,
                               ["wgl", "gT"], [PK[4 + jo]])
                        act(sig[:, jo, :], PS[4 + jo][:, :], AF.Sigmoid, [PK[4 + jo], "bglu"], ["sig%d" % jo], bias=bgl[:, jo:jo + 1])
                    for jo in range(4):
                        wb = load_w(512 + jo * 128)
                        proj_fm(wb, PS[1 + jo % 2][:, :], LCTX + q * 512, 512, PK[1 + jo % 2])
                        act(zt, PS[1 + jo % 2][:, :], AF.Silu, [PK[1 + jo % 2]], ["zt"])
                        tt("dve", sig[:, jo, :], sig[:, jo, :], zt, ALU.mult, ["sig%d" % jo, "zt"], ["sig%d" % jo])
                    for jo in range(4):
                        tt("pool", gT[:, jo, q * 512:(q + 1) * 512], gT[:, jo, q * 512:(q + 1) * 512], sig[:, jo, :], ALU.mult,
                           ["gT", "sig%d" % jo], ["gT"])
                if s == 0:
                    dump("mix_s5", mixT[:, 0, 0:512], ["gT"])
            if "gdn" in stages:
                barrier()
                o = [0]

                def carve(n, dt=F32):
                    words = n if dt == F32 else (n + 1) // 2
                    a = arena[:, o[0]:o[0] + words]
                    o[0] += words
                    assert o[0] <= ARENA_WORDS, o[0]
                    return a.bitcast(BF16)[:, 0:n] if dt == BF16 else a

                def c3(n_, a_, b_):
                    return carve(n_ * a_ * b_).rearrange("p (n a b) -> p n a b", n=n_, a=a_)
                wgb = carve(128, BF16).rearrange("p (k c) -> p k c", k=8)
                gpar = carve(32)
                normw = carve(128)
                cwt = carve(108).rearrange("p (t k) -> p t k", t=12)
                G_ = {nm: carve(288).rearrange("p (c e) -> p c e", e=8) for nm in ("beta", "g", "gcum", "koutc")}
                G_["gam"] = G_["gcum"]
                XTf = carve(LTOT)
                XT = [carve(LTOT, BF16) for _ in range(3)]
                lpad = carve(34 * 66, BF16).rearrange("p (r c) -> p r c", c=66)
                cpad = carve(258, BF16)
                dgt = carve(9 * 128, BF16).rearrange("p (k c) -> p k c", k=9)
                tmp5 = carve(512)
                tmp5b = carve(512)
                rsb = carve(32)
                Oacc = carve(32 * 128).rearrange("p (c v) -> p c v", v=128)
                Sst = [carve(128) for _ in range(2)]
                NB = 4
                NI = 2 * NB
                mk = lambda w_, dt=F32: carve(NI * w_, dt).rearrange("p (n c) -> p n c", n=NI)
                sh_ = {"Lg": mk(64), "dec": mk(64), "Pm": mk(64), "Lgh": mk(64, BF16), "Lgl": mk(64, BF16),
                       "Qa": mk(64, BF16), "QTa": mk(64, BF16), "Qb": mk(64, BF16), "QTb": mk(64, BF16), "Pmb": mk(64, BF16),
                       "V": mk(128, BF16), "Kg": mk(128, BF16)}
                Av = [dict(sh_, **{"wT": mk(64), "QgT": mk(64), "gbc": mk(64), "QKT": mk(64, BF16), "K": mk(128, BF16), "u0": mk(128)})
                      for _ in range(2)]
                A64v = Av
                A128v = Av
                ubuf = [carve(128, BF16) for _ in range(4)]
                dma(xst[0][:, 0:128], D["wgate"].rearrange("p k c -> p (k c)"), w=["xst0"])
                cp("dve", wgb.rearrange("p k c -> p (k c)"), xst[0][:, 0:128], ["xst0"], ["wgb"])
                dma(gpar[0:64, 0:8], D["alog"][:, :], w=["gpar"])
                dma(gpar[0:64, 8:16], D["dtb"][:, :], w=["gpar"])
                dma(normw[0:64, :], D["normw"][:, :], w=["normw"])
                dma(cwt[:], D["convw"][:, :, :], w=["cwt"])
                act(gpar[0:64, 0:8], gpar[0:64, 0:8], AF.Exp, ["gpar"], ["gpar"])
                ts("dve", gpar[0:64, 0:8], gpar[0:64, 0:8], -1.0, ALU.mult, ["gpar"], ["gpar"])
                P.op("dve", lambda e: e.memset(lpad[:, :, :], 0.0), writes=["lpad"])
                P.op("dve", lambda e: e.memset(cpad[:, :], 0.0), writes=["cpad"])
                for c_ in range(36):
                    bank, cc = (0, c_) if c_ < 32 else (3, c_ - 32)
                    for k in range(8):
                        mm(PS[bank][0:64, cc * 16:(cc + 1) * 16], hT[:, k, c_ * 64:(c_ + 1) * 64], wgb[:, k, :], k == 0, k == 7,
                           ["hT", "wgb"], [PK[bank]])
                for (bank, c0, nch) in ((0, 0, 32), (3, 32, 4)):
                    pv = PS[bank][0:64, 0:nch * 16].rearrange("p (c e) -> p c e", e=16)
                    act(G_["beta"][0:64, c0:c0 + nch, :], pv[:, :, 0:8], AF.Sigmoid, [PK[bank]], ["gates"])
                    tt("dve", G_["g"][0:64, c0:c0 + nch, :], pv[:, :, 8:16],
                       gpar[0:64, 8:16].unsqueeze(1).to_broadcast([64, nch, 8]), ALU.add, [PK[bank], "gpar"], ["gates"])
                gall = lambda nm: G_[nm][0:64, :, :]
                act(gall("g"), gall("g"), AF.Exp, ["gates"], ["gates"])
                act(gall("g"), gall("g"), AF.Ln, ["gates"], ["gates"], bias=1.0)
                tt("dve", gall("g"), gall("g"), gpar[0:64, 0:8].unsqueeze(1).to_broadcast([64, 36, 8]), ALU.mult, ["gates", "gpar"], ["gates"])
                gflat = G_["g"][0:64, :, :].rearrange("p c e -> p (c e)")
                for d in range(2):
                    for hh in range(2):
                        pass
                    mm(PS[1][0:64, 0:288], TRI[d], gflat, True, True, ["masks", "gates"], [PK[1]])
                    pv = PS[1][0:64, 0:288].rearrange("p (c e) -> p c e", e=8)
                    cp("dve", G_["gcum"][0:64, :, d * 4:(d + 1) * 4], pv[:, :, d * 4:(d + 1) * 4], [PK[1]], ["gates"])
                mm(PS[2][0:64, 0:288], ones[0:64, 0:64], gflat, True, True, ["ones", "gates"], [PK[2]])
                tt("dve", gall("koutc"), PS[2][0:64, 0:288].rearrange("p (c e) -> p c e", e=8), gall("gcum"), ALU.subtract, [PK[2], "gates"], ["gates"])
                act(gall("koutc"), gall("koutc"), AF.Exp, ["gates"], ["gates"])
                if s == 0:
                    dump("g", G_["g"][0:64, :, :].rearrange("p c e -> p (c e)"), ["gates"])
                    dump("gcum", G_["gcum"][0:64, :, :].rearrange("p c e -> p (c e)"), ["gates"])
                act(gall("gam"), gall("gcum"), AF.Exp, ["gates"], ["gates"])
                FORD = list(range(36))
                BORD = [3, 2, 1, 0] + list(range(35, 3, -1))
                import os as _os2
                for h in range(int(_os2.environ.get('GDN_NHEADS', 4))):
                    for t in range(3):
                        tile_i = t * 4 + h
                        wb = load_w(1024 + t * 512 + h * 128)
                        proj_fm(wb, PS[1][:, 0:256], 0, 256, PK[1])
                        cp("act", cpad[:, 1:257], PS[1][:, 0:256], [PK[1]], ["cpad"])
                        for q in range(4):
                            pk = 2 + q % 2
                            proj_fm(wb, PS[pk][:, :], 256 + q * 512, 512, PK[pk])
                            cp("act", lpad[:, 1 + 8 * q:9 + 8 * q, 1:65], PS[pk][:, :].rearrange("p (r c) -> p r c", c=64), [PK[pk]], ["lpad"])
                        xk = "XT%d" % t
                        for tap in range(9):
                            ts("dve", dgt[:, tap, :], identb[:, :], cwt[:, tile_i, tap:tap + 1], ALU.mult, ["identb", "cwt"], ["dgt"])
                        for kx in range(3):
                            mm(PS[4][:, 0:256], dgt[:, 3 + kx, :], cpad[:, kx:kx + 256], kx == 0, kx == 2, ["dgt", "cpad"], [PK[4]])
                        if t < 2:
                            act(XTf[:, 0:256], PS[4][:, 0:256], AF.Silu, [PK[4]], ["XTf"])
                        else:
                            act(XT[t][:, 0:256], PS[4][:, 0:256], AF.Silu, [PK[4]], [xk])
                        for q in range(4):
                            pk = 4 + (q + 1) % 2
                            for tap in range(9):
                                ky, kx = tap // 3, tap % 3
                                mm(PS[pk][:, :], dgt[:, tap, :], lpad[:, ky + 8 * q:ky + 8 * q + 8, kx:kx + 64], tap == 0, tap == 8,
                                   ["dgt", "lpad"], [PK[pk]])
                            if t < 2:
                                act(XTf[:, 256 + q * 512:256 + (q + 1) * 512], PS[pk][:, :], AF.Silu, [PK[pk]], ["XTf"])
                            else:
                                act(XT[t][:, 256 + q * 512:256 + (q + 1) * 512], PS[pk][:, :], AF.Silu, [PK[pk]], [xk])
                        if t < 2:
                            for bi, (t0, nt) in enumerate(TOKBLK):
                                pk = 1 + bi % 3
                                tq = tmp5 if bi % 2 == 0 else tmp5b
                                tqk = "tmp5" if bi % 2 == 0 else "tmp5b"
                                tt("dve", tq[:, 0:nt], XTf[:, t0:t0 + nt], XTf[:, t0:t0 + nt], ALU.mult, ["XTf"], [tqk])
                                mm(PS[pk][:, 0:nt], ones[:, :], tq[:, 0:nt], True, True, ["ones", tqk], [PK[pk]])
                                act(tq[:, 0:nt], PS[pk][:, 0:nt], AF.Ln, [PK[pk]], [tqk], bias=1e-6)
                                act(tq[:, 0:nt], tq[:, 0:nt], AF.Exp, [tqk], [tqk], scale=-0.5)
                                stt("dve", XT[t][:, t0:t0 + nt], XTf[:, t0:t0 + nt], (128.0 ** -0.5) if t == 0 else 1.0, tq[:, 0:nt],
                                    ALU.mult, ALU.mult, ["XTf", tqk], [xk])
                    qT, kT, vT = XT
                    if s == 0 and h == 0:
                        dump("qn", qT[:, 0:512], ["XT0"])
                        dump("kn", kT[:, 256:768], ["XT1"])
                        dump("vv", vT[:, 256:768], ["XT2"])
                    for d in range(2):
                        P.op("dve", (lambda d=d: lambda e: e.memset(Sst[d][:, :], 0.0))(), writes=["S%d" % d])

                    def batch_info(b):
                        i0_ = b * NB
                        return [(FORD[i0_], NB), (min(BORD[i0_:i0_ + NB]), NB)]

                    def stageA_batch(b):
                        info = batch_info(b)
                        par = b % 2
                        A64, A128 = A64v[par], A128v[par]
                        KY = lambda s_: s_ + str(par)
                        items = []
                        for d in range(2):
                            cmin, nch = info[d]
                            for q_ in range(nch):
                                items.append((d, cmin + q_, d * NB + q_))
                        e8 = lambda d: d * 4 + h
                        gcols = lambda nm, d: G_[nm][0:64, info[d][0]:info[d][0] + NB, e8(d)]
                        v = lambda nm, d: A64[nm][0:64, d * NB:(d + 1) * NB, :]
                        v2 = lambda nm, d: A128[nm][0:64, d * NB:(d + 1) * NB, :]
                        ps3 = lambda bank, d, w_: PS[bank][0:64, :].rearrange("p (n c) -> p n c", c=w_)[:, (d * NB if w_ == 64 else 0):(d * NB if w_ == 64 else 0) + NB, :]
                        for d in range(2):
                            tt("dve", v("Lg", d), TRI[d].unsqueeze(1).to_broadcast([64, NB, 64]),
                               gcols("g", d).unsqueeze(2).to_broadcast([64, NB, 64]), ALU.mult, ["masks", "gates"], ["aLg"])
                        cp("act", A64["Lgh"][0:64, :, :], A64["Lg"][0:64, :, :], ["aLg"], ["aLgh"])
                        tt("dve", A64["Lgl"][0:64, :, :], A64["Lg"][0:64, :, :], A64["Lgh"][0:64, :, :], ALU.subtract, ["aLg", "aLgh"], ["aLgl"])
                        yield
                        for (d, c_, it) in items:
                            mm(PS[2][0:64, it * 64:(it + 1) * 64], SUb[d], A64["Lgh"][0:64, it, :], True, False, ["masksb", "aLgh"], [PK[2]])
                            mm(PS[2][0:64, it * 64:(it + 1) * 64], SUb[d], A64["Lgl"][0:64, it, :], False, True, ["masksb", "aLgl"], [PK[2]])
                            mm(PS[3][:, it * 64:(it + 1) * 64], onesb[:, :], A64["Lgh"][0:64, it, :], True, False, ["onesb", "aLgh"], [PK[3]])
                            mm(PS[3][:, it * 64:(it + 1) * 64], onesb[:, :], A64["Lgl"][0:64, it, :], False, True, ["onesb", "aLgl"], [PK[3]])
                        act(A64["dec"][0:64, :, :], PS[2][0:64, :].rearrange("p (n c) -> p n c", c=64), AF.Exp, [PK[2]], ["adec"])
                        act(A64["gbc"][:, :, :], PS[3][:, :].rearrange("p (n c) -> p n c", c=64), AF.Exp, [PK[3]], [KY("agbc")])
                        yield
                        for (d, c_, it) in items:
                            tok = slice(c_ * 64, (c_ + 1) * 64)
                            mm(PS[0][0:64, it * 64:(it + 1) * 64], kT[:, tok], kT[:, tok], True, True, ["XT1"], [PK[0]])
                            mm(PS[1][0:64, it * 64:(it + 1) * 64], kT[:, tok], qT[:, tok], True, True, ["XT1", "XT0"], [PK[1]])
                        for d in range(2):
                            tt("pool", v("Lg", d), v("dec", d), MI[d].unsqueeze(1).to_broadcast([64, NB, 64]), ALU.mult, ["adec", "masks"], ["aLg"])
                            tt("pool", v("dec", d), v("dec", d), MS[d].unsqueeze(1).to_broadcast([64, NB, 64]), ALU.mult, ["adec", "masks"], ["adec"])
                        yield
                        for d in range(2):
                            tt("dve", v("Pm", d), PS[0][0:64, :].rearrange("p (n c) -> p n c", c=64)[:, d * NB:(d + 1) * NB, :],
                               gcols("beta", d).unsqueeze(2).to_broadcast([64, NB, 64]), ALU.mult, [PK[0], "gates"], ["aPm"])
                        tt("dve", A64["dec"][0:64, :, :], A64["Pm"][0:64, :, :], A64["dec"][0:64, :, :], ALU.mult, ["aPm", "adec"], ["adec"])
                        tt("dve", A64["QKT"][0:64, :, :], PS[1][0:64, :].rearrange("p (n c) -> p n c", c=64), A64["Lg"][0:64, :, :], ALU.mult,
                           [PK[1], "aLg"], [KY("aQKT")])
                        tt("dve", A64["Pm"][0:64, :, :], ident[0:64, 0:64].unsqueeze(1).to_broadcast([64, NI, 64]), A64["dec"][0:64, :, :],
                           ALU.subtract, ["ident", "adec"], ["aPm"])
                        cp("pool", A64["Pmb"][0:64, :, :], A64["Pm"][0:64, :, :], ["aPm"], ["aPmb"])
                        cp("act", A64["Qa"][0:64, :, :], A64["dec"][0:64, :, :], ["adec"], ["aQa"])
                        yield
                        for (d, c_, it) in items:
                            mm(PS[4][0:64, it * 64:(it + 1) * 64], A64["Qa"][0:64, it, :], identb[0:64, 0:64], True, True, ["aQa", "identb"], [PK[4]])
                        cp("act", A64["QTa"][0:64, :, :], PS[4][0:64, :].rearrange("p (n c) -> p n c", c=64), [PK[4]], ["aQTa"])
                        yield
                        for (d, c_, it) in items:
                            tok = slice(c_ * 64, (c_ + 1) * 64)
                            mm(PS[2 + d][0:64, (it % NB) * 128:(it % NB + 1) * 128], kT[:, tok], identb[:, :], True, True, ["XT1", "identb"], [PK[2 + d]])
                            mm(PS[4 + d][0:64, (it % NB) * 128:(it % NB + 1) * 128], vT[:, tok], identb[:, :], True, True, ["XT2", "identb"], [PK[4 + d]])
                        for d in range(2):
                            cp("act", v2("K", d), PS[2 + d][0:64, :].rearrange("p (n c) -> p n c", c=128), [PK[2 + d]], [KY("aK")])
                            cp("act", v2("V", d), PS[4 + d][0:64, :].rearrange("p (n c) -> p n c", c=128), [PK[4 + d]], ["aV"])
                        for d in range(2):
                            tt("pool", v2("Kg", d), v2("K", d), gcols("gam", d).unsqueeze(2).to_broadcast([64, NB, 128]), ALU.mult, [KY("aK"), "gates"], ["aKg"])
                            tt("pool", v2("K", d), v2("K", d), gcols("koutc", d).unsqueeze(2).to_broadcast([64, NB, 128]), ALU.mult, [KY("aK"), "gates"], [KY("aK")])
                            cmin = info[d][0]
                            tt("pool", A64["QgT"][:, d * NB:(d + 1) * NB, :], qT[:, cmin * 64:(cmin + NB) * 64].rearrange("p (n c) -> p n c", c=64),
                               A64["gbc"][:, d * NB:(d + 1) * NB, :], ALU.mult, ["XT0", KY("agbc")], [KY("aQgT")])
                        yield
                        Q, QT, Qn, QTn = "Qa", "QTa", "Qb", "QTb"
                        kq = {"Qa": "aQa", "QTa": "aQTa", "Qb": "aQb", "QTb": "aQTb"}
                        for lvl in range(5):
                            for (d, c_, it) in items:
                                mm(PS[0][0:64, it * 64:(it + 1) * 64], A64[QT][0:64, it, :], A64[Q][0:64, it, :], True, True, [kq[Q], kq[QT]], [PK[0]])
                                mm(PS[1][0:64, it * 64:(it + 1) * 64], A64[Q][0:64, it, :], A64[QT][0:64, it, :], True, True, [kq[Q], kq[QT]], [PK[1]])
                            yield
                            cp("act", A64[Qn][0:64, :, :], PS[0][0:64, :].rearrange("p (n c) -> p n c", c=64), [PK[0]], [kq[Qn]])
                            cp("dve", A64[QTn][0:64, :, :], PS[1][0:64, :].rearrange("p (n c) -> p n c", c=64), [PK[1]], [kq[QTn]])
                            for (d, c_, it) in items:
                                mm(PS[5][0:64, it * 64:(it + 1) * 64], A64[QTn][0:64, it, :], A64["Pmb"][0:64, it, :], True, True, [kq[QTn], "aPmb"], [PK[5]])
                            tt("dve", A64["Pm"][0:64, :, :], A64["Pm"][0:64, :, :], PS[5][0:64, :].rearrange("p (n c) -> p n c", c=64), ALU.add,
                               ["aPm", PK[5]], ["aPm"])
                            cp("act", A64["Pmb"][0:64, :, :], A64["Pm"][0:64, :, :], ["aPm"], ["aPmb"])
                            yield
                            Q, QT, Qn, QTn = Qn, QTn, Q, QT
                        yield
                        for (d, c_, it) in items:
                            mm(PS[2 + d][0:64, (it % NB) * 128:(it % NB + 1) * 128], A64["Pmb"][0:64, it, :], A128["V"][0:64, it, :], True, True,
                               ["aPmb", "aV"], [PK[2 + d]])
                            mm(PS[4][:, it * 64:(it + 1) * 64], A128["Kg"][0:64, it, :], A64["Pmb"][0:64, it, :], True, True, ["aKg", "aPmb"], [PK[4]])
                        for d in range(2):
                            tt("dve", v2("u0", d), PS[2 + d][0:64, :].rearrange("p (n c) -> p n c", c=128),
                               gcols("beta", d).unsqueeze(2).to_broadcast([64, NB, 128]), ALU.mult, [PK[2 + d], "gates"], [KY("au0")])
                        act(A64["wT"][:, :, :], PS[4][:, :].rearrange("p (n c) -> p n c", c=64), AF.Identity, [PK[4]], [KY("awT")], scale=-1.0)

                    def stageC_gen(i, d):
                        b = i // NB
                        info = batch_info(b)
                        par = b % 2
                        A64, A128 = A64v[par], A128v[par]
                        KY = lambda s_: s_ + str(par)
                        c_ = FORD[i] if d == 0 else BORD[i]
                        it = d * NB + (c_ - info[d][0])
                        e8 = d * 4 + h
                        col = lambda nm: G_[nm][0:64, c_, e8:e8 + 1]
                        pC = PS[6 + d]
                        kC = PK[6 + d]
                        Sk = "S%d" % d
                        S_ = Sst[d]
                        ub = ubuf[(i % 2) * 2 + d]
                        uk = "u%d" % ((i % 2) * 2 + d)
                        mm(pC[0:64, 0:128], A64["wT"][:, it, :], S_[:, :], True, True, [KY("awT"), Sk], [kC])
                        yield
                        stt("dve", ub[0:64, :], pC[0:64, 0:128], col("beta"), A128["u0"][0:64, it, :], ALU.mult, ALU.add, [kC, "gates", KY("au0")], [uk])
                        yield
                        if c_ >= 4:
                            cl = c_ - 4
                            mm(pC[0:64, 128:256], A64["QgT"][:, it, :], S_[:, :], True, False, [KY("aQgT"), Sk], [kC])
                            mm(pC[0:64, 128:256], A64["QKT"][0:64, it, :], ub[0:64, :], False, True, [KY("aQKT"), uk], [kC])
                        mm(pC[:, 256:384], A128["K"][0:64, it, :], ub[0:64, :], True, True, [KY("aK"), uk], [kC])
                        yield
                        lastcol = 63 if d == 0 else 0
                        stt("dve", S_[:, :], S_[:, :], A64["gbc"][:, it, lastcol:lastcol + 1], pC[:, 256:384], ALU.mult, ALU.add,
                            [Sk, KY("agbc"), kC], [Sk])
                        yield
                        if c_ >= 4:
                            firstw = (d == 0 and cl < 16) or (d == 1 and cl >= 16)
                            ok = "O%d" % cl
                            if firstw:
                                cp("act", Oacc[0:64, cl, :], pC[0:64, 128:256], [kC], [ok])
                            else:
                                tt("dve", Oacc[0:64, cl, :], Oacc[0:64, cl, :], pC[0:64, 128:256], ALU.add, [kC, ok], [ok])
                        yield

                    def chain_d(d, steps):
                        for i in steps:
                            yield from stageC_gen(i, d)

                    def stageC_steps(steps):
                        alive = [chain_d(0, steps), chain_d(1, steps)]
                        while alive:
                            for g_ in list(alive):
                                try:
                                    next(g_)
                                except StopIteration:
                                    alive.remove(g_)

                    def chain_both(steps):
                        alive = [chain_d(0, steps), chain_d(1, steps)]
                        while alive:
                            for g_ in list(alive):
                                try:
                                    next(g_)
                                except StopIteration:
                                    alive.remove(g_)
                            yield

                    def run_all(g_):
                        for _ in g_:
                            pass
                    NBT = 36 // NB
                    run_all(stageA_batch(0))
                    for b in range(NBT):
                        gC = chain_both(list(range(b * NB, (b + 1) * NB)))
                        gA = stageA_batch(b + 1) if b + 1 < NBT else iter(())
                        doneA = doneC = False
                        while not (doneA and doneC):
                            for _ in range(2):
                                if not doneA:
                                    try:
                                        next(gA)
                                    except StopIteration:
                                        doneA = True
                            if not doneC:
                                try:
                                    next(gC)
                                except StopIteration:
                                    doneC = True
                    if s == 0 and h == 0:
                        dump("O0", Oacc[0:64, 0, :], ["O0"])
                        dump("O31", Oacc[0:64, 31, :], ["O31"])
                    wb = load_w(2560 + h * 128)
                    for blk in range(8):
                        par_ = blk % 2
                        okeys = ["O%d" % (blk * 4 + cc) for cc in range(4)]
                        O4 = Oacc[0:64, blk * 4:blk * 4 + 4, :]
                        sq4 = XTf[0:64, par_ * 512:(par_ + 1) * 512].rearrange("p (c v) -> p c v", v=128)
                        rs_ = rsb[0:64, 4 * blk:4 * blk + 4]
                        ksq, krs = "fsq%d" % par_, "rs%d" % blk
                        tt("pool", sq4, O4, O4, ALU.mult, okeys + ["XTf"], [ksq])
                        P.op("dve", (lambda sq4=sq4, rs_=rs_: lambda e: e.tensor_reduce(out=rs_, in_=sq4, axis=AX.X, op=ALU.add))(),
                             reads=[ksq, "XTf"], writes=[krs])
                        ts("dve", rs_, rs_, 1.0 / 128.0, ALU.mult, [krs], [krs], s2=1e-6, op1=ALU.add)
                        act(rs_, rs_, AF.Sqrt, [krs], [krs])
                        P.op("dve", (lambda rs_=rs_: lambda e: e.reciprocal(out=rs_, in_=rs_))(), reads=[krs], writes=[krs])
                    for blk in range(8):
                        par_ = blk % 2
                        okeys = ["O%d" % (blk * 4 + cc) for cc in range(4)]
                        O4 = Oacc[0:64, blk * 4:blk * 4 + 4, :]
                        z4 = XTf[0:64, 1024 + par_ * 512:1024 + (par_ + 1) * 512].rearrange("p (c v) -> p c v", v=128)
                        rs_ = rsb[0:64, 4 * blk:4 * blk + 4]
                        kz, krs = "fz%d" % par_, "rs%d" % blk
                        pz, pt = (1, 2) if par_ == 0 else (3, 4)
                        for cc in range(4):
                            tk = slice(LCTX + (blk * 4 + cc) * 64, LCTX + (blk * 4 + cc + 1) * 64)
                            for k in range(8):
                                mm(PS[pz][0:64, cc * 128:(cc + 1) * 128], hT[:, k, tk], wbf[wb][:, k, :], k == 0, k == 7, ["hT", "wbf%d" % wb], [PK[pz]])
                        act(z4, PS[pz][0:64, :].rearrange("p (c v) -> p c v", v=128), AF.Silu, [PK[pz], "XTf"], [kz])
                        tt("dve", O4, O4, rs_.unsqueeze(2).to_broadcast([64, 4, 128]), ALU.mult, okeys + [krs], okeys)
                        tt("pool", O4, O4, normw[0:64, :].unsqueeze(1).to_broadcast([64, 4, 128]), ALU.mult, okeys + ["normw"], okeys)
                        tt("dve", O4, O4, z4, ALU.mult, okeys + [kz, "XTf"], okeys)
                        for cc in range(4):
                            tr(PS[pt][:, cc * 64:(cc + 1) * 64], Oacc[0:64, blk * 4 + cc, :], ident[0:64, 0:64], okeys + ["ident"], [PK[pt]])
                        cp("act", mixT[:, 4 + h, blk * 256:(blk + 1) * 256], PS[pt][:, 0:256], [PK[pt]], ["mixG"])
                    if s == 0 and h == 0:
                        dump("mix_g", mixT[:, 4, 0:512], ["mixG"])

            if "out" in stages:
                barrier()
                o = [0]

                def carve(n, dt=F32):
                    words = n if dt == F32 else (n + 1) // 2
                    a = arena[:, o[0]:o[0] + words]
                    o[0] += words
                    assert o[0] <= ARENA_WORDS, o[0]
                    return a.bitcast(BF16)[:, 0:n] if dt == BF16 else a
                woutb = carve(8192, BF16).rearrange("p (k c) -> p k c", k=8)
                gate_bc = carve(1024)
                lng = carve(1024)
                lnb = carve(1024)
                wg = carve(4096).rearrange("p (k c) -> p k c", k=8)
                silucb = carve(1024).rearrange("p (k c) -> p k c", k=8)
                rbuf = [carve(1024) for _ in range(2)]
                stats = carve(16)
                dma(lng, D["lng"][:, :], w=["lng"])
                dma(lnb, D["lnb"][:, :], w=["lnb"])
                dma(gate_bc, D["bgate_bc"][:, :], w=["gate_bc"])
                wout_v = D["w_out"].rearrange("(k p) c -> p k c", p=128)
                for k in range(8):
                    dma(xst[k % 2][:], wout_v[:, k, :], w=["xst%d" % (k % 2)])
                    cp("pool", woutb[:, k, :], xst[k % 2][:], ["xst%d" % (k % 2)], ["woutb"])
                for k in range(8):
                    ts("dve", silucb[:, k, :], ones[:, :], siluc[:, k, s:s + 1], ALU.mult, ["ones", "siluc"], ["silucb"])
                for half in range(2):
                    dma(wg[:], wada[:, :, 2048 + half * 512:2048 + (half + 1) * 512], w=["wg"])
                    for k in range(8):
                        mm(PS[3][:, :], silucb[:, k, :], wg[:, k, :], k == 0, k == 7, ["silucb", "wg"], [PK[3]])
                    tt("dve", gate_bc[:, half * 512:(half + 1) * 512], gate_bc[:, half * 512:(half + 1) * 512], PS[3][:, :], ALU.add,
                       ["gate_bc", PK[3]], ["gate_bc"])
                if s == 0:
                    dump("gate", gate_bc[:, 0:512], ["gate_bc"])
                for ti in range(16):
                    b = ti % 2
                    rk = "r%d" % b
                    dma(xst[b][:], D["x2"][s, ti * 128:(ti + 1) * 128, :], w=["xst%d" % b])
                    for half in range(2):
                        pk = 4 + half + 2 * b
                        for k in range(8):
                            mm(PS[pk][:, :], mixT[:, k, ti * 128:(ti + 1) * 128], woutb[:, k, half * 512:(half + 1) * 512], k == 0, k == 7,
                               ["mixT", "woutb"], [PK[pk]])
                        tt("dve", rbuf[b][:, half * 512:(half + 1) * 512], PS[pk][:, :], gate_bc[:, half * 512:(half + 1) * 512], ALU.mult,
                           [PK[pk], "gate_bc"], [rk])
                    stt("pool", rbuf[b][:, :], xst[b][:, :], DEEP_ALPHA, rbuf[b][:, :], ALU.mult, ALU.add, ["xst%d" % b, rk], [rk])
                    st6 = stats[:, 0:12].rearrange("p (c e) -> p c e", e=6)
                    for half in range(2):
                        P.op("dve", (lambda b=b, half=half, st6=st6: lambda e: e.bn_stats(out=st6[:, half, :], in_=rbuf[b][:, half * 512:(half + 1) * 512]))(),
                             reads=[rk], writes=["stats"])
                    P.op("dve", (lambda st6=st6: lambda e: e.bn_aggr(out=stats[:, 12:14], in_=st6))(), reads=["stats"], writes=["mv"])
                    act(stats[:, 14:15], stats[:, 13:14], AF.Sqrt, ["mv"], ["mv2"], bias=1e-5)
                    P.op("dve", lambda e: e.reciprocal(out=stats[:, 14:15], in_=stats[:, 14:15]), reads=["mv2"], writes=["mv2"])
                    stt("dve", stats[:, 15:16], stats[:, 12:13], -1.0, stats[:, 14:15], ALU.mult, ALU.mult, ["mv", "mv2"], ["mv2"])
                    act(rbuf[b][:, :], rbuf[b][:, :], AF.Identity, [rk, "mv2"], [rk], bias=stats[:, 15:16], scale=stats[:, 14:15])
                    tt("pool", rbuf[b][:, :], rbuf[b][:, :], lng, ALU.mult, [rk, "lng"], [rk])
                    tt("dve", rbuf[b][:, :], rbuf[b][:, :], lnb, ALU.add, [rk, "lnb"], [rk])
                    dma(yout[s, ti * 128:(ti + 1) * 128, :], rbuf[b][:, :], r=[rk])
        P.emit(nc)
    return nc, P


def _core_inputs(inp, sh, core):
    m = dict(sh)
    f = lambda a: np.ascontiguousarray(np.asarray(a, dtype=np.float32))
    b0 = core * NSEQ
    m["x2"] = f(inp["x"][b0:b0 + NSEQ])
    m["ctx2"] = f(inp["ctx"][b0:b0 + NSEQ])
    cc = np.stack([np.asarray(inp["c"][b0], np.float32), np.asarray(inp["c"][b0 + 1], np.float32),
                   np.asarray(inp["c_ctx"], np.float32)], axis=0)
    m["cT"] = f(cc.reshape(3, 8, 128).transpose(2, 1, 0))
    return m


_CACHE = {}


def kernel(**inputs):
    if "nc" not in _CACHE:
        _CACHE["nc"] = build_program()[0]
    nc = _CACHE["nc"]
    sh = _prep_shared(inputs)
    maps = [_core_inputs(inputs, sh, c) for c in range(8)]
    res = run_bass_kernel_spmd(nc, maps, core_ids=list(range(8)))
    out = np.concatenate([r["yout"] for r in res.results], axis=0)
    return out.astype(np.float32)
```

```python
import contextlib
import math
import numpy as np
import concourse.bass as bass
import concourse.mybir as mybir
from concourse.bass_utils import run_bass_kernel_spmd

F32 = mybir.dt.float32
BF16 = mybir.dt.bfloat16
ALU = mybir.AluOpType
AF = mybir.ActivationFunctionType
AX = mybir.AxisListType

ENG_NAMES = ["pe", "act", "dve", "pool", "sp"]
SAME_ENGINE_SYNC = True
N_DMA_SEMS = 12

NSEQ = 2
LCTX = 256
LLAT = 2048
LTOT = LCTX + LLAT
DEEP_ALPHA = 2.0 ** 0.25


class Prog:
    def __init__(self):
        self.ops = []
        self.last_w = {}
        self.readers = {}
        self.barrier_idx = None

    def barrier(self, eng, fn):
        deps = set(self.last_w.values())
        for v in self.readers.values():
            deps.update(v)
        if self.barrier_idx is not None:
            deps.add(self.barrier_idx)
        idx = len(self.ops)
        self.ops.append(dict(eng=eng, fn=fn, deps=deps, dma=False))
        self.barrier_idx = idx
        self.last_w = {}
        self.readers = {}
        return idx

    def op(self, eng, fn, reads=(), writes=(), dma=False):
        writes = list(writes) + [k for k in reads if k.startswith("ps")]
        idx = len(self.ops)
        deps = set()
        if self.barrier_idx is not None:
            deps.add(self.barrier_idx)
        for k in reads:
            if k in self.last_w:
                deps.add(self.last_w[k])
        for k in writes:
            if k in self.last_w:
                deps.add(self.last_w[k])
            deps.update(self.readers.get(k, ()))
        self.ops.append(dict(eng=eng, fn=fn, deps=deps, dma=dma))
        for k in reads:
            self.readers.setdefault(k, []).append(idx)
        for k in writes:
            self.last_w[k] = idx
            self.readers[k] = []
        return idx

    def emit(self, nc):
        ops = self.ops
        pos = {}
        seqcount = {e: 0 for e in ENG_NAMES}
        for i, o in enumerate(ops):
            if not o["dma"]:
                seqcount[o["eng"]] += 1
                pos[i] = seqcount[o["eng"]]
        dma_ops = [i for i, o in enumerate(ops) if o["dma"]]
        dma_sem = {}
        dma_val = {}
        semcnt = [0] * N_DMA_SEMS
        prev_on_sem = {}
        for n, i in enumerate(dma_ops):
            s = n % N_DMA_SEMS
            semcnt[s] += 16
            dma_sem[i] = s
            dma_val[i] = semcnt[s]
            if s in prev_on_sem:
                ops[i]["deps"].add(prev_on_sem[s])
            prev_on_sem[s] = i
        known = {e: {p: 0 for p in ENG_NAMES} for e in ENG_NAMES}
        known_dma = {e: [0] * N_DMA_SEMS for e in ENG_NAMES}
        flagged = set()
        waits = [[] for _ in ops]
        for i, o in enumerate(ops):
            e = o["eng"]
            for d in sorted(o["deps"]):
                od = ops[d]
                if od["dma"]:
                    s = dma_sem[d]
                    if dma_val[d] > known_dma[e][s]:
                        known_dma[e][s] = dma_val[d]
                        waits[i].append(("dma", s, dma_val[d]))
                else:
                    p = od["eng"]
                    if p == e and (e == "pe" or not SAME_ENGINE_SYNC):
                        continue
                    if pos[d] > known[e][p]:
                        known[e][p] = pos[d]
                        flagged.add(d)
                        waits[i].append(("eng", p, d))
        cnt = {e: 0 for e in ENG_NAMES}
        val = {}
        for i, o in enumerate(ops):
            if i in flagged:
                cnt[o["eng"]] += 1
                val[i] = cnt[o["eng"]]
        per_eng = {e: [] for e in ENG_NAMES}
        for i, o in enumerate(ops):
            per_eng[o["eng"]].append(i)
        self.stats = {e: len(per_eng[e]) for e in ENG_NAMES}

        with contextlib.ExitStack() as st:
            esem = {e: st.enter_context(nc.semaphore("s_" + e)) for e in ENG_NAMES}
            dsem = [st.enter_context(nc.semaphore("d_%d" % k)) for k in range(N_DMA_SEMS)]
            block = st.enter_context(nc.Block())

            def run(e, engobj):
                for i in per_eng[e]:
                    o = ops[i]
                    mx = {}
                    for w in waits[i]:
                        if w[0] == "dma":
                            key = ("d", w[1]); v = w[2]
                        else:
                            key = ("e", w[1]); v = val[w[2]]
                        mx[key] = max(mx.get(key, 0), v)
                    for key, v in mx.items():
                        sem = dsem[key[1]] if key[0] == "d" else esem[key[1]]
                        engobj.wait_ge(sem, v)
                    ins = o["fn"](engobj)
                    if o["dma"]:
                        ins.then_inc(dsem[dma_sem[i]], 16)
                    elif i in flagged:
                        ins.then_inc(esem[e], 1)
                if e == "sp":
                    for s in range(N_DMA_SEMS):
                        if semcnt[s] > 0:
                            engobj.wait_ge(dsem[s], semcnt[s])

            @block.tensor
            def _(eng):
                run("pe", eng)

            @block.scalar
            def _(eng):
                run("act", eng)

            @block.vector
            def _(eng):
                run("dve", eng)

            @block.gpsimd
            def _(eng):
                run("pool", eng)

            @block.sync
            def _(eng):
                run("sp", eng)


def _consts():
    c = {}
    c["ident"] = np.eye(128, dtype=np.float32)
    k = np.arange(64)
    tri = np.stack([(k[:, None] <= k[None, :]), (k[:, None] >= k[None, :])]).astype(np.float32)
    su = np.stack([(k[:, None] > k[None, :]), (k[:, None] < k[None, :])]).astype(np.float32)
    mi = np.stack([(k[None, :] >= k[:, None]), (k[None, :] <= k[:, None])]).astype(np.float32)
    ms = np.stack([(k[None, :] > k[:, None]), (k[None, :] < k[:, None])]).astype(np.float32)
    c["masks"] = np.concatenate([tri[0], tri[1], su[0], su[1], mi[0], mi[1], ms[0], ms[1]], axis=1).astype(np.float32)
    c["ones"] = np.ones((128, 128), np.float32)
    return c


def _prep_shared(inp):
    f = lambda a: np.ascontiguousarray(np.asarray(a, dtype=np.float32))
    sh = {}
    sh["w_ada"] = f(inp["w_ada"][0])
    b_ada = f(inp["b_ada"][0])
    sh["bcol"] = f(b_ada.reshape(24, 128).T)
    sh["bgate_bc"] = f(np.broadcast_to(b_ada[2048:3072][None, :], (128, 1024)))
    w_in = f(inp["w_in"][0])
    sh["w_in"] = w_in
    sh["wgate"] = f(w_in[:, 3072:3088].reshape(8, 128, 16).transpose(1, 0, 2))
    lre = f(inp["s5_lambda_re"][0]); lim = f(inp["s5_lambda_im"][0]); ldt = f(inp["s5_log_dt"][0])

    def smaj(a):
        o = np.zeros((128, 32), np.float32)
        for pair in range(16):
            for d in range(2):
                for gl in range(2):
                    o[gl * 64:(gl + 1) * 64, pair * 2 + d] = a[d, 2 * pair + gl, :]
        return o
    sh["lam_re"] = smaj(lre)
    sh["lam_im"] = smaj(lim)
    sh["logdt"] = smaj(np.broadcast_to(ldt[:, :, None], (2, 32, 64)))
    bre = f(inp["s5_b_re"][0]); bim = f(inp["s5_b_im"][0])
    cre = f(inp["s5_c_re"][0]); cim = f(inp["s5_c_im"][0])
    Bp = np.zeros((128, 4, 4, 2, 2, 128), np.float32)
    Cp = np.zeros((128, 4, 4, 2, 2, 128), np.float32)
    for j in range(4):
        for gp in range(4):
            for gl in range(2):
                gq = 2 * gp + gl
                g = 8 * j + gq
                for d in range(2):
                    Bp[gq * 16:(gq + 1) * 16, j, gp, d, 0, gl * 64:(gl + 1) * 64] = bre[d, g].T
                    Bp[gq * 16:(gq + 1) * 16, j, gp, d, 1, gl * 64:(gl + 1) * 64] = bim[d, g].T
                    Cp[gl * 64:(gl + 1) * 64, j, gp, d, 0, gq * 16:(gq + 1) * 16] = cre[d, g].T
                    Cp[gl * 64:(gl + 1) * 64, j, gp, d, 1, gq * 16:(gq + 1) * 16] = cim[d, g].T
    sh["Bp"] = f(Bp.reshape(128, 8192))
    sh["Cp"] = f(Cp.reshape(128, 8192))
    sh["dcol"] = f(inp["s5_d"][0].reshape(4, 128).T)
    sh["wglu"] = f(inp["w_glu"][0].reshape(4, 128, 512).transpose(1, 0, 2))
    sh["bglu"] = f(inp["b_glu"][0].reshape(4, 128).T)
    sh["convw"] = f(inp["conv_w"][0].reshape(9, 12, 128).transpose(2, 1, 0))
    sh["alog"] = f(np.broadcast_to(inp["gdn_a_log"][0].reshape(1, 8), (64, 8)))
    sh["dtb"] = f(np.broadcast_to(inp["gdn_dt_bias"][0].reshape(1, 8), (64, 8)))
    sh["normw"] = f(np.broadcast_to(inp["gdn_norm_w"][0].reshape(1, 128), (64, 128)))
    sh["w_out"] = f(inp["w_out"][0])
    sh["lng"] = f(np.broadcast_to(inp["ln_g"][0][None, :], (128, 1024)))
    sh["lnb"] = f(np.broadcast_to(inp["ln_b"][0][None, :], (128, 1024)))
    sh.update(_consts())
    return sh


IN_SHAPES = {
    "x2": [NSEQ, LLAT, 1024], "ctx2": [NSEQ, LCTX, 1024], "cT": [128, 8, 3],
    "w_ada": [1024, 3072], "bcol": [128, 24], "bgate_bc": [128, 1024], "w_in": [1024, 3088],
    "wgate": [128, 8, 16], "lam_re": [128, 32], "lam_im": [128, 32], "logdt": [128, 32],
    "Bp": [128, 8192], "Cp": [128, 8192], "dcol": [128, 4], "wglu": [128, 4, 512], "bglu": [128, 4],
    "convw": [128, 12, 9], "alog": [64, 8], "dtb": [64, 8], "normw": [64, 128],
    "w_out": [1024, 1024], "lng": [128, 1024], "lnb": [128, 1024],
    "ident": [128, 128], "masks": [64, 512], "ones": [128, 128],
}


def build_program(dbg=None, stages=("s5", "gdn", "out")):
    nc = bass.Bass("TRN2", target_bir_lowering=False)
    D = {k: nc.dram_tensor(k, v, F32, kind="ExternalInput").ap() for k, v in IN_SHAPES.items()}
    yout = nc.dram_tensor("yout", [NSEQ, LLAT, 1024], F32, kind="ExternalOutput").ap()
    dbg_t = {}
    if dbg:
        for k, shp in dbg.items():
            dbg_t[k] = nc.dram_tensor("dbg_" + k, shp, F32, kind="ExternalOutput").ap()
    P = Prog()
    st = contextlib.ExitStack()
    uid = [0]

    def sb(name, shape, dt=F32):
        return st.enter_context(nc.sbuf_tensor("sb_" + name, shape, dt))

    def psum(name):
        return st.enter_context(nc.psum_tensor(name, [128, 512], F32))

    with st:
        hT = sb("hT", [128, 8, LTOT], BF16)
        mixT = sb("mixT", [128, 8, LLAT], BF16)
        ident = sb("ident", [128, 128])
        ones = sb("ones", [128, 128])
        masks = sb("masks", [64, 512])
        identb = sb("identb", [128, 128], BF16)
        masksb = sb("masksb", [64, 512], BF16)
        onesb = sb("onesb", [64, 128], BF16)
        xst = [sb("xst%d" % i, [128, 1024]) for i in range(2)]
        wst = [sb("wst%d" % i, [128, 8, 128]) for i in range(2)]
        wbf = [sb("wbf%d" % i, [128, 8, 128], BF16) for i in range(2)]
        modsc = sb("modsc", [128, 16, 3])
        siluc = sb("siluc", [128, 8, 3])
        dbgs = sb("dbgs", [128, 512]) if dbg else None
        bcol = sb("bcol", [128, 24])
        small = sb("small", [128, 64])
        ARENA_WORDS = 28800
        arena = sb("arena", [128, ARENA_WORDS])
        PS = [psum("ps%d" % i) for i in range(8)]
        PK = ["ps%d" % i for i in range(8)]

        def dma(out, in_, r=(), w=()):
            P.op("sp", lambda e: e.dma_start(out=out, in_=in_), reads=r, writes=w, dma=True)

        def act(out, in_, func, r, w, bias=None, scale=None):
            kw = {}
            if bias is not None:
                kw["bias"] = bias
            if scale is not None:
                kw["scale"] = scale
            P.op("act", lambda e: e.activation(out=out, in_=in_, func=func, **kw), reads=r, writes=w)

        def tt(eng, out, in0, in1, op, r, w):
            P.op(eng, lambda e: e.tensor_tensor(out=out, in0=in0, in1=in1, op=op), reads=r, writes=w)

        def ts(eng, out, in0, s1, op0, r, w, s2=None, op1=None):
            if op1 is None:
                P.op(eng, lambda e: e.tensor_scalar(out=out, in0=in0, scalar1=s1, scalar2=None, op0=op0), reads=r, writes=w)
            else:
                P.op(eng, lambda e: e.tensor_scalar(out=out, in0=in0, scalar1=s1, scalar2=s2, op0=op0, op1=op1), reads=r, writes=w)

        def stt(eng, out, in0, scalar, in1, op0, op1, r, w):
            eng = "dve"
            P.op(eng, lambda e: e.scalar_tensor_tensor(out=out, in0=in0, scalar=scalar, in1=in1, op0=op0, op1=op1), reads=r, writes=w)

        def cp(eng, out, in_, r, w):
            if eng == "act":
                act(out, in_, AF.Copy, r, w)
            else:
                P.op(eng, lambda e: e.tensor_copy(out=out, in_=in_), reads=r, writes=w)

        def mm(out, lhsT, rhs, start, stop, r, w):
            P.op("pe", lambda e: e.matmul(out, lhsT=lhsT, rhs=rhs, start=start, stop=stop), reads=r, writes=w)

        def tr(out, in_, idn, r, w):
            P.op("pe", lambda e: e.transpose(out, in_, idn), reads=r, writes=w)

        def dump(name, ap, r):
            if name in dbg_t:
                if ap.dtype == BF16:
                    n = ap.shape[-1]
                    cp("dve", dbgs[:, 0:n], ap, r, ["dbgs"])
                    dma(dbg_t[name], dbgs[:, 0:n], r=["dbgs"])
                else:
                    dma(dbg_t[name], ap, r=r)

        def barrier():
            P.barrier("dve", lambda e: e.memset(small[:, 63:64], 0.0))

        dma(ident[:], D["ident"][:, :], w=["ident"])
        dma(ones[:], D["ones"][:, :], w=["ones"])
        dma(masks[:], D["masks"][:, :], w=["masks"])
        dma(bcol[:], D["bcol"][:, :], w=["bcol"])
        cp("dve", identb[:], ident[:], ["ident"], ["identb"])
        cp("dve", masksb[:], masks[:], ["masks"], ["masksb"])
        cp("dve", onesb[:], ones[0:64, :], ["ones"], ["onesb"])
        SUb = [masksb[:, 128:192], masksb[:, 192:256]]
        TRI = [masks[:, 0:64], masks[:, 64:128]]
        SU = [masks[:, 128:192], masks[:, 192:256]]
        MI = [masks[:, 256:320], masks[:, 320:384]]
        MS = [masks[:, 384:448], masks[:, 448:512]]

        dma(siluc[:], D["cT"][:, :, :], w=["siluc"])
        act(siluc[:], siluc[:], AF.Silu, ["siluc"], ["siluc"])
        wada = D["w_ada"].rearrange("(k p) c -> p k c", p=128)
        for t in range(16):
            b = t % 2
            dma(wst[b][:], wada[:, :, t * 128:(t + 1) * 128], w=["wst%d" % b])
            for k in range(8):
                mm(PS[0][:, t * 4:t * 4 + 3], wst[b][:, k, :], siluc[:, k, :], k == 0, k == 7, ["wst%d" % b, "siluc"], [PK[0]])
        for t in range(16):
            ts("dve", modsc[:, t, :], PS[0][:, t * 4:t * 4 + 3], bcol[:, t:t + 1], ALU.add, [PK[0], "bcol"], ["modsc"],
               s2=(1.0 if t >= 8 else 0.0), op1=ALU.add)

        win = D["w_in"].rearrange("(k p) c -> p k c", p=128)
        wcount = [0]

        def load_w(c0, ncols=128):
            b = wcount[0] % 2
            wcount[0] += 1
            dma(wst[b][:, :, 0:ncols], win[:, :, c0:c0 + ncols], w=["wst%d" % b])
            cp("pool", wbf[b][:, :, 0:ncols], wst[b][:, :, 0:ncols], ["wst%d" % b], ["wbf%d" % b])
            return b

        def proj_fm(b, ps_ap, tok0, ntok, pk):
            for k in range(8):
                mm(ps_ap, wbf[b][:, k, :], hT[:, k, tok0:tok0 + ntok], k == 0, k == 7, ["wbf%d" % b, "hT"], [pk])

        TOKBLK = [(0, 256), (256, 512), (768, 512), (1280, 512), (1792, 512)]

        for s in range(NSEQ):
            barrier()
            for tt_i in range(18):
                b = tt_i % 2
                if tt_i < 2:
                    src = D["ctx2"][s, tt_i * 128:(tt_i + 1) * 128, :]
                    jcol = 2
                else:
                    src = D["x2"][s, (tt_i - 2) * 128:(tt_i - 1) * 128, :]
                    jcol = s
                dma(xst[b][:], src, w=["xst%d" % b])
                for half in range(2):
                    pk = 1 + half
                    for kk in range(4):
                        k = half * 4 + kk
                        tr(PS[pk][:, kk * 128:(kk + 1) * 128], xst[b][:, k * 128:(k + 1) * 128], ident[:], ["xst%d" % b, "ident"], [PK[pk]])
                    for kk in range(4):
                        k = half * 4 + kk
                        act(hT[:, k, tt_i * 128:(tt_i + 1) * 128], PS[pk][:, kk * 128:(kk + 1) * 128], AF.Identity,
                            [PK[pk], "modsc"], ["hT"], bias=modsc[:, k, jcol:jcol + 1], scale=modsc[:, 8 + k, jcol:jcol + 1])
            if s == 0:
                dump("hT", hT[:, 0, 0:512], ["hT"])

            if "s5" in stages:
                barrier()
                o = [0]

                def carve(n, dt=F32):
                    words = n if dt == F32 else (n + 1) // 2
                    a = arena[:, o[0]:o[0] + words]
                    o[0] += words
                    assert o[0] <= ARENA_WORDS, o[0]
                    return a.bitcast(BF16)[:, 0:n] if dt == BF16 else a
                Bpb = carve(2048, BF16)
                Cpb = carve(2048, BF16)
                uT = carve(4 * LTOT, BF16).rearrange("p (j t) -> p j t", j=4)
                o_hb = o[0]
                HbD = [[carve(LTOT) for _ in range(2)] for _ in range(2)]
                HbB = [[carve(LLAT, BF16) for _ in range(2)] for _ in range(2)]
                XaD = [[carve(288) for _ in range(2)] for _ in range(2)]
                XbD = [[carve(288) for _ in range(2)] for _ in range(2)]
                prm = carve(32 * 40).rearrange("p (q c) -> p q c", c=32)
                prm2 = carve(32 * 28).rearrange("p (q c) -> p q c", c=32)
                ytmp = carve(512)
                LR, LI, DT, ER, TH, AR, AI, CR, CI, T0, T1, T2 = range(12)
                PW = 13
                NP = PW + 16
                P.op("dve", lambda e: e.memset(prm[:, :, :], 0.0), writes=["prm"])
                for q_, nm in ((LR, "lam_re"), (LI, "lam_im"), (DT, "logdt")):
                    dma(prm[:, q_, :], D[nm][:, :], w=["prm"])

                def ptt(dst, a_, b_, op):
                    tt("dve", prm[:, dst, :], prm[:, a_, :], prm[:, b_, :], op, ["prm"], ["prm"])

                def pts(dst, a_, s1, op0, s2=None, op1=None, eng="dve"):
                    ts(eng, prm[:, dst, :], prm[:, a_, :], s1, op0, ["prm"], ["prm"], s2=s2, op1=op1)
                act(prm[:, DT, :], prm[:, DT, :], AF.Exp, ["prm"], ["prm"])
                ptt(ER, LR, DT, ALU.mult)
                act(prm[:, ER, :], prm[:, ER, :], AF.Exp, ["prm"], ["prm"])
                ptt(TH, LI, DT, ALU.mult)
                I32 = mybir.dt.int32
                kint = prm[:, 12, :].bitcast(I32)
                PI_C = 3.14159
                for (dst, shift_) in ((AI, 0.0), (AR, 0.5 * math.pi)):
                    pts(T0, TH, 1.0 / (2 * math.pi), ALU.mult, s2=shift_ / (2 * math.pi), op1=ALU.add)
                    P.op("dve", lambda e: e.tensor_copy(out=kint, in_=prm[:, T0, :]), reads=["prm"], writes=["prm"])
                    P.op("dve", lambda e: e.tensor_copy(out=prm[:, T1, :], in_=kint), reads=["prm"], writes=["prm"])
                    pts(T2, TH, shift_, ALU.add)
                    stt("dve", prm[:, T0, :], prm[:, T1, :], -2 * math.pi, prm[:, T2, :], ALU.mult, ALU.add, ["prm"], ["prm"])
                    pts(T0, T0, -PI_C, ALU.max, s2=PI_C, op1=ALU.min)
                    act(prm[:, dst, :], prm[:, T0, :], AF.Sin, ["prm"], ["prm"])
                ptt(AR, AR, ER, ALU.mult)
                ptt(AI, AI, ER, ALU.mult)
                pts(T0, AR, -1.0, ALU.add)
                ptt(T1, T0, LR, ALU.mult)
                ptt(T2, AI, LI, ALU.mult)
                ptt(CR, T1, T2, ALU.add)
                ptt(T1, AI, LR, ALU.mult)
                ptt(T2, T0, LI, ALU.mult)
                ptt(CI, T1, T2, ALU.subtract)
                ptt(T1, LR, LR, ALU.mult)
                ptt(T2, LI, LI, ALU.mult)
                ptt(T1, T1, T2, ALU.add)
                P.op("dve", lambda e: e.reciprocal(out=prm[:, T1, :], in_=prm[:, T1, :]), reads=["prm"], writes=["prm"])
                ptt(CR, CR, T1, ALU.mult)
                ptt(CI, CI, T1, ALU.mult)

                def cmul(dst_r, dst_i, ar_, ai_, br_, bi_):
                    ptt(T0, ar_, br_, ALU.mult)
                    ptt(T1, ai_, bi_, ALU.mult)
                    ptt(T2, ar_, bi_, ALU.mult)
                    ptt(dst_r, T0, T1, ALU.subtract)
                    ptt(T0, ai_, br_, ALU.mult)
                    ptt(dst_i, T2, T0, ALU.add)
                pts(PW, AR, 1.0, ALU.mult)
                pts(PW + 1, AI, 1.0, ALU.mult)
                for k in range(2, 9):
                    cmul(PW + 2 * (k - 1), PW + 2 * (k - 1) + 1, PW + 2 * (k - 2), PW + 2 * (k - 2) + 1, AR, AI)
                for k in range(1, 9):
                    pts(NP + k - 1, PW + 2 * (k - 1) + 1, -1.0, ALU.mult)
                ts("dve", prm2[:, 0, :], prm[:, PW + 14, :], 1.0, ALU.mult, ["prm"], ["prm2"])
                ts("dve", prm2[:, 1, :], prm[:, PW + 15, :], 1.0, ALU.mult, ["prm"], ["prm2"])
                for m in range(9):
                    if m > 0:
                        r0, i0_ = prm2[:, 3 * (m - 1), :], prm2[:, 3 * (m - 1) + 1, :]
                        tt("dve", prm[:, T0, :], r0, r0, ALU.mult, ["prm2", "prm"], ["prm"])
                        tt("dve", prm[:, T1, :], i0_, i0_, ALU.mult, ["prm2", "prm"], ["prm"])
                        tt("dve", prm2[:, 3 * m, :], prm[:, T0, :], prm[:, T1, :], ALU.subtract, ["prm"], ["prm2"])
                        tt("dve", prm[:, T0, :], r0, i0_, ALU.mult, ["prm2", "prm"], ["prm"])
                        ts("dve", prm2[:, 3 * m + 1, :], prm[:, T0, :], 2.0, ALU.mult, ["prm"], ["prm2"])
                    ts("dve", prm2[:, 3 * m + 2, :], prm2[:, 3 * m + 1, :], -1.0, ALU.mult, ["prm2"], ["prm2"])
                if s == 0:
                    dump("prm", prm[:, 0:40, :], ["prm"])
                for j in range(4):
                    wb = load_w(j * 128)
                    for bi, (t0, nt) in enumerate(TOKBLK):
                        pk = 1 + bi % 2
                        proj_fm(wb, PS[pk][:, 0:nt], t0, nt, PK[pk])
                        cp("act", uT[:, j, t0:t0 + nt], PS[pk][:, 0:nt], [PK[pk]], ["uT"])
                if s == 0:
                    dump("uT", uT[:, 0, 0:512], ["uT"])
                dcol = small[:, 0:4]
                dma(dcol, D["dcol"][:, :], w=["dcol"])
                gT = mixT[:, 0:4, :]
                Bpv = Bpb.rearrange("p (g d r c) -> p g d r c", g=4, d=2, r=2)
                Cpv = Cpb.rearrange("p (g d r c) -> p g d r c", g=4, d=2, r=2)
                for j in range(4):
                    for pc in range(2):
                        dma(xst[pc][:], D["Bp"][:, j * 2048 + pc * 1024:j * 2048 + (pc + 1) * 1024], w=["xst%d" % pc])
                        cp("pool", Bpb[:, pc * 1024:(pc + 1) * 1024], xst[pc][:], ["xst%d" % pc], ["Bpb"])
                    for gp in range(4):
                        b = gp % 2
                        pcg = j * 4 + gp
                        dma(xst[b][:, 0:512], D["Cp"][:, pcg * 512:(pcg + 1) * 512], w=["xst%d" % b])
                        for d in range(2):
                            col = pcg * 2 + d
                            cre_ = xst[b][:, d * 256:d * 256 + 128]
                            cim_ = xst[b][:, d * 256 + 128:d * 256 + 256]
                            t_ = xst[b][:, 512 + d * 128:512 + (d + 1) * 128]
                            base = gp * 512 + d * 256
                            xk = "xst%d" % b
                            ts("dve", t_, cim_, prm[:, CI, col:col + 1], ALU.mult, [xk, "prm"], [xk])
                            stt("dve", Cpb[:, base:base + 128], cre_, prm[:, CR, col:col + 1], t_, ALU.mult, ALU.subtract, [xk, "prm"], ["Cpb"])
                            ts("dve", t_, cre_, prm[:, CI, col:col + 1], ALU.mult, [xk, "prm"], [xk])
                            stt("dve", t_, cim_, prm[:, CR, col:col + 1], t_, ALU.mult, ALU.add, [xk, "prm"], [xk])
                            ts("dve", Cpb[:, base + 128:base + 256], t_, -1.0, ALU.mult, [xk], ["Cpb"])
                    YP = [PS[4 + q] for q in range(4)]
                    YK = [PK[4 + q] for q in range(4)]
                    rcount = [0, 0, 0, 0]

                    def s5_body(gp, d):
                        pair = j * 4 + gp
                        col = pair * 2 + d
                        E = "dve"
                        Hb = HbD[d]
                        hk = "Hb%d" % d
                        Xa, Xb = XaD[d], XbD[d]
                        for ri in range(2):
                            for bi, (t0, nt) in enumerate(TOKBLK):
                                pk = 1 + (bi + ri) % 3
                                mm(PS[pk][:, 0:nt], Bpv[:, gp, d, ri, :], uT[:, j, t0:t0 + nt], True, True, ["Bpb", "uT"], [PK[pk]])
                                cp("act", Hb[ri].rearrange("p (s c) -> p c s", s=8)[:, t0 // 8:(t0 + nt) // 8, :],
                                   PS[pk][:, 0:nt].rearrange("p (c s) -> p c s", s=8), [PK[pk]], [hk])
                        yield "hold"
                        Hr = Hb[0].rearrange("p (s c) -> p c s", s=8)
                        Hi = Hb[1].rearrange("p (s c) -> p c s", s=8)
                        a_r = prm[:, PW, col:col + 1]
                        a_i = prm[:, PW + 1, col:col + 1]
                        a_ni = prm[:, NP, col:col + 1]
                        order = range(1, 8) if d == 0 else range(6, -1, -1)
                        for s_ in order:
                            sp_ = s_ - 1 if d == 0 else s_ + 1
                            stt(E, Hr[:, :, s_], Hr[:, :, sp_], a_r, Hr[:, :, s_], ALU.mult, ALU.add, [hk, "prm"], [hk])
                            yield
                            stt(E, Hi[:, :, s_], Hr[:, :, sp_], a_i, Hi[:, :, s_], ALU.mult, ALU.add, [hk, "prm"], [hk])
                            yield
                            stt(E, Hr[:, :, s_], Hi[:, :, sp_], a_ni, Hr[:, :, s_], ALU.mult, ALU.add, [hk, "prm"], [hk])
                            yield
                            stt(E, Hi[:, :, s_], Hi[:, :, sp_], a_r, Hi[:, :, s_], ALU.mult, ALU.add, [hk, "prm"], [hk])
                            yield
                        se = 7 if d == 0 else 0
                        xak, xbk = "Xa%d" % d, "Xb%d" % d
                        for ri, Hv in enumerate((Hr, Hi)):
                            if d == 0:
                                cp(E, Xa[ri][:, 0:288], Hv[:, :, se], [hk], [xak])
                                yield
                            else:
                                cp(E, Xa[ri][:, 0:256], Hv[:, 32:288, se], [hk], [xak])
                                yield
                                cp(E, Xa[ri][:, 256:288], Hv[:, 0:32, se], [hk], [xak])
                                yield
                        cur, nxt, ck, nk = Xa, Xb, xak, xbk
                        for m in range(9):
                            sh_ = 1 << m
                            A_r = prm2[:, 3 * m, col:col + 1]
                            A_i = prm2[:, 3 * m + 1, col:col + 1]
                            A_ni = prm2[:, 3 * m + 2, col:col + 1]
                            n_ = 288 - sh_
                            if d == 0:
                                dst = slice(sh_, 288); srcs = slice(0, n_); keep = slice(0, sh_)
                            else:
                                dst = slice(0, n_); srcs = slice(sh_, 288); keep = slice(n_, 288)
                            stt(E, nxt[0][:, dst], cur[0][:, srcs], A_r, cur[0][:, dst], ALU.mult, ALU.add, [ck, "prm2"], [nk])
                            yield
                            stt(E, nxt[0][:, dst], cur[1][:, srcs], A_ni, nxt[0][:, dst], ALU.mult, ALU.add, [ck, nk, "prm2"], [nk])
                            yield
                            stt(E, nxt[1][:, dst], cur[0][:, srcs], A_i, cur[1][:, dst], ALU.mult, ALU.add, [ck, "prm2"], [nk])
                            yield
                            stt(E, nxt[1][:, dst], cur[1][:, srcs], A_r, nxt[1][:, dst], ALU.mult, ALU.add, [ck, nk, "prm2"], [nk])
                            yield
                            cp(E, nxt[0][:, keep], cur[0][:, keep], [ck], [nk])
                            yield
                            cp(E, nxt[1][:, keep], cur[1][:, keep], [ck], [nk])
                            yield
                            cur, nxt, ck, nk = nxt, cur, nk, ck
                        HBr = HbB[d][0].rearrange("p (s c) -> p c s", s=8)
                        HBi = HbB[d][1].rearrange("p (s c) -> p c s", s=8)
                        bk = "HbB%d" % d
                        for s_ in range(8):
                            kpow = s_ + 1 if d == 0 else 8 - s_
                            p_r = prm[:, PW + 2 * (kpow - 1), col:col + 1]
                            p_i = prm[:, PW + 2 * (kpow - 1) + 1, col:col + 1]
                            p_ni = prm[:, NP + kpow - 1, col:col + 1]
                            if d == 0:
                                Hs_r, Hs_i = cur[0][:, 31:287], cur[1][:, 31:287]
                            else:
                                Hs_r, Hs_i = cur[0][:, 1:257], cur[1][:, 1:257]
                            stt(E, Hr[:, 32:288, s_], Hs_r, p_r, Hr[:, 32:288, s_], ALU.mult, ALU.add, [hk, ck, "prm"], [hk])
                            yield
                            stt(E, HBr[:, :, s_], Hs_i, p_ni, Hr[:, 32:288, s_], ALU.mult, ALU.add, [hk, ck, "prm"], [bk])
                            yield
                            stt(E, Hi[:, 32:288, s_], Hs_r, p_i, Hi[:, 32:288, s_], ALU.mult, ALU.add, [hk, ck, "prm"], [hk])
                            yield
                            stt(E, HBi[:, :, s_], Hs_i, p_r, Hi[:, 32:288, s_], ALU.mult, ALU.add, [hk, ck, "prm"], [bk])
                            yield
                        for q in range(4):
                            for ri in range(2):
                                mm(YP[q][:, :], Cpv[:, gp, d, ri, :], HbB[d][ri].rearrange("p (s c) -> p c s", s=8)[:, q * 64:(q + 1) * 64, :],
                                   rcount[q] == 0, rcount[q] == 15, ["Cpb", bk], [YK[q]])
                                rcount[q] += 1

                    def s5_stream(d):
                        for gp in range(4):
                            yield from s5_body(gp, d)
                    g0, g1 = s5_stream(0), s5_stream(1)
                    for _ in range(60):
                        next(g0)
                    alive = [g0, g1]
                    hold = {id(g0): 0, id(g1): 0}
                    while alive:
                        progressed = False
                        for g_ in list(alive):
                            if hold[id(g_)] > 0 and len(alive) > 1:
                                hold[id(g_)] -= 1
                                continue
                            try:
                                r_ = next(g_)
                                progressed = True
                                if r_ == "hold":
                                    hold[id(g_)] = 40
                            except StopIteration:
                                alive.remove(g_)
                        if not progressed:
                            for k_ in hold:
                                hold[k_] = 0
                    for q in range(4):
                        stt("dve", ytmp, uT[:, j, LCTX + q * 512:LCTX + (q + 1) * 512], dcol[:, j:j + 1], YP[q][:, :], ALU.mult, ALU.add,
                            ["uT", "dcol", YK[q]], ["ytmp"])
                        if s == 0 and j == 0 and q == 0:
                            dump("y0", ytmp, ["ytmp"])
                        act(gT[:, j, q * 512:(q + 1) * 512], ytmp, AF.Gelu, ["ytmp"], ["gT"])
                barrier()
                o[0] = o_hb
                zt = carve(512)
                sig = carve(4 * 512).rearrange("p (j c) -> p j c", j=4)
                wgl = carve(2048, BF16).rearrange("p (j c) -> p j c", j=4)
                bgl = small[:, 4:8]
                dma(bgl, D["bglu"][:, :], w=["bglu"])
                for j in range(4):
                    dma(xst[j % 2][:, 0:512], D["wglu"][:, j, :], w=["xst%d" % (j % 2)])
                    cp("pool", wgl[:, j, :], xst[j % 2][:, 0:512], ["xst%d" % (j % 2)], ["wgl"])
                for q in range(4):
                    for jo in range(4):
                        for ji in range(4):
                            mm(PS[4 + jo][:, :], wgl[:, ji, jo * 128:(jo + 1) * 128], gT[:, ji, q * 512:(q + 1) * 512], ji == 0, ji == 3,
                               ["wgl", "gT"], [PK[4 + jo]])
                        act(sig[:, jo, :], PS[4 + jo][:, :], AF.Sigmoid, [PK[4 + jo], "bglu"], ["sig%d" % jo], bias=bgl[:, jo:jo + 1])
                    for jo in range(4):
                        wb = load_w(512 + jo * 128)
                        proj_fm(wb, PS[1 + jo % 2][:, :], LCTX + q * 512, 512, PK[1 + jo % 2])
                        act(zt, PS[1 + jo % 2][:, :], AF.Silu, [PK[1 + jo % 2]], ["zt"])
                        tt("dve", sig[:, jo, :], sig[:, jo, :], zt, ALU.mult, ["sig%d" % jo, "zt"], ["sig%d" % jo])
                    for jo in range(4):
                        tt("pool", gT[:, jo, q * 512:(q + 1) * 512], gT[:, jo, q * 512:(q + 1) * 512], sig[:, jo, :], ALU.mult,
                           ["gT", "sig%d" % jo], ["gT"])
                if s == 0:
                    dump("mix_s5", mixT[:, 0, 0:512], ["gT"])
            if "gdn" in stages:
                barrier()
                o = [0]

                def carve(n, dt=F32):
                    words = n if dt == F32 else (n + 1) // 2
                    a = arena[:, o[0]:o[0] + words]
                    o[0] += words
                    assert o[0] <= ARENA_WORDS, o[0]
                    return a.bitcast(BF16)[:, 0:n] if dt == BF16 else a

                def c3(n_, a_, b_):
                    return carve(n_ * a_ * b_).rearrange("p (n a b) -> p n a b", n=n_, a=a_)
                wgb = carve(128, BF16).rearrange("p (k c) -> p k c", k=8)
                gpar = carve(32)
                normw = carve(128)
                cwt = carve(108).rearrange("p (t k) -> p t k", t=12)
                G_ = {nm: carve(288).rearrange("p (c e) -> p c e", e=8) for nm in ("beta", "g", "gcum", "koutc")}
                G_["gam"] = G_["gcum"]
                XTf = carve(LTOT)
                XT = [carve(LTOT, BF16) for _ in range(3)]
                lpad = carve(34 * 66, BF16).rearrange("p (r c) -> p r c", c=66)
                cpad = carve(258, BF16)
                dgt = carve(9 * 128, BF16).rearrange("p (k c) -> p k c", k=9)
                tmp5 = carve(512)
                tmp5b = carve(512)
                rsb = carve(32)
                Oacc = carve(32 * 128).rearrange("p (c v) -> p c v", v=128)
                Sst = [carve(128) for _ in range(2)]
                NB = 4
                NI = 2 * NB
                mk = lambda w_, dt=F32: carve(NI * w_, dt).rearrange("p (n c) -> p n c", n=NI)
                sh_ = {"Lg": mk(64), "dec": mk(64), "Pm": mk(64), "Lgh": mk(64, BF16), "Lgl": mk(64, BF16),
                       "Qa": mk(64, BF16), "QTa": mk(64, BF16), "Qb": mk(64, BF16), "QTb": mk(64, BF16), "Pmb": mk(64, BF16),
                       "V": mk(128, BF16), "Kg": mk(128, BF16)}
                Av = [dict(sh_, **{"wT": mk(64), "QgT": mk(64), "gbc": mk(64), "QKT": mk(64, BF16), "K": mk(128, BF16), "u0": mk(128)})
                      for _ in range(2)]
                A64v = Av
                A128v = Av
                ubuf = [carve(128, BF16) for _ in range(4)]
                dma(xst[0][:, 0:128], D["wgate"].rearrange("p k c -> p (k c)"), w=["xst0"])
                cp("dve", wgb.rearrange("p k c -> p (k c)"), xst[0][:, 0:128], ["xst0"], ["wgb"])
                dma(gpar[0:64, 0:8], D["alog"][:, :], w=["gpar"])
                dma(gpar[0:64, 8:16], D["dtb"][:, :], w=["gpar"])
                dma(normw[0:64, :], D["normw"][:, :], w=["normw"])
                dma(cwt[:], D["convw"][:, :, :], w=["cwt"])
                act(gpar[0:64, 0:8], gpar[0:64, 0:8], AF.Exp, ["gpar"], ["gpar"])
                ts("dve", gpar[0:64, 0:8], gpar[0:64, 0:8], -1.0, ALU.mult, ["gpar"], ["gpar"])
                P.op("dve", lambda e: e.memset(lpad[:, :, :], 0.0), writes=["lpad"])
                P.op("dve", lambda e: e.memset(cpad[:, :], 0.0), writes=["cpad"])
                for c_ in range(36):
                    bank, cc = (0, c_) if c_ < 32 else (3, c_ - 32)
                    for k in range(8):
                        mm(PS[bank][0:64, cc * 16:(cc + 1) * 16], hT[:, k, c_ * 64:(c_ + 1) * 64], wgb[:, k, :], k == 0, k == 7,
                           ["hT", "wgb"], [PK[bank]])
                for (bank, c0, nch) in ((0, 0, 32), (3, 32, 4)):
                    pv = PS[bank][0:64, 0:nch * 16].rearrange("p (c e) -> p c e", e=16)
                    act(G_["beta"][0:64, c0:c0 + nch, :], pv[:, :, 0:8], AF.Sigmoid, [PK[bank]], ["gates"])
                    tt("dve", G_["g"][0:64, c0:c0 + nch, :], pv[:, :, 8:16],
                       gpar[0:64, 8:16].unsqueeze(1).to_broadcast([64, nch, 8]), ALU.add, [PK[bank], "gpar"], ["gates"])
                gall = lambda nm: G_[nm][0:64, :, :]
                act(gall("g"), gall("g"), AF.Exp, ["gates"], ["gates"])
                act(gall("g"), gall("g"), AF.Ln, ["gates"], ["gates"], bias=1.0)
                tt("dve", gall("g"), gall("g"), gpar[0:64, 0:8].unsqueeze(1).to_broadcast([64, 36, 8]), ALU.mult, ["gates", "gpar"], ["gates"])
                gflat = G_["g"][0:64, :, :].rearrange("p c e -> p (c e)")
                for d in range(2):
                    for hh in range(2):
                        pass
                    mm(PS[1][0:64, 0:288], TRI[d], gflat, True, True, ["masks", "gates"], [PK[1]])
                    pv = PS[1][0:64, 0:288].rearrange("p (c e) -> p c e", e=8)
                    cp("dve", G_["gcum"][0:64, :, d * 4:(d + 1) * 4], pv[:, :, d * 4:(d + 1) * 4], [PK[1]], ["gates"])
                mm(PS[2][0:64, 0:288], ones[0:64, 0:64], gflat, True, True, ["ones", "gates"], [PK[2]])
                tt("dve", gall("koutc"), PS[2][0:64, 0:288].rearrange("p (c e) -> p c e", e=8), gall("gcum"), ALU.subtract, [PK[2], "gates"], ["gates"])
                act(gall("koutc"), gall("koutc"), AF.Exp, ["gates"], ["gates"])
                if s == 0:
                    dump("g", G_["g"][0:64, :, :].rearrange("p c e -> p (c e)"), ["gates"])
                    dump("gcum", G_["gcum"][0:64, :, :].rearrange("p c e -> p (c e)"), ["gates"])
                act(gall("gam"), gall("gcum"), AF.Exp, ["gates"], ["gates"])
                FORD = list(range(36))
                BORD = [3, 2, 1, 0] + list(range(35, 3, -1))
                import os as _os2
                for h in range(int(_os2.environ.get('GDN_NHEADS', 4))):
                    for t in range(3):
                        tile_i = t * 4 + h
                        wb = load_w(1024 + t * 512 + h * 128)
                        proj_fm(wb, PS[1][:, 0:256], 0, 256, PK[1])
                        cp("act", cpad[:, 1:257], PS[1][:, 0:256], [PK[1]], ["cpad"])
                        for q in range(4):
                            pk = 2 + q % 2
                            proj_fm(wb, PS[pk][:, :], 256 + q * 512, 512, PK[pk])
                            cp("act", lpad[:, 1 + 8 * q:9 + 8 * q, 1:65], PS[pk][:, :].rearrange("p (r c) -> p r c", c=64), [PK[pk]], ["lpad"])
                        xk = "XT%d" % t
                        for tap in range(9):
                            ts("dve", dgt[:, tap, :], identb[:, :], cwt[:, tile_i, tap:tap + 1], ALU.mult, ["identb", "cwt"], ["dgt"])
                        for kx in range(3):
                            mm(PS[4][:, 0:256], dgt[:, 3 + kx, :], cpad[:, kx:kx + 256], kx == 0, kx == 2, ["dgt", "cpad"], [PK[4]])
                        if t < 2:
                            act(XTf[:, 0:256], PS[4][:, 0:256], AF.Silu, [PK[4]], ["XTf"])
                        else:
                            act(XT[t][:, 0:256], PS[4][:, 0:256], AF.Silu, [PK[4]], [xk])
                        for q in range(4):
                            pk = 4 + (q + 1) % 2
                            for tap in range(9):
                                ky, kx = tap // 3, tap % 3
                                mm(PS[pk][:, :], dgt[:, tap, :], lpad[:, ky + 8 * q:ky + 8 * q + 8, kx:kx + 64], tap == 0, tap == 8,
                                   ["dgt", "lpad"], [PK[pk]])
                            if t < 2:
                                act(XTf[:, 256 + q * 512:256 + (q + 1) * 512], PS[pk][:, :], AF.Silu, [PK[pk]], ["XTf"])
                            else:
                                act(XT[t][:, 256 + q * 512:256 + (q + 1) * 512], PS[pk][:, :], AF.Silu, [PK[pk]], [xk])
                        if t < 2:
                            for bi, (t0, nt) in enumerate(TOKBLK):
                                pk = 1 + bi % 3
                                tq = tmp5 if bi % 2 == 0 else tmp5b
                                tqk = "tmp5" if bi % 2 == 0 else "tmp5b"
                                tt("pool", tq[:, 0:nt], XTf[:, t0:t0 + nt], XTf[:, t0:t0 + nt], ALU.mult, ["XTf"], [tqk])
                                mm(PS[pk][:, 0:nt], ones[:, :], tq[:, 0:nt], True, True, ["ones", tqk], [PK[pk]])
                                act(tq[:, 0:nt], PS[pk][:, 0:nt], AF.Ln, [PK[pk]], [tqk], bias=1e-6)
                                act(tq[:, 0:nt], tq[:, 0:nt], AF.Exp, [tqk], [tqk], scale=-0.5)
                                stt("dve", XT[t][:, t0:t0 + nt], XTf[:, t0:t0 + nt], (128.0 ** -0.5) if t == 0 else 1.0, tq[:, 0:nt],
                                    ALU.mult, ALU.mult, ["XTf", tqk], [xk])
                    qT, kT, vT = XT
                    if s == 0 and h == 0:
                        dump("qn", qT[:, 0:512], ["XT0"])
                        dump("kn", kT[:, 256:768], ["XT1"])
                        dump("vv", vT[:, 256:768], ["XT2"])
                    for d in range(2):
                        P.op("dve", (lambda d=d: lambda e: e.memset(Sst[d][:, :], 0.0))(), writes=["S%d" % d])

                    def batch_info(b):
                        i0_ = b * NB
                        return [(FORD[i0_], NB), (min(BORD[i0_:i0_ + NB]), NB)]

                    def stageA_batch(b):
                        info = batch_info(b)
                        par = b % 2
                        A64, A128 = A64v[par], A128v[par]
                        KY = lambda s_: s_ + str(par)
                        items = []
                        for d in range(2):
                            cmin, nch = info[d]
                            for q_ in range(nch):
                                items.append((d, cmin + q_, d * NB + q_))
                        e8 = lambda d: d * 4 + h
                        gcols = lambda nm, d: G_[nm][0:64, info[d][0]:info[d][0] + NB, e8(d)]
                        v = lambda nm, d: A64[nm][0:64, d * NB:(d + 1) * NB, :]
                        v2 = lambda nm, d: A128[nm][0:64, d * NB:(d + 1) * NB, :]
                        ps3 = lambda bank, d, w_: PS[bank][0:64, :].rearrange("p (n c) -> p n c", c=w_)[:, (d * NB if w_ == 64 else 0):(d * NB if w_ == 64 else 0) + NB, :]
                        for d in range(2):
                            tt("dve", v("Lg", d), TRI[d].unsqueeze(1).to_broadcast([64, NB, 64]),
                               gcols("g", d).unsqueeze(2).to_broadcast([64, NB, 64]), ALU.mult, ["masks", "gates"], ["aLg"])
                        cp("act", A64["Lgh"][0:64, :, :], A64["Lg"][0:64, :, :], ["aLg"], ["aLgh"])
                        tt("dve", A64["Lgl"][0:64, :, :], A64["Lg"][0:64, :, :], A64["Lgh"][0:64, :, :], ALU.subtract, ["aLg", "aLgh"], ["aLgl"])
                        yield
                        for (d, c_, it) in items:
                            mm(PS[2][0:64, it * 64:(it + 1) * 64], SUb[d], A64["Lgh"][0:64, it, :], True, False, ["masksb", "aLgh"], [PK[2]])
                            mm(PS[2][0:64, it * 64:(it + 1) * 64], SUb[d], A64["Lgl"][0:64, it, :], False, True, ["masksb", "aLgl"], [PK[2]])
                            mm(PS[3][:, it * 64:(it + 1) * 64], onesb[:, :], A64["Lgh"][0:64, it, :], True, False, ["onesb", "aLgh"], [PK[3]])
                            mm(PS[3][:, it * 64:(it + 1) * 64], onesb[:, :], A64["Lgl"][0:64, it, :], False, True, ["onesb", "aLgl"], [PK[3]])
                        act(A64["dec"][0:64, :, :], PS[2][0:64, :].rearrange("p (n c) -> p n c", c=64), AF.Exp, [PK[2]], ["adec"])
                        act(A64["gbc"][:, :, :], PS[3][:, :].rearrange("p (n c) -> p n c", c=64), AF.Exp, [PK[3]], [KY("agbc")])
                        yield
                        for (d, c_, it) in items:
                            tok = slice(c_ * 64, (c_ + 1) * 64)
                            mm(PS[0][0:64, it * 64:(it + 1) * 64], kT[:, tok], kT[:, tok], True, True, ["XT1"], [PK[0]])
                            mm(PS[1][0:64, it * 64:(it + 1) * 64], kT[:, tok], qT[:, tok], True, True, ["XT1", "XT0"], [PK[1]])
                        for d in range(2):
                            tt("dve", v("Lg", d), v("dec", d), MI[d].unsqueeze(1).to_broadcast([64, NB, 64]), ALU.mult, ["adec", "masks"], ["aLg"])
                            tt("dve", v("dec", d), v("dec", d), MS[d].unsqueeze(1).to_broadcast([64, NB, 64]), ALU.mult, ["adec", "masks"], ["adec"])
                        yield
                        for d in range(2):
                            tt("dve", v("Pm", d), PS[0][0:64, :].rearrange("p (n c) -> p n c", c=64)[:, d * NB:(d + 1) * NB, :],
                               gcols("beta", d).unsqueeze(2).to_broadcast([64, NB, 64]), ALU.mult, [PK[0], "gates"], ["aPm"])
                        tt("dve", A64["dec"][0:64, :, :], A64["Pm"][0:64, :, :], A64["dec"][0:64, :, :], ALU.mult, ["aPm", "adec"], ["adec"])
                        tt("dve", A64["QKT"][0:64, :, :], PS[1][0:64, :].rearrange("p (n c) -> p n c", c=64), A64["Lg"][0:64, :, :], ALU.mult,
                           [PK[1], "aLg"], [KY("aQKT")])
                        tt("dve", A64["Pm"][0:64, :, :], ident[0:64, 0:64].unsqueeze(1).to_broadcast([64, NI, 64]), A64["dec"][0:64, :, :],
                           ALU.subtract, ["ident", "adec"], ["aPm"])
                        cp("dve", A64["Pmb"][0:64, :, :], A64["Pm"][0:64, :, :], ["aPm"], ["aPmb"])
                        cp("act", A64["Qa"][0:64, :, :], A64["dec"][0:64, :, :], ["adec"], ["aQa"])
                        yield
                        for (d, c_, it) in items:
                            mm(PS[4][0:64, it * 64:(it + 1) * 64], A64["Qa"][0:64, it, :], identb[0:64, 0:64], True, True, ["aQa", "identb"], [PK[4]])
                        cp("act", A64["QTa"][0:64, :, :], PS[4][0:64, :].rearrange("p (n c) -> p n c", c=64), [PK[4]], ["aQTa"])
                        yield
                        for (d, c_, it) in items:
                            tok = slice(c_ * 64, (c_ + 1) * 64)
                            mm(PS[2 + d][0:64, (it % NB) * 128:(it % NB + 1) * 128], kT[:, tok], identb[:, :], True, True, ["XT1", "identb"], [PK[2 + d]])
                            mm(PS[4 + d][0:64, (it % NB) * 128:(it % NB + 1) * 128], vT[:, tok], identb[:, :], True, True, ["XT2", "identb"], [PK[4 + d]])
                        for d in range(2):
                            cp("act", v2("K", d), PS[2 + d][0:64, :].rearrange("p (n c) -> p n c", c=128), [PK[2 + d]], [KY("aK")])
                            cp("act", v2("V", d), PS[4 + d][0:64, :].rearrange("p (n c) -> p n c", c=128), [PK[4 + d]], ["aV"])
                        for d in range(2):
                            tt("pool", v2("Kg", d), v2("K", d), gcols("gam", d).unsqueeze(2).to_broadcast([64, NB, 128]), ALU.mult, [KY("aK"), "gates"], ["aKg"])
                            tt("pool", v2("K", d), v2("K", d), gcols("koutc", d).unsqueeze(2).to_broadcast([64, NB, 128]), ALU.mult, [KY("aK"), "gates"], [KY("aK")])
                            cmin = info[d][0]
                            tt("pool", A64["QgT"][:, d * NB:(d + 1) * NB, :], qT[:, cmin * 64:(cmin + NB) * 64].rearrange("p (n c) -> p n c", c=64),
                               A64["gbc"][:, d * NB:(d + 1) * NB, :], ALU.mult, ["XT0", KY("agbc")], [KY("aQgT")])
                        yield
                        Q, QT, Qn, QTn = "Qa", "QTa", "Qb", "QTb"
                        kq = {"Qa": "aQa", "QTa": "aQTa", "Qb": "aQb", "QTb": "aQTb"}
                        for lvl in range(5):
                            for (d, c_, it) in items:
                                mm(PS[0][0:64, it * 64:(it + 1) * 64], A64[QT][0:64, it, :], A64[Q][0:64, it, :], True, True, [kq[Q], kq[QT]], [PK[0]])
                                mm(PS[1][0:64, it * 64:(it + 1) * 64], A64[Q][0:64, it, :], A64[QT][0:64, it, :], True, True, [kq[Q], kq[QT]], [PK[1]])
                            yield
                            cp("act", A64[Qn][0:64, :, :], PS[0][0:64, :].rearrange("p (n c) -> p n c", c=64), [PK[0]], [kq[Qn]])
                            cp("dve", A64[QTn][0:64, :, :], PS[1][0:64, :].rearrange("p (n c) -> p n c", c=64), [PK[1]], [kq[QTn]])
                            for (d, c_, it) in items:
                                mm(PS[5][0:64, it * 64:(it + 1) * 64], A64[QTn][0:64, it, :], A64["Pmb"][0:64, it, :], True, True, [kq[QTn], "aPmb"], [PK[5]])
                            tt("dve", A64["Pm"][0:64, :, :], A64["Pm"][0:64, :, :], PS[5][0:64, :].rearrange("p (n c) -> p n c", c=64), ALU.add,
                               ["aPm", PK[5]], ["aPm"])
                            cp("act", A64["Pmb"][0:64, :, :], A64["Pm"][0:64, :, :], ["aPm"], ["aPmb"])
                            yield
                            Q, QT, Qn, QTn = Qn, QTn, Q, QT
                        yield
                        for (d, c_, it) in items:
                            mm(PS[2 + d][0:64, (it % NB) * 128:(it % NB + 1) * 128], A64["Pmb"][0:64, it, :], A128["V"][0:64, it, :], True, True,
                               ["aPmb", "aV"], [PK[2 + d]])
                            mm(PS[4][:, it * 64:(it + 1) * 64], A128["Kg"][0:64, it, :], A64["Pmb"][0:64, it, :], True, True, ["aKg", "aPmb"], [PK[4]])
                        for d in range(2):
                            tt("dve", v2("u0", d), PS[2 + d][0:64, :].rearrange("p (n c) -> p n c", c=128),
                               gcols("beta", d).unsqueeze(2).to_broadcast([64, NB, 128]), ALU.mult, [PK[2 + d], "gates"], [KY("au0")])
                        act(A64["wT"][:, :, :], PS[4][:, :].rearrange("p (n c) -> p n c", c=64), AF.Identity, [PK[4]], [KY("awT")], scale=-1.0)

                    def stageC_gen(i, d):
                        b = i // NB
                        info = batch_info(b)
                        par = b % 2
                        A64, A128 = A64v[par], A128v[par]
                        KY = lambda s_: s_ + str(par)
                        c_ = FORD[i] if d == 0 else BORD[i]
                        it = d * NB + (c_ - info[d][0])
                        e8 = d * 4 + h
                        col = lambda nm: G_[nm][0:64, c_, e8:e8 + 1]
                        pC = PS[6 + d]
                        kC = PK[6 + d]
                        Sk = "S%d" % d
                        S_ = Sst[d]
                        ub = ubuf[(i % 2) * 2 + d]
                        uk = "u%d" % ((i % 2) * 2 + d)
                        mm(pC[0:64, 0:128], A64["wT"][:, it, :], S_[:, :], True, True, [KY("awT"), Sk], [kC])
                        yield
                        stt("dve", ub[0:64, :], pC[0:64, 0:128], col("beta"), A128["u0"][0:64, it, :], ALU.mult, ALU.add, [kC, "gates", KY("au0")], [uk])
                        yield
                        if c_ >= 4:
                            cl = c_ - 4
                            mm(pC[0:64, 128:256], A64["QgT"][:, it, :], S_[:, :], True, False, [KY("aQgT"), Sk], [kC])
                            mm(pC[0:64, 128:256], A64["QKT"][0:64, it, :], ub[0:64, :], False, True, [KY("aQKT"), uk], [kC])
                        mm(pC[:, 256:384], A128["K"][0:64, it, :], ub[0:64, :], True, True, [KY("aK"), uk], [kC])
                        yield
                        lastcol = 63 if d == 0 else 0
                        stt("dve", S_[:, :], S_[:, :], A64["gbc"][:, it, lastcol:lastcol + 1], pC[:, 256:384], ALU.mult, ALU.add,
                            [Sk, KY("agbc"), kC], [Sk])
                        yield
                        if c_ >= 4:
                            firstw = (d == 0 and cl < 16) or (d == 1 and cl >= 16)
                            ok = "O%d" % cl
                            if firstw:
                                cp("act", Oacc[0:64, cl, :], pC[0:64, 128:256], [kC], [ok])
                            else:
                                tt("dve", Oacc[0:64, cl, :], Oacc[0:64, cl, :], pC[0:64, 128:256], ALU.add, [kC, ok], [ok])
                        yield

                    def chain_d(d, steps):
                        for i in steps:
                            yield from stageC_gen(i, d)

                    def stageC_steps(steps):
                        alive = [chain_d(0, steps), chain_d(1, steps)]
                        while alive:
                            for g_ in list(alive):
                                try:
                                    next(g_)
                                except StopIteration:
                                    alive.remove(g_)

                    def chain_both(steps):
                        alive = [chain_d(0, steps), chain_d(1, steps)]
                        while alive:
                            for g_ in list(alive):
                                try:
                                    next(g_)
                                except StopIteration:
                                    alive.remove(g_)
                            yield

                    def run_all(g_):
                        for _ in g_:
                            pass
                    NBT = 36 // NB
                    run_all(stageA_batch(0))
                    for b in range(NBT):
                        gC = chain_both(list(range(b * NB, (b + 1) * NB)))
                        gA = stageA_batch(b + 1) if b + 1 < NBT else iter(())
                        doneA = doneC = False
                        while not (doneA and doneC):
                            for _ in range(2):
                                if not doneA:
                                    try:
                                        next(gA)
                                    except StopIteration:
                                        doneA = True
                            if not doneC:
                                try:
                                    next(gC)
                                except StopIteration:
                                    doneC = True
                    if s == 0 and h == 0:
                        dump("O0", Oacc[0:64, 0, :], ["O0"])
                        dump("O31", Oacc[0:64, 31, :], ["O31"])
                    wb = load_w(2560 + h * 128)
                    for blk in range(8):
                        par_ = blk % 2
                        okeys = ["O%d" % (blk * 4 + cc) for cc in range(4)]
                        O4 = Oacc[0:64, blk * 4:blk * 4 + 4, :]
                        sq4 = XTf[0:64, par_ * 512:(par_ + 1) * 512].rearrange("p (c v) -> p c v", v=128)
                        rs_ = rsb[0:64, 4 * blk:4 * blk + 4]
                        ksq, krs = "fsq%d" % par_, "rs%d" % blk
                        tt("pool", sq4, O4, O4, ALU.mult, okeys + ["XTf"], [ksq])
                        P.op("dve", (lambda sq4=sq4, rs_=rs_: lambda e: e.tensor_reduce(out=rs_, in_=sq4, axis=AX.X, op=ALU.add))(),
                             reads=[ksq, "XTf"], writes=[krs])
                        ts("dve", rs_, rs_, 1.0 / 128.0, ALU.mult, [krs], [krs], s2=1e-6, op1=ALU.add)
                        act(rs_, rs_, AF.Sqrt, [krs], [krs])
                        P.op("dve", (lambda rs_=rs_: lambda e: e.reciprocal(out=rs_, in_=rs_))(), reads=[krs], writes=[krs])
                    for blk in range(8):
                        par_ = blk % 2
                        okeys = ["O%d" % (blk * 4 + cc) for cc in range(4)]
                        O4 = Oacc[0:64, blk * 4:blk * 4 + 4, :]
                        z4 = XTf[0:64, 1024 + par_ * 512:1024 + (par_ + 1) * 512].rearrange("p (c v) -> p c v", v=128)
                        rs_ = rsb[0:64, 4 * blk:4 * blk + 4]
                        kz, krs = "fz%d" % par_, "rs%d" % blk
                        pz, pt = (1, 2) if par_ == 0 else (3, 4)
                        for cc in range(4):
                            tk = slice(LCTX + (blk * 4 + cc) * 64, LCTX + (blk * 4 + cc + 1) * 64)
                            for k in range(8):
                                mm(PS[pz][0:64, cc * 128:(cc + 1) * 128], hT[:, k, tk], wbf[wb][:, k, :], k == 0, k == 7, ["hT", "wbf%d" % wb], [PK[pz]])
                        act(z4, PS[pz][0:64, :].rearrange("p (c v) -> p c v", v=128), AF.Silu, [PK[pz], "XTf"], [kz])
                        tt("dve", O4, O4, rs_.unsqueeze(2).to_broadcast([64, 4, 128]), ALU.mult, okeys + [krs], okeys)
                        tt("pool", O4, O4, normw[0:64, :].unsqueeze(1).to_broadcast([64, 4, 128]), ALU.mult, okeys + ["normw"], okeys)
                        tt("dve", O4, O4, z4, ALU.mult, okeys + [kz, "XTf"], okeys)
                        for cc in range(4):
                            tr(PS[pt][:, cc * 64:(cc + 1) * 64], Oacc[0:64, blk * 4 + cc, :], ident[0:64, 0:64], okeys + ["ident"], [PK[pt]])
                        cp("act", mixT[:, 4 + h, blk * 256:(blk + 1) * 256], PS[pt][:, 0:256], [PK[pt]], ["mixG"])
                    if s == 0 and h == 0:
                        dump("mix_g", mixT[:, 4, 0:512], ["mixG"])

            if "out" in stages:
                barrier()
                o = [0]

                def carve(n, dt=F32):
                    words = n if dt == F32 else (n + 1) // 2
                    a = arena[:, o[0]:o[0] + words]
                    o[0] += words
                    assert o[0] <= ARENA_WORDS, o[0]
                    return a.bitcast(BF16)[:, 0:n] if dt == BF16 else a
                woutb = carve(8192, BF16).rearrange("p (k c) -> p k c", k=8)
                gate_bc = carve(1024)
                lng = carve(1024)
                lnb = carve(1024)
                wg = carve(4096).rearrange("p (k c) -> p k c", k=8)
                silucb = carve(1024).rearrange("p (k c) -> p k c", k=8)
                rbuf = [carve(1024) for _ in range(2)]
                stats = carve(16)
                dma(lng, D["lng"][:, :], w=["lng"])
                dma(lnb, D["lnb"][:, :], w=["lnb"])
                dma(gate_bc, D["bgate_bc"][:, :], w=["gate_bc"])
                wout_v = D["w_out"].rearrange("(k p) c -> p k c", p=128)
                for k in range(8):
                    dma(xst[k % 2][:], wout_v[:, k, :], w=["xst%d" % (k % 2)])
                    cp("pool", woutb[:, k, :], xst[k % 2][:], ["xst%d" % (k % 2)], ["woutb"])
                for k in range(8):
                    ts("dve", silucb[:, k, :], ones[:, :], siluc[:, k, s:s + 1], ALU.mult, ["ones", "siluc"], ["silucb"])
                for half in range(2):
                    dma(wg[:], wada[:, :, 2048 + half * 512:2048 + (half + 1) * 512], w=["wg"])
                    for k in range(8):
                        mm(PS[3][:, :], silucb[:, k, :], wg[:, k, :], k == 0, k == 7, ["silucb", "wg"], [PK[3]])
                    tt("dve", gate_bc[:, half * 512:(half + 1) * 512], gate_bc[:, half * 512:(half + 1) * 512], PS[3][:, :], ALU.add,
                       ["gate_bc", PK[3]], ["gate_bc"])
                if s == 0:
                    dump("gate", gate_bc[:, 0:512], ["gate_bc"])
                for ti in range(16):
                    b = ti % 2
                    rk = "r%d" % b
                    dma(xst[b][:], D["x2"][s, ti * 128:(ti + 1) * 128, :], w=["xst%d" % b])
                    for half in range(2):
                        pk = 4 + half + 2 * b
                        for k in range(8):
                            mm(PS[pk][:, :], mixT[:, k, ti * 128:(ti + 1) * 128], woutb[:, k, half * 512:(half + 1) * 512], k == 0, k == 7,
                               ["mixT", "woutb"], [PK[pk]])
                        tt("dve", rbuf[b][:, half * 512:(half + 1) * 512], PS[pk][:, :], gate_bc[:, half * 512:(half + 1) * 512], ALU.mult,
                           [PK[pk], "gate_bc"], [rk])
                    stt("pool", rbuf[b][:, :], xst[b][:, :], DEEP_ALPHA, rbuf[b][:, :], ALU.mult, ALU.add, ["xst%d" % b, rk], [rk])
                    st6 = stats[:, 0:12].rearrange("p (c e) -> p c e", e=6)
                    for half in range(2):
                        P.op("dve", (lambda b=b, half=half, st6=st6: lambda e: e.bn_stats(out=st6[:, half, :], in_=rbuf[b][:, half * 512:(half + 1) * 512]))(),
                             reads=[rk], writes=["stats"])
                    P.op("dve", (lambda st6=st6: lambda e: e.bn_aggr(out=stats[:, 12:14], in_=st6))(), reads=["stats"], writes=["mv"])
                    act(stats[:, 14:15], stats[:, 13:14], AF.Sqrt, ["mv"], ["mv2"], bias=1e-5)
                    P.op("dve", lambda e: e.reciprocal(out=stats[:, 14:15], in_=stats[:, 14:15]), reads=["mv2"], writes=["mv2"])
                    stt("dve", stats[:, 15:16], stats[:, 12:13], -1.0, stats[:, 14:15], ALU.mult, ALU.mult, ["mv", "mv2"], ["mv2"])
                    act(rbuf[b][:, :], rbuf[b][:, :], AF.Identity, [rk, "mv2"], [rk], bias=stats[:, 15:16], scale=stats[:, 14:15])
                    tt("pool", rbuf[b][:, :], rbuf[b][:, :], lng, ALU.mult, [rk, "lng"], [rk])
                    tt("dve", rbuf[b][:, :], rbuf[b][:, :], lnb, ALU.add, [rk, "lnb"], [rk])
                    dma(yout[s, ti * 128:(ti + 1) * 128, :], rbuf[b][:, :], r=[rk])
        P.emit(nc)
    return nc, P


def _core_inputs(inp, sh, core):
    m = dict(sh)
    f = lambda a: np.ascontiguousarray(np.asarray(a, dtype=np.float32))
    b0 = core * NSEQ
    m["x2"] = f(inp["x"][b0:b0 + NSEQ])
    m["ctx2"] = f(inp["ctx"][b0:b0 + NSEQ])
    cc = np.stack([np.asarray(inp["c"][b0], np.float32), np.asarray(inp["c"][b0 + 1], np.float32),
                   np.asarray(inp["c_ctx"], np.float32)], axis=0)
    m["cT"] = f(cc.reshape(3, 8, 128).transpose(2, 1, 0))
    return m


_CACHE = {}


def kernel(**inputs):
    if "nc" not in _CACHE:
        _CACHE["nc"] = build_program()[0]
    nc = _CACHE["nc"]
    sh = _prep_shared(inputs)
    maps = [_core_inputs(inputs, sh, c) for c in range(8)]
    res = run_bass_kernel_spmd(nc, maps, core_ids=list(range(8)))
    out = np.concatenate([r["yout"] for r in res.results], axis=0)
    return out.astype(np.float32)
```

```python
import contextlib
import math
import numpy as np
import concourse.bass as bass
import concourse.mybir as mybir
from concourse.bass_utils import run_bass_kernel_spmd

F32 = mybir.dt.float32
BF16 = mybir.dt.bfloat16
ALU = mybir.AluOpType
AF = mybir.ActivationFunctionType
AX = mybir.AxisListType

ENG_NAMES = ["pe", "act", "dve", "pool", "sp"]
SAME_ENGINE_SYNC = True
N_DMA_SEMS = 12

NSEQ = 2
LCTX = 256
LLAT = 2048
LTOT = LCTX + LLAT
DEEP_ALPHA = 2.0 ** 0.25


class Prog:
    def __init__(self):
        self.ops = []
        self.last_w = {}
        self.readers = {}
        self.barrier_idx = None

    def barrier(self, eng, fn):
        deps = set(self.last_w.values())
        for v in self.readers.values():
            deps.update(v)
        if self.barrier_idx is not None:
            deps.add(self.barrier_idx)
        idx = len(self.ops)
        self.ops.append(dict(eng=eng, fn=fn, deps=deps, dma=False))
        self.barrier_idx = idx
        self.last_w = {}
        self.readers = {}
        return idx

    def op(self, eng, fn, reads=(), writes=(), dma=False):
        writes = list(writes) + [k for k in reads if k.startswith("ps")]
        idx = len(self.ops)
        deps = set()
        if self.barrier_idx is not None:
            deps.add(self.barrier_idx)
        for k in reads:
            if k in self.last_w:
                deps.add(self.last_w[k])
        for k in writes:
            if k in self.last_w:
                deps.add(self.last_w[k])
            deps.update(self.readers.get(k, ()))
        self.ops.append(dict(eng=eng, fn=fn, deps=deps, dma=dma))
        for k in reads:
            self.readers.setdefault(k, []).append(idx)
        for k in writes:
            self.last_w[k] = idx
            self.readers[k] = []
        return idx

    def emit(self, nc):
        ops = self.ops
        pos = {}
        seqcount = {e: 0 for e in ENG_NAMES}
        for i, o in enumerate(ops):
            if not o["dma"]:
                seqcount[o["eng"]] += 1
                pos[i] = seqcount[o["eng"]]
        dma_ops = [i for i, o in enumerate(ops) if o["dma"]]
        dma_sem = {}
        dma_val = {}
        semcnt = [0] * N_DMA_SEMS
        prev_on_sem = {}
        for n, i in enumerate(dma_ops):
            s = n % N_DMA_SEMS
            semcnt[s] += 16
            dma_sem[i] = s
            dma_val[i] = semcnt[s]
            if s in prev_on_sem:
                ops[i]["deps"].add(prev_on_sem[s])
            prev_on_sem[s] = i
        known = {e: {p: 0 for p in ENG_NAMES} for e in ENG_NAMES}
        known_dma = {e: [0] * N_DMA_SEMS for e in ENG_NAMES}
        flagged = set()
        waits = [[] for _ in ops]
        for i, o in enumerate(ops):
            e = o["eng"]
            for d in sorted(o["deps"]):
                od = ops[d]
                if od["dma"]:
                    s = dma_sem[d]
                    if dma_val[d] > known_dma[e][s]:
                        known_dma[e][s] = dma_val[d]
                        waits[i].append(("dma", s, dma_val[d]))
                else:
                    p = od["eng"]
                    if p == e and (e == "pe" or not SAME_ENGINE_SYNC):
                        continue
                    if pos[d] > known[e][p]:
                        known[e][p] = pos[d]
                        flagged.add(d)
                        waits[i].append(("eng", p, d))
        cnt = {e: 0 for e in ENG_NAMES}
        val = {}
        for i, o in enumerate(ops):
            if i in flagged:
                cnt[o["eng"]] += 1
                val[i] = cnt[o["eng"]]
        per_eng = {e: [] for e in ENG_NAMES}
        for i, o in enumerate(ops):
            per_eng[o["eng"]].append(i)
        self.stats = {e: len(per_eng[e]) for e in ENG_NAMES}

        with contextlib.ExitStack() as st:
            esem = {e: st.enter_context(nc.semaphore("s_" + e)) for e in ENG_NAMES}
            dsem = [st.enter_context(nc.semaphore("d_%d" % k)) for k in range(N_DMA_SEMS)]
            block = st.enter_context(nc.Block())

            def run(e, engobj):
                for i in per_eng[e]:
                    o = ops[i]
                    mx = {}
                    for w in waits[i]:
                        if w[0] == "dma":
                            key = ("d", w[1]); v = w[2]
                        else:
                            key = ("e", w[1]); v = val[w[2]]
                        mx[key] = max(mx.get(key, 0), v)
                    for key, v in mx.items():
                        sem = dsem[key[1]] if key[0] == "d" else esem[key[1]]
                        engobj.wait_ge(sem, v)
                    ins = o["fn"](engobj)
                    if o["dma"]:
                        ins.then_inc(dsem[dma_sem[i]], 16)
                    elif i in flagged:
                        ins.then_inc(esem[e], 1)
                if e == "sp":
                    for s in range(N_DMA_SEMS):
                        if semcnt[s] > 0:
                            engobj.wait_ge(dsem[s], semcnt[s])

            @block.tensor
            def _(eng):
                run("pe", eng)

            @block.scalar
            def _(eng):
                run("act", eng)

            @block.vector
            def _(eng):
                run("dve", eng)

            @block.gpsimd
            def _(eng):
                run("pool", eng)

            @block.sync
            def _(eng):
                run("sp", eng)


def _consts():
    c = {}
    c["ident"] = np.eye(128, dtype=np.float32)
    k = np.arange(64)
    tri = np.stack([(k[:, None] <= k[None, :]), (k[:, None] >= k[None, :])]).astype(np.float32)
    su = np.stack([(k[:, None] > k[None, :]), (k[:, None] < k[None, :])]).astype(np.float32)
    mi = np.stack([(k[None, :] >= k[:, None]), (k[None, :] <= k[:, None])]).astype(np.float32)
    ms = np.stack([(k[None, :] > k[:, None]), (k[None, :] < k[:, None])]).astype(np.float32)
    c["masks"] = np.concatenate([tri[0], tri[1], su[0], su[1], mi[0], mi[1], ms[0], ms[1]], axis=1).astype(np.float32)
    c["ones"] = np.ones((128, 128), np.float32)
    return c


def _prep_shared(inp):
    f = lambda a: np.ascontiguousarray(np.asarray(a, dtype=np.float32))
    sh = {}
    sh["w_ada"] = f(inp["w_ada"][0])
    b_ada = f(inp["b_ada"][0])
    sh["bcol"] = f(b_ada.reshape(24, 128).T)
    sh["bgate_bc"] = f(np.broadcast_to(b_ada[2048:3072][None, :], (128, 1024)))
    w_in = f(inp["w_in"][0])
    sh["w_in"] = w_in
    sh["wgate"] = f(w_in[:, 3072:3088].reshape(8, 128, 16).transpose(1, 0, 2))
    lre = f(inp["s5_lambda_re"][0]); lim = f(inp["s5_lambda_im"][0]); ldt = f(inp["s5_log_dt"][0])

    def smaj(a):
        o = np.zeros((128, 32), np.float32)
        for pair in range(16):
            for d in range(2):
                for gl in range(2):
                    o[gl * 64:(gl + 1) * 64, pair * 2 + d] = a[d, 2 * pair + gl, :]
        return o
    sh["lam_re"] = smaj(lre)
    sh["lam_im"] = smaj(lim)
    sh["logdt"] = smaj(np.broadcast_to(ldt[:, :, None], (2, 32, 64)))
    bre = f(inp["s5_b_re"][0]); bim = f(inp["s5_b_im"][0])
    cre = f(inp["s5_c_re"][0]); cim = f(inp["s5_c_im"][0])
    Bp = np.zeros((128, 4, 4, 2, 2, 128), np.float32)
    Cp = np.zeros((128, 4, 4, 2, 2, 128), np.float32)
    for j in range(4):
        for gp in range(4):
            for gl in range(2):
                gq = 2 * gp + gl
                g = 8 * j + gq
                for d in range(2):
                    Bp[gq * 16:(gq + 1) * 16, j, gp, d, 0, gl * 64:(gl + 1) * 64] = bre[d, g].T
                    Bp[gq * 16:(gq + 1) * 16, j, gp, d, 1, gl * 64:(gl + 1) * 64] = bim[d, g].T
                    Cp[gl * 64:(gl + 1) * 64, j, gp, d, 0, gq * 16:(gq + 1) * 16] = cre[d, g].T
                    Cp[gl * 64:(gl + 1) * 64, j, gp, d, 1, gq * 16:(gq + 1) * 16] = cim[d, g].T
    sh["Bp"] = f(Bp.reshape(128, 8192))
    sh["Cp"] = f(Cp.reshape(128, 8192))
    sh["dcol"] = f(inp["s5_d"][0].reshape(4, 128).T)
    sh["wglu"] = f(inp["w_glu"][0].reshape(4, 128, 512).transpose(1, 0, 2))
    sh["bglu"] = f(inp["b_glu"][0].reshape(4, 128).T)
    sh["convw"] = f(inp["conv_w"][0].reshape(9, 12, 128).transpose(2, 1, 0))
    sh["alog"] = f(np.broadcast_to(inp["gdn_a_log"][0].reshape(1, 8), (64, 8)))
    sh["dtb"] = f(np.broadcast_to(inp["gdn_dt_bias"][0].reshape(1, 8), (64, 8)))
    sh["normw"] = f(np.broadcast_to(inp["gdn_norm_w"][0].reshape(1, 128), (64, 128)))
    sh["w_out"] = f(inp["w_out"][0])
    sh["lng"] = f(np.broadcast_to(inp["ln_g"][0][None, :], (128, 1024)))
    sh["lnb"] = f(np.broadcast_to(inp["ln_b"][0][None, :], (128, 1024)))
    sh.update(_consts())
    return sh


IN_SHAPES = {
    "x2": [NSEQ, LLAT, 1024], "ctx2": [NSEQ, LCTX, 1024], "cT": [128, 8, 3],
    "w_ada": [1024, 3072], "bcol": [128, 24], "bgate_bc": [128, 1024], "w_in": [1024, 3088],
    "wgate": [128, 8, 16], "lam_re": [128, 32], "lam_im": [128, 32], "logdt": [128, 32],
    "Bp": [128, 8192], "Cp": [128, 8192], "dcol": [128, 4], "wglu": [128, 4, 512], "bglu": [128, 4],
    "convw": [128, 12, 9], "alog": [64, 8], "dtb": [64, 8], "normw": [64, 128],
    "w_out": [1024, 1024], "lng": [128, 1024], "lnb": [128, 1024],
    "ident": [128, 128], "masks": [64, 512], "ones": [128, 128],
}


def build_program(dbg=None, stages=("s5", "gdn", "out")):
    nc = bass.Bass("TRN2", target_bir_lowering=False)
    D = {k: nc.dram_tensor(k, v, F32, kind="ExternalInput").ap() for k, v in IN_SHAPES.items()}
    yout = nc.dram_tensor("yout", [NSEQ, LLAT, 1024], F32, kind="ExternalOutput").ap()
    dbg_t = {}
    if dbg:
        for k, shp in dbg.items():
            dbg_t[k] = nc.dram_tensor("dbg_" + k, shp, F32, kind="ExternalOutput").ap()
    P = Prog()
    st = contextlib.ExitStack()
    uid = [0]

    def sb(name, shape, dt=F32):
        return st.enter_context(nc.sbuf_tensor("sb_" + name, shape, dt))

    def psum(name):
        return st.enter_context(nc.psum_tensor(name, [128, 512], F32))

    with st:
        hT = sb("hT", [128, 8, LTOT], BF16)
        mixT = sb("mixT", [128, 8, LLAT], BF16)
        ident = sb("ident", [128, 128])
        ones = sb("ones", [128, 128])
        masks = sb("masks", [64, 512])
        identb = sb("identb", [128, 128], BF16)
        masksb = sb("masksb", [64, 512], BF16)
        onesb = sb("onesb", [64, 128], BF16)
        xst = [sb("xst%d" % i, [128, 1024]) for i in range(2)]
        wst = [sb("wst%d" % i, [128, 8, 128]) for i in range(2)]
        wbf = [sb("wbf%d" % i, [128, 8, 128], BF16) for i in range(2)]
        modsc = sb("modsc", [128, 16, 3])
        siluc = sb("siluc", [128, 8, 3])
        dbgs = sb("dbgs", [128, 512]) if dbg else None
        bcol = sb("bcol", [128, 24])
        small = sb("small", [128, 64])
        ARENA_WORDS = 28800
        arena = sb("arena", [128, ARENA_WORDS])
        PS = [psum("ps%d" % i) for i in range(8)]
        PK = ["ps%d" % i for i in range(8)]

        def dma(out, in_, r=(), w=()):
            P.op("sp", lambda e: e.dma_start(out=out, in_=in_), reads=r, writes=w, dma=True)

        def act(out, in_, func, r, w, bias=None, scale=None):
            kw = {}
            if bias is not None:
                kw["bias"] = bias
            if scale is not None:
                kw["scale"] = scale
            P.op("act", lambda e: e.activation(out=out, in_=in_, func=func, **kw), reads=r, writes=w)

        def tt(eng, out, in0, in1, op, r, w):
            P.op(eng, lambda e: e.tensor_tensor(out=out, in0=in0, in1=in1, op=op), reads=r, writes=w)

        def ts(eng, out, in0, s1, op0, r, w, s2=None, op1=None):
            if op1 is None:
                P.op(eng, lambda e: e.tensor_scalar(out=out, in0=in0, scalar1=s1, scalar2=None, op0=op0), reads=r, writes=w)
            else:
                P.op(eng, lambda e: e.tensor_scalar(out=out, in0=in0, scalar1=s1, scalar2=s2, op0=op0, op1=op1), reads=r, writes=w)

        def stt(eng, out, in0, scalar, in1, op0, op1, r, w):
            eng = "dve"
            P.op(eng, lambda e: e.scalar_tensor_tensor(out=out, in0=in0, scalar=scalar, in1=in1, op0=op0, op1=op1), reads=r, writes=w)

        def cp(eng, out, in_, r, w):
            if eng == "act":
                act(out, in_, AF.Copy, r, w)
            else:
                P.op(eng, lambda e: e.tensor_copy(out=out, in_=in_), reads=r, writes=w)

        def mm(out, lhsT, rhs, start, stop, r, w):
            P.op("pe", lambda e: e.matmul(out, lhsT=lhsT, rhs=rhs, start=start, stop=stop), reads=r, writes=w)

        def tr(out, in_, idn, r, w):
            P.op("pe", lambda e: e.transpose(out, in_, idn), reads=r, writes=w)

        def dump(name, ap, r):
            if name in dbg_t:
                if ap.dtype == BF16:
                    n = ap.shape[-1]
                    cp("dve", dbgs[:, 0:n], ap, r, ["dbgs"])
                    dma(dbg_t[name], dbgs[:, 0:n], r=["dbgs"])
                else:
                    dma(dbg_t[name], ap, r=r)

        def barrier():
            P.barrier("dve", lambda e: e.memset(small[:, 63:64], 0.0))

        dma(ident[:], D["ident"][:, :], w=["ident"])
        dma(ones[:], D["ones"][:, :], w=["ones"])
        dma(masks[:], D["masks"][:, :], w=["masks"])
        dma(bcol[:], D["bcol"][:, :], w=["bcol"])
        cp("dve", identb[:], ident[:], ["ident"], ["identb"])
        cp("dve", masksb[:], masks[:], ["masks"], ["masksb"])
        cp("dve", onesb[:], ones[0:64, :], ["ones"], ["onesb"])
        SUb = [masksb[:, 128:192], masksb[:, 192:256]]
        TRI = [masks[:, 0:64], masks[:, 64:128]]
        SU = [masks[:, 128:192], masks[:, 192:256]]
        MI = [masks[:, 256:320], masks[:, 320:384]]
        MS = [masks[:, 384:448], masks[:, 448:512]]

        dma(siluc[:], D["cT"][:, :, :], w=["siluc"])
        act(siluc[:], siluc[:], AF.Silu, ["siluc"], ["siluc"])
        wada = D["w_ada"].rearrange("(k p) c -> p k c", p=128)
        for t in range(16):
            b = t % 2
            dma(wst[b][:], wada[:, :, t * 128:(t + 1) * 128], w=["wst%d" % b])
            for k in range(8):
                mm(PS[0][:, t * 4:t * 4 + 3], wst[b][:, k, :], siluc[:, k, :], k == 0, k == 7, ["wst%d" % b, "siluc"], [PK[0]])
        for t in range(16):
            ts("dve", modsc[:, t, :], PS[0][:, t * 4:t * 4 + 3], bcol[:, t:t + 1], ALU.add, [PK[0], "bcol"], ["modsc"],
               s2=(1.0 if t >= 8 else 0.0), op1=ALU.add)

        win = D["w_in"].rearrange("(k p) c -> p k c", p=128)
        wcount = [0]

        def load_w(c0, ncols=128):
            b = wcount[0] % 2
            wcount[0] += 1
            dma(wst[b][:, :, 0:ncols], win[:, :, c0:c0 + ncols], w=["wst%d" % b])
            cp("pool", wbf[b][:, :, 0:ncols], wst[b][:, :, 0:ncols], ["wst%d" % b], ["wbf%d" % b])
            return b

        def proj_fm(b, ps_ap, tok0, ntok, pk):
            for k in range(8):
                mm(ps_ap, wbf[b][:, k, :], hT[:, k, tok0:tok0 + ntok], k == 0, k == 7, ["wbf%d" % b, "hT"], [pk])

        TOKBLK = [(0, 256), (256, 512), (768, 512), (1280, 512), (1792, 512)]

        for s in range(NSEQ):
            barrier()
            for tt_i in range(18):
                b = tt_i % 2
                if tt_i < 2:
                    src = D["ctx2"][s, tt_i * 128:(tt_i + 1) * 128, :]
                    jcol = 2
                else:
                    src = D["x2"][s, (tt_i - 2) * 128:(tt_i - 1) * 128, :]
                    jcol = s
                dma(xst[b][:], src, w=["xst%d" % b])
                for half in range(2):
                    pk = 1 + half
                    for kk in range(4):
                        k = half * 4 + kk
                        tr(PS[pk][:, kk * 128:(kk + 1) * 128], xst[b][:, k * 128:(k + 1) * 128], ident[:], ["xst%d" % b, "ident"], [PK[pk]])
                    for kk in range(4):
                        k = half * 4 + kk
                        act(hT[:, k, tt_i * 128:(tt_i + 1) * 128], PS[pk][:, kk * 128:(kk + 1) * 128], AF.Identity,
                            [PK[pk], "modsc"], ["hT"], bias=modsc[:, k, jcol:jcol + 1], scale=modsc[:, 8 + k, jcol:jcol + 1])
            if s == 0:
                dump("hT", hT[:, 0, 0:512], ["hT"])

            if "s5" in stages:
                barrier()
                o = [0]

                def carve(n, dt=F32):
                    words = n if dt == F32 else (n + 1) // 2
                    a = arena[:, o[0]:o[0] + words]
                    o[0] += words
                    assert o[0] <= ARENA_WORDS, o[0]
                    return a.bitcast(BF16)[:, 0:n] if dt == BF16 else a
                Bpb = carve(2048, BF16)
                Cpb = carve(2048, BF16)
                uT = carve(4 * LTOT, BF16).rearrange("p (j t) -> p j t", j=4)
                o_hb = o[0]
                HbD = [[carve(LTOT) for _ in range(2)] for _ in range(2)]
                HbB = [[carve(LLAT, BF16) for _ in range(2)] for _ in range(2)]
                XPAD = 256
                XaF = [[carve(288 + XPAD) for _ in range(2)] for _ in range(2)]
                XbF = [[carve(288 + XPAD) for _ in range(2)] for _ in range(2)]
                for d_ in range(2):
                    for r_ in range(2):
                        for buf_, nm_ in ((XaF, "Xa%d" % d_), (XbF, "Xb%d" % d_)):
                            P.op("pool", (lambda t_=buf_[d_][r_]: lambda e: e.memset(t_[:, :], 0.0))(), writes=[nm_])
                XOFF = [XPAD, 0]
                XaD = [[XaF[d_][r_][:, XOFF[d_]:XOFF[d_] + 288] for r_ in range(2)] for d_ in range(2)]
                XbD = [[XbF[d_][r_][:, XOFF[d_]:XOFF[d_] + 288] for r_ in range(2)] for d_ in range(2)]
                prm = carve(32 * 40).rearrange("p (q c) -> p q c", c=32)
                prm2 = carve(32 * 28).rearrange("p (q c) -> p q c", c=32)
                ytmp = carve(512)
                LR, LI, DT, ER, TH, AR, AI, CR, CI, T0, T1, T2 = range(12)
                PW = 13
                NP = PW + 16
                P.op("dve", lambda e: e.memset(prm[:, :, :], 0.0), writes=["prm"])
                for q_, nm in ((LR, "lam_re"), (LI, "lam_im"), (DT, "logdt")):
                    dma(prm[:, q_, :], D[nm][:, :], w=["prm"])

                def ptt(dst, a_, b_, op):
                    tt("dve", prm[:, dst, :], prm[:, a_, :], prm[:, b_, :], op, ["prm"], ["prm"])

                def pts(dst, a_, s1, op0, s2=None, op1=None, eng="dve"):
                    ts(eng, prm[:, dst, :], prm[:, a_, :], s1, op0, ["prm"], ["prm"], s2=s2, op1=op1)
                act(prm[:, DT, :], prm[:, DT, :], AF.Exp, ["prm"], ["prm"])
                ptt(ER, LR, DT, ALU.mult)
                act(prm[:, ER, :], prm[:, ER, :], AF.Exp, ["prm"], ["prm"])
                ptt(TH, LI, DT, ALU.mult)
                I32 = mybir.dt.int32
                kint = prm[:, 12, :].bitcast(I32)
                PI_C = 3.14159
                for (dst, shift_) in ((AI, 0.0), (AR, 0.5 * math.pi)):
                    pts(T0, TH, 1.0 / (2 * math.pi), ALU.mult, s2=shift_ / (2 * math.pi), op1=ALU.add)
                    P.op("dve", lambda e: e.tensor_copy(out=kint, in_=prm[:, T0, :]), reads=["prm"], writes=["prm"])
                    P.op("dve", lambda e: e.tensor_copy(out=prm[:, T1, :], in_=kint), reads=["prm"], writes=["prm"])
                    pts(T2, TH, shift_, ALU.add)
                    stt("dve", prm[:, T0, :], prm[:, T1, :], -2 * math.pi, prm[:, T2, :], ALU.mult, ALU.add, ["prm"], ["prm"])
                    pts(T0, T0, -PI_C, ALU.max, s2=PI_C, op1=ALU.min)
                    act(prm[:, dst, :], prm[:, T0, :], AF.Sin, ["prm"], ["prm"])
                ptt(AR, AR, ER, ALU.mult)
                ptt(AI, AI, ER, ALU.mult)
                pts(T0, AR, -1.0, ALU.add)
                ptt(T1, T0, LR, ALU.mult)
                ptt(T2, AI, LI, ALU.mult)
                ptt(CR, T1, T2, ALU.add)
                ptt(T1, AI, LR, ALU.mult)
                ptt(T2, T0, LI, ALU.mult)
                ptt(CI, T1, T2, ALU.subtract)
                ptt(T1, LR, LR, ALU.mult)
                ptt(T2, LI, LI, ALU.mult)
                ptt(T1, T1, T2, ALU.add)
                P.op("dve", lambda e: e.reciprocal(out=prm[:, T1, :], in_=prm[:, T1, :]), reads=["prm"], writes=["prm"])
                ptt(CR, CR, T1, ALU.mult)
                ptt(CI, CI, T1, ALU.mult)

                def cmul(dst_r, dst_i, ar_, ai_, br_, bi_):
                    ptt(T0, ar_, br_, ALU.mult)
                    ptt(T1, ai_, bi_, ALU.mult)
                    ptt(T2, ar_, bi_, ALU.mult)
                    ptt(dst_r, T0, T1, ALU.subtract)
                    ptt(T0, ai_, br_, ALU.mult)
                    ptt(dst_i, T2, T0, ALU.add)
                pts(PW, AR, 1.0, ALU.mult)
                pts(PW + 1, AI, 1.0, ALU.mult)
                for k in range(2, 9):
                    cmul(PW + 2 * (k - 1), PW + 2 * (k - 1) + 1, PW + 2 * (k - 2), PW + 2 * (k - 2) + 1, AR, AI)
                for k in range(1, 9):
                    pts(NP + k - 1, PW + 2 * (k - 1) + 1, -1.0, ALU.mult)
                ts("dve", prm2[:, 0, :], prm[:, PW + 14, :], 1.0, ALU.mult, ["prm"], ["prm2"])
                ts("dve", prm2[:, 1, :], prm[:, PW + 15, :], 1.0, ALU.mult, ["prm"], ["prm2"])
                for m in range(9):
                    if m > 0:
                        r0, i0_ = prm2[:, 3 * (m - 1), :], prm2[:, 3 * (m - 1) + 1, :]
                        tt("dve", prm[:, T0, :], r0, r0, ALU.mult, ["prm2", "prm"], ["prm"])
                        tt("dve", prm[:, T1, :], i0_, i0_, ALU.mult, ["prm2", "prm"], ["prm"])
                        tt("dve", prm2[:, 3 * m, :], prm[:, T0, :], prm[:, T1, :], ALU.subtract, ["prm"], ["prm2"])
                        tt("dve", prm[:, T0, :], r0, i0_, ALU.mult, ["prm2", "prm"], ["prm"])
                        ts("dve", prm2[:, 3 * m + 1, :], prm[:, T0, :], 2.0, ALU.mult, ["prm"], ["prm2"])
                    ts("dve", prm2[:, 3 * m + 2, :], prm2[:, 3 * m + 1, :], -1.0, ALU.mult, ["prm2"], ["prm2"])
                if s == 0:
                    dump("prm", prm[:, 0:40, :], ["prm"])
                for j in range(4):
                    wb = load_w(j * 128)
                    for bi, (t0, nt) in enumerate(TOKBLK):
                        pk = 1 + bi % 2
                        proj_fm(wb, PS[pk][:, 0:nt], t0, nt, PK[pk])
                        cp("act", uT[:, j, t0:t0 + nt], PS[pk][:, 0:nt], [PK[pk]], ["uT"])
                if s == 0:
                    dump("uT", uT[:, 0, 0:512], ["uT"])
                dcol = small[:, 0:4]
                dma(dcol, D["dcol"][:, :], w=["dcol"])
                gT = mixT[:, 0:4, :]
                Bpv = Bpb.rearrange("p (g d r c) -> p g d r c", g=4, d=2, r=2)
                Cpv = Cpb.rearrange("p (g d r c) -> p g d r c", g=4, d=2, r=2)
                for j in range(4):
                    for pc in range(2):
                        dma(xst[pc][:], D["Bp"][:, j * 2048 + pc * 1024:j * 2048 + (pc + 1) * 1024], w=["xst%d" % pc])
                        cp("pool", Bpb[:, pc * 1024:(pc + 1) * 1024], xst[pc][:], ["xst%d" % pc], ["Bpb"])
                    for gp in range(4):
                        b = gp % 2
                        pcg = j * 4 + gp
                        dma(xst[b][:, 0:512], D["Cp"][:, pcg * 512:(pcg + 1) * 512], w=["xst%d" % b])
                        for d in range(2):
                            col = pcg * 2 + d
                            cre_ = xst[b][:, d * 256:d * 256 + 128]
                            cim_ = xst[b][:, d * 256 + 128:d * 256 + 256]
                            t_ = xst[b][:, 512 + d * 128:512 + (d + 1) * 128]
                            base = gp * 512 + d * 256
                            xk = "xst%d" % b
                            ts("dve", t_, cim_, prm[:, CI, col:col + 1], ALU.mult, [xk, "prm"], [xk])
                            stt("dve", Cpb[:, base:base + 128], cre_, prm[:, CR, col:col + 1], t_, ALU.mult, ALU.subtract, [xk, "prm"], ["Cpb"])
                            ts("dve", t_, cre_, prm[:, CI, col:col + 1], ALU.mult, [xk, "prm"], [xk])
                            stt("dve", t_, cim_, prm[:, CR, col:col + 1], t_, ALU.mult, ALU.add, [xk, "prm"], [xk])
                            ts("dve", Cpb[:, base + 128:base + 256], t_, -1.0, ALU.mult, [xk], ["Cpb"])
                    YP = [PS[4 + q] for q in range(4)]
                    YK = [PK[4 + q] for q in range(4)]
                    rcount = [0, 0, 0, 0]

                    def s5_body(gp, d):
                        pair = j * 4 + gp
                        col = pair * 2 + d
                        E = "dve"
                        Hb = HbD[d]
                        hk = "Hb%d" % d
                        Xa, Xb = XaD[d], XbD[d]
                        for ri in range(2):
                            for bi, (t0, nt) in enumerate(TOKBLK):
                                pk = 1 + (bi + ri) % 3
                                mm(PS[pk][:, 0:nt], Bpv[:, gp, d, ri, :], uT[:, j, t0:t0 + nt], True, True, ["Bpb", "uT"], [PK[pk]])
                                cp("act", Hb[ri].rearrange("p (s c) -> p c s", s=8)[:, t0 // 8:(t0 + nt) // 8, :],
                                   PS[pk][:, 0:nt].rearrange("p (c s) -> p c s", s=8), [PK[pk]], [hk])
                        yield "hold"
                        Hr = Hb[0].rearrange("p (s c) -> p c s", s=8)
                        Hi = Hb[1].rearrange("p (s c) -> p c s", s=8)
                        a_r = prm[:, PW, col:col + 1]
                        a_i = prm[:, PW + 1, col:col + 1]
                        a_ni = prm[:, NP, col:col + 1]
                        order = range(1, 8) if d == 0 else range(6, -1, -1)
                        for s_ in order:
                            sp_ = s_ - 1 if d == 0 else s_ + 1
                            stt(E, Hr[:, :, s_], Hr[:, :, sp_], a_r, Hr[:, :, s_], ALU.mult, ALU.add, [hk, "prm"], [hk])
                            yield
                            stt(E, Hi[:, :, s_], Hr[:, :, sp_], a_i, Hi[:, :, s_], ALU.mult, ALU.add, [hk, "prm"], [hk])
                            yield
                            stt(E, Hr[:, :, s_], Hi[:, :, sp_], a_ni, Hr[:, :, s_], ALU.mult, ALU.add, [hk, "prm"], [hk])
                            yield
                            stt(E, Hi[:, :, s_], Hi[:, :, sp_], a_r, Hi[:, :, s_], ALU.mult, ALU.add, [hk, "prm"], [hk])
                            yield
                        se = 7 if d == 0 else 0
                        xak, xbk = "Xa%d" % d, "Xb%d" % d
                        for ri, Hv in enumerate((Hr, Hi)):
                            if d == 0:
                                cp(E, Xa[ri][:, 0:288], Hv[:, :, se], [hk], [xak])
                                yield
                            else:
                                cp(E, Xa[ri][:, 0:256], Hv[:, 32:288, se], [hk], [xak])
                                yield
                                cp(E, Xa[ri][:, 256:288], Hv[:, 0:32, se], [hk], [xak])
                                yield
                        cur, nxt, ck, nk = Xa, Xb, xak, xbk
                        curF, nxtF = XaF[d], XbF[d]
                        for m in range(9):
                            sh_ = 1 << m
                            A_r = prm2[:, 3 * m, col:col + 1]
                            A_i = prm2[:, 3 * m + 1, col:col + 1]
                            A_ni = prm2[:, 3 * m + 2, col:col + 1]
                            off_ = XPAD - sh_ if d == 0 else sh_
                            dst = slice(0, 288)
                            s0_ = curF[0][:, off_:off_ + 288]
                            s1_ = curF[1][:, off_:off_ + 288]
                            stt(E, nxt[0][:, dst], s0_, A_r, cur[0][:, dst], ALU.mult, ALU.add, [ck, "prm2"], [nk])
                            yield
                            stt(E, nxt[0][:, dst], s1_, A_ni, nxt[0][:, dst], ALU.mult, ALU.add, [ck, nk, "prm2"], [nk])
                            yield
                            stt(E, nxt[1][:, dst], s0_, A_i, cur[1][:, dst], ALU.mult, ALU.add, [ck, "prm2"], [nk])
                            yield
                            stt(E, nxt[1][:, dst], s1_, A_r, nxt[1][:, dst], ALU.mult, ALU.add, [ck, nk, "prm2"], [nk])
                            yield
                            cur, nxt, ck, nk = nxt, cur, nk, ck
                            curF, nxtF = nxtF, curF
                        HBr = HbB[d][0].rearrange("p (s c) -> p c s", s=8)
                        HBi = HbB[d][1].rearrange("p (s c) -> p c s", s=8)
                        bk = "HbB%d" % d
                        for s_ in range(8):
                            kpow = s_ + 1 if d == 0 else 8 - s_
                            p_r = prm[:, PW + 2 * (kpow - 1), col:col + 1]
                            p_i = prm[:, PW + 2 * (kpow - 1) + 1, col:col + 1]
                            p_ni = prm[:, NP + kpow - 1, col:col + 1]
                            if d == 0:
                                Hs_r, Hs_i = cur[0][:, 31:287], cur[1][:, 31:287]
                            else:
                                Hs_r, Hs_i = cur[0][:, 1:257], cur[1][:, 1:257]
                            stt(E, Hr[:, 32:288, s_], Hs_r, p_r, Hr[:, 32:288, s_], ALU.mult, ALU.add, [hk, ck, "prm"], [hk])
                            yield
                            stt(E, HBr[:, :, s_], Hs_i, p_ni, Hr[:, 32:288, s_], ALU.mult, ALU.add, [hk, ck, "prm"], [bk])
                            yield
                            stt(E, Hi[:, 32:288, s_], Hs_r, p_i, Hi[:, 32:288, s_], ALU.mult, ALU.add, [hk, ck, "prm"], [hk])
                            yield
                            stt(E, HBi[:, :, s_], Hs_i, p_r, Hi[:, 32:288, s_], ALU.mult, ALU.add, [hk, ck, "prm"], [bk])
                            yield
                        for q in range(4):
                            for ri in range(2):
                                mm(YP[q][:, :], Cpv[:, gp, d, ri, :], HbB[d][ri].rearrange("p (s c) -> p c s", s=8)[:, q * 64:(q + 1) * 64, :],
                                   rcount[q] == 0, rcount[q] == 15, ["Cpb", bk], [YK[q]])
                                rcount[q] += 1

                    def s5_stream(d):
                        for gp in range(4):
                            yield from s5_body(gp, d)
                    g0, g1 = s5_stream(0), s5_stream(1)
                    for _ in range(44):
                        next(g0)
                    alive = [g0, g1]
                    hold = {id(g0): 0, id(g1): 0}
                    while alive:
                        progressed = False
                        for g_ in list(alive):
                            if hold[id(g_)] > 0 and len(alive) > 1:
                                hold[id(g_)] -= 1
                                continue
                            try:
                                r_ = next(g_)
                                progressed = True
                                if r_ == "hold":
                                    hold[id(g_)] = 32
                            except StopIteration:
                                alive.remove(g_)
                        if not progressed:
                            for k_ in hold:
                                hold[k_] = 0
                    for q in range(4):
                        stt("dve", ytmp, uT[:, j, LCTX + q * 512:LCTX + (q + 1) * 512], dcol[:, j:j + 1], YP[q][:, :], ALU.mult, ALU.add,
                            ["uT", "dcol", YK[q]], ["ytmp"])
                        if s == 0 and j == 0 and q == 0:
                            dump("y0", ytmp, ["ytmp"])
                        act(gT[:, j, q * 512:(q + 1) * 512], ytmp, AF.Gelu, ["ytmp"], ["gT"])
                barrier()
                o[0] = o_hb
                zt = carve(512)
                sig = carve(4 * 512).rearrange("p (j c) -> p j c", j=4)
                wgl = carve(2048, BF16).rearrange("p (j c) -> p j c", j=4)
                bgl = small[:, 4:8]
                dma(bgl, D["bglu"][:, :], w=["bglu"])
                for j in range(4):
                    dma(xst[j % 2][:, 0:512], D["wglu"][:, j, :], w=["xst%d" % (j % 2)])
                    cp("pool", wgl[:, j, :], xst[j % 2][:, 0:512], ["xst%d" % (j % 2)], ["wgl"])
                for q in range(4):
                    for jo in range(4):
                        for ji in range(4):
                            mm(PS[4 + jo][:, :], wgl[:, ji, jo * 128:(jo + 1) * 128], gT[:, ji, q * 512:(q + 1) * 512], ji == 0, ji == 3,
                               ["wgl", "gT"], [PK[4 + jo]])
                        act(sig[:, jo, :], PS[4 + jo][:, :], AF.Sigmoid, [PK[4 + jo], "bglu"], ["sig%d" % jo], bias=bgl[:, jo:jo + 1])
                    for jo in range(4):
                        wb = load_w(512 + jo * 128)
                        proj_fm(wb, PS[1 + jo % 2][:, :], LCTX + q * 512, 512, PK[1 + jo % 2])
                        act(zt, PS[1 + jo % 2][:, :], AF.Silu, [PK[1 + jo % 2]], ["zt"])
                        tt("dve", sig[:, jo, :], sig[:, jo, :], zt, ALU.mult, ["sig%d" % jo, "zt"], ["sig%d" % jo])
                    for jo in range(4):
                        tt("pool", gT[:, jo, q * 512:(q + 1) * 512], gT[:, jo, q * 512:(q + 1) * 512], sig[:, jo, :], ALU.mult,
                           ["gT", "sig%d" % jo], ["gT"])
                if s == 0:
                    dump("mix_s5", mixT[:, 0, 0:512], ["gT"])
            if "gdn" in stages:
                barrier()
                o = [0]

                def carve(n, dt=F32):
                    words = n if dt == F32 else (n + 1) // 2
                    a = arena[:, o[0]:o[0] + words]
                    o[0] += words
                    assert o[0] <= ARENA_WORDS, o[0]
                    return a.bitcast(BF16)[:, 0:n] if dt == BF16 else a

                def c3(n_, a_, b_):
                    return carve(n_ * a_ * b_).rearrange("p (n a b) -> p n a b", n=n_, a=a_)
                wgb = carve(128, BF16).rearrange("p (k c) -> p k c", k=8)
                gpar = carve(32)
                normw = carve(128)
                cwt = carve(108).rearrange("p (t k) -> p t k", t=12)
                G_ = {nm: carve(288).rearrange("p (c e) -> p c e", e=8) for nm in ("beta", "g", "gcum", "koutc")}
                G_["gam"] = G_["gcum"]
                XTf = carve(LTOT)
                XT = [carve(LTOT, BF16) for _ in range(3)]
                lpad = carve(34 * 66, BF16).rearrange("p (r c) -> p r c", c=66)
                cpad = carve(258, BF16)
                dgt = carve(9 * 128, BF16).rearrange("p (k c) -> p k c", k=9)
                tmp5 = carve(512)
                tmp5b = carve(512)
                rsb = carve(32)
                Oacc = carve(32 * 128).rearrange("p (c v) -> p c v", v=128)
                Sst = [carve(128) for _ in range(2)]
                NB = 4
                NI = 2 * NB
                mk = lambda w_, dt=F32: carve(NI * w_, dt).rearrange("p (n c) -> p n c", n=NI)
                sh_ = {"Lg": mk(64), "dec": mk(64), "Pm": mk(64), "Lgh": mk(64, BF16), "Lgl": mk(64, BF16),
                       "Qa": mk(64, BF16), "QTa": mk(64, BF16), "Qb": mk(64, BF16), "QTb": mk(64, BF16), "Pmb": mk(64, BF16),
                       "V": mk(128, BF16), "Kg": mk(128, BF16)}
                Av = [dict(sh_, **{"wT": mk(64), "QgT": mk(64), "gbc": mk(64), "QKT": mk(64, BF16), "K": mk(128, BF16), "u0": mk(128)})
                      for _ in range(2)]
                A64v = Av
                A128v = Av
                ubuf = [carve(128, BF16) for _ in range(4)]
                dma(xst[0][:, 0:128], D["wgate"].rearrange("p k c -> p (k c)"), w=["xst0"])
                cp("dve", wgb.rearrange("p k c -> p (k c)"), xst[0][:, 0:128], ["xst0"], ["wgb"])
                dma(gpar[0:64, 0:8], D["alog"][:, :], w=["gpar"])
                dma(gpar[0:64, 8:16], D["dtb"][:, :], w=["gpar"])
                dma(normw[0:64, :], D["normw"][:, :], w=["normw"])
                dma(cwt[:], D["convw"][:, :, :], w=["cwt"])
                act(gpar[0:64, 0:8], gpar[0:64, 0:8], AF.Exp, ["gpar"], ["gpar"])
                ts("dve", gpar[0:64, 0:8], gpar[0:64, 0:8], -1.0, ALU.mult, ["gpar"], ["gpar"])
                P.op("dve", lambda e: e.memset(lpad[:, :, :], 0.0), writes=["lpad"])
                P.op("dve", lambda e: e.memset(cpad[:, :], 0.0), writes=["cpad"])
                for c_ in range(36):
                    bank, cc = (0, c_) if c_ < 32 else (3, c_ - 32)
                    for k in range(8):
                        mm(PS[bank][0:64, cc * 16:(cc + 1) * 16], hT[:, k, c_ * 64:(c_ + 1) * 64], wgb[:, k, :], k == 0, k == 7,
                           ["hT", "wgb"], [PK[bank]])
                for (bank, c0, nch) in ((0, 0, 32), (3, 32, 4)):
                    pv = PS[bank][0:64, 0:nch * 16].rearrange("p (c e) -> p c e", e=16)
                    act(G_["beta"][0:64, c0:c0 + nch, :], pv[:, :, 0:8], AF.Sigmoid, [PK[bank]], ["gates"])
                    tt("dve", G_["g"][0:64, c0:c0 + nch, :], pv[:, :, 8:16],
                       gpar[0:64, 8:16].unsqueeze(1).to_broadcast([64, nch, 8]), ALU.add, [PK[bank], "gpar"], ["gates"])
                gall = lambda nm: G_[nm][0:64, :, :]
                act(gall("g"), gall("g"), AF.Exp, ["gates"], ["gates"])
                act(gall("g"), gall("g"), AF.Ln, ["gates"], ["gates"], bias=1.0)
                tt("dve", gall("g"), gall("g"), gpar[0:64, 0:8].unsqueeze(1).to_broadcast([64, 36, 8]), ALU.mult, ["gates", "gpar"], ["gates"])
                gflat = G_["g"][0:64, :, :].rearrange("p c e -> p (c e)")
                for d in range(2):
                    for hh in range(2):
                        pass
                    mm(PS[1][0:64, 0:288], TRI[d], gflat, True, True, ["masks", "gates"], [PK[1]])
                    pv = PS[1][0:64, 0:288].rearrange("p (c e) -> p c e", e=8)
                    cp("dve", G_["gcum"][0:64, :, d * 4:(d + 1) * 4], pv[:, :, d * 4:(d + 1) * 4], [PK[1]], ["gates"])
                mm(PS[2][0:64, 0:288], ones[0:64, 0:64], gflat, True, True, ["ones", "gates"], [PK[2]])
                tt("dve", gall("koutc"), PS[2][0:64, 0:288].rearrange("p (c e) -> p c e", e=8), gall("gcum"), ALU.subtract, [PK[2], "gates"], ["gates"])
                act(gall("koutc"), gall("koutc"), AF.Exp, ["gates"], ["gates"])
                if s == 0:
                    dump("g", G_["g"][0:64, :, :].rearrange("p c e -> p (c e)"), ["gates"])
                    dump("gcum", G_["gcum"][0:64, :, :].rearrange("p c e -> p (c e)"), ["gates"])
                act(gall("gam"), gall("gcum"), AF.Exp, ["gates"], ["gates"])
                FORD = list(range(36))
                BORD = [3, 2, 1, 0] + list(range(35, 3, -1))
                import os as _os2
                for h in range(int(_os2.environ.get('GDN_NHEADS', 4))):
                    for t in range(3):
                        tile_i = t * 4 + h
                        wb = load_w(1024 + t * 512 + h * 128)
                        proj_fm(wb, PS[1][:, 0:256], 0, 256, PK[1])
                        cp("act", cpad[:, 1:257], PS[1][:, 0:256], [PK[1]], ["cpad"])
                        for q in range(4):
                            pk = 2 + q % 2
                            proj_fm(wb, PS[pk][:, :], 256 + q * 512, 512, PK[pk])
                            cp("act", lpad[:, 1 + 8 * q:9 + 8 * q, 1:65], PS[pk][:, :].rearrange("p (r c) -> p r c", c=64), [PK[pk]], ["lpad"])
                        xk = "XT%d" % t
                        for tap in range(9):
                            ts("dve", dgt[:, tap, :], identb[:, :], cwt[:, tile_i, tap:tap + 1], ALU.mult, ["identb", "cwt"], ["dgt"])
                        for kx in range(3):
                            mm(PS[4][:, 0:256], dgt[:, 3 + kx, :], cpad[:, kx:kx + 256], kx == 0, kx == 2, ["dgt", "cpad"], [PK[4]])
                        if t < 2:
                            act(XTf[:, 0:256], PS[4][:, 0:256], AF.Silu, [PK[4]], ["XTf"])
                        else:
                            act(XT[t][:, 0:256], PS[4][:, 0:256], AF.Silu, [PK[4]], [xk])
                        for q in range(4):
                            pk = 4 + (q + 1) % 2
                            for tap in range(9):
                                ky, kx = tap // 3, tap % 3
                                mm(PS[pk][:, :], dgt[:, tap, :], lpad[:, ky + 8 * q:ky + 8 * q + 8, kx:kx + 64], tap == 0, tap == 8,
                                   ["dgt", "lpad"], [PK[pk]])
                            if t < 2:
                                act(XTf[:, 256 + q * 512:256 + (q + 1) * 512], PS[pk][:, :], AF.Silu, [PK[pk]], ["XTf"])
                            else:
                                act(XT[t][:, 256 + q * 512:256 + (q + 1) * 512], PS[pk][:, :], AF.Silu, [PK[pk]], [xk])
                        if t < 2:
                            for bi, (t0, nt) in enumerate(TOKBLK):
                                pk = 1 + bi % 3
                                tq = tmp5 if bi % 2 == 0 else tmp5b
                                tqk = "tmp5" if bi % 2 == 0 else "tmp5b"
                                tt("pool", tq[:, 0:nt], XTf[:, t0:t0 + nt], XTf[:, t0:t0 + nt], ALU.mult, ["XTf"], [tqk])
                                mm(PS[pk][:, 0:nt], ones[:, :], tq[:, 0:nt], True, True, ["ones", tqk], [PK[pk]])
                                act(tq[:, 0:nt], PS[pk][:, 0:nt], AF.Ln, [PK[pk]], [tqk], bias=1e-6)
                                act(tq[:, 0:nt], tq[:, 0:nt], AF.Exp, [tqk], [tqk], scale=-0.5)
                                stt("dve", XT[t][:, t0:t0 + nt], XTf[:, t0:t0 + nt], (128.0 ** -0.5) if t == 0 else 1.0, tq[:, 0:nt],
                                    ALU.mult, ALU.mult, ["XTf", tqk], [xk])
                    qT, kT, vT = XT
                    if s == 0 and h == 0:
                        dump("qn", qT[:, 0:512], ["XT0"])
                        dump("kn", kT[:, 256:768], ["XT1"])
                        dump("vv", vT[:, 256:768], ["XT2"])
                    for d in range(2):
                        P.op("dve", (lambda d=d: lambda e: e.memset(Sst[d][:, :], 0.0))(), writes=["S%d" % d])

                    def batch_info(b):
                        i0_ = b * NB
                        return [(FORD[i0_], NB), (min(BORD[i0_:i0_ + NB]), NB)]

                    def stageA_batch(b):
                        info = batch_info(b)
                        par = b % 2
                        A64, A128 = A64v[par], A128v[par]
                        KY = lambda s_: s_ + str(par)
                        items = []
                        for d in range(2):
                            cmin, nch = info[d]
                            for q_ in range(nch):
                                items.append((d, cmin + q_, d * NB + q_))
                        e8 = lambda d: d * 4 + h
                        gcols = lambda nm, d: G_[nm][0:64, info[d][0]:info[d][0] + NB, e8(d)]
                        v = lambda nm, d: A64[nm][0:64, d * NB:(d + 1) * NB, :]
                        v2 = lambda nm, d: A128[nm][0:64, d * NB:(d + 1) * NB, :]
                        ps3 = lambda bank, d, w_: PS[bank][0:64, :].rearrange("p (n c) -> p n c", c=w_)[:, (d * NB if w_ == 64 else 0):(d * NB if w_ == 64 else 0) + NB, :]
                        for d in range(2):
                            tt("dve", v("Lg", d), TRI[d].unsqueeze(1).to_broadcast([64, NB, 64]),
                               gcols("g", d).unsqueeze(2).to_broadcast([64, NB, 64]), ALU.mult, ["masks", "gates"], ["aLg"])
                        cp("act", A64["Lgh"][0:64, :, :], A64["Lg"][0:64, :, :], ["aLg"], ["aLgh"])
                        tt("dve", A64["Lgl"][0:64, :, :], A64["Lg"][0:64, :, :], A64["Lgh"][0:64, :, :], ALU.subtract, ["aLg", "aLgh"], ["aLgl"])
                        yield
                        for (d, c_, it) in items:
                            mm(PS[2][0:64, it * 64:(it + 1) * 64], SUb[d], A64["Lgh"][0:64, it, :], True, False, ["masksb", "aLgh"], [PK[2]])
                            mm(PS[2][0:64, it * 64:(it + 1) * 64], SUb[d], A64["Lgl"][0:64, it, :], False, True, ["masksb", "aLgl"], [PK[2]])
                            mm(PS[3][:, it * 64:(it + 1) * 64], onesb[:, :], A64["Lgh"][0:64, it, :], True, False, ["onesb", "aLgh"], [PK[3]])
                            mm(PS[3][:, it * 64:(it + 1) * 64], onesb[:, :], A64["Lgl"][0:64, it, :], False, True, ["onesb", "aLgl"], [PK[3]])
                        act(A64["dec"][0:64, :, :], PS[2][0:64, :].rearrange("p (n c) -> p n c", c=64), AF.Exp, [PK[2]], ["adec"])
                        act(A64["gbc"][:, :, :], PS[3][:, :].rearrange("p (n c) -> p n c", c=64), AF.Exp, [PK[3]], [KY("agbc")])
                        yield
                        for (d, c_, it) in items:
                            tok = slice(c_ * 64, (c_ + 1) * 64)
                            mm(PS[0][0:64, it * 64:(it + 1) * 64], kT[:, tok], kT[:, tok], True, True, ["XT1"], [PK[0]])
                            mm(PS[1][0:64, it * 64:(it + 1) * 64], kT[:, tok], qT[:, tok], True, True, ["XT1", "XT0"], [PK[1]])
                        for d in range(2):
                            tt("dve", v("Lg", d), v("dec", d), MI[d].unsqueeze(1).to_broadcast([64, NB, 64]), ALU.mult, ["adec", "masks"], ["aLg"])
                            tt("dve", v("dec", d), v("dec", d), MS[d].unsqueeze(1).to_broadcast([64, NB, 64]), ALU.mult, ["adec", "masks"], ["adec"])
                        yield
                        for d in range(2):
                            tt("dve", v("Pm", d), PS[0][0:64, :].rearrange("p (n c) -> p n c", c=64)[:, d * NB:(d + 1) * NB, :],
                               gcols("beta", d).unsqueeze(2).to_broadcast([64, NB, 64]), ALU.mult, [PK[0], "gates"], ["aPm"])
                        tt("dve", A64["dec"][0:64, :, :], A64["Pm"][0:64, :, :], A64["dec"][0:64, :, :], ALU.mult, ["aPm", "adec"], ["adec"])
                        tt("dve", A64["QKT"][0:64, :, :], PS[1][0:64, :].rearrange("p (n c) -> p n c", c=64), A64["Lg"][0:64, :, :], ALU.mult,
                           [PK[1], "aLg"], [KY("aQKT")])
                        tt("dve", A64["Pm"][0:64, :, :], ident[0:64, 0:64].unsqueeze(1).to_broadcast([64, NI, 64]), A64["dec"][0:64, :, :],
                           ALU.subtract, ["ident", "adec"], ["aPm"])
                        cp("dve", A64["Pmb"][0:64, :, :], A64["Pm"][0:64, :, :], ["aPm"], ["aPmb"])
                        cp("act", A64["Qa"][0:64, :, :], A64["dec"][0:64, :, :], ["adec"], ["aQa"])
                        yield
                        for (d, c_, it) in items:
                            mm(PS[4][0:64, it * 64:(it + 1) * 64], A64["Qa"][0:64, it, :], identb[0:64, 0:64], True, True, ["aQa", "identb"], [PK[4]])
                        cp("act", A64["QTa"][0:64, :, :], PS[4][0:64, :].rearrange("p (n c) -> p n c", c=64), [PK[4]], ["aQTa"])
                        yield
                        for (d, c_, it) in items:
                            tok = slice(c_ * 64, (c_ + 1) * 64)
                            mm(PS[2 + d][0:64, (it % NB) * 128:(it % NB + 1) * 128], kT[:, tok], identb[:, :], True, True, ["XT1", "identb"], [PK[2 + d]])
                            mm(PS[4 + d][0:64, (it % NB) * 128:(it % NB + 1) * 128], vT[:, tok], identb[:, :], True, True, ["XT2", "identb"], [PK[4 + d]])
                        for d in range(2):
                            cp("act", v2("K", d), PS[2 + d][0:64, :].rearrange("p (n c) -> p n c", c=128), [PK[2 + d]], [KY("aK")])
                            cp("act", v2("V", d), PS[4 + d][0:64, :].rearrange("p (n c) -> p n c", c=128), [PK[4 + d]], ["aV"])
                        for d in range(2):
                            tt("pool", v2("Kg", d), v2("K", d), gcols("gam", d).unsqueeze(2).to_broadcast([64, NB, 128]), ALU.mult, [KY("aK"), "gates"], ["aKg"])
                            tt("pool", v2("K", d), v2("K", d), gcols("koutc", d).unsqueeze(2).to_broadcast([64, NB, 128]), ALU.mult, [KY("aK"), "gates"], [KY("aK")])
                            cmin = info[d][0]
                            tt("pool", A64["QgT"][:, d * NB:(d + 1) * NB, :], qT[:, cmin * 64:(cmin + NB) * 64].rearrange("p (n c) -> p n c", c=64),
                               A64["gbc"][:, d * NB:(d + 1) * NB, :], ALU.mult, ["XT0", KY("agbc")], [KY("aQgT")])
                        yield
                        Q, QT, Qn, QTn = "Qa", "QTa", "Qb", "QTb"
                        kq = {"Qa": "aQa", "QTa": "aQTa", "Qb": "aQb", "QTb": "aQTb"}
                        for lvl in range(5):
                            for (d, c_, it) in items:
                                mm(PS[0][0:64, it * 64:(it + 1) * 64], A64[QT][0:64, it, :], A64[Q][0:64, it, :], True, True, [kq[Q], kq[QT]], [PK[0]])
                                mm(PS[1][0:64, it * 64:(it + 1) * 64], A64[Q][0:64, it, :], A64[QT][0:64, it, :], True, True, [kq[Q], kq[QT]], [PK[1]])
                            yield
                            cp("act", A64[Qn][0:64, :, :], PS[0][0:64, :].rearrange("p (n c) -> p n c", c=64), [PK[0]], [kq[Qn]])
                            cp("dve", A64[QTn][0:64, :, :], PS[1][0:64, :].rearrange("p (n c) -> p n c", c=64), [PK[1]], [kq[QTn]])
                            for (d, c_, it) in items:
                                mm(PS[5][0:64, it * 64:(it + 1) * 64], A64[QTn][0:64, it, :], A64["Pmb"][0:64, it, :], True, True, [kq[QTn], "aPmb"], [PK[5]])
                            tt("dve", A64["Pm"][0:64, :, :], A64["Pm"][0:64, :, :], PS[5][0:64, :].rearrange("p (n c) -> p n c", c=64), ALU.add,
                               ["aPm", PK[5]], ["aPm"])
                            cp("act", A64["Pmb"][0:64, :, :], A64["Pm"][0:64, :, :], ["aPm"], ["aPmb"])
                            yield
                            Q, QT, Qn, QTn = Qn, QTn, Q, QT
                        yield
                        for (d, c_, it) in items:
                            mm(PS[2 + d][0:64, (it % NB) * 128:(it % NB + 1) * 128], A64["Pmb"][0:64, it, :], A128["V"][0:64, it, :], True, True,
                               ["aPmb", "aV"], [PK[2 + d]])
                            mm(PS[4][:, it * 64:(it + 1) * 64], A128["Kg"][0:64, it, :], A64["Pmb"][0:64, it, :], True, True, ["aKg", "aPmb"], [PK[4]])
                        for d in range(2):
                            tt("dve", v2("u0", d), PS[2 + d][0:64, :].rearrange("p (n c) -> p n c", c=128),
                               gcols("beta", d).unsqueeze(2).to_broadcast([64, NB, 128]), ALU.mult, [PK[2 + d], "gates"], [KY("au0")])
                        act(A64["wT"][:, :, :], PS[4][:, :].rearrange("p (n c) -> p n c", c=64), AF.Identity, [PK[4]], [KY("awT")], scale=-1.0)

                    def stageC_gen(i, d):
                        b = i // NB
                        info = batch_info(b)
                        par = b % 2
                        A64, A128 = A64v[par], A128v[par]
                        KY = lambda s_: s_ + str(par)
                        c_ = FORD[i] if d == 0 else BORD[i]
                        it = d * NB + (c_ - info[d][0])
                        e8 = d * 4 + h
                        col = lambda nm: G_[nm][0:64, c_, e8:e8 + 1]
                        pC = PS[6 + d]
                        kC = PK[6 + d]
                        Sk = "S%d" % d
                        S_ = Sst[d]
                        ub = ubuf[(i % 2) * 2 + d]
                        uk = "u%d" % ((i % 2) * 2 + d)
                        mm(pC[0:64, 0:128], A64["wT"][:, it, :], S_[:, :], True, True, [KY("awT"), Sk], [kC])
                        yield
                        stt("dve", ub[0:64, :], pC[0:64, 0:128], col("beta"), A128["u0"][0:64, it, :], ALU.mult, ALU.add, [kC, "gates", KY("au0")], [uk])
                        yield
                        if c_ >= 4:
                            cl = c_ - 4
                            mm(pC[0:64, 128:256], A64["QgT"][:, it, :], S_[:, :], True, False, [KY("aQgT"), Sk], [kC])
                            mm(pC[0:64, 128:256], A64["QKT"][0:64, it, :], ub[0:64, :], False, True, [KY("aQKT"), uk], [kC])
                        mm(pC[:, 256:384], A128["K"][0:64, it, :], ub[0:64, :], True, True, [KY("aK"), uk], [kC])
                        yield
                        lastcol = 63 if d == 0 else 0
                        stt("dve", S_[:, :], S_[:, :], A64["gbc"][:, it, lastcol:lastcol + 1], pC[:, 256:384], ALU.mult, ALU.add,
                            [Sk, KY("agbc"), kC], [Sk])
                        yield
                        if c_ >= 4:
                            firstw = (d == 0 and cl < 16) or (d == 1 and cl >= 16)
                            ok = "O%d" % cl
                            if firstw:
                                cp("act", Oacc[0:64, cl, :], pC[0:64, 128:256], [kC], [ok])
                            else:
                                tt("dve", Oacc[0:64, cl, :], Oacc[0:64, cl, :], pC[0:64, 128:256], ALU.add, [kC, ok], [ok])
                        yield

                    def chain_d(d, steps):
                        for i in steps:
                            yield from stageC_gen(i, d)

                    def stageC_steps(steps):
                        alive = [chain_d(0, steps), chain_d(1, steps)]
                        while alive:
                            for g_ in list(alive):
                                try:
                                    next(g_)
                                except StopIteration:
                                    alive.remove(g_)

                    def chain_both(steps):
                        alive = [chain_d(0, steps), chain_d(1, steps)]
                        while alive:
                            for g_ in list(alive):
                                try:
                                    next(g_)
                                except StopIteration:
                                    alive.remove(g_)
                            yield

                    def run_all(g_):
                        for _ in g_:
                            pass
                    NBT = 36 // NB
                    run_all(stageA_batch(0))
                    for b in range(NBT):
                        gC = chain_both(list(range(b * NB, (b + 1) * NB)))
                        gA = stageA_batch(b + 1) if b + 1 < NBT else iter(())
                        doneA = doneC = False
                        while not (doneA and doneC):
                            for _ in range(2):
                                if not doneA:
                                    try:
                                        next(gA)
                                    except StopIteration:
                                        doneA = True
                            if not doneC:
                                try:
                                    next(gC)
                                except StopIteration:
                                    doneC = True
                    if s == 0 and h == 0:
                        dump("O0", Oacc[0:64, 0, :], ["O0"])
                        dump("O31", Oacc[0:64, 31, :], ["O31"])
                    wb = load_w(2560 + h * 128)
                    for blk in range(8):
                        par_ = blk % 2
                        okeys = ["O%d" % (blk * 4 + cc) for cc in range(4)]
                        O4 = Oacc[0:64, blk * 4:blk * 4 + 4, :]
                        sq4 = XTf[0:64, par_ * 512:(par_ + 1) * 512].rearrange("p (c v) -> p c v", v=128)
                        rs_ = rsb[0:64, 4 * blk:4 * blk + 4]
                        ksq, krs = "fsq%d" % par_, "rs%d" % blk
                        tt("pool", sq4, O4, O4, ALU.mult, okeys + ["XTf"], [ksq])
                        P.op("dve", (lambda sq4=sq4, rs_=rs_: lambda e: e.tensor_reduce(out=rs_, in_=sq4, axis=AX.X, op=ALU.add))(),
                             reads=[ksq, "XTf"], writes=[krs])
                        ts("dve", rs_, rs_, 1.0 / 128.0, ALU.mult, [krs], [krs], s2=1e-6, op1=ALU.add)
                        act(rs_, rs_, AF.Sqrt, [krs], [krs])
                        P.op("dve", (lambda rs_=rs_: lambda e: e.reciprocal(out=rs_, in_=rs_))(), reads=[krs], writes=[krs])
                    for blk in range(8):
                        par_ = blk % 2
                        okeys = ["O%d" % (blk * 4 + cc) for cc in range(4)]
                        O4 = Oacc[0:64, blk * 4:blk * 4 + 4, :]
                        z4 = XTf[0:64, 1024 + par_ * 512:1024 + (par_ + 1) * 512].rearrange("p (c v) -> p c v", v=128)
                        rs_ = rsb[0:64, 4 * blk:4 * blk + 4]
                        kz, krs = "fz%d" % par_, "rs%d" % blk
                        pz, pt = (1, 2) if par_ == 0 else (3, 4)
                        for cc in range(4):
                            tk = slice(LCTX + (blk * 4 + cc) * 64, LCTX + (blk * 4 + cc + 1) * 64)
                            for k in range(8):
                                mm(PS[pz][0:64, cc * 128:(cc + 1) * 128], hT[:, k, tk], wbf[wb][:, k, :], k == 0, k == 7, ["hT", "wbf%d" % wb], [PK[pz]])
                        act(z4, PS[pz][0:64, :].rearrange("p (c v) -> p c v", v=128), AF.Silu, [PK[pz], "XTf"], [kz])
                        tt("dve", O4, O4, rs_.unsqueeze(2).to_broadcast([64, 4, 128]), ALU.mult, okeys + [krs], okeys)
                        tt("pool", O4, O4, normw[0:64, :].unsqueeze(1).to_broadcast([64, 4, 128]), ALU.mult, okeys + ["normw"], okeys)
                        tt("dve", O4, O4, z4, ALU.mult, okeys + [kz, "XTf"], okeys)
                        for cc in range(4):
                            tr(PS[pt][:, cc * 64:(cc + 1) * 64], Oacc[0:64, blk * 4 + cc, :], ident[0:64, 0:64], okeys + ["ident"], [PK[pt]])
                        cp("act", mixT[:, 4 + h, blk * 256:(blk + 1) * 256], PS[pt][:, 0:256], [PK[pt]], ["mixG"])
                    if s == 0 and h == 0:
                        dump("mix_g", mixT[:, 4, 0:512], ["mixG"])

            if "out" in stages:
                barrier()
                o = [0]

                def carve(n, dt=F32):
                    words = n if dt == F32 else (n + 1) // 2
                    a = arena[:, o[0]:o[0] + words]
                    o[0] += words
                    assert o[0] <= ARENA_WORDS, o[0]
                    return a.bitcast(BF16)[:, 0:n] if dt == BF16 else a
                woutb = carve(8192, BF16).rearrange("p (k c) -> p k c", k=8)
                gate_bc = carve(1024)
                lng = carve(1024)
                lnb = carve(1024)
                wg = carve(4096).rearrange("p (k c) -> p k c", k=8)
                silucb = carve(1024).rearrange("p (k c) -> p k c", k=8)
                rbuf = [carve(1024) for _ in range(2)]
                stats = carve(16)
                dma(lng, D["lng"][:, :], w=["lng"])
                dma(lnb, D["lnb"][:, :], w=["lnb"])
                dma(gate_bc, D["bgate_bc"][:, :], w=["gate_bc"])
                wout_v = D["w_out"].rearrange("(k p) c -> p k c", p=128)
                for k in range(8):
                    dma(xst[k % 2][:], wout_v[:, k, :], w=["xst%d" % (k % 2)])
                    cp("pool", woutb[:, k, :], xst[k % 2][:], ["xst%d" % (k % 2)], ["woutb"])
                for k in range(8):
                    ts("dve", silucb[:, k, :], ones[:, :], siluc[:, k, s:s + 1], ALU.mult, ["ones", "siluc"], ["silucb"])
                for half in range(2):
                    dma(wg[:], wada[:, :, 2048 + half * 512:2048 + (half + 1) * 512], w=["wg"])
                    for k in range(8):
                        mm(PS[3][:, :], silucb[:, k, :], wg[:, k, :], k == 0, k == 7, ["silucb", "wg"], [PK[3]])
                    tt("dve", gate_bc[:, half * 512:(half + 1) * 512], gate_bc[:, half * 512:(half + 1) * 512], PS[3][:, :], ALU.add,
                       ["gate_bc", PK[3]], ["gate_bc"])
                if s == 0:
                    dump("gate", gate_bc[:, 0:512], ["gate_bc"])
                for ti in range(16):
                    b = ti % 2
                    rk = "r%d" % b
                    dma(xst[b][:], D["x2"][s, ti * 128:(ti + 1) * 128, :], w=["xst%d" % b])
                    for half in range(2):
                        pk = 4 + half + 2 * b
                        for k in range(8):
                            mm(PS[pk][:, :], mixT[:, k, ti * 128:(ti + 1) * 128], woutb[:, k, half * 512:(half + 1) * 512], k == 0, k == 7,
                               ["mixT", "woutb"], [PK[pk]])
                        tt("dve", rbuf[b][:, half * 512:(half + 1) * 512], PS[pk][:, :], gate_bc[:, half * 512:(half + 1) * 512], ALU.mult,
                           [PK[pk], "gate_bc"], [rk])
                    stt("pool", rbuf[b][:, :], xst[b][:, :], DEEP_ALPHA, rbuf[b][:, :], ALU.mult, ALU.add, ["xst%d" % b, rk], [rk])
                    st6 = stats[:, 0:12].rearrange("p (c e) -> p c e", e=6)
                    for half in range(2):
                        P.op("dve", (lambda b=b, half=half, st6=st6: lambda e: e.bn_stats(out=st6[:, half, :], in_=rbuf[b][:, half * 512:(half + 1) * 512]))(),
                             reads=[rk], writes=["stats"])
                    P.op("dve", (lambda st6=st6: lambda e: e.bn_aggr(out=stats[:, 12:14], in_=st6))(), reads=["stats"], writes=["mv"])
                    act(stats[:, 14:15], stats[:, 13:14], AF.Sqrt, ["mv"], ["mv2"], bias=1e-5)
                    P.op("dve", lambda e: e.reciprocal(out=stats[:, 14:15], in_=stats[:, 14:15]), reads=["mv2"], writes=["mv2"])
                    stt("dve", stats[:, 15:16], stats[:, 12:13], -1.0, stats[:, 14:15], ALU.mult, ALU.mult, ["mv", "mv2"], ["mv2"])
                    act(rbuf[b][:, :], rbuf[b][:, :], AF.Identity, [rk, "mv2"], [rk], bias=stats[:, 15:16], scale=stats[:, 14:15])
                    tt("pool", rbuf[b][:, :], rbuf[b][:, :], lng, ALU.mult, [rk, "lng"], [rk])
                    tt("dve", rbuf[b][:, :], rbuf[b][:, :], lnb, ALU.add, [rk, "lnb"], [rk])
                    dma(yout[s, ti * 128:(ti + 1) * 128, :], rbuf[b][:, :], r=[rk])
        P.emit(nc)
    return nc, P


def _core_inputs(inp, sh, core):
    m = dict(sh)
    f = lambda a: np.ascontiguousarray(np.asarray(a, dtype=np.float32))
    b0 = core * NSEQ
    m["x2"] = f(inp["x"][b0:b0 + NSEQ])
    m["ctx2"] = f(inp["ctx"][b0:b0 + NSEQ])
    cc = np.stack([np.asarray(inp["c"][b0], np.float32), np.asarray(inp["c"][b0 + 1], np.float32),
                   np.asarray(inp["c_ctx"], np.float32)], axis=0)
    m["cT"] = f(cc.reshape(3, 8, 128).transpose(2, 1, 0))
    return m


_CACHE = {}


def kernel(**inputs):
    if "nc" not in _CACHE:
        _CACHE["nc"] = build_program()[0]
    nc = _CACHE["nc"]
    sh = _prep_shared(inputs)
    maps = [_core_inputs(inputs, sh, c) for c in range(8)]
    res = run_bass_kernel_spmd(nc, maps, core_ids=list(range(8)))
    out = np.concatenate([r["yout"] for r in res.results], axis=0)
    return out.astype(np.float32)
```

```python
import contextlib
import math
import numpy as np
import concourse.bass as bass
import concourse.mybir as mybir
from concourse.bass_utils import run_bass_kernel_spmd

F32 = mybir.dt.float32
BF16 = mybir.dt.bfloat16
ALU = mybir.AluOpType
AF = mybir.ActivationFunctionType
AX = mybir.AxisListType

ENG_NAMES = ["pe", "act", "dve", "pool", "sp"]
SAME_ENGINE_SYNC = True
N_DMA_SEMS = 12

NSEQ = 2
LCTX = 256
LLAT = 2048
LTOT = LCTX + LLAT
DEEP_ALPHA = 2.0 ** 0.25


class Prog:
    def __init__(self):
        self.ops = []
        self.last_w = {}
        self.readers = {}
        self.barrier_idx = None

    def barrier(self, eng, fn):
        deps = set(self.last_w.values())
        for v in self.readers.values():
            deps.update(v)
        if self.barrier_idx is not None:
            deps.add(self.barrier_idx)
        idx = len(self.ops)
        self.ops.append(dict(eng=eng, fn=fn, deps=deps, dma=False))
        self.barrier_idx = idx
        self.last_w = {}
        self.readers = {}
        return idx

    def op(self, eng, fn, reads=(), writes=(), dma=False):
        writes = list(writes) + [k for k in reads if k.startswith("ps")]
        idx = len(self.ops)
        deps = set()
        if self.barrier_idx is not None:
            deps.add(self.barrier_idx)
        for k in reads:
            if k in self.last_w:
                deps.add(self.last_w[k])
        for k in writes:
            if k in self.last_w:
                deps.add(self.last_w[k])
            deps.update(self.readers.get(k, ()))
        self.ops.append(dict(eng=eng, fn=fn, deps=deps, dma=dma))
        for k in reads:
            self.readers.setdefault(k, []).append(idx)
        for k in writes:
            self.last_w[k] = idx
            self.readers[k] = []
        return idx

    def emit(self, nc):
        ops = self.ops
        pos = {}
        seqcount = {e: 0 for e in ENG_NAMES}
        for i, o in enumerate(ops):
            if not o["dma"]:
                seqcount[o["eng"]] += 1
                pos[i] = seqcount[o["eng"]]
        dma_ops = [i for i, o in enumerate(ops) if o["dma"]]
        dma_sem = {}
        dma_val = {}
        semcnt = [0] * N_DMA_SEMS
        prev_on_sem = {}
        for n, i in enumerate(dma_ops):
            s = n % N_DMA_SEMS
            semcnt[s] += 16
            dma_sem[i] = s
            dma_val[i] = semcnt[s]
            if s in prev_on_sem:
                ops[i]["deps"].add(prev_on_sem[s])
            prev_on_sem[s] = i
        known = {e: {p: 0 for p in ENG_NAMES} for e in ENG_NAMES}
        known_dma = {e: [0] * N_DMA_SEMS for e in ENG_NAMES}
        flagged = set()
        waits = [[] for _ in ops]
        for i, o in enumerate(ops):
            e = o["eng"]
            for d in sorted(o["deps"]):
                od = ops[d]
                if od["dma"]:
                    s = dma_sem[d]
                    if dma_val[d] > known_dma[e][s]:
                        known_dma[e][s] = dma_val[d]
                        waits[i].append(("dma", s, dma_val[d]))
                else:
                    p = od["eng"]
                    if p == e and (e == "pe" or not SAME_ENGINE_SYNC):
                        continue
                    if pos[d] > known[e][p]:
                        known[e][p] = pos[d]
                        flagged.add(d)
                        waits[i].append(("eng", p, d))
        cnt = {e: 0 for e in ENG_NAMES}
        val = {}
        for i, o in enumerate(ops):
            if i in flagged:
                cnt[o["eng"]] += 1
                val[i] = cnt[o["eng"]]
        per_eng = {e: [] for e in ENG_NAMES}
        for i, o in enumerate(ops):
            per_eng[o["eng"]].append(i)
        self.stats = {e: len(per_eng[e]) for e in ENG_NAMES}

        with contextlib.ExitStack() as st:
            esem = {e: st.enter_context(nc.semaphore("s_" + e)) for e in ENG_NAMES}
            dsem = [st.enter_context(nc.semaphore("d_%d" % k)) for k in range(N_DMA_SEMS)]
            block = st.enter_context(nc.Block())

            def run(e, engobj):
                for i in per_eng[e]:
                    o = ops[i]
                    mx = {}
                    for w in waits[i]:
                        if w[0] == "dma":
                            key = ("d", w[1]); v = w[2]
                        else:
                            key = ("e", w[1]); v = val[w[2]]
                        mx[key] = max(mx.get(key, 0), v)
                    for key, v in mx.items():
                        sem = dsem[key[1]] if key[0] == "d" else esem[key[1]]
                        engobj.wait_ge(sem, v)
                    ins = o["fn"](engobj)
                    if o["dma"]:
                        ins.then_inc(dsem[dma_sem[i]], 16)
                    elif i in flagged:
                        ins.then_inc(esem[e], 1)
                if e == "sp":
                    for s in range(N_DMA_SEMS):
                        if semcnt[s] > 0:
                            engobj.wait_ge(dsem[s], semcnt[s])

            @block.tensor
            def _(eng):
                run("pe", eng)

            @block.scalar
            def _(eng):
                run("act", eng)

            @block.vector
            def _(eng):
                run("dve", eng)

            @block.gpsimd
            def _(eng):
                run("pool", eng)

            @block.sync
            def _(eng):
                run("sp", eng)


def _consts():
    c = {}
    c["ident"] = np.eye(128, dtype=np.float32)
    k = np.arange(64)
    tri = np.stack([(k[:, None] <= k[None, :]), (k[:, None] >= k[None, :])]).astype(np.float32)
    su = np.stack([(k[:, None] > k[None, :]), (k[:, None] < k[None, :])]).astype(np.float32)
    mi = np.stack([(k[None, :] >= k[:, None]), (k[None, :] <= k[:, None])]).astype(np.float32)
    ms = np.stack([(k[None, :] > k[:, None]), (k[None, :] < k[:, None])]).astype(np.float32)
    c["masks"] = np.concatenate([tri[0], tri[1], su[0], su[1], mi[0], mi[1], ms[0], ms[1]], axis=1).astype(np.float32)
    c["ones"] = np.ones((128, 128), np.float32)
    return c


def _prep_shared(inp):
    f = lambda a: np.ascontiguousarray(np.asarray(a, dtype=np.float32))
    sh = {}
    sh["w_ada"] = f(inp["w_ada"][0])
    b_ada = f(inp["b_ada"][0])
    sh["bcol"] = f(b_ada.reshape(24, 128).T)
    sh["bgate_bc"] = f(np.broadcast_to(b_ada[2048:3072][None, :], (128, 1024)))
    w_in = f(inp["w_in"][0])
    sh["w_in"] = w_in
    sh["wgate"] = f(w_in[:, 3072:3088].reshape(8, 128, 16).transpose(1, 0, 2))
    lre = f(inp["s5_lambda_re"][0]); lim = f(inp["s5_lambda_im"][0]); ldt = f(inp["s5_log_dt"][0])

    def smaj(a):
        o = np.zeros((128, 32), np.float32)
        for pair in range(16):
            for d in range(2):
                for gl in range(2):
                    o[gl * 64:(gl + 1) * 64, pair * 2 + d] = a[d, 2 * pair + gl, :]
        return o
    sh["lam_re"] = smaj(lre)
    sh["lam_im"] = smaj(lim)
    sh["logdt"] = smaj(np.broadcast_to(ldt[:, :, None], (2, 32, 64)))
    bre = f(inp["s5_b_re"][0]); bim = f(inp["s5_b_im"][0])
    cre = f(inp["s5_c_re"][0]); cim = f(inp["s5_c_im"][0])
    Bp = np.zeros((128, 4, 4, 2, 2, 128), np.float32)
    Cp = np.zeros((128, 4, 4, 2, 2, 128), np.float32)
    for j in range(4):
        for gp in range(4):
            for gl in range(2):
                gq = 2 * gp + gl
                g = 8 * j + gq
                for d in range(2):
                    Bp[gq * 16:(gq + 1) * 16, j, gp, d, 0, gl * 64:(gl + 1) * 64] = bre[d, g].T
                    Bp[gq * 16:(gq + 1) * 16, j, gp, d, 1, gl * 64:(gl + 1) * 64] = bim[d, g].T
                    Cp[gl * 64:(gl + 1) * 64, j, gp, d, 0, gq * 16:(gq + 1) * 16] = cre[d, g].T
                    Cp[gl * 64:(gl + 1) * 64, j, gp, d, 1, gq * 16:(gq + 1) * 16] = cim[d, g].T
    sh["Bp"] = f(Bp.reshape(128, 8192))
    sh["Cp"] = f(Cp.reshape(128, 8192))
    sh["dcol"] = f(inp["s5_d"][0].reshape(4, 128).T)
    sh["wglu"] = f(inp["w_glu"][0].reshape(4, 128, 512).transpose(1, 0, 2))
    sh["bglu"] = f(inp["b_glu"][0].reshape(4, 128).T)
    sh["convw"] = f(inp["conv_w"][0].reshape(9, 12, 128).transpose(2, 1, 0))
    sh["alog"] = f(np.broadcast_to(inp["gdn_a_log"][0].reshape(1, 8), (64, 8)))
    sh["dtb"] = f(np.broadcast_to(inp["gdn_dt_bias"][0].reshape(1, 8), (64, 8)))
    sh["normw"] = f(np.broadcast_to(inp["gdn_norm_w"][0].reshape(1, 128), (64, 128)))
    sh["w_out"] = f(inp["w_out"][0])
    sh["lng"] = f(np.broadcast_to(inp["ln_g"][0][None, :], (128, 1024)))
    sh["lnb"] = f(np.broadcast_to(inp["ln_b"][0][None, :], (128, 1024)))
    sh.update(_consts())
    return sh


IN_SHAPES = {
    "x2": [NSEQ, LLAT, 1024], "ctx2": [NSEQ, LCTX, 1024], "cT": [128, 8, 3],
    "w_ada": [1024, 3072], "bcol": [128, 24], "bgate_bc": [128, 1024], "w_in": [1024, 3088],
    "wgate": [128, 8, 16], "lam_re": [128, 32], "lam_im": [128, 32], "logdt": [128, 32],
    "Bp": [128, 8192], "Cp": [128, 8192], "dcol": [128, 4], "wglu": [128, 4, 512], "bglu": [128, 4],
    "convw": [128, 12, 9], "alog": [64, 8], "dtb": [64, 8], "normw": [64, 128],
    "w_out": [1024, 1024], "lng": [128, 1024], "lnb": [128, 1024],
    "ident": [128, 128], "masks": [64, 512], "ones": [128, 128],
}


def build_program(dbg=None, stages=("s5", "gdn", "out")):
    nc = bass.Bass("TRN2", target_bir_lowering=False)
    D = {k: nc.dram_tensor(k, v, F32, kind="ExternalInput").ap() for k, v in IN_SHAPES.items()}
    yout = nc.dram_tensor("yout", [NSEQ, LLAT, 1024], F32, kind="ExternalOutput").ap()
    dbg_t = {}
    if dbg:
        for k, shp in dbg.items():
            dbg_t[k] = nc.dram_tensor("dbg_" + k, shp, F32, kind="ExternalOutput").ap()
    P = Prog()
    st = contextlib.ExitStack()
    uid = [0]

    def sb(name, shape, dt=F32):
        return st.enter_context(nc.sbuf_tensor("sb_" + name, shape, dt))

    def psum(name):
        return st.enter_context(nc.psum_tensor(name, [128, 512], F32))

    with st:
        hT = sb("hT", [128, 8, LTOT], BF16)
        mixT = sb("mixT", [128, 8, LLAT], BF16)
        ident = sb("ident", [128, 128])
        ones = sb("ones", [128, 128])
        masks = sb("masks", [64, 512])
        identb = sb("identb", [128, 128], BF16)
        masksb = sb("masksb", [64, 512], BF16)
        onesb = sb("onesb", [64, 128], BF16)
        xst = [sb("xst%d" % i, [128, 1024]) for i in range(2)]
        wst = [sb("wst%d" % i, [128, 8, 128]) for i in range(2)]
        wbf = [sb("wbf%d" % i, [128, 8, 128], BF16) for i in range(2)]
        modsc = sb("modsc", [128, 16, 3])
        siluc = sb("siluc", [128, 8, 3])
        dbgs = sb("dbgs", [128, 512]) if dbg else None
        bcol = sb("bcol", [128, 24])
        small = sb("small", [128, 64])
        ARENA_WORDS = 28800
        arena = sb("arena", [128, ARENA_WORDS])
        PS = [psum("ps%d" % i) for i in range(8)]
        PK = ["ps%d" % i for i in range(8)]

        def dma(out, in_, r=(), w=()):
            P.op("sp", lambda e: e.dma_start(out=out, in_=in_), reads=r, writes=w, dma=True)

        def act(out, in_, func, r, w, bias=None, scale=None):
            kw = {}
            if bias is not None:
                kw["bias"] = bias
            if scale is not None:
                kw["scale"] = scale
            P.op("act", lambda e: e.activation(out=out, in_=in_, func=func, **kw), reads=r, writes=w)

        def tt(eng, out, in0, in1, op, r, w):
            P.op(eng, lambda e: e.tensor_tensor(out=out, in0=in0, in1=in1, op=op), reads=r, writes=w)

        def ts(eng, out, in0, s1, op0, r, w, s2=None, op1=None):
            if op1 is None:
                P.op(eng, lambda e: e.tensor_scalar(out=out, in0=in0, scalar1=s1, scalar2=None, op0=op0), reads=r, writes=w)
            else:
                P.op(eng, lambda e: e.tensor_scalar(out=out, in0=in0, scalar1=s1, scalar2=s2, op0=op0, op1=op1), reads=r, writes=w)

        def stt(eng, out, in0, scalar, in1, op0, op1, r, w):
            eng = "dve"
            P.op(eng, lambda e: e.scalar_tensor_tensor(out=out, in0=in0, scalar=scalar, in1=in1, op0=op0, op1=op1), reads=r, writes=w)

        def cp(eng, out, in_, r, w):
            if eng == "act":
                act(out, in_, AF.Copy, r, w)
            else:
                P.op(eng, lambda e: e.tensor_copy(out=out, in_=in_), reads=r, writes=w)

        def mm(out, lhsT, rhs, start, stop, r, w):
            P.op("pe", lambda e: e.matmul(out, lhsT=lhsT, rhs=rhs, start=start, stop=stop), reads=r, writes=w)

        def tr(out, in_, idn, r, w):
            P.op("pe", lambda e: e.transpose(out, in_, idn), reads=r, writes=w)

        def dump(name, ap, r):
            if name in dbg_t:
                if ap.dtype == BF16:
                    n = ap.shape[-1]
                    cp("dve", dbgs[:, 0:n], ap, r, ["dbgs"])
                    dma(dbg_t[name], dbgs[:, 0:n], r=["dbgs"])
                else:
                    dma(dbg_t[name], ap, r=r)

        def barrier():
            P.barrier("dve", lambda e: e.memset(small[:, 63:64], 0.0))

        dma(ident[:], D["ident"][:, :], w=["ident"])
        dma(ones[:], D["ones"][:, :], w=["ones"])
        dma(masks[:], D["masks"][:, :], w=["masks"])
        dma(bcol[:], D["bcol"][:, :], w=["bcol"])
        cp("dve", identb[:], ident[:], ["ident"], ["identb"])
        cp("dve", masksb[:], masks[:], ["masks"], ["masksb"])
        cp("dve", onesb[:], ones[0:64, :], ["ones"], ["onesb"])
        SUb = [masksb[:, 128:192], masksb[:, 192:256]]
        TRI = [masks[:, 0:64], masks[:, 64:128]]
        SU = [masks[:, 128:192], masks[:, 192:256]]
        MI = [masks[:, 256:320], masks[:, 320:384]]
        MS = [masks[:, 384:448], masks[:, 448:512]]

        dma(siluc[:], D["cT"][:, :, :], w=["siluc"])
        act(siluc[:], siluc[:], AF.Silu, ["siluc"], ["siluc"])
        wada = D["w_ada"].rearrange("(k p) c -> p k c", p=128)
        for t in range(16):
            b = t % 2
            dma(wst[b][:], wada[:, :, t * 128:(t + 1) * 128], w=["wst%d" % b])
            for k in range(8):
                mm(PS[0][:, t * 4:t * 4 + 3], wst[b][:, k, :], siluc[:, k, :], k == 0, k == 7, ["wst%d" % b, "siluc"], [PK[0]])
        for t in range(16):
            ts("dve", modsc[:, t, :], PS[0][:, t * 4:t * 4 + 3], bcol[:, t:t + 1], ALU.add, [PK[0], "bcol"], ["modsc"],
               s2=(1.0 if t >= 8 else 0.0), op1=ALU.add)

        win = D["w_in"].rearrange("(k p) c -> p k c", p=128)
        wcount = [0]

        def load_w(c0, ncols=128):
            b = wcount[0] % 2
            wcount[0] += 1
            dma(wst[b][:, :, 0:ncols], win[:, :, c0:c0 + ncols], w=["wst%d" % b])
            cp("pool", wbf[b][:, :, 0:ncols], wst[b][:, :, 0:ncols], ["wst%d" % b], ["wbf%d" % b])
            return b

        def proj_fm(b, ps_ap, tok0, ntok, pk):
            for k in range(8):
                mm(ps_ap, wbf[b][:, k, :], hT[:, k, tok0:tok0 + ntok], k == 0, k == 7, ["wbf%d" % b, "hT"], [pk])

        TOKBLK = [(0, 256), (256, 512), (768, 512), (1280, 512), (1792, 512)]

        for s in range(NSEQ):
            barrier()
            for tt_i in range(18):
                b = tt_i % 2
                if tt_i < 2:
                    src = D["ctx2"][s, tt_i * 128:(tt_i + 1) * 128, :]
                    jcol = 2
                else:
                    src = D["x2"][s, (tt_i - 2) * 128:(tt_i - 1) * 128, :]
                    jcol = s
                dma(xst[b][:], src, w=["xst%d" % b])
                for half in range(2):
                    pk = 1 + half
                    for kk in range(4):
                        k = half * 4 + kk
                        tr(PS[pk][:, kk * 128:(kk + 1) * 128], xst[b][:, k * 128:(k + 1) * 128], ident[:], ["xst%d" % b, "ident"], [PK[pk]])
                    for kk in range(4):
                        k = half * 4 + kk
                        act(hT[:, k, tt_i * 128:(tt_i + 1) * 128], PS[pk][:, kk * 128:(kk + 1) * 128], AF.Identity,
                            [PK[pk], "modsc"], ["hT"], bias=modsc[:, k, jcol:jcol + 1], scale=modsc[:, 8 + k, jcol:jcol + 1])
            if s == 0:
                dump("hT", hT[:, 0, 0:512], ["hT"])

            if "s5" in stages:
                barrier()
                o = [0]

                def carve(n, dt=F32):
                    words = n if dt == F32 else (n + 1) // 2
                    a = arena[:, o[0]:o[0] + words]
                    o[0] += words
                    assert o[0] <= ARENA_WORDS, o[0]
                    return a.bitcast(BF16)[:, 0:n] if dt == BF16 else a
                Bpb = carve(2048, BF16)
                Cpb = carve(2048, BF16)
                uT = carve(4 * LTOT, BF16).rearrange("p (j t) -> p j t", j=4)
                o_hb = o[0]
                HbD = [[carve(LTOT) for _ in range(2)] for _ in range(2)]
                HbB = [[carve(LLAT, BF16) for _ in range(2)] for _ in range(2)]
                XPAD = 256
                XaF = [[carve(288 + XPAD) for _ in range(2)] for _ in range(2)]
                XbF = [[carve(288 + XPAD) for _ in range(2)] for _ in range(2)]
                for d_ in range(2):
                    for r_ in range(2):
                        for buf_, nm_ in ((XaF, "Xa%d" % d_), (XbF, "Xb%d" % d_)):
                            P.op("pool", (lambda t_=buf_[d_][r_]: lambda e: e.memset(t_[:, :], 0.0))(), writes=[nm_])
                XOFF = [XPAD, 0]
                XaD = [[XaF[d_][r_][:, XOFF[d_]:XOFF[d_] + 288] for r_ in range(2)] for d_ in range(2)]
                XbD = [[XbF[d_][r_][:, XOFF[d_]:XOFF[d_] + 288] for r_ in range(2)] for d_ in range(2)]
                prm = carve(32 * 40).rearrange("p (q c) -> p q c", c=32)
                prm2 = carve(32 * 28).rearrange("p (q c) -> p q c", c=32)
                ytmp = carve(512)
                LR, LI, DT, ER, TH, AR, AI, CR, CI, T0, T1, T2 = range(12)
                PW = 13
                NP = PW + 16
                P.op("dve", lambda e: e.memset(prm[:, :, :], 0.0), writes=["prm"])
                for q_, nm in ((LR, "lam_re"), (LI, "lam_im"), (DT, "logdt")):
                    dma(prm[:, q_, :], D[nm][:, :], w=["prm"])

                def ptt(dst, a_, b_, op):
                    tt("dve", prm[:, dst, :], prm[:, a_, :], prm[:, b_, :], op, ["prm"], ["prm"])

                def pts(dst, a_, s1, op0, s2=None, op1=None, eng="dve"):
                    ts(eng, prm[:, dst, :], prm[:, a_, :], s1, op0, ["prm"], ["prm"], s2=s2, op1=op1)
                act(prm[:, DT, :], prm[:, DT, :], AF.Exp, ["prm"], ["prm"])
                ptt(ER, LR, DT, ALU.mult)
                act(prm[:, ER, :], prm[:, ER, :], AF.Exp, ["prm"], ["prm"])
                ptt(TH, LI, DT, ALU.mult)
                I32 = mybir.dt.int32
                kint = prm[:, 12, :].bitcast(I32)
                PI_C = 3.14159
                for (dst, shift_) in ((AI, 0.0), (AR, 0.5 * math.pi)):
                    pts(T0, TH, 1.0 / (2 * math.pi), ALU.mult, s2=shift_ / (2 * math.pi), op1=ALU.add)
                    P.op("dve", lambda e: e.tensor_copy(out=kint, in_=prm[:, T0, :]), reads=["prm"], writes=["prm"])
                    P.op("dve", lambda e: e.tensor_copy(out=prm[:, T1, :], in_=kint), reads=["prm"], writes=["prm"])
                    pts(T2, TH, shift_, ALU.add)
                    stt("dve", prm[:, T0, :], prm[:, T1, :], -2 * math.pi, prm[:, T2, :], ALU.mult, ALU.add, ["prm"], ["prm"])
                    pts(T0, T0, -PI_C, ALU.max, s2=PI_C, op1=ALU.min)
                    act(prm[:, dst, :], prm[:, T0, :], AF.Sin, ["prm"], ["prm"])
                ptt(AR, AR, ER, ALU.mult)
                ptt(AI, AI, ER, ALU.mult)
                pts(T0, AR, -1.0, ALU.add)
                ptt(T1, T0, LR, ALU.mult)
                ptt(T2, AI, LI, ALU.mult)
                ptt(CR, T1, T2, ALU.add)
                ptt(T1, AI, LR, ALU.mult)
                ptt(T2, T0, LI, ALU.mult)
                ptt(CI, T1, T2, ALU.subtract)
                ptt(T1, LR, LR, ALU.mult)
                ptt(T2, LI, LI, ALU.mult)
                ptt(T1, T1, T2, ALU.add)
                P.op("dve", lambda e: e.reciprocal(out=prm[:, T1, :], in_=prm[:, T1, :]), reads=["prm"], writes=["prm"])
                ptt(CR, CR, T1, ALU.mult)
                ptt(CI, CI, T1, ALU.mult)

                def cmul(dst_r, dst_i, ar_, ai_, br_, bi_):
                    ptt(T0, ar_, br_, ALU.mult)
                    ptt(T1, ai_, bi_, ALU.mult)
                    ptt(T2, ar_, bi_, ALU.mult)
                    ptt(dst_r, T0, T1, ALU.subtract)
                    ptt(T0, ai_, br_, ALU.mult)
                    ptt(dst_i, T2, T0, ALU.add)
                pts(PW, AR, 1.0, ALU.mult)
                pts(PW + 1, AI, 1.0, ALU.mult)
                for k in range(2, 9):
                    cmul(PW + 2 * (k - 1), PW + 2 * (k - 1) + 1, PW + 2 * (k - 2), PW + 2 * (k - 2) + 1, AR, AI)
                for k in range(1, 9):
                    pts(NP + k - 1, PW + 2 * (k - 1) + 1, -1.0, ALU.mult)
                ts("dve", prm2[:, 0, :], prm[:, PW + 14, :], 1.0, ALU.mult, ["prm"], ["prm2"])
                ts("dve", prm2[:, 1, :], prm[:, PW + 15, :], 1.0, ALU.mult, ["prm"], ["prm2"])
                for m in range(9):
                    if m > 0:
                        r0, i0_ = prm2[:, 3 * (m - 1), :], prm2[:, 3 * (m - 1) + 1, :]
                        tt("dve", prm[:, T0, :], r0, r0, ALU.mult, ["prm2", "prm"], ["prm"])
                        tt("dve", prm[:, T1, :], i0_, i0_, ALU.mult, ["prm2", "prm"], ["prm"])
                        tt("dve", prm2[:, 3 * m, :], prm[:, T0, :], prm[:, T1, :], ALU.subtract, ["prm"], ["prm2"])
                        tt("dve", prm[:, T0, :], r0, i0_, ALU.mult, ["prm2", "prm"], ["prm"])
                        ts("dve", prm2[:, 3 * m + 1, :], prm[:, T0, :], 2.0, ALU.mult, ["prm"], ["prm2"])
                    ts("dve", prm2[:, 3 * m + 2, :], prm2[:, 3 * m + 1, :], -1.0, ALU.mult, ["prm2"], ["prm2"])
                if s == 0:
                    dump("prm", prm[:, 0:40, :], ["prm"])
                for j in range(4):
                    wb = load_w(j * 128)
                    for bi, (t0, nt) in enumerate(TOKBLK):
                        pk = 1 + bi % 2
                        proj_fm(wb, PS[pk][:, 0:nt], t0, nt, PK[pk])
                        cp("act", uT[:, j, t0:t0 + nt], PS[pk][:, 0:nt], [PK[pk]], ["uT"])
                if s == 0:
                    dump("uT", uT[:, 0, 0:512], ["uT"])
                dcol = small[:, 0:4]
                dma(dcol, D["dcol"][:, :], w=["dcol"])
                gT = mixT[:, 0:4, :]
                Bpv = Bpb.rearrange("p (g d r c) -> p g d r c", g=4, d=2, r=2)
                Cpv = Cpb.rearrange("p (g d r c) -> p g d r c", g=4, d=2, r=2)
                for j in range(4):
                    for pc in range(2):
                        dma(xst[pc][:], D["Bp"][:, j * 2048 + pc * 1024:j * 2048 + (pc + 1) * 1024], w=["xst%d" % pc])
                        cp("pool", Bpb[:, pc * 1024:(pc + 1) * 1024], xst[pc][:], ["xst%d" % pc], ["Bpb"])
                    for gp in range(4):
                        b = gp % 2
                        pcg = j * 4 + gp
                        dma(xst[b][:, 0:512], D["Cp"][:, pcg * 512:(pcg + 1) * 512], w=["xst%d" % b])
                        for d in range(2):
                            col = pcg * 2 + d
                            cre_ = xst[b][:, d * 256:d * 256 + 128]
                            cim_ = xst[b][:, d * 256 + 128:d * 256 + 256]
                            t_ = xst[b][:, 512 + d * 128:512 + (d + 1) * 128]
                            base = gp * 512 + d * 256
                            xk = "xst%d" % b
                            ts("dve", t_, cim_, prm[:, CI, col:col + 1], ALU.mult, [xk, "prm"], [xk])
                            stt("dve", Cpb[:, base:base + 128], cre_, prm[:, CR, col:col + 1], t_, ALU.mult, ALU.subtract, [xk, "prm"], ["Cpb"])
                            ts("dve", t_, cre_, prm[:, CI, col:col + 1], ALU.mult, [xk, "prm"], [xk])
                            stt("dve", t_, cim_, prm[:, CR, col:col + 1], t_, ALU.mult, ALU.add, [xk, "prm"], [xk])
                            ts("dve", Cpb[:, base + 128:base + 256], t_, -1.0, ALU.mult, [xk], ["Cpb"])
                    YP = [PS[4 + q] for q in range(4)]
                    YK = [PK[4 + q] for q in range(4)]
                    rcount = [0, 0, 0, 0]

                    def s5_body(gp, d):
                        pair = j * 4 + gp
                        col = pair * 2 + d
                        E = "dve"
                        Hb = HbD[d]
                        hk = "Hb%d" % d
                        Xa, Xb = XaD[d], XbD[d]
                        for ri in range(2):
                            for bi, (t0, nt) in enumerate(TOKBLK):
                                pk = 1 + (bi + ri) % 3
                                mm(PS[pk][:, 0:nt], Bpv[:, gp, d, ri, :], uT[:, j, t0:t0 + nt], True, True, ["Bpb", "uT"], [PK[pk]])
                                cp("act", Hb[ri].rearrange("p (s c) -> p c s", s=8)[:, t0 // 8:(t0 + nt) // 8, :],
                                   PS[pk][:, 0:nt].rearrange("p (c s) -> p c s", s=8), [PK[pk]], [hk])
                        yield "hold"
                        Hr = Hb[0].rearrange("p (s c) -> p c s", s=8)
                        Hi = Hb[1].rearrange("p (s c) -> p c s", s=8)
                        a_r = prm[:, PW, col:col + 1]
                        a_i = prm[:, PW + 1, col:col + 1]
                        a_ni = prm[:, NP, col:col + 1]
                        order = range(1, 8) if d == 0 else range(6, -1, -1)
                        for s_ in order:
                            sp_ = s_ - 1 if d == 0 else s_ + 1
                            stt(E, Hr[:, :, s_], Hr[:, :, sp_], a_r, Hr[:, :, s_], ALU.mult, ALU.add, [hk, "prm"], [hk])
                            yield
                            stt(E, Hi[:, :, s_], Hr[:, :, sp_], a_i, Hi[:, :, s_], ALU.mult, ALU.add, [hk, "prm"], [hk])
                            yield
                            stt(E, Hr[:, :, s_], Hi[:, :, sp_], a_ni, Hr[:, :, s_], ALU.mult, ALU.add, [hk, "prm"], [hk])
                            yield
                            stt(E, Hi[:, :, s_], Hi[:, :, sp_], a_r, Hi[:, :, s_], ALU.mult, ALU.add, [hk, "prm"], [hk])
                            yield
                        se = 7 if d == 0 else 0
                        xak, xbk = "Xa%d" % d, "Xb%d" % d
                        for ri, Hv in enumerate((Hr, Hi)):
                            if d == 0:
                                cp(E, Xa[ri][:, 0:288], Hv[:, :, se], [hk], [xak])
                                yield
                            else:
                                cp(E, Xa[ri][:, 0:256], Hv[:, 32:288, se], [hk], [xak])
                                yield
                                cp(E, Xa[ri][:, 256:288], Hv[:, 0:32, se], [hk], [xak])
                                yield
                        cur, nxt, ck, nk = Xa, Xb, xak, xbk
                        curF, nxtF = XaF[d], XbF[d]
                        for m in range(9):
                            sh_ = 1 << m
                            A_r = prm2[:, 3 * m, col:col + 1]
                            A_i = prm2[:, 3 * m + 1, col:col + 1]
                            A_ni = prm2[:, 3 * m + 2, col:col + 1]
                            off_ = XPAD - sh_ if d == 0 else sh_
                            dst = slice(0, 288)
                            s0_ = curF[0][:, off_:off_ + 288]
                            s1_ = curF[1][:, off_:off_ + 288]
                            stt(E, nxt[0][:, dst], s0_, A_r, cur[0][:, dst], ALU.mult, ALU.add, [ck, "prm2"], [nk])
                            yield
                            stt(E, nxt[0][:, dst], s1_, A_ni, nxt[0][:, dst], ALU.mult, ALU.add, [ck, nk, "prm2"], [nk])
                            yield
                            stt(E, nxt[1][:, dst], s0_, A_i, cur[1][:, dst], ALU.mult, ALU.add, [ck, "prm2"], [nk])
                            yield
                            stt(E, nxt[1][:, dst], s1_, A_r, nxt[1][:, dst], ALU.mult, ALU.add, [ck, nk, "prm2"], [nk])
                            yield
                            cur, nxt, ck, nk = nxt, cur, nk, ck
                            curF, nxtF = nxtF, curF
                        HBr = HbB[d][0].rearrange("p (s c) -> p c s", s=8)
                        HBi = HbB[d][1].rearrange("p (s c) -> p c s", s=8)
                        bk = "HbB%d" % d
                        for s_ in range(8):
                            kpow = s_ + 1 if d == 0 else 8 - s_
                            p_r = prm[:, PW + 2 * (kpow - 1), col:col + 1]
                            p_i = prm[:, PW + 2 * (kpow - 1) + 1, col:col + 1]
                            p_ni = prm[:, NP + kpow - 1, col:col + 1]
                            if d == 0:
                                Hs_r, Hs_i = cur[0][:, 31:287], cur[1][:, 31:287]
                            else:
                                Hs_r, Hs_i = cur[0][:, 1:257], cur[1][:, 1:257]
                            stt(E, Hr[:, 32:288, s_], Hs_r, p_r, Hr[:, 32:288, s_], ALU.mult, ALU.add, [hk, ck, "prm"], [hk])
                            yield
                            stt(E, HBr[:, :, s_], Hs_i, p_ni, Hr[:, 32:288, s_], ALU.mult, ALU.add, [hk, ck, "prm"], [bk])
                            yield
                            stt(E, Hi[:, 32:288, s_], Hs_r, p_i, Hi[:, 32:288, s_], ALU.mult, ALU.add, [hk, ck, "prm"], [hk])
                            yield
                            stt(E, HBi[:, :, s_], Hs_i, p_r, Hi[:, 32:288, s_], ALU.mult, ALU.add, [hk, ck, "prm"], [bk])
                            yield
                        for q in range(4):
                            for ri in range(2):
                                mm(YP[q][:, :], Cpv[:, gp, d, ri, :], HbB[d][ri].rearrange("p (s c) -> p c s", s=8)[:, q * 64:(q + 1) * 64, :],
                                   rcount[q] == 0, rcount[q] == 15, ["Cpb", bk], [YK[q]])
                                rcount[q] += 1

                    def s5_stream(d):
                        for gp in range(4):
                            yield from s5_body(gp, d)
                    g0, g1 = s5_stream(0), s5_stream(1)
                    for _ in range(40):
                        next(g0)
                    alive = [g0, g1]
                    hold = {id(g0): 0, id(g1): 0}
                    while alive:
                        progressed = False
                        for g_ in list(alive):
                            if hold[id(g_)] > 0 and len(alive) > 1:
                                hold[id(g_)] -= 1
                                continue
                            try:
                                r_ = next(g_)
                                progressed = True
                                if r_ == "hold":
                                    hold[id(g_)] = 30
                            except StopIteration:
                                alive.remove(g_)
                        if not progressed:
                            for k_ in hold:
                                hold[k_] = 0
                    for q in range(4):
                        stt("dve", ytmp, uT[:, j, LCTX + q * 512:LCTX + (q + 1) * 512], dcol[:, j:j + 1], YP[q][:, :], ALU.mult, ALU.add,
                            ["uT", "dcol", YK[q]], ["ytmp"])
                        if s == 0 and j == 0 and q == 0:
                            dump("y0", ytmp, ["ytmp"])
                        act(gT[:, j, q * 512:(q + 1) * 512], ytmp, AF.Gelu, ["ytmp"], ["gT"])
                barrier()
                o[0] = o_hb
                zt = carve(512)
                sig = carve(4 * 512).rearrange("p (j c) -> p j c", j=4)
                wgl = carve(2048, BF16).rearrange("p (j c) -> p j c", j=4)
                bgl = small[:, 4:8]
                dma(bgl, D["bglu"][:, :], w=["bglu"])
                for j in range(4):
                    dma(xst[j % 2][:, 0:512], D["wglu"][:, j, :], w=["xst%d" % (j % 2)])
                    cp("pool", wgl[:, j, :], xst[j % 2][:, 0:512], ["xst%d" % (j % 2)], ["wgl"])
                for q in range(4):
                    for jo in range(4):
                        for ji in range(4):
                            mm(PS[4 + jo][:, :], wgl[:, ji, jo * 128:(jo + 1) * 128], gT[:, ji, q * 512:(q + 1) * 512], ji == 0, ji == 3,
                               ["wgl", "gT"], [PK[4 + jo]])
                        act(sig[:, jo, :], PS[4 + jo][:, :], AF.Sigmoid, [PK[4 + jo], "bglu"], ["sig%d" % jo], bias=bgl[:, jo:jo + 1])
                    for jo in range(4):
                        wb = load_w(512 + jo * 128)
                        proj_fm(wb, PS[1 + jo % 2][:, :], LCTX + q * 512, 512, PK[1 + jo % 2])
                        act(zt, PS[1 + jo % 2][:, :], AF.Silu, [PK[1 + jo % 2]], ["zt"])
                        tt("dve", sig[:, jo, :], sig[:, jo, :], zt, ALU.mult, ["sig%d" % jo, "zt"], ["sig%d" % jo])
                    for jo in range(4):
                        tt("pool", gT[:, jo, q * 512:(q + 1) * 512], gT[:, jo, q * 512:(q + 1) * 512], sig[:, jo, :], ALU.mult,
                           ["gT", "sig%d" % jo], ["gT"])
                if s == 0:
                    dump("mix_s5", mixT[:, 0, 0:512], ["gT"])
            if "gdn" in stages:
                barrier()
                o = [0]

                def carve(n, dt=F32):
                    words = n if dt == F32 else (n + 1) // 2
                    a = arena[:, o[0]:o[0] + words]
                    o[0] += words
                    assert o[0] <= ARENA_WORDS, o[0]
                    return a.bitcast(BF16)[:, 0:n] if dt == BF16 else a

                def c3(n_, a_, b_):
                    return carve(n_ * a_ * b_).rearrange("p (n a b) -> p n a b", n=n_, a=a_)
                wgb = carve(128, BF16).rearrange("p (k c) -> p k c", k=8)
                gpar = carve(32)
                normw = carve(128)
                cwt = carve(108).rearrange("p (t k) -> p t k", t=12)
                G_ = {nm: carve(288).rearrange("p (c e) -> p c e", e=8) for nm in ("beta", "g", "gcum", "koutc")}
                G_["gam"] = G_["gcum"]
                XTf = carve(LTOT)
                XT = [carve(LTOT, BF16) for _ in range(3)]
                lpad = carve(34 * 66, BF16).rearrange("p (r c) -> p r c", c=66)
                cpad = carve(258, BF16)
                dgt = carve(9 * 128, BF16).rearrange("p (k c) -> p k c", k=9)
                tmp5 = carve(512)
                tmp5b = carve(512)
                rsb = carve(32)
                Oacc = carve(32 * 128).rearrange("p (c v) -> p c v", v=128)
                Sst = [carve(128) for _ in range(2)]
                NB = 4
                NI = 2 * NB
                mk = lambda w_, dt=F32: carve(NI * w_, dt).rearrange("p (n c) -> p n c", n=NI)
                sh_ = {"Lg": mk(64), "dec": mk(64), "Pm": mk(64), "Lgh": mk(64, BF16), "Lgl": mk(64, BF16),
                       "Qa": mk(64, BF16), "QTa": mk(64, BF16), "Qb": mk(64, BF16), "QTb": mk(64, BF16), "Pmb": mk(64, BF16),
                       "V": mk(128, BF16), "Kg": mk(128, BF16)}
                Av = [dict(sh_, **{"wT": mk(64), "QgT": mk(64), "gbc": mk(64), "QKT": mk(64, BF16), "K": mk(128, BF16), "u0": mk(128)})
                      for _ in range(2)]
                A64v = Av
                A128v = Av
                ubuf = [carve(128, BF16) for _ in range(4)]
                dma(xst[0][:, 0:128], D["wgate"].rearrange("p k c -> p (k c)"), w=["xst0"])
                cp("dve", wgb.rearrange("p k c -> p (k c)"), xst[0][:, 0:128], ["xst0"], ["wgb"])
                dma(gpar[0:64, 0:8], D["alog"][:, :], w=["gpar"])
                dma(gpar[0:64, 8:16], D["dtb"][:, :], w=["gpar"])
                dma(normw[0:64, :], D["normw"][:, :], w=["normw"])
                dma(cwt[:], D["convw"][:, :, :], w=["cwt"])
                act(gpar[0:64, 0:8], gpar[0:64, 0:8], AF.Exp, ["gpar"], ["gpar"])
                ts("dve", gpar[0:64, 0:8], gpar[0:64, 0:8], -1.0, ALU.mult, ["gpar"], ["gpar"])
                P.op("dve", lambda e: e.memset(lpad[:, :, :], 0.0), writes=["lpad"])
                P.op("dve", lambda e: e.memset(cpad[:, :], 0.0), writes=["cpad"])
                for c_ in range(36):
                    bank, cc = (0, c_) if c_ < 32 else (3, c_ - 32)
                    for k in range(8):
                        mm(PS[bank][0:64, cc * 16:(cc + 1) * 16], hT[:, k, c_ * 64:(c_ + 1) * 64], wgb[:, k, :], k == 0, k == 7,
                           ["hT", "wgb"], [PK[bank]])
                for (bank, c0, nch) in ((0, 0, 32), (3, 32, 4)):
                    pv = PS[bank][0:64, 0:nch * 16].rearrange("p (c e) -> p c e", e=16)
                    act(G_["beta"][0:64, c0:c0 + nch, :], pv[:, :, 0:8], AF.Sigmoid, [PK[bank]], ["gates"])
                    tt("dve", G_["g"][0:64, c0:c0 + nch, :], pv[:, :, 8:16],
                       gpar[0:64, 8:16].unsqueeze(1).to_broadcast([64, nch, 8]), ALU.add, [PK[bank], "gpar"], ["gates"])
                gall = lambda nm: G_[nm][0:64, :, :]
                act(gall("g"), gall("g"), AF.Exp, ["gates"], ["gates"])
                act(gall("g"), gall("g"), AF.Ln, ["gates"], ["gates"], bias=1.0)
                tt("dve", gall("g"), gall("g"), gpar[0:64, 0:8].unsqueeze(1).to_broadcast([64, 36, 8]), ALU.mult, ["gates", "gpar"], ["gates"])
                gflat = G_["g"][0:64, :, :].rearrange("p c e -> p (c e)")
                for d in range(2):
                    for hh in range(2):
                        pass
                    mm(PS[1][0:64, 0:288], TRI[d], gflat, True, True, ["masks", "gates"], [PK[1]])
                    pv = PS[1][0:64, 0:288].rearrange("p (c e) -> p c e", e=8)
                    cp("dve", G_["gcum"][0:64, :, d * 4:(d + 1) * 4], pv[:, :, d * 4:(d + 1) * 4], [PK[1]], ["gates"])
                mm(PS[2][0:64, 0:288], ones[0:64, 0:64], gflat, True, True, ["ones", "gates"], [PK[2]])
                tt("dve", gall("koutc"), PS[2][0:64, 0:288].rearrange("p (c e) -> p c e", e=8), gall("gcum"), ALU.subtract, [PK[2], "gates"], ["gates"])
                act(gall("koutc"), gall("koutc"), AF.Exp, ["gates"], ["gates"])
                if s == 0:
                    dump("g", G_["g"][0:64, :, :].rearrange("p c e -> p (c e)"), ["gates"])
                    dump("gcum", G_["gcum"][0:64, :, :].rearrange("p c e -> p (c e)"), ["gates"])
                act(gall("gam"), gall("gcum"), AF.Exp, ["gates"], ["gates"])
                FORD = list(range(36))
                BORD = [3, 2, 1, 0] + list(range(35, 3, -1))
                import os as _os2
                for h in range(int(_os2.environ.get('GDN_NHEADS', 4))):
                    for t in range(3):
                        tile_i = t * 4 + h
                        wb = load_w(1024 + t * 512 + h * 128)
                        proj_fm(wb, PS[1][:, 0:256], 0, 256, PK[1])
                        cp("act", cpad[:, 1:257], PS[1][:, 0:256], [PK[1]], ["cpad"])
                        for q in range(4):
                            pk = 2 + q % 2
                            proj_fm(wb, PS[pk][:, :], 256 + q * 512, 512, PK[pk])
                            cp("act", lpad[:, 1 + 8 * q:9 + 8 * q, 1:65], PS[pk][:, :].rearrange("p (r c) -> p r c", c=64), [PK[pk]], ["lpad"])
                        xk = "XT%d" % t
                        for tap in range(9):
                            ts("dve", dgt[:, tap, :], identb[:, :], cwt[:, tile_i, tap:tap + 1], ALU.mult, ["identb", "cwt"], ["dgt"])
                        for kx in range(3):
                            mm(PS[4][:, 0:256], dgt[:, 3 + kx, :], cpad[:, kx:kx + 256], kx == 0, kx == 2, ["dgt", "cpad"], [PK[4]])
                        if t < 2:
                            act(XTf[:, 0:256], PS[4][:, 0:256], AF.Silu, [PK[4]], ["XTf"])
                        else:
                            act(XT[t][:, 0:256], PS[4][:, 0:256], AF.Silu, [PK[4]], [xk])
                        for q in range(4):
                            pk = 4 + (q + 1) % 2
                            for tap in range(9):
                                ky, kx = tap // 3, tap % 3
                                mm(PS[pk][:, :], dgt[:, tap, :], lpad[:, ky + 8 * q:ky + 8 * q + 8, kx:kx + 64], tap == 0, tap == 8,
                                   ["dgt", "lpad"], [PK[pk]])
                            if t < 2:
                                act(XTf[:, 256 + q * 512:256 + (q + 1) * 512], PS[pk][:, :], AF.Silu, [PK[pk]], ["XTf"])
                            else:
                                act(XT[t][:, 256 + q * 512:256 + (q + 1) * 512], PS[pk][:, :], AF.Silu, [PK[pk]], [xk])
                        if t < 2:
                            for bi, (t0, nt) in enumerate(TOKBLK):
                                pk = 1 + bi % 3
                                tq = tmp5 if bi % 2 == 0 else tmp5b
                                tqk = "tmp5" if bi % 2 == 0 else "tmp5b"
                                tt("pool", tq[:, 0:nt], XTf[:, t0:t0 + nt], XTf[:, t0:t0 + nt], ALU.mult, ["XTf"], [tqk])
                                mm(PS[pk][:, 0:nt], ones[:, :], tq[:, 0:nt], True, True, ["ones", tqk], [PK[pk]])
                                act(tq[:, 0:nt], PS[pk][:, 0:nt], AF.Ln, [PK[pk]], [tqk], bias=1e-6)
                                act(tq[:, 0:nt], tq[:, 0:nt], AF.Exp, [tqk], [tqk], scale=-0.5)
                                stt("dve", XT[t][:, t0:t0 + nt], XTf[:, t0:t0 + nt], (128.0 ** -0.5) if t == 0 else 1.0, tq[:, 0:nt],
                                    ALU.mult, ALU.mult, ["XTf", tqk], [xk])
                    qT, kT, vT = XT
                    if s == 0 and h == 0:
                        dump("qn", qT[:, 0:512], ["XT0"])
                        dump("kn", kT[:, 256:768], ["XT1"])
                        dump("vv", vT[:, 256:768], ["XT2"])
                    for d in range(2):
                        P.op("dve", (lambda d=d: lambda e: e.memset(Sst[d][:, :], 0.0))(), writes=["S%d" % d])

                    def batch_info(b):
                        i0_ = b * NB
                        return [(FORD[i0_], NB), (min(BORD[i0_:i0_ + NB]), NB)]

                    def stageA_batch(b):
                        info = batch_info(b)
                        par = b % 2
                        A64, A128 = A64v[par], A128v[par]
                        KY = lambda s_: s_ + str(par)
                        items = []
                        for d in range(2):
                            cmin, nch = info[d]
                            for q_ in range(nch):
                                items.append((d, cmin + q_, d * NB + q_))
                        e8 = lambda d: d * 4 + h
                        gcols = lambda nm, d: G_[nm][0:64, info[d][0]:info[d][0] + NB, e8(d)]
                        v = lambda nm, d: A64[nm][0:64, d * NB:(d + 1) * NB, :]
                        v2 = lambda nm, d: A128[nm][0:64, d * NB:(d + 1) * NB, :]
                        ps3 = lambda bank, d, w_: PS[bank][0:64, :].rearrange("p (n c) -> p n c", c=w_)[:, (d * NB if w_ == 64 else 0):(d * NB if w_ == 64 else 0) + NB, :]
                        for d in range(2):
                            tt("dve", v("Lg", d), TRI[d].unsqueeze(1).to_broadcast([64, NB, 64]),
                               gcols("g", d).unsqueeze(2).to_broadcast([64, NB, 64]), ALU.mult, ["masks", "gates"], ["aLg"])
                        cp("act", A64["Lgh"][0:64, :, :], A64["Lg"][0:64, :, :], ["aLg"], ["aLgh"])
                        tt("dve", A64["Lgl"][0:64, :, :], A64["Lg"][0:64, :, :], A64["Lgh"][0:64, :, :], ALU.subtract, ["aLg", "aLgh"], ["aLgl"])
                        yield
                        for (d, c_, it) in items:
                            mm(PS[2][0:64, it * 64:(it + 1) * 64], SUb[d], A64["Lgh"][0:64, it, :], True, False, ["masksb", "aLgh"], [PK[2]])
                            mm(PS[2][0:64, it * 64:(it + 1) * 64], SUb[d], A64["Lgl"][0:64, it, :], False, True, ["masksb", "aLgl"], [PK[2]])
                            mm(PS[3][:, it * 64:(it + 1) * 64], onesb[:, :], A64["Lgh"][0:64, it, :], True, False, ["onesb", "aLgh"], [PK[3]])
                            mm(PS[3][:, it * 64:(it + 1) * 64], onesb[:, :], A64["Lgl"][0:64, it, :], False, True, ["onesb", "aLgl"], [PK[3]])
                        act(A64["dec"][0:64, :, :], PS[2][0:64, :].rearrange("p (n c) -> p n c", c=64), AF.Exp, [PK[2]], ["adec"])
                        act(A64["gbc"][:, :, :], PS[3][:, :].rearrange("p (n c) -> p n c", c=64), AF.Exp, [PK[3]], [KY("agbc")])
                        yield
                        for (d, c_, it) in items:
                            tok = slice(c_ * 64, (c_ + 1) * 64)
                            mm(PS[0][0:64, it * 64:(it + 1) * 64], kT[:, tok], kT[:, tok], True, True, ["XT1"], [PK[0]])
                            mm(PS[1][0:64, it * 64:(it + 1) * 64], kT[:, tok], qT[:, tok], True, True, ["XT1", "XT0"], [PK[1]])
                        for d in range(2):
                            tt("dve", v("Lg", d), v("dec", d), MI[d].unsqueeze(1).to_broadcast([64, NB, 64]), ALU.mult, ["adec", "masks"], ["aLg"])
                            tt("dve", v("dec", d), v("dec", d), MS[d].unsqueeze(1).to_broadcast([64, NB, 64]), ALU.mult, ["adec", "masks"], ["adec"])
                        yield
                        for d in range(2):
                            tt("dve", v("Pm", d), PS[0][0:64, :].rearrange("p (n c) -> p n c", c=64)[:, d * NB:(d + 1) * NB, :],
                               gcols("beta", d).unsqueeze(2).to_broadcast([64, NB, 64]), ALU.mult, [PK[0], "gates"], ["aPm"])
                        tt("dve", A64["dec"][0:64, :, :], A64["Pm"][0:64, :, :], A64["dec"][0:64, :, :], ALU.mult, ["aPm", "adec"], ["adec"])
                        tt("dve", A64["QKT"][0:64, :, :], PS[1][0:64, :].rearrange("p (n c) -> p n c", c=64), A64["Lg"][0:64, :, :], ALU.mult,
                           [PK[1], "aLg"], [KY("aQKT")])
                        tt("dve", A64["Pm"][0:64, :, :], ident[0:64, 0:64].unsqueeze(1).to_broadcast([64, NI, 64]), A64["dec"][0:64, :, :],
                           ALU.subtract, ["ident", "adec"], ["aPm"])
                        cp("dve", A64["Pmb"][0:64, :, :], A64["Pm"][0:64, :, :], ["aPm"], ["aPmb"])
                        cp("act", A64["Qa"][0:64, :, :], A64["dec"][0:64, :, :], ["adec"], ["aQa"])
                        yield
                        for (d, c_, it) in items:
                            mm(PS[4][0:64, it * 64:(it + 1) * 64], A64["Qa"][0:64, it, :], identb[0:64, 0:64], True, True, ["aQa", "identb"], [PK[4]])
                        cp("act", A64["QTa"][0:64, :, :], PS[4][0:64, :].rearrange("p (n c) -> p n c", c=64), [PK[4]], ["aQTa"])
                        yield
                        for (d, c_, it) in items:
                            tok = slice(c_ * 64, (c_ + 1) * 64)
                            mm(PS[2 + d][0:64, (it % NB) * 128:(it % NB + 1) * 128], kT[:, tok], identb[:, :], True, True, ["XT1", "identb"], [PK[2 + d]])
                            mm(PS[4 + d][0:64, (it % NB) * 128:(it % NB + 1) * 128], vT[:, tok], identb[:, :], True, True, ["XT2", "identb"], [PK[4 + d]])
                        for d in range(2):
                            cp("act", v2("K", d), PS[2 + d][0:64, :].rearrange("p (n c) -> p n c", c=128), [PK[2 + d]], [KY("aK")])
                            cp("act", v2("V", d), PS[4 + d][0:64, :].rearrange("p (n c) -> p n c", c=128), [PK[4 + d]], ["aV"])
                        for d in range(2):
                            tt("pool", v2("Kg", d), v2("K", d), gcols("gam", d).unsqueeze(2).to_broadcast([64, NB, 128]), ALU.mult, [KY("aK"), "gates"], ["aKg"])
                            tt("pool", v2("K", d), v2("K", d), gcols("koutc", d).unsqueeze(2).to_broadcast([64, NB, 128]), ALU.mult, [KY("aK"), "gates"], [KY("aK")])
                            cmin = info[d][0]
                            tt("pool", A64["QgT"][:, d * NB:(d + 1) * NB, :], qT[:, cmin * 64:(cmin + NB) * 64].rearrange("p (n c) -> p n c", c=64),
                               A64["gbc"][:, d * NB:(d + 1) * NB, :], ALU.mult, ["XT0", KY("agbc")], [KY("aQgT")])
                        yield
                        Q, QT, Qn, QTn = "Qa", "QTa", "Qb", "QTb"
                        kq = {"Qa": "aQa", "QTa": "aQTa", "Qb": "aQb", "QTb": "aQTb"}
                        for lvl in range(5):
                            for (d, c_, it) in items:
                                mm(PS[0][0:64, it * 64:(it + 1) * 64], A64[QT][0:64, it, :], A64[Q][0:64, it, :], True, True, [kq[Q], kq[QT]], [PK[0]])
                                mm(PS[1][0:64, it * 64:(it + 1) * 64], A64[Q][0:64, it, :], A64[QT][0:64, it, :], True, True, [kq[Q], kq[QT]], [PK[1]])
                            yield
                            cp("act", A64[Qn][0:64, :, :], PS[0][0:64, :].rearrange("p (n c) -> p n c", c=64), [PK[0]], [kq[Qn]])
                            cp("dve", A64[QTn][0:64, :, :], PS[1][0:64, :].rearrange("p (n c) -> p n c", c=64), [PK[1]], [kq[QTn]])
                            for (d, c_, it) in items:
                                mm(PS[5][0:64, it * 64:(it + 1) * 64], A64[QTn][0:64, it, :], A64["Pmb"][0:64, it, :], True, True, [kq[QTn], "aPmb"], [PK[5]])
                            tt("dve", A64["Pm"][0:64, :, :], A64["Pm"][0:64, :, :], PS[5][0:64, :].rearrange("p (n c) -> p n c", c=64), ALU.add,
                               ["aPm", PK[5]], ["aPm"])
                            cp("act", A64["Pmb"][0:64, :, :], A64["Pm"][0:64, :, :], ["aPm"], ["aPmb"])
                            yield
                            Q, QT, Qn, QTn = Qn, QTn, Q, QT
                        yield
                        for (d, c_, it) in items:
                            mm(PS[2 + d][0:64, (it % NB) * 128:(it % NB + 1) * 128], A64["Pmb"][0:64, it, :], A128["V"][0:64, it, :], True, True,
                               ["aPmb", "aV"], [PK[2 + d]])
                            mm(PS[4][:, it * 64:(it + 1) * 64], A128["Kg"][0:64, it, :], A64["Pmb"][0:64, it, :], True, True, ["aKg", "aPmb"], [PK[4]])
                        for d in range(2):
                            tt("dve", v2("u0", d), PS[2 + d][0:64, :].rearrange("p (n c) -> p n c", c=128),
                               gcols("beta", d).unsqueeze(2).to_broadcast([64, NB, 128]), ALU.mult, [PK[2 + d], "gates"], [KY("au0")])
                        act(A64["wT"][:, :, :], PS[4][:, :].rearrange("p (n c) -> p n c", c=64), AF.Identity, [PK[4]], [KY("awT")], scale=-1.0)

                    def stageC_gen(i, d):
                        b = i // NB
                        info = batch_info(b)
                        par = b % 2
                        A64, A128 = A64v[par], A128v[par]
                        KY = lambda s_: s_ + str(par)
                        c_ = FORD[i] if d == 0 else BORD[i]
                        it = d * NB + (c_ - info[d][0])
                        e8 = d * 4 + h
                        col = lambda nm: G_[nm][0:64, c_, e8:e8 + 1]
                        pC = PS[6 + d]
                        kC = PK[6 + d]
                        Sk = "S%d" % d
                        S_ = Sst[d]
                        ub = ubuf[(i % 2) * 2 + d]
                        uk = "u%d" % ((i % 2) * 2 + d)
                        mm(pC[0:64, 0:128], A64["wT"][:, it, :], S_[:, :], True, True, [KY("awT"), Sk], [kC])
                        yield
                        stt("dve", ub[0:64, :], pC[0:64, 0:128], col("beta"), A128["u0"][0:64, it, :], ALU.mult, ALU.add, [kC, "gates", KY("au0")], [uk])
                        yield
                        if c_ >= 4:
                            cl = c_ - 4
                            mm(pC[0:64, 128:256], A64["QgT"][:, it, :], S_[:, :], True, False, [KY("aQgT"), Sk], [kC])
                            mm(pC[0:64, 128:256], A64["QKT"][0:64, it, :], ub[0:64, :], False, True, [KY("aQKT"), uk], [kC])
                        mm(pC[:, 256:384], A128["K"][0:64, it, :], ub[0:64, :], True, True, [KY("aK"), uk], [kC])
                        yield
                        lastcol = 63 if d == 0 else 0
                        stt("dve", S_[:, :], S_[:, :], A64["gbc"][:, it, lastcol:lastcol + 1], pC[:, 256:384], ALU.mult, ALU.add,
                            [Sk, KY("agbc"), kC], [Sk])
                        yield
                        if c_ >= 4:
                            firstw = (d == 0 and cl < 16) or (d == 1 and cl >= 16)
                            ok = "O%d" % cl
                            if firstw:
                                cp("act", Oacc[0:64, cl, :], pC[0:64, 128:256], [kC], [ok])
                            else:
                                tt("dve", Oacc[0:64, cl, :], Oacc[0:64, cl, :], pC[0:64, 128:256], ALU.add, [kC, ok], [ok])
                        yield

                    def chain_d(d, steps):
                        for i in steps:
                            yield from stageC_gen(i, d)

                    def stageC_steps(steps):
                        alive = [chain_d(0, steps), chain_d(1, steps)]
                        while alive:
                            for g_ in list(alive):
                                try:
                                    next(g_)
                                except StopIteration:
                                    alive.remove(g_)

                    def chain_both(steps):
                        alive = [chain_d(0, steps), chain_d(1, steps)]
                        while alive:
                            for g_ in list(alive):
                                try:
                                    next(g_)
                                except StopIteration:
                                    alive.remove(g_)
                            yield

                    def run_all(g_):
                        for _ in g_:
                            pass
                    NBT = 36 // NB
                    run_all(stageA_batch(0))
                    for b in range(NBT):
                        gC = chain_both(list(range(b * NB, (b + 1) * NB)))
                        gA = stageA_batch(b + 1) if b + 1 < NBT else iter(())
                        doneA = doneC = False
                        while not (doneA and doneC):
                            for _ in range(3):
                                if not doneA:
                                    try:
                                        next(gA)
                                    except StopIteration:
                                        doneA = True
                            if not doneC:
                                try:
                                    next(gC)
                                except StopIteration:
                                    doneC = True
                    if s == 0 and h == 0:
                        dump("O0", Oacc[0:64, 0, :], ["O0"])
                        dump("O31", Oacc[0:64, 31, :], ["O31"])
                    wb = load_w(2560 + h * 128)
                    for blk in range(8):
                        par_ = blk % 2
                        okeys = ["O%d" % (blk * 4 + cc) for cc in range(4)]
                        O4 = Oacc[0:64, blk * 4:blk * 4 + 4, :]
                        sq4 = XTf[0:64, par_ * 512:(par_ + 1) * 512].rearrange("p (c v) -> p c v", v=128)
                        rs_ = rsb[0:64, 4 * blk:4 * blk + 4]
                        ksq, krs = "fsq%d" % par_, "rs%d" % blk
                        tt("pool", sq4, O4, O4, ALU.mult, okeys + ["XTf"], [ksq])
                        P.op("dve", (lambda sq4=sq4, rs_=rs_: lambda e: e.tensor_reduce(out=rs_, in_=sq4, axis=AX.X, op=ALU.add))(),
                             reads=[ksq, "XTf"], writes=[krs])
                        ts("dve", rs_, rs_, 1.0 / 128.0, ALU.mult, [krs], [krs], s2=1e-6, op1=ALU.add)
                        act(rs_, rs_, AF.Sqrt, [krs], [krs])
                        P.op("dve", (lambda rs_=rs_: lambda e: e.reciprocal(out=rs_, in_=rs_))(), reads=[krs], writes=[krs])
                    for blk in range(8):
                        par_ = blk % 2
                        okeys = ["O%d" % (blk * 4 + cc) for cc in range(4)]
                        O4 = Oacc[0:64, blk * 4:blk * 4 + 4, :]
                        z4 = XTf[0:64, 1024 + par_ * 512:1024 + (par_ + 1) * 512].rearrange("p (c v) -> p c v", v=128)
                        rs_ = rsb[0:64, 4 * blk:4 * blk + 4]
                        kz, krs = "fz%d" % par_, "rs%d" % blk
                        pz, pt = (1, 2) if par_ == 0 else (3, 4)
                        for cc in range(4):
                            tk = slice(LCTX + (blk * 4 + cc) * 64, LCTX + (blk * 4 + cc + 1) * 64)
                            for k in range(8):
                                mm(PS[pz][0:64, cc * 128:(cc + 1) * 128], hT[:, k, tk], wbf[wb][:, k, :], k == 0, k == 7, ["hT", "wbf%d" % wb], [PK[pz]])
                        act(z4, PS[pz][0:64, :].rearrange("p (c v) -> p c v", v=128), AF.Silu, [PK[pz], "XTf"], [kz])
                        tt("dve", O4, O4, rs_.unsqueeze(2).to_broadcast([64, 4, 128]), ALU.mult, okeys + [krs], okeys)
                        tt("pool", O4, O4, normw[0:64, :].unsqueeze(1).to_broadcast([64, 4, 128]), ALU.mult, okeys + ["normw"], okeys)
                        tt("dve", O4, O4, z4, ALU.mult, okeys + [kz, "XTf"], okeys)
                        for cc in range(4):
                            tr(PS[pt][:, cc * 64:(cc + 1) * 64], Oacc[0:64, blk * 4 + cc, :], ident[0:64, 0:64], okeys + ["ident"], [PK[pt]])
                        cp("act", mixT[:, 4 + h, blk * 256:(blk + 1) * 256], PS[pt][:, 0:256], [PK[pt]], ["mixG"])
                    if s == 0 and h == 0:
                        dump("mix_g", mixT[:, 4, 0:512], ["mixG"])

            if "out" in stages:
                barrier()
                o = [0]

                def carve(n, dt=F32):
                    words = n if dt == F32 else (n + 1) // 2
                    a = arena[:, o[0]:o[0] + words]
                    o[0] += words
                    assert o[0] <= ARENA_WORDS, o[0]
                    return a.bitcast(BF16)[:, 0:n] if dt == BF16 else a
                woutb = carve(8192, BF16).rearrange("p (k c) -> p k c", k=8)
                gate_bc = carve(1024)
                lng = carve(1024)
                lnb = carve(1024)
                wg = carve(4096).rearrange("p (k c) -> p k c", k=8)
                silucb = carve(1024).rearrange("p (k c) -> p k c", k=8)
                rbuf = [carve(1024) for _ in range(2)]
                stats = carve(16)
                dma(lng, D["lng"][:, :], w=["lng"])
                dma(lnb, D["lnb"][:, :], w=["lnb"])
                dma(gate_bc, D["bgate_bc"][:, :], w=["gate_bc"])
                wout_v = D["w_out"].rearrange("(k p) c -> p k c", p=128)
                for k in range(8):
                    dma(xst[k % 2][:], wout_v[:, k, :], w=["xst%d" % (k % 2)])
                    cp("pool", woutb[:, k, :], xst[k % 2][:], ["xst%d" % (k % 2)], ["woutb"])
                for k in range(8):
                    ts("dve", silucb[:, k, :], ones[:, :], siluc[:, k, s:s + 1], ALU.mult, ["ones", "siluc"], ["silucb"])
                for half in range(2):
                    dma(wg[:], wada[:, :, 2048 + half * 512:2048 + (half + 1) * 512], w=["wg"])
                    for k in range(8):
                        mm(PS[3][:, :], silucb[:, k, :], wg[:, k, :], k == 0, k == 7, ["silucb", "wg"], [PK[3]])
                    tt("dve", gate_bc[:, half * 512:(half + 1) * 512], gate_bc[:, half * 512:(half + 1) * 512], PS[3][:, :], ALU.add,
                       ["gate_bc", PK[3]], ["gate_bc"])
                if s == 0:
                    dump("gate", gate_bc[:, 0:512], ["gate_bc"])
                for ti in range(16):
                    b = ti % 2
                    rk = "r%d" % b
                    dma(xst[b][:], D["x2"][s, ti * 128:(ti + 1) * 128, :], w=["xst%d" % b])
                    for half in range(2):
                        pk = 4 + half + 2 * b
                        for k in range(8):
                            mm(PS[pk][:, :], mixT[:, k, ti * 128:(ti + 1) * 128], woutb[:, k, half * 512:(half + 1) * 512], k == 0, k == 7,
                               ["mixT", "woutb"], [PK[pk]])
                        tt("dve", rbuf[b][:, half * 512:(half + 1) * 512], PS[pk][:, :], gate_bc[:, half * 512:(half + 1) * 512], ALU.mult,
                           [PK[pk], "gate_bc"], [rk])
                    stt("pool", rbuf[b][:, :], xst[b][:, :], DEEP_ALPHA, rbuf[b][:, :], ALU.mult, ALU.add, ["xst%d" % b, rk], [rk])
                    st6 = stats[:, 0:12].rearrange("p (c e) -> p c e", e=6)
                    for half in range(2):
                        P.op("dve", (lambda b=b, half=half, st6=st6: lambda e: e.bn_stats(out=st6[:, half, :], in_=rbuf[b][:, half * 512:(half + 1) * 512]))(),
                             reads=[rk], writes=["stats"])
                    P.op("dve", (lambda st6=st6: lambda e: e.bn_aggr(out=stats[:, 12:14], in_=st6))(), reads=["stats"], writes=["mv"])
                    act(stats[:, 14:15], stats[:, 13:14], AF.Sqrt, ["mv"], ["mv2"], bias=1e-5)
                    P.op("dve", lambda e: e.reciprocal(out=stats[:, 14:15], in_=stats[:, 14:15]), reads=["mv2"], writes=["mv2"])
                    stt("dve", stats[:, 15:16], stats[:, 12:13], -1.0, stats[:, 14:15], ALU.mult, ALU.mult, ["mv", "mv2"], ["mv2"])
                    act(rbuf[b][:, :], rbuf[b][:, :], AF.Identity, [rk, "mv2"], [rk], bias=stats[:, 15:16], scale=stats[:, 14:15])
                    tt("pool", rbuf[b][:, :], rbuf[b][:, :], lng, ALU.mult, [rk, "lng"], [rk])
                    tt("dve", rbuf[b][:, :], rbuf[b][:, :], lnb, ALU.add, [rk, "lnb"], [rk])
                    dma(yout[s, ti * 128:(ti + 1) * 128, :], rbuf[b][:, :], r=[rk])
        P.emit(nc)
    return nc, P


def _core_inputs(inp, sh, core):
    m = dict(sh)
    f = lambda a: np.ascontiguousarray(np.asarray(a, dtype=np.float32))
    b0 = core * NSEQ
    m["x2"] = f(inp["x"][b0:b0 + NSEQ])
    m["ctx2"] = f(inp["ctx"][b0:b0 + NSEQ])
    cc = np.stack([np.asarray(inp["c"][b0], np.float32), np.asarray(inp["c"][b0 + 1], np.float32),
                   np.asarray(inp["c_ctx"], np.float32)], axis=0)
    m["cT"] = f(cc.reshape(3, 8, 128).transpose(2, 1, 0))
    return m


_CACHE = {}


def kernel(**inputs):
    if "nc" not in _CACHE:
        _CACHE["nc"] = build_program()[0]
    nc = _CACHE["nc"]
    sh = _prep_shared(inputs)
    maps = [_core_inputs(inputs, sh, c) for c in range(8)]
    res = run_bass_kernel_spmd(nc, maps, core_ids=list(range(8)))
    out = np.concatenate([r["yout"] for r in res.results], axis=0)
    return out.astype(np.float32)
```

```python
import contextlib
import math
import numpy as np
import concourse.bass as bass
import concourse.mybir as mybir
from concourse.bass_utils import run_bass_kernel_spmd

F32 = mybir.dt.float32
BF16 = mybir.dt.bfloat16
ALU = mybir.AluOpType
AF = mybir.ActivationFunctionType
AX = mybir.AxisListType

ENG_NAMES = ["pe", "act", "dve", "pool", "sp"]
SAME_ENGINE_SYNC = True
N_DMA_SEMS = 12

NSEQ = 2
LCTX = 256
LLAT = 2048
LTOT = LCTX + LLAT
DEEP_ALPHA = 2.0 ** 0.25


class Prog:
    def __init__(self):
        self.ops = []
        self.last_w = {}
        self.readers = {}
        self.barrier_idx = None

    def barrier(self, eng, fn):
        deps = set(self.last_w.values())
        for v in self.readers.values():
            deps.update(v)
        if self.barrier_idx is not None:
            deps.add(self.barrier_idx)
        idx = len(self.ops)
        self.ops.append(dict(eng=eng, fn=fn, deps=deps, dma=False))
        self.barrier_idx = idx
        self.last_w = {}
        self.readers = {}
        return idx

    def op(self, eng, fn, reads=(), writes=(), dma=False):
        writes = list(writes) + [k for k in reads if k.startswith("ps")]
        idx = len(self.ops)
        deps = set()
        if self.barrier_idx is not None:
            deps.add(self.barrier_idx)
        for k in reads:
            if k in self.last_w:
                deps.add(self.last_w[k])
        for k in writes:
            if k in self.last_w:
                deps.add(self.last_w[k])
            deps.update(self.readers.get(k, ()))
        self.ops.append(dict(eng=eng, fn=fn, deps=deps, dma=dma))
        for k in reads:
            self.readers.setdefault(k, []).append(idx)
        for k in writes:
            self.last_w[k] = idx
            self.readers[k] = []
        return idx

    def emit(self, nc):
        ops = self.ops
        pos = {}
        seqcount = {e: 0 for e in ENG_NAMES}
        for i, o in enumerate(ops):
            if not o["dma"]:
                seqcount[o["eng"]] += 1
                pos[i] = seqcount[o["eng"]]
        dma_ops = [i for i, o in enumerate(ops) if o["dma"]]
        dma_sem = {}
        dma_val = {}
        semcnt = [0] * N_DMA_SEMS
        prev_on_sem = {}
        for n, i in enumerate(dma_ops):
            s = n % N_DMA_SEMS
            semcnt[s] += 16
            dma_sem[i] = s
            dma_val[i] = semcnt[s]
            if s in prev_on_sem:
                ops[i]["deps"].add(prev_on_sem[s])
            prev_on_sem[s] = i
        known = {e: {p: 0 for p in ENG_NAMES} for e in ENG_NAMES}
        known_dma = {e: [0] * N_DMA_SEMS for e in ENG_NAMES}
        flagged = set()
        waits = [[] for _ in ops]
        for i, o in enumerate(ops):
            e = o["eng"]
            for d in sorted(o["deps"]):
                od = ops[d]
                if od["dma"]:
                    s = dma_sem[d]
                    if dma_val[d] > known_dma[e][s]:
                        known_dma[e][s] = dma_val[d]
                        waits[i].append(("dma", s, dma_val[d]))
                else:
                    p = od["eng"]
                    if p == e and (e == "pe" or not SAME_ENGINE_SYNC):
                        continue
                    if pos[d] > known[e][p]:
                        known[e][p] = pos[d]
                        flagged.add(d)
                        waits[i].append(("eng", p, d))
        cnt = {e: 0 for e in ENG_NAMES}
        val = {}
        for i, o in enumerate(ops):
            if i in flagged:
                cnt[o["eng"]] += 1
                val[i] = cnt[o["eng"]]
        per_eng = {e: [] for e in ENG_NAMES}
        for i, o in enumerate(ops):
            per_eng[o["eng"]].append(i)
        self.stats = {e: len(per_eng[e]) for e in ENG_NAMES}

        with contextlib.ExitStack() as st:
            esem = {e: st.enter_context(nc.semaphore("s_" + e)) for e in ENG_NAMES}
            dsem = [st.enter_context(nc.semaphore("d_%d" % k)) for k in range(N_DMA_SEMS)]
            block = st.enter_context(nc.Block())

            def run(e, engobj):
                for i in per_eng[e]:
                    o = ops[i]
                    mx = {}
                    for w in waits[i]:
                        if w[0] == "dma":
                            key = ("d", w[1]); v = w[2]
                        else:
                            key = ("e", w[1]); v = val[w[2]]
                        mx[key] = max(mx.get(key, 0), v)
                    for key, v in mx.items():
                        sem = dsem[key[1]] if key[0] == "d" else esem[key[1]]
                        engobj.wait_ge(sem, v)
                    ins = o["fn"](engobj)
                    if o["dma"]:
                        ins.then_inc(dsem[dma_sem[i]], 16)
                    elif i in flagged:
                        ins.then_inc(esem[e], 1)
                if e == "sp":
                    for s in range(N_DMA_SEMS):
                        if semcnt[s] > 0:
                            engobj.wait_ge(dsem[s], semcnt[s])

            @block.tensor
            def _(eng):
                run("pe", eng)

            @block.scalar
            def _(eng):
                run("act", eng)

            @block.vector
            def _(eng):
                run("dve", eng)

            @block.gpsimd
            def _(eng):
                run("pool", eng)

            @block.sync
            def _(eng):
                run("sp", eng)


def _consts():
    c = {}
    c["ident"] = np.eye(128, dtype=np.float32)
    k = np.arange(64)
    tri = np.stack([(k[:, None] <= k[None, :]), (k[:, None] >= k[None, :])]).astype(np.float32)
    su = np.stack([(k[:, None] > k[None, :]), (k[:, None] < k[None, :])]).astype(np.float32)
    mi = np.stack([(k[None, :] >= k[:, None]), (k[None, :] <= k[:, None])]).astype(np.float32)
    ms = np.stack([(k[None, :] > k[:, None]), (k[None, :] < k[:, None])]).astype(np.float32)
    c["masks"] = np.concatenate([tri[0], tri[1], su[0], su[1], mi[0], mi[1], ms[0], ms[1]], axis=1).astype(np.float32)
    c["ones"] = np.ones((128, 128), np.float32)
    return c


def _prep_shared(inp):
    f = lambda a: np.ascontiguousarray(np.asarray(a, dtype=np.float32))
    sh = {}
    sh["w_ada"] = f(inp["w_ada"][0])
    b_ada = f(inp["b_ada"][0])
    sh["bcol"] = f(b_ada.reshape(24, 128).T)
    sh["bgate_bc"] = f(np.broadcast_to(b_ada[2048:3072][None, :], (128, 1024)))
    w_in = f(inp["w_in"][0])
    sh["w_in"] = w_in
    sh["wgate"] = f(w_in[:, 3072:3088].reshape(8, 128, 16).transpose(1, 0, 2))
    lre = f(inp["s5_lambda_re"][0]); lim = f(inp["s5_lambda_im"][0]); ldt = f(inp["s5_log_dt"][0])

    def smaj(a):
        o = np.zeros((128, 32), np.float32)
        for pair in range(16):
            for d in range(2):
                for gl in range(2):
                    o[gl * 64:(gl + 1) * 64, pair * 2 + d] = a[d, 2 * pair + gl, :]
        return o
    sh["lam_re"] = smaj(lre)
    sh["lam_im"] = smaj(lim)
    sh["logdt"] = smaj(np.broadcast_to(ldt[:, :, None], (2, 32, 64)))
    bre = f(inp["s5_b_re"][0]); bim = f(inp["s5_b_im"][0])
    cre = f(inp["s5_c_re"][0]); cim = f(inp["s5_c_im"][0])
    Bp = np.zeros((128, 4, 4, 2, 2, 128), np.float32)
    Cp = np.zeros((128, 4, 4, 2, 2, 128), np.float32)
    for j in range(4):
        for gp in range(4):
            for gl in range(2):
                gq = 2 * gp + gl
                g = 8 * j + gq
                for d in range(2):
                    Bp[gq * 16:(gq + 1) * 16, j, gp, d, 0, gl * 64:(gl + 1) * 64] = bre[d, g].T
                    Bp[gq * 16:(gq + 1) * 16, j, gp, d, 1, gl * 64:(gl + 1) * 64] = bim[d, g].T
                    Cp[gl * 64:(gl + 1) * 64, j, gp, d, 0, gq * 16:(gq + 1) * 16] = cre[d, g].T
                    Cp[gl * 64:(gl + 1) * 64, j, gp, d, 1, gq * 16:(gq + 1) * 16] = cim[d, g].T
    sh["Bp"] = f(Bp.reshape(128, 8192))
    sh["Cp"] = f(Cp.reshape(128, 8192))
    sh["dcol"] = f(inp["s5_d"][0].reshape(4, 128).T)
    sh["wglu"] = f(inp["w_glu"][0].reshape(4, 128, 512).transpose(1, 0, 2))
    sh["bglu"] = f(inp["b_glu"][0].reshape(4, 128).T)
    sh["convw"] = f(inp["conv_w"][0].reshape(9, 12, 128).transpose(2, 1, 0))
    sh["alog"] = f(np.broadcast_to(inp["gdn_a_log"][0].reshape(1, 8), (64, 8)))
    sh["dtb"] = f(np.broadcast_to(inp["gdn_dt_bias"][0].reshape(1, 8), (64, 8)))
    sh["normw"] = f(np.broadcast_to(inp["gdn_norm_w"][0].reshape(1, 128), (64, 128)))
    sh["w_out"] = f(inp["w_out"][0])
    sh["lng"] = f(np.broadcast_to(inp["ln_g"][0][None, :], (128, 1024)))
    sh["lnb"] = f(np.broadcast_to(inp["ln_b"][0][None, :], (128, 1024)))
    sh.update(_consts())
    return sh


IN_SHAPES = {
    "x2": [NSEQ, LLAT, 1024], "ctx2": [NSEQ, LCTX, 1024], "cT": [128, 8, 3],
    "w_ada": [1024, 3072], "bcol": [128, 24], "bgate_bc": [128, 1024], "w_in": [1024, 3088],
    "wgate": [128, 8, 16], "lam_re": [128, 32], "lam_im": [128, 32], "logdt": [128, 32],
    "Bp": [128, 8192], "Cp": [128, 8192], "dcol": [128, 4], "wglu": [128, 4, 512], "bglu": [128, 4],
    "convw": [128, 12, 9], "alog": [64, 8], "dtb": [64, 8], "normw": [64, 128],
    "w_out": [1024, 1024], "lng": [128, 1024], "lnb": [128, 1024],
    "ident": [128, 128], "masks": [64, 512], "ones": [128, 128],
}


def build_program(dbg=None, stages=("s5", "gdn", "out")):
    nc = bass.Bass("TRN2", target_bir_lowering=False)
    D = {k: nc.dram_tensor(k, v, F32, kind="ExternalInput").ap() for k, v in IN_SHAPES.items()}
    yout = nc.dram_tensor("yout", [NSEQ, LLAT, 1024], F32, kind="ExternalOutput").ap()
    dbg_t = {}
    if dbg:
        for k, shp in dbg.items():
            dbg_t[k] = nc.dram_tensor("dbg_" + k, shp, F32, kind="ExternalOutput").ap()
    P = Prog()
    st = contextlib.ExitStack()
    uid = [0]

    def sb(name, shape, dt=F32):
        return st.enter_context(nc.sbuf_tensor("sb_" + name, shape, dt))

    def psum(name):
        return st.enter_context(nc.psum_tensor(name, [128, 512], F32))

    with st:
        hT = sb("hT", [128, 8, LTOT], BF16)
        mixT = sb("mixT", [128, 8, LLAT], BF16)
        ident = sb("ident", [128, 128])
        ones = sb("ones", [128, 128])
        masks = sb("masks", [64, 512])
        identb = sb("identb", [128, 128], BF16)
        masksb = sb("masksb", [64, 512], BF16)
        onesb = sb("onesb", [64, 128], BF16)
        xst = [sb("xst%d" % i, [128, 1024]) for i in range(2)]
        wst = [sb("wst%d" % i, [128, 8, 128]) for i in range(2)]
        wbf = [sb("wbf%d" % i, [128, 8, 128], BF16) for i in range(2)]
        modsc = sb("modsc", [128, 16, 3])
        siluc = sb("siluc", [128, 8, 3])
        dbgs = sb("dbgs", [128, 512]) if dbg else None
        bcol = sb("bcol", [128, 24])
        small = sb("small", [128, 64])
        ARENA_WORDS = 28800
        arena = sb("arena", [128, ARENA_WORDS])
        PS = [psum("ps%d" % i) for i in range(8)]
        PK = ["ps%d" % i for i in range(8)]

        def dma(out, in_, r=(), w=()):
            P.op("sp", lambda e: e.dma_start(out=out, in_=in_), reads=r, writes=w, dma=True)

        def act(out, in_, func, r, w, bias=None, scale=None):
            kw = {}
            if bias is not None:
                kw["bias"] = bias
            if scale is not None:
                kw["scale"] = scale
            P.op("act", lambda e: e.activation(out=out, in_=in_, func=func, **kw), reads=r, writes=w)

        def tt(eng, out, in0, in1, op, r, w):
            P.op(eng, lambda e: e.tensor_tensor(out=out, in0=in0, in1=in1, op=op), reads=r, writes=w)

        def ts(eng, out, in0, s1, op0, r, w, s2=None, op1=None):
            if op1 is None:
                P.op(eng, lambda e: e.tensor_scalar(out=out, in0=in0, scalar1=s1, scalar2=None, op0=op0), reads=r, writes=w)
            else:
                P.op(eng, lambda e: e.tensor_scalar(out=out, in0=in0, scalar1=s1, scalar2=s2, op0=op0, op1=op1), reads=r, writes=w)

        def stt(eng, out, in0, scalar, in1, op0, op1, r, w):
            eng = "dve"
            P.op(eng, lambda e: e.scalar_tensor_tensor(out=out, in0=in0, scalar=scalar, in1=in1, op0=op0, op1=op1), reads=r, writes=w)

        def cp(eng, out, in_, r, w):
            if eng == "act":
                act(out, in_, AF.Copy, r, w)
            else:
                P.op(eng, lambda e: e.tensor_copy(out=out, in_=in_), reads=r, writes=w)

        def mm(out, lhsT, rhs, start, stop, r, w):
            P.op("pe", lambda e: e.matmul(out, lhsT=lhsT, rhs=rhs, start=start, stop=stop), reads=r, writes=w)

        def tr(out, in_, idn, r, w):
            P.op("pe", lambda e: e.transpose(out, in_, idn), reads=r, writes=w)

        def dump(name, ap, r):
            if name in dbg_t:
                if ap.dtype == BF16:
                    n = ap.shape[-1]
                    cp("dve", dbgs[:, 0:n], ap, r, ["dbgs"])
                    dma(dbg_t[name], dbgs[:, 0:n], r=["dbgs"])
                else:
                    dma(dbg_t[name], ap, r=r)

        def barrier():
            P.barrier("dve", lambda e: e.memset(small[:, 63:64], 0.0))

        dma(ident[:], D["ident"][:, :], w=["ident"])
        dma(ones[:], D["ones"][:, :], w=["ones"])
        dma(masks[:], D["masks"][:, :], w=["masks"])
        dma(bcol[:], D["bcol"][:, :], w=["bcol"])
        cp("dve", identb[:], ident[:], ["ident"], ["identb"])
        cp("dve", masksb[:], masks[:], ["masks"], ["masksb"])
        cp("dve", onesb[:], ones[0:64, :], ["ones"], ["onesb"])
        SUb = [masksb[:, 128:192], masksb[:, 192:256]]
        TRI = [masks[:, 0:64], masks[:, 64:128]]
        SU = [masks[:, 128:192], masks[:, 192:256]]
        MI = [masks[:, 256:320], masks[:, 320:384]]
        MS = [masks[:, 384:448], masks[:, 448:512]]

        dma(siluc[:], D["cT"][:, :, :], w=["siluc"])
        act(siluc[:], siluc[:], AF.Silu, ["siluc"], ["siluc"])
        wada = D["w_ada"].rearrange("(k p) c -> p k c", p=128)
        for t in range(16):
            b = t % 2
            dma(wst[b][:], wada[:, :, t * 128:(t + 1) * 128], w=["wst%d" % b])
            for k in range(8):
                mm(PS[0][:, t * 4:t * 4 + 3], wst[b][:, k, :], siluc[:, k, :], k == 0, k == 7, ["wst%d" % b, "siluc"], [PK[0]])
        for t in range(16):
            ts("dve", modsc[:, t, :], PS[0][:, t * 4:t * 4 + 3], bcol[:, t:t + 1], ALU.add, [PK[0], "bcol"], ["modsc"],
               s2=(1.0 if t >= 8 else 0.0), op1=ALU.add)

        win = D["w_in"].rearrange("(k p) c -> p k c", p=128)
        wcount = [0]

        def load_w(c0, ncols=128):
            b = wcount[0] % 2
            wcount[0] += 1
            dma(wst[b][:, :, 0:ncols], win[:, :, c0:c0 + ncols], w=["wst%d" % b])
            cp("pool", wbf[b][:, :, 0:ncols], wst[b][:, :, 0:ncols], ["wst%d" % b], ["wbf%d" % b])
            return b

        def proj_fm(b, ps_ap, tok0, ntok, pk):
            for k in range(8):
                mm(ps_ap, wbf[b][:, k, :], hT[:, k, tok0:tok0 + ntok], k == 0, k == 7, ["wbf%d" % b, "hT"], [pk])

        TOKBLK = [(0, 256), (256, 512), (768, 512), (1280, 512), (1792, 512)]

        for s in range(NSEQ):
            barrier()
            for tt_i in range(18):
                b = tt_i % 2
                if tt_i < 2:
                    src = D["ctx2"][s, tt_i * 128:(tt_i + 1) * 128, :]
                    jcol = 2
                else:
                    src = D["x2"][s, (tt_i - 2) * 128:(tt_i - 1) * 128, :]
                    jcol = s
                dma(xst[b][:], src, w=["xst%d" % b])
                for half in range(2):
                    pk = 1 + half
                    for kk in range(4):
                        k = half * 4 + kk
                        tr(PS[pk][:, kk * 128:(kk + 1) * 128], xst[b][:, k * 128:(k + 1) * 128], ident[:], ["xst%d" % b, "ident"], [PK[pk]])
                    for kk in range(4):
                        k = half * 4 + kk
                        act(hT[:, k, tt_i * 128:(tt_i + 1) * 128], PS[pk][:, kk * 128:(kk + 1) * 128], AF.Identity,
                            [PK[pk], "modsc"], ["hT"], bias=modsc[:, k, jcol:jcol + 1], scale=modsc[:, 8 + k, jcol:jcol + 1])
            if s == 0:
                dump("hT", hT[:, 0, 0:512], ["hT"])

            if "s5" in stages:
                barrier()
                o = [0]

                def carve(n, dt=F32):
                    words = n if dt == F32 else (n + 1) // 2
                    a = arena[:, o[0]:o[0] + words]
                    o[0] += words
                    assert o[0] <= ARENA_WORDS, o[0]
                    return a.bitcast(BF16)[:, 0:n] if dt == BF16 else a
                Bpb = carve(2048, BF16)
                Cpb = carve(2048, BF16)
                uT = carve(4 * LTOT, BF16).rearrange("p (j t) -> p j t", j=4)
                o_hb = o[0]
                HbD = [[carve(LTOT) for _ in range(2)] for _ in range(2)]
                HbB = [[carve(LLAT, BF16) for _ in range(2)] for _ in range(2)]
                XPAD = 256
                XaF = [[carve(288 + XPAD) for _ in range(2)] for _ in range(2)]
                XbF = [[carve(288 + XPAD) for _ in range(2)] for _ in range(2)]
                for d_ in range(2):
                    for r_ in range(2):
                        for buf_, nm_ in ((XaF, "Xa%d" % d_), (XbF, "Xb%d" % d_)):
                            P.op("pool", (lambda t_=buf_[d_][r_]: lambda e: e.memset(t_[:, :], 0.0))(), writes=[nm_])
                XOFF = [XPAD, 0]
                XaD = [[XaF[d_][r_][:, XOFF[d_]:XOFF[d_] + 288] for r_ in range(2)] for d_ in range(2)]
                XbD = [[XbF[d_][r_][:, XOFF[d_]:XOFF[d_] + 288] for r_ in range(2)] for d_ in range(2)]
                prm = carve(32 * 40).rearrange("p (q c) -> p q c", c=32)
                prm2 = carve(32 * 28).rearrange("p (q c) -> p q c", c=32)
                ytmp = carve(512)
                LR, LI, DT, ER, TH, AR, AI, CR, CI, T0, T1, T2 = range(12)
                PW = 13
                NP = PW + 16
                P.op("dve", lambda e: e.memset(prm[:, :, :], 0.0), writes=["prm"])
                for q_, nm in ((LR, "lam_re"), (LI, "lam_im"), (DT, "logdt")):
                    dma(prm[:, q_, :], D[nm][:, :], w=["prm"])

                def ptt(dst, a_, b_, op):
                    tt("dve", prm[:, dst, :], prm[:, a_, :], prm[:, b_, :], op, ["prm"], ["prm"])

                def pts(dst, a_, s1, op0, s2=None, op1=None, eng="dve"):
                    ts(eng, prm[:, dst, :], prm[:, a_, :], s1, op0, ["prm"], ["prm"], s2=s2, op1=op1)
                act(prm[:, DT, :], prm[:, DT, :], AF.Exp, ["prm"], ["prm"])
                ptt(ER, LR, DT, ALU.mult)
                act(prm[:, ER, :], prm[:, ER, :], AF.Exp, ["prm"], ["prm"])
                ptt(TH, LI, DT, ALU.mult)
                I32 = mybir.dt.int32
                kint = prm[:, 12, :].bitcast(I32)
                PI_C = 3.14159
                for (dst, shift_) in ((AI, 0.0), (AR, 0.5 * math.pi)):
                    pts(T0, TH, 1.0 / (2 * math.pi), ALU.mult, s2=shift_ / (2 * math.pi), op1=ALU.add)
                    P.op("dve", lambda e: e.tensor_copy(out=kint, in_=prm[:, T0, :]), reads=["prm"], writes=["prm"])
                    P.op("dve", lambda e: e.tensor_copy(out=prm[:, T1, :], in_=kint), reads=["prm"], writes=["prm"])
                    pts(T2, TH, shift_, ALU.add)
                    stt("dve", prm[:, T0, :], prm[:, T1, :], -2 * math.pi, prm[:, T2, :], ALU.mult, ALU.add, ["prm"], ["prm"])
                    pts(T0, T0, -PI_C, ALU.max, s2=PI_C, op1=ALU.min)
                    act(prm[:, dst, :], prm[:, T0, :], AF.Sin, ["prm"], ["prm"])
                ptt(AR, AR, ER, ALU.mult)
                ptt(AI, AI, ER, ALU.mult)
                pts(T0, AR, -1.0, ALU.add)
                ptt(T1, T0, LR, ALU.mult)
                ptt(T2, AI, LI, ALU.mult)
                ptt(CR, T1, T2, ALU.add)
                ptt(T1, AI, LR, ALU.mult)
                ptt(T2, T0, LI, ALU.mult)
                ptt(CI, T1, T2, ALU.subtract)
                ptt(T1, LR, LR, ALU.mult)
                ptt(T2, LI, LI, ALU.mult)
                ptt(T1, T1, T2, ALU.add)
                P.op("dve", lambda e: e.reciprocal(out=prm[:, T1, :], in_=prm[:, T1, :]), reads=["prm"], writes=["prm"])
                ptt(CR, CR, T1, ALU.mult)
                ptt(CI, CI, T1, ALU.mult)

                def cmul(dst_r, dst_i, ar_, ai_, br_, bi_):
                    ptt(T0, ar_, br_, ALU.mult)
                    ptt(T1, ai_, bi_, ALU.mult)
                    ptt(T2, ar_, bi_, ALU.mult)
                    ptt(dst_r, T0, T1, ALU.subtract)
                    ptt(T0, ai_, br_, ALU.mult)
                    ptt(dst_i, T2, T0, ALU.add)
                pts(PW, AR, 1.0, ALU.mult)
                pts(PW + 1, AI, 1.0, ALU.mult)
                for k in range(2, 9):
                    cmul(PW + 2 * (k - 1), PW + 2 * (k - 1) + 1, PW + 2 * (k - 2), PW + 2 * (k - 2) + 1, AR, AI)
                for k in range(1, 9):
                    pts(NP + k - 1, PW + 2 * (k - 1) + 1, -1.0, ALU.mult)
                ts("dve", prm2[:, 0, :], prm[:, PW + 14, :], 1.0, ALU.mult, ["prm"], ["prm2"])
                ts("dve", prm2[:, 1, :], prm[:, PW + 15, :], 1.0, ALU.mult, ["prm"], ["prm2"])
                for m in range(9):
                    if m > 0:
                        r0, i0_ = prm2[:, 3 * (m - 1), :], prm2[:, 3 * (m - 1) + 1, :]
                        tt("dve", prm[:, T0, :], r0, r0, ALU.mult, ["prm2", "prm"], ["prm"])
                        tt("dve", prm[:, T1, :], i0_, i0_, ALU.mult, ["prm2", "prm"], ["prm"])
                        tt("dve", prm2[:, 3 * m, :], prm[:, T0, :], prm[:, T1, :], ALU.subtract, ["prm"], ["prm2"])
                        tt("dve", prm[:, T0, :], r0, i0_, ALU.mult, ["prm2", "prm"], ["prm"])
                        ts("dve", prm2[:, 3 * m + 1, :], prm[:, T0, :], 2.0, ALU.mult, ["prm"], ["prm2"])
                    ts("dve", prm2[:, 3 * m + 2, :], prm2[:, 3 * m + 1, :], -1.0, ALU.mult, ["prm2"], ["prm2"])
                if s == 0:
                    dump("prm", prm[:, 0:40, :], ["prm"])
                for j in range(4):
                    wb = load_w(j * 128)
                    for bi, (t0, nt) in enumerate(TOKBLK):
                        pk = 1 + bi % 2
                        proj_fm(wb, PS[pk][:, 0:nt], t0, nt, PK[pk])
                        cp("act", uT[:, j, t0:t0 + nt], PS[pk][:, 0:nt], [PK[pk]], ["uT"])
                if s == 0:
                    dump("uT", uT[:, 0, 0:512], ["uT"])
                dcol = small[:, 0:4]
                dma(dcol, D["dcol"][:, :], w=["dcol"])
                gT = mixT[:, 0:4, :]
                Bpv = Bpb.rearrange("p (g d r c) -> p g d r c", g=4, d=2, r=2)
                Cpv = Cpb.rearrange("p (g d r c) -> p g d r c", g=4, d=2, r=2)
                for j in range(4):
                    for pc in range(2):
                        dma(xst[pc][:], D["Bp"][:, j * 2048 + pc * 1024:j * 2048 + (pc + 1) * 1024], w=["xst%d" % pc])
                        cp("pool", Bpb[:, pc * 1024:(pc + 1) * 1024], xst[pc][:], ["xst%d" % pc], ["Bpb"])
                    for gp in range(4):
                        b = gp % 2
                        pcg = j * 4 + gp
                        dma(xst[b][:, 0:512], D["Cp"][:, pcg * 512:(pcg + 1) * 512], w=["xst%d" % b])
                        for d in range(2):
                            col = pcg * 2 + d
                            cre_ = xst[b][:, d * 256:d * 256 + 128]
                            cim_ = xst[b][:, d * 256 + 128:d * 256 + 256]
                            t_ = xst[b][:, 512 + d * 128:512 + (d + 1) * 128]
                            base = gp * 512 + d * 256
                            xk = "xst%d" % b
                            ts("dve", t_, cim_, prm[:, CI, col:col + 1], ALU.mult, [xk, "prm"], [xk])
                            stt("dve", Cpb[:, base:base + 128], cre_, prm[:, CR, col:col + 1], t_, ALU.mult, ALU.subtract, [xk, "prm"], ["Cpb"])
                            ts("dve", t_, cre_, prm[:, CI, col:col + 1], ALU.mult, [xk, "prm"], [xk])
                            stt("dve", t_, cim_, prm[:, CR, col:col + 1], t_, ALU.mult, ALU.add, [xk, "prm"], [xk])
                            ts("dve", Cpb[:, base + 128:base + 256], t_, -1.0, ALU.mult, [xk], ["Cpb"])
                    YP = [PS[4 + q] for q in range(4)]
                    YK = [PK[4 + q] for q in range(4)]
                    rcount = [0, 0, 0, 0]

                    def s5_body(gp, d):
                        pair = j * 4 + gp
                        col = pair * 2 + d
                        E = "dve"
                        Hb = HbD[d]
                        hk = "Hb%d" % d
                        Xa, Xb = XaD[d], XbD[d]
                        for ri in range(2):
                            for bi, (t0, nt) in enumerate(TOKBLK):
                                pk = 1 + (bi + ri) % 3
                                mm(PS[pk][:, 0:nt], Bpv[:, gp, d, ri, :], uT[:, j, t0:t0 + nt], True, True, ["Bpb", "uT"], [PK[pk]])
                                cp("act", Hb[ri].rearrange("p (s c) -> p c s", s=8)[:, t0 // 8:(t0 + nt) // 8, :],
                                   PS[pk][:, 0:nt].rearrange("p (c s) -> p c s", s=8), [PK[pk]], [hk])
                        yield "hold"
                        Hr = Hb[0].rearrange("p (s c) -> p c s", s=8)
                        Hi = Hb[1].rearrange("p (s c) -> p c s", s=8)
                        a_r = prm[:, PW, col:col + 1]
                        a_i = prm[:, PW + 1, col:col + 1]
                        a_ni = prm[:, NP, col:col + 1]
                        order = range(1, 8) if d == 0 else range(6, -1, -1)
                        for s_ in order:
                            sp_ = s_ - 1 if d == 0 else s_ + 1
                            stt(E, Hr[:, :, s_], Hr[:, :, sp_], a_r, Hr[:, :, s_], ALU.mult, ALU.add, [hk, "prm"], [hk])
                            yield
                            stt(E, Hi[:, :, s_], Hr[:, :, sp_], a_i, Hi[:, :, s_], ALU.mult, ALU.add, [hk, "prm"], [hk])
                            yield
                            stt(E, Hr[:, :, s_], Hi[:, :, sp_], a_ni, Hr[:, :, s_], ALU.mult, ALU.add, [hk, "prm"], [hk])
                            yield
                            stt(E, Hi[:, :, s_], Hi[:, :, sp_], a_r, Hi[:, :, s_], ALU.mult, ALU.add, [hk, "prm"], [hk])
                            yield
                        se = 7 if d == 0 else 0
                        xak, xbk = "Xa%d" % d, "Xb%d" % d
                        for ri, Hv in enumerate((Hr, Hi)):
                            if d == 0:
                                cp(E, Xa[ri][:, 0:288], Hv[:, :, se], [hk], [xak])
                                yield
                            else:
                                cp(E, Xa[ri][:, 0:256], Hv[:, 32:288, se], [hk], [xak])
                                yield
                                cp(E, Xa[ri][:, 256:288], Hv[:, 0:32, se], [hk], [xak])
                                yield
                        cur, nxt, ck, nk = Xa, Xb, xak, xbk
                        curF, nxtF = XaF[d], XbF[d]
                        for m in range(9):
                            sh_ = 1 << m
                            A_r = prm2[:, 3 * m, col:col + 1]
                            A_i = prm2[:, 3 * m + 1, col:col + 1]
                            A_ni = prm2[:, 3 * m + 2, col:col + 1]
                            off_ = XPAD - sh_ if d == 0 else sh_
                            dst = slice(0, 288)
                            s0_ = curF[0][:, off_:off_ + 288]
                            s1_ = curF[1][:, off_:off_ + 288]
                            stt(E, nxt[0][:, dst], s0_, A_r, cur[0][:, dst], ALU.mult, ALU.add, [ck, "prm2"], [nk])
                            yield
                            stt(E, nxt[0][:, dst], s1_, A_ni, nxt[0][:, dst], ALU.mult, ALU.add, [ck, nk, "prm2"], [nk])
                            yield
                            stt(E, nxt[1][:, dst], s0_, A_i, cur[1][:, dst], ALU.mult, ALU.add, [ck, "prm2"], [nk])
                            yield
                            stt(E, nxt[1][:, dst], s1_, A_r, nxt[1][:, dst], ALU.mult, ALU.add, [ck, nk, "prm2"], [nk])
                            yield
                            cur, nxt, ck, nk = nxt, cur, nk, ck
                            curF, nxtF = nxtF, curF
                        HBr = HbB[d][0].rearrange("p (s c) -> p c s", s=8)
                        HBi = HbB[d][1].rearrange("p (s c) -> p c s", s=8)
                        bk = "HbB%d" % d
                        for s_ in range(8):
                            kpow = s_ + 1 if d == 0 else 8 - s_
                            p_r = prm[:, PW + 2 * (kpow - 1), col:col + 1]
                            p_i = prm[:, PW + 2 * (kpow - 1) + 1, col:col + 1]
                            p_ni = prm[:, NP + kpow - 1, col:col + 1]
                            if d == 0:
                                Hs_r, Hs_i = cur[0][:, 31:287], cur[1][:, 31:287]
                            else:
                                Hs_r, Hs_i = cur[0][:, 1:257], cur[1][:, 1:257]
                            stt(E, Hr[:, 32:288, s_], Hs_r, p_r, Hr[:, 32:288, s_], ALU.mult, ALU.add, [hk, ck, "prm"], [hk])
                            yield
                            stt(E, HBr[:, :, s_], Hs_i, p_ni, Hr[:, 32:288, s_], ALU.mult, ALU.add, [hk, ck, "prm"], [bk])
                            yield
                            stt(E, Hi[:, 32:288, s_], Hs_r, p_i, Hi[:, 32:288, s_], ALU.mult, ALU.add, [hk, ck, "prm"], [hk])
                            yield
                            stt(E, HBi[:, :, s_], Hs_i, p_r, Hi[:, 32:288, s_], ALU.mult, ALU.add, [hk, ck, "prm"], [bk])
                            yield
                        for q in range(4):
                            for ri in range(2):
                                mm(YP[q][:, :], Cpv[:, gp, d, ri, :], HbB[d][ri].rearrange("p (s c) -> p c s", s=8)[:, q * 64:(q + 1) * 64, :],
                                   rcount[q] == 0, rcount[q] == 15, ["Cpb", bk], [YK[q]])
                                rcount[q] += 1

                    def s5_stream(d):
                        for gp in range(4):
                            yield from s5_body(gp, d)
                    g0, g1 = s5_stream(0), s5_stream(1)
                    for _ in range(36):
                        next(g0)
                    alive = [g0, g1]
                    hold = {id(g0): 0, id(g1): 0}
                    while alive:
                        progressed = False
                        for g_ in list(alive):
                            if hold[id(g_)] > 0 and len(alive) > 1:
                                hold[id(g_)] -= 1
                                continue
                            try:
                                r_ = next(g_)
                                progressed = True
                                if r_ == "hold":
                                    hold[id(g_)] = 28
                            except StopIteration:
                                alive.remove(g_)
                        if not progressed:
                            for k_ in hold:
                                hold[k_] = 0
                    for q in range(4):
                        stt("dve", ytmp, uT[:, j, LCTX + q * 512:LCTX + (q + 1) * 512], dcol[:, j:j + 1], YP[q][:, :], ALU.mult, ALU.add,
                            ["uT", "dcol", YK[q]], ["ytmp"])
                        if s == 0 and j == 0 and q == 0:
                            dump("y0", ytmp, ["ytmp"])
                        act(gT[:, j, q * 512:(q + 1) * 512], ytmp, AF.Gelu, ["ytmp"], ["gT"])
                barrier()
                o[0] = o_hb
                zt = carve(512)
                sig = carve(4 * 512).rearrange("p (j c) -> p j c", j=4)
                wgl = carve(2048, BF16).rearrange("p (j c) -> p j c", j=4)
                bgl = small[:, 4:8]
                dma(bgl, D["bglu"][:, :], w=["bglu"])
                for j in range(4):
                    dma(xst[j % 2][:, 0:512], D["wglu"][:, j, :], w=["xst%d" % (j % 2)])
                    cp("pool", wgl[:, j, :], xst[j % 2][:, 0:512], ["xst%d" % (j % 2)], ["wgl"])
                for q in range(4):
                    for jo in range(4):
                        for ji in range(4):
                            mm(PS[4 + jo][:, :], wgl[:, ji, jo * 128:(jo + 1) * 128], gT[:, ji, q * 512:(q + 1) * 512], ji == 0, ji == 3,
                               ["wgl", "gT"], [PK[4 + jo]])
                        act(sig[:, jo, :], PS[4 + jo][:, :], AF.Sigmoid, [PK[4 + jo], "bglu"], ["sig%d" % jo], bias=bgl[:, jo:jo + 1])
                    for jo in range(4):
                        wb = load_w(512 + jo * 128)
                        proj_fm(wb, PS[1 + jo % 2][:, :], LCTX + q * 512, 512, PK[1 + jo % 2])
                        act(zt, PS[1 + jo % 2][:, :], AF.Silu, [PK[1 + jo % 2]], ["zt"])
                        tt("dve", sig[:, jo, :], sig[:, jo, :], zt, ALU.mult, ["sig%d" % jo, "zt"], ["sig%d" % jo])
                    for jo in range(4):
                        tt("pool", gT[:, jo, q * 512:(q + 1) * 512], gT[:, jo, q * 512:(q + 1) * 512], sig[:, jo, :], ALU.mult,
                           ["gT", "sig%d" % jo], ["gT"])
                if s == 0:
                    dump("mix_s5", mixT[:, 0, 0:512], ["gT"])
            if "gdn" in stages:
                barrier()
                o = [0]

                def carve(n, dt=F32):
                    words = n if dt == F32 else (n + 1) // 2
                    a = arena[:, o[0]:o[0] + words]
                    o[0] += words
                    assert o[0] <= ARENA_WORDS, o[0]
                    return a.bitcast(BF16)[:, 0:n] if dt == BF16 else a

                def c3(n_, a_, b_):
                    return carve(n_ * a_ * b_).rearrange("p (n a b) -> p n a b", n=n_, a=a_)
                wgb = carve(128, BF16).rearrange("p (k c) -> p k c", k=8)
                gpar = carve(32)
                normw = carve(128)
                cwt = carve(108).rearrange("p (t k) -> p t k", t=12)
                G_ = {nm: carve(288).rearrange("p (c e) -> p c e", e=8) for nm in ("beta", "g", "gcum", "koutc")}
                G_["gam"] = G_["gcum"]
                XTf = carve(LTOT)
                XT = [carve(LTOT, BF16) for _ in range(3)]
                lpad = carve(34 * 66, BF16).rearrange("p (r c) -> p r c", c=66)
                cpad = carve(258, BF16)
                dgt = carve(9 * 128, BF16).rearrange("p (k c) -> p k c", k=9)
                tmp5 = carve(512)
                tmp5b = carve(512)
                rsb = carve(32)
                Oacc = carve(32 * 128).rearrange("p (c v) -> p c v", v=128)
                Sst = [carve(128) for _ in range(2)]
                NB = 4
                NI = 2 * NB
                mk = lambda w_, dt=F32: carve(NI * w_, dt).rearrange("p (n c) -> p n c", n=NI)
                sh_ = {"Lg": mk(64), "dec": mk(64), "Pm": mk(64), "Lgh": mk(64, BF16), "Lgl": mk(64, BF16),
                       "Qa": mk(64, BF16), "QTa": mk(64, BF16), "Qb": mk(64, BF16), "QTb": mk(64, BF16), "Pmb": mk(64, BF16),
                       "V": mk(128, BF16), "Kg": mk(128, BF16)}
                Av = [dict(sh_, **{"wT": mk(64), "QgT": mk(64), "gbc": mk(64), "QKT": mk(64, BF16), "K": mk(128, BF16), "u0": mk(128)})
                      for _ in range(2)]
                A64v = Av
                A128v = Av
                ubuf = [carve(128, BF16) for _ in range(4)]
                dma(xst[0][:, 0:128], D["wgate"].rearrange("p k c -> p (k c)"), w=["xst0"])
                cp("dve", wgb.rearrange("p k c -> p (k c)"), xst[0][:, 0:128], ["xst0"], ["wgb"])
                dma(gpar[0:64, 0:8], D["alog"][:, :], w=["gpar"])
                dma(gpar[0:64, 8:16], D["dtb"][:, :], w=["gpar"])
                dma(normw[0:64, :], D["normw"][:, :], w=["normw"])
                dma(cwt[:], D["convw"][:, :, :], w=["cwt"])
                act(gpar[0:64, 0:8], gpar[0:64, 0:8], AF.Exp, ["gpar"], ["gpar"])
                ts("dve", gpar[0:64, 0:8], gpar[0:64, 0:8], -1.0, ALU.mult, ["gpar"], ["gpar"])
                P.op("dve", lambda e: e.memset(lpad[:, :, :], 0.0), writes=["lpad"])
                P.op("dve", lambda e: e.memset(cpad[:, :], 0.0), writes=["cpad"])
                for c_ in range(36):
                    bank, cc = (0, c_) if c_ < 32 else (3, c_ - 32)
                    for k in range(8):
                        mm(PS[bank][0:64, cc * 16:(cc + 1) * 16], hT[:, k, c_ * 64:(c_ + 1) * 64], wgb[:, k, :], k == 0, k == 7,
                           ["hT", "wgb"], [PK[bank]])
                for (bank, c0, nch) in ((0, 0, 32), (3, 32, 4)):
                    pv = PS[bank][0:64, 0:nch * 16].rearrange("p (c e) -> p c e", e=16)
                    act(G_["beta"][0:64, c0:c0 + nch, :], pv[:, :, 0:8], AF.Sigmoid, [PK[bank]], ["gates"])
                    tt("dve", G_["g"][0:64, c0:c0 + nch, :], pv[:, :, 8:16],
                       gpar[0:64, 8:16].unsqueeze(1).to_broadcast([64, nch, 8]), ALU.add, [PK[bank], "gpar"], ["gates"])
                gall = lambda nm: G_[nm][0:64, :, :]
                act(gall("g"), gall("g"), AF.Exp, ["gates"], ["gates"])
                act(gall("g"), gall("g"), AF.Ln, ["gates"], ["gates"], bias=1.0)
                tt("dve", gall("g"), gall("g"), gpar[0:64, 0:8].unsqueeze(1).to_broadcast([64, 36, 8]), ALU.mult, ["gates", "gpar"], ["gates"])
                gflat = G_["g"][0:64, :, :].rearrange("p c e -> p (c e)")
                for d in range(2):
                    for hh in range(2):
                        pass
                    mm(PS[1][0:64, 0:288], TRI[d], gflat, True, True, ["masks", "gates"], [PK[1]])
                    pv = PS[1][0:64, 0:288].rearrange("p (c e) -> p c e", e=8)
                    cp("dve", G_["gcum"][0:64, :, d * 4:(d + 1) * 4], pv[:, :, d * 4:(d + 1) * 4], [PK[1]], ["gates"])
                mm(PS[2][0:64, 0:288], ones[0:64, 0:64], gflat, True, True, ["ones", "gates"], [PK[2]])
                tt("dve", gall("koutc"), PS[2][0:64, 0:288].rearrange("p (c e) -> p c e", e=8), gall("gcum"), ALU.subtract, [PK[2], "gates"], ["gates"])
                act(gall("koutc"), gall("koutc"), AF.Exp, ["gates"], ["gates"])
                if s == 0:
                    dump("g", G_["g"][0:64, :, :].rearrange("p c e -> p (c e)"), ["gates"])
                    dump("gcum", G_["gcum"][0:64, :, :].rearrange("p c e -> p (c e)"), ["gates"])
                act(gall("gam"), gall("gcum"), AF.Exp, ["gates"], ["gates"])
                FORD = list(range(36))
                BORD = [3, 2, 1, 0] + list(range(35, 3, -1))
                import os as _os2
                for h in range(int(_os2.environ.get('GDN_NHEADS', 4))):
                    for t in range(3):
                        tile_i = t * 4 + h
                        wb = load_w(1024 + t * 512 + h * 128)
                        proj_fm(wb, PS[1][:, 0:256], 0, 256, PK[1])
                        cp("act", cpad[:, 1:257], PS[1][:, 0:256], [PK[1]], ["cpad"])
                        for q in range(4):
                            pk = 2 + q % 2
                            proj_fm(wb, PS[pk][:, :], 256 + q * 512, 512, PK[pk])
                            cp("act", lpad[:, 1 + 8 * q:9 + 8 * q, 1:65], PS[pk][:, :].rearrange("p (r c) -> p r c", c=64), [PK[pk]], ["lpad"])
                        xk = "XT%d" % t
                        for tap in range(9):
                            ts("dve", dgt[:, tap, :], identb[:, :], cwt[:, tile_i, tap:tap + 1], ALU.mult, ["identb", "cwt"], ["dgt"])
                        for kx in range(3):
                            mm(PS[4][:, 0:256], dgt[:, 3 + kx, :], cpad[:, kx:kx + 256], kx == 0, kx == 2, ["dgt", "cpad"], [PK[4]])
                        if t < 2:
                            act(XTf[:, 0:256], PS[4][:, 0:256], AF.Silu, [PK[4]], ["XTf"])
                        else:
                            act(XT[t][:, 0:256], PS[4][:, 0:256], AF.Silu, [PK[4]], [xk])
                        for q in range(4):
                            pk = 4 + (q + 1) % 2
                            for tap in range(9):
                                ky, kx = tap // 3, tap % 3
                                mm(PS[pk][:, :], dgt[:, tap, :], lpad[:, ky + 8 * q:ky + 8 * q + 8, kx:kx + 64], tap == 0, tap == 8,
                                   ["dgt", "lpad"], [PK[pk]])
                            if t < 2:
                                act(XTf[:, 256 + q * 512:256 + (q + 1) * 512], PS[pk][:, :], AF.Silu, [PK[pk]], ["XTf"])
                            else:
                                act(XT[t][:, 256 + q * 512:256 + (q + 1) * 512], PS[pk][:, :], AF.Silu, [PK[pk]], [xk])
                        if t < 2:
                            for bi, (t0, nt) in enumerate(TOKBLK):
                                pk = 1 + bi % 3
                                tq = tmp5 if bi % 2 == 0 else tmp5b
                                tqk = "tmp5" if bi % 2 == 0 else "tmp5b"
                                tt("pool", tq[:, 0:nt], XTf[:, t0:t0 + nt], XTf[:, t0:t0 + nt], ALU.mult, ["XTf"], [tqk])
                                mm(PS[pk][:, 0:nt], ones[:, :], tq[:, 0:nt], True, True, ["ones", tqk], [PK[pk]])
                                act(tq[:, 0:nt], PS[pk][:, 0:nt], AF.Ln, [PK[pk]], [tqk], bias=1e-6)
                                act(tq[:, 0:nt], tq[:, 0:nt], AF.Exp, [tqk], [tqk], scale=-0.5)
                                stt("dve", XT[t][:, t0:t0 + nt], XTf[:, t0:t0 + nt], (128.0 ** -0.5) if t == 0 else 1.0, tq[:, 0:nt],
                                    ALU.mult, ALU.mult, ["XTf", tqk], [xk])
                    qT, kT, vT = XT
                    if s == 0 and h == 0:
                        dump("qn", qT[:, 0:512], ["XT0"])
                        dump("kn", kT[:, 256:768], ["XT1"])
                        dump("vv", vT[:, 256:768], ["XT2"])
                    for d in range(2):
                        P.op("dve", (lambda d=d: lambda e: e.memset(Sst[d][:, :], 0.0))(), writes=["S%d" % d])

                    def batch_info(b):
                        i0_ = b * NB
                        return [(FORD[i0_], NB), (min(BORD[i0_:i0_ + NB]), NB)]

                    def stageA_batch(b):
                        info = batch_info(b)
                        par = b % 2
                        A64, A128 = A64v[par], A128v[par]
                        KY = lambda s_: s_ + str(par)
                        items = []
                        for d in range(2):
                            cmin, nch = info[d]
                            for q_ in range(nch):
                                items.append((d, cmin + q_, d * NB + q_))
                        e8 = lambda d: d * 4 + h
                        gcols = lambda nm, d: G_[nm][0:64, info[d][0]:info[d][0] + NB, e8(d)]
                        v = lambda nm, d: A64[nm][0:64, d * NB:(d + 1) * NB, :]
                        v2 = lambda nm, d: A128[nm][0:64, d * NB:(d + 1) * NB, :]
                        ps3 = lambda bank, d, w_: PS[bank][0:64, :].rearrange("p (n c) -> p n c", c=w_)[:, (d * NB if w_ == 64 else 0):(d * NB if w_ == 64 else 0) + NB, :]
                        for d in range(2):
                            tt("dve", v("Lg", d), TRI[d].unsqueeze(1).to_broadcast([64, NB, 64]),
                               gcols("g", d).unsqueeze(2).to_broadcast([64, NB, 64]), ALU.mult, ["masks", "gates"], ["aLg"])
                        cp("act", A64["Lgh"][0:64, :, :], A64["Lg"][0:64, :, :], ["aLg"], ["aLgh"])
                        tt("dve", A64["Lgl"][0:64, :, :], A64["Lg"][0:64, :, :], A64["Lgh"][0:64, :, :], ALU.subtract, ["aLg", "aLgh"], ["aLgl"])
                        yield
                        for (d, c_, it) in items:
                            mm(PS[2][0:64, it * 64:(it + 1) * 64], SUb[d], A64["Lgh"][0:64, it, :], True, False, ["masksb", "aLgh"], [PK[2]])
                            mm(PS[2][0:64, it * 64:(it + 1) * 64], SUb[d], A64["Lgl"][0:64, it, :], False, True, ["masksb", "aLgl"], [PK[2]])
                            mm(PS[3][:, it * 64:(it + 1) * 64], onesb[:, :], A64["Lgh"][0:64, it, :], True, False, ["onesb", "aLgh"], [PK[3]])
                            mm(PS[3][:, it * 64:(it + 1) * 64], onesb[:, :], A64["Lgl"][0:64, it, :], False, True, ["onesb", "aLgl"], [PK[3]])
                        act(A64["dec"][0:64, :, :], PS[2][0:64, :].rearrange("p (n c) -> p n c", c=64), AF.Exp, [PK[2]], ["adec"])
                        act(A64["gbc"][:, :, :], PS[3][:, :].rearrange("p (n c) -> p n c", c=64), AF.Exp, [PK[3]], [KY("agbc")])
                        yield
                        for (d, c_, it) in items:
                            tok = slice(c_ * 64, (c_ + 1) * 64)
                            mm(PS[0][0:64, it * 64:(it + 1) * 64], kT[:, tok], kT[:, tok], True, True, ["XT1"], [PK[0]])
                            mm(PS[1][0:64, it * 64:(it + 1) * 64], kT[:, tok], qT[:, tok], True, True, ["XT1", "XT0"], [PK[1]])
                        for d in range(2):
                            tt("dve", v("Lg", d), v("dec", d), MI[d].unsqueeze(1).to_broadcast([64, NB, 64]), ALU.mult, ["adec", "masks"], ["aLg"])
                            tt("dve", v("dec", d), v("dec", d), MS[d].unsqueeze(1).to_broadcast([64, NB, 64]), ALU.mult, ["adec", "masks"], ["adec"])
                        yield
                        for d in range(2):
                            tt("dve", v("Pm", d), PS[0][0:64, :].rearrange("p (n c) -> p n c", c=64)[:, d * NB:(d + 1) * NB, :],
                               gcols("beta", d).unsqueeze(2).to_broadcast([64, NB, 64]), ALU.mult, [PK[0], "gates"], ["aPm"])
                        tt("dve", A64["dec"][0:64, :, :], A64["Pm"][0:64, :, :], A64["dec"][0:64, :, :], ALU.mult, ["aPm", "adec"], ["adec"])
                        tt("dve", A64["QKT"][0:64, :, :], PS[1][0:64, :].rearrange("p (n c) -> p n c", c=64), A64["Lg"][0:64, :, :], ALU.mult,
                           [PK[1], "aLg"], [KY("aQKT")])
                        tt("dve", A64["Pm"][0:64, :, :], ident[0:64, 0:64].unsqueeze(1).to_broadcast([64, NI, 64]), A64["dec"][0:64, :, :],
                           ALU.subtract, ["ident", "adec"], ["aPm"])
                        cp("dve", A64["Pmb"][0:64, :, :], A64["Pm"][0:64, :, :], ["aPm"], ["aPmb"])
                        cp("act", A64["Qa"][0:64, :, :], A64["dec"][0:64, :, :], ["adec"], ["aQa"])
                        yield
                        for (d, c_, it) in items:
                            mm(PS[4][0:64, it * 64:(it + 1) * 64], A64["Qa"][0:64, it, :], identb[0:64, 0:64], True, True, ["aQa", "identb"], [PK[4]])
                        cp("act", A64["QTa"][0:64, :, :], PS[4][0:64, :].rearrange("p (n c) -> p n c", c=64), [PK[4]], ["aQTa"])
                        yield
                        for (d, c_, it) in items:
                            tok = slice(c_ * 64, (c_ + 1) * 64)
                            mm(PS[2 + d][0:64, (it % NB) * 128:(it % NB + 1) * 128], kT[:, tok], identb[:, :], True, True, ["XT1", "identb"], [PK[2 + d]])
                            mm(PS[4 + d][0:64, (it % NB) * 128:(it % NB + 1) * 128], vT[:, tok], identb[:, :], True, True, ["XT2", "identb"], [PK[4 + d]])
                        for d in range(2):
                            cp("act", v2("K", d), PS[2 + d][0:64, :].rearrange("p (n c) -> p n c", c=128), [PK[2 + d]], [KY("aK")])
                            cp("act", v2("V", d), PS[4 + d][0:64, :].rearrange("p (n c) -> p n c", c=128), [PK[4 + d]], ["aV"])
                        for d in range(2):
                            tt("pool", v2("Kg", d), v2("K", d), gcols("gam", d).unsqueeze(2).to_broadcast([64, NB, 128]), ALU.mult, [KY("aK"), "gates"], ["aKg"])
                            tt("pool", v2("K", d), v2("K", d), gcols("koutc", d).unsqueeze(2).to_broadcast([64, NB, 128]), ALU.mult, [KY("aK"), "gates"], [KY("aK")])
                            cmin = info[d][0]
                            tt("pool", A64["QgT"][:, d * NB:(d + 1) * NB, :], qT[:, cmin * 64:(cmin + NB) * 64].rearrange("p (n c) -> p n c", c=64),
                               A64["gbc"][:, d * NB:(d + 1) * NB, :], ALU.mult, ["XT0", KY("agbc")], [KY("aQgT")])
                        yield
                        Q, QT, Qn, QTn = "Qa", "QTa", "Qb", "QTb"
                        kq = {"Qa": "aQa", "QTa": "aQTa", "Qb": "aQb", "QTb": "aQTb"}
                        for lvl in range(5):
                            for (d, c_, it) in items:
                                mm(PS[0][0:64, it * 64:(it + 1) * 64], A64[QT][0:64, it, :], A64[Q][0:64, it, :], True, True, [kq[Q], kq[QT]], [PK[0]])
                                mm(PS[1][0:64, it * 64:(it + 1) * 64], A64[Q][0:64, it, :], A64[QT][0:64, it, :], True, True, [kq[Q], kq[QT]], [PK[1]])
                            yield
                            cp("act", A64[Qn][0:64, :, :], PS[0][0:64, :].rearrange("p (n c) -> p n c", c=64), [PK[0]], [kq[Qn]])
                            cp("dve", A64[QTn][0:64, :, :], PS[1][0:64, :].rearrange("p (n c) -> p n c", c=64), [PK[1]], [kq[QTn]])
                            for (d, c_, it) in items:
                                mm(PS[5][0:64, it * 64:(it + 1) * 64], A64[QTn][0:64, it, :], A64["Pmb"][0:64, it, :], True, True, [kq[QTn], "aPmb"], [PK[5]])
                            tt("dve", A64["Pm"][0:64, :, :], A64["Pm"][0:64, :, :], PS[5][0:64, :].rearrange("p (n c) -> p n c", c=64), ALU.add,
                               ["aPm", PK[5]], ["aPm"])
                            cp("act", A64["Pmb"][0:64, :, :], A64["Pm"][0:64, :, :], ["aPm"], ["aPmb"])
                            yield
                            Q, QT, Qn, QTn = Qn, QTn, Q, QT
                        yield
                        for (d, c_, it) in items:
                            mm(PS[2 + d][0:64, (it % NB) * 128:(it % NB + 1) * 128], A64["Pmb"][0:64, it, :], A128["V"][0:64, it, :], True, True,
                               ["aPmb", "aV"], [PK[2 + d]])
                            mm(PS[4][:, it * 64:(it + 1) * 64], A128["Kg"][0:64, it, :], A64["Pmb"][0:64, it, :], True, True, ["aKg", "aPmb"], [PK[4]])
                        for d in range(2):
                            tt("dve", v2("u0", d), PS[2 + d][0:64, :].rearrange("p (n c) -> p n c", c=128),
                               gcols("beta", d).unsqueeze(2).to_broadcast([64, NB, 128]), ALU.mult, [PK[2 + d], "gates"], [KY("au0")])
                        act(A64["wT"][:, :, :], PS[4][:, :].rearrange("p (n c) -> p n c", c=64), AF.Identity, [PK[4]], [KY("awT")], scale=-1.0)

                    def stageC_gen(i, d):
                        b = i // NB
                        info = batch_info(b)
                        par = b % 2
                        A64, A128 = A64v[par], A128v[par]
                        KY = lambda s_: s_ + str(par)
                        c_ = FORD[i] if d == 0 else BORD[i]
                        it = d * NB + (c_ - info[d][0])
                        e8 = d * 4 + h
                        col = lambda nm: G_[nm][0:64, c_, e8:e8 + 1]
                        pC = PS[6 + d]
                        kC = PK[6 + d]
                        Sk = "S%d" % d
                        S_ = Sst[d]
                        ub = ubuf[(i % 2) * 2 + d]
                        uk = "u%d" % ((i % 2) * 2 + d)
                        mm(pC[0:64, 0:128], A64["wT"][:, it, :], S_[:, :], True, True, [KY("awT"), Sk], [kC])
                        yield
                        stt("dve", ub[0:64, :], pC[0:64, 0:128], col("beta"), A128["u0"][0:64, it, :], ALU.mult, ALU.add, [kC, "gates", KY("au0")], [uk])
                        yield
                        if c_ >= 4:
                            cl = c_ - 4
                            mm(pC[0:64, 128:256], A64["QgT"][:, it, :], S_[:, :], True, False, [KY("aQgT"), Sk], [kC])
                            mm(pC[0:64, 128:256], A64["QKT"][0:64, it, :], ub[0:64, :], False, True, [KY("aQKT"), uk], [kC])
                        mm(pC[:, 256:384], A128["K"][0:64, it, :], ub[0:64, :], True, True, [KY("aK"), uk], [kC])
                        yield
                        lastcol = 63 if d == 0 else 0
                        stt("dve", S_[:, :], S_[:, :], A64["gbc"][:, it, lastcol:lastcol + 1], pC[:, 256:384], ALU.mult, ALU.add,
                            [Sk, KY("agbc"), kC], [Sk])
                        yield
                        if c_ >= 4:
                            firstw = (d == 0 and cl < 16) or (d == 1 and cl >= 16)
                            ok = "O%d" % cl
                            if firstw:
                                cp("act", Oacc[0:64, cl, :], pC[0:64, 128:256], [kC], [ok])
                            else:
                                tt("dve", Oacc[0:64, cl, :], Oacc[0:64, cl, :], pC[0:64, 128:256], ALU.add, [kC, ok], [ok])
                        yield

                    def chain_d(d, steps):
                        for i in steps:
                            yield from stageC_gen(i, d)

                    def stageC_steps(steps):
                        alive = [chain_d(0, steps), chain_d(1, steps)]
                        while alive:
                            for g_ in list(alive):
                                try:
                                    next(g_)
                                except StopIteration:
                                    alive.remove(g_)

                    def chain_both(steps):
                        alive = [chain_d(0, steps), chain_d(1, steps)]
                        while alive:
                            for g_ in list(alive):
                                try:
                                    next(g_)
                                except StopIteration:
                                    alive.remove(g_)
                            yield

                    def run_all(g_):
                        for _ in g_:
                            pass
                    NBT = 36 // NB
                    run_all(stageA_batch(0))
                    for b in range(NBT):
                        gC = chain_both(list(range(b * NB, (b + 1) * NB)))
                        gA = stageA_batch(b + 1) if b + 1 < NBT else iter(())
                        doneA = doneC = False
                        while not (doneA and doneC):
                            for _ in range(3):
                                if not doneA:
                                    try:
                                        next(gA)
                                    except StopIteration:
                                        doneA = True
                            if not doneC:
                                try:
                                    next(gC)
                                except StopIteration:
                                    doneC = True
                    if s == 0 and h == 0:
                        dump("O0", Oacc[0:64, 0, :], ["O0"])
                        dump("O31", Oacc[0:64, 31, :], ["O31"])
                    wb = load_w(2560 + h * 128)
                    for blk in range(8):
                        par_ = blk % 2
                        okeys = ["O%d" % (blk * 4 + cc) for cc in range(4)]
                        O4 = Oacc[0:64, blk * 4:blk * 4 + 4, :]
                        sq4 = XTf[0:64, par_ * 512:(par_ + 1) * 512].rearrange("p (c v) -> p c v", v=128)
                        rs_ = rsb[0:64, 4 * blk:4 * blk + 4]
                        ksq, krs = "fsq%d" % par_, "rs%d" % blk
                        tt("pool", sq4, O4, O4, ALU.mult, okeys + ["XTf"], [ksq])
                        P.op("dve", (lambda sq4=sq4, rs_=rs_: lambda e: e.tensor_reduce(out=rs_, in_=sq4, axis=AX.X, op=ALU.add))(),
                             reads=[ksq, "XTf"], writes=[krs])
                        ts("dve", rs_, rs_, 1.0 / 128.0, ALU.mult, [krs], [krs], s2=1e-6, op1=ALU.add)
                        act(rs_, rs_, AF.Sqrt, [krs], [krs])
                        P.op("dve", (lambda rs_=rs_: lambda e: e.reciprocal(out=rs_, in_=rs_))(), reads=[krs], writes=[krs])
                    for blk in range(8):
                        par_ = blk % 2
                        okeys = ["O%d" % (blk * 4 + cc) for cc in range(4)]
                        O4 = Oacc[0:64, blk * 4:blk * 4 + 4, :]
                        z4 = XTf[0:64, 1024 + par_ * 512:1024 + (par_ + 1) * 512].rearrange("p (c v) -> p c v", v=128)
                        rs_ = rsb[0:64, 4 * blk:4 * blk + 4]
                        kz, krs = "fz%d" % par_, "rs%d" % blk
                        pz, pt = (1, 2) if par_ == 0 else (3, 4)
                        for cc in range(4):
                            tk = slice(LCTX + (blk * 4 + cc) * 64, LCTX + (blk * 4 + cc + 1) * 64)
                            for k in range(8):
                                mm(PS[pz][0:64, cc * 128:(cc + 1) * 128], hT[:, k, tk], wbf[wb][:, k, :], k == 0, k == 7, ["hT", "wbf%d" % wb], [PK[pz]])
                        act(z4, PS[pz][0:64, :].rearrange("p (c v) -> p c v", v=128), AF.Silu, [PK[pz], "XTf"], [kz])
                        tt("dve", O4, O4, rs_.unsqueeze(2).to_broadcast([64, 4, 128]), ALU.mult, okeys + [krs], okeys)
                        tt("pool", O4, O4, normw[0:64, :].unsqueeze(1).to_broadcast([64, 4, 128]), ALU.mult, okeys + ["normw"], okeys)
                        tt("dve", O4, O4, z4, ALU.mult, okeys + [kz, "XTf"], okeys)
                        for cc in range(4):
                            tr(PS[pt][:, cc * 64:(cc + 1) * 64], Oacc[0:64, blk * 4 + cc, :], ident[0:64, 0:64], okeys + ["ident"], [PK[pt]])
                        cp("act", mixT[:, 4 + h, blk * 256:(blk + 1) * 256], PS[pt][:, 0:256], [PK[pt]], ["mixG"])
                    if s == 0 and h == 0:
                        dump("mix_g", mixT[:, 4, 0:512], ["mixG"])

            if "out" in stages:
                barrier()
                o = [0]

                def carve(n, dt=F32):
                    words = n if dt == F32 else (n + 1) // 2
                    a = arena[:, o[0]:o[0] + words]
                    o[0] += words
                    assert o[0] <= ARENA_WORDS, o[0]
                    return a.bitcast(BF16)[:, 0:n] if dt == BF16 else a
                woutb = carve(8192, BF16).rearrange("p (k c) -> p k c", k=8)
                gate_bc = carve(1024)
                lng = carve(1024)
                lnb = carve(1024)
                wg = carve(4096).rearrange("p (k c) -> p k c", k=8)
                silucb = carve(1024).rearrange("p (k c) -> p k c", k=8)
                rbuf = [carve(1024) for _ in range(2)]
                stats = carve(16)
                dma(lng, D["lng"][:, :], w=["lng"])
                dma(lnb, D["lnb"][:, :], w=["lnb"])
                dma(gate_bc, D["bgate_bc"][:, :], w=["gate_bc"])
                wout_v = D["w_out"].rearrange("(k p) c -> p k c", p=128)
                for k in range(8):
                    dma(xst[k % 2][:], wout_v[:, k, :], w=["xst%d" % (k % 2)])
                    cp("pool", woutb[:, k, :], xst[k % 2][:], ["xst%d" % (k % 2)], ["woutb"])
                for k in range(8):
                    ts("dve", silucb[:, k, :], ones[:, :], siluc[:, k, s:s + 1], ALU.mult, ["ones", "siluc"], ["silucb"])
                for half in range(2):
                    dma(wg[:], wada[:, :, 2048 + half * 512:2048 + (half + 1) * 512], w=["wg"])
                    for k in range(8):
                        mm(PS[3][:, :], silucb[:, k, :], wg[:, k, :], k == 0, k == 7, ["silucb", "wg"], [PK[3]])
                    tt("dve", gate_bc[:, half * 512:(half + 1) * 512], gate_bc[:, half * 512:(half + 1) * 512], PS[3][:, :], ALU.add,
                       ["gate_bc", PK[3]], ["gate_bc"])
                if s == 0:
                    dump("gate", gate_bc[:, 0:512], ["gate_bc"])
                for ti in range(16):
                    b = ti % 2
                    rk = "r%d" % b
                    dma(xst[b][:], D["x2"][s, ti * 128:(ti + 1) * 128, :], w=["xst%d" % b])
                    for half in range(2):
                        pk = 4 + half + 2 * b
                        for k in range(8):
                            mm(PS[pk][:, :], mixT[:, k, ti * 128:(ti + 1) * 128], woutb[:, k, half * 512:(half + 1) * 512], k == 0, k == 7,
                               ["mixT", "woutb"], [PK[pk]])
                        tt("dve", rbuf[b][:, half * 512:(half + 1) * 512], PS[pk][:, :], gate_bc[:, half * 512:(half + 1) * 512], ALU.mult,
                           [PK[pk], "gate_bc"], [rk])
                    stt("pool", rbuf[b][:, :], xst[b][:, :], DEEP_ALPHA, rbuf[b][:, :], ALU.mult, ALU.add, ["xst%d" % b, rk], [rk])
                    st6 = stats[:, 0:12].rearrange("p (c e) -> p c e", e=6)
                    for half in range(2):
                        P.op("dve", (lambda b=b, half=half, st6=st6: lambda e: e.bn_stats(out=st6[:, half, :], in_=rbuf[b][:, half * 512:(half + 1) * 512]))(),
                             reads=[rk], writes=["stats"])
                    P.op("dve", (lambda st6=st6: lambda e: e.bn_aggr(out=stats[:, 12:14], in_=st6))(), reads=["stats"], writes=["mv"])
                    act(stats[:, 14:15], stats[:, 13:14], AF.Sqrt, ["mv"], ["mv2"], bias=1e-5)
                    P.op("dve", lambda e: e.reciprocal(out=stats[:, 14:15], in_=stats[:, 14:15]), reads=["mv2"], writes=["mv2"])
                    stt("dve", stats[:, 15:16], stats[:, 12:13], -1.0, stats[:, 14:15], ALU.mult, ALU.mult, ["mv", "mv2"], ["mv2"])
                    act(rbuf[b][:, :], rbuf[b][:, :], AF.Identity, [rk, "mv2"], [rk], bias=stats[:, 15:16], scale=stats[:, 14:15])
                    tt("pool", rbuf[b][:, :], rbuf[b][:, :], lng, ALU.mult, [rk, "lng"], [rk])
                    tt("dve", rbuf[b][:, :], rbuf[b][:, :], lnb, ALU.add, [rk, "lnb"], [rk])
                    dma(yout[s, ti * 128:(ti + 1) * 128, :], rbuf[b][:, :], r=[rk])
        P.emit(nc)
    return nc, P


def _core_inputs(inp, sh, core):
    m = dict(sh)
    f = lambda a: np.ascontiguousarray(np.asarray(a, dtype=np.float32))
    b0 = core * NSEQ
    m["x2"] = f(inp["x"][b0:b0 + NSEQ])
    m["ctx2"] = f(inp["ctx"][b0:b0 + NSEQ])
    cc = np.stack([np.asarray(inp["c"][b0], np.float32), np.asarray(inp["c"][b0 + 1], np.float32),
                   np.asarray(inp["c_ctx"], np.float32)], axis=0)
    m["cT"] = f(cc.reshape(3, 8, 128).transpose(2, 1, 0))
    return m


_CACHE = {}


def kernel(**inputs):
    if "nc" not in _CACHE:
        _CACHE["nc"] = build_program()[0]
    nc = _CACHE["nc"]
    sh = _prep_shared(inputs)
    maps = [_core_inputs(inputs, sh, c) for c in range(8)]
    res = run_bass_kernel_spmd(nc, maps, core_ids=list(range(8)))
    out = np.concatenate([r["yout"] for r in res.results], axis=0)
    return out.astype(np.float32)
```
